# Optimizing a Trainium2 kernel written in Bass

```python
import math
import jax, jax.numpy as jnp
from jax import lax
import numpy as np

D_MODEL = 2048
BATCH = 8
SEQ = 2048
DEPTH = 2
DEC_BATCH = 128
DEC_SEQ = 4
PAST_LEN = 8192
PAGE_SIZE = 128

N_MIXERS = 4
D_MIX = D_MODEL
D_GROUP = D_MIX // N_MIXERS
POOL_WINDOWS = (2, 4, 8, 16)
POOL_GROUPS = len(POOL_WINDOWS)
POOL_CH = D_GROUP // POOL_GROUPS
POOL_PAD = max(POOL_WINDOWS) - 1
CONV_WIDTH = 3
SWA_WINDOW = 128
SWA_BLOCK = SWA_WINDOW
SWA_HEAD_DIM = 64
SWA_Q_HEADS = D_GROUP // SWA_HEAD_DIM
SWA_KV_HEADS = 2
SWA_GROUP = SWA_Q_HEADS // SWA_KV_HEADS
SWA_KV_DIM = SWA_KV_HEADS * SWA_HEAD_DIM
SWA_SCALE = 1.0 / math.sqrt(SWA_HEAD_DIM)
MEM_TOKENS = 256
MEM_HEADS = 4
MEM_HEAD_DIM = D_GROUP // MEM_HEADS
MEM_SCALE = 1.0 / math.sqrt(MEM_HEAD_DIM)
D_FF = 4 * D_MODEL
RMS_EPS = 1e-6
SPLIT_OFFSETS = (D_GROUP, 2 * D_GROUP, 3 * D_GROUP, 4 * D_GROUP, 5 * D_GROUP,
                 5 * D_GROUP + SWA_KV_DIM, 5 * D_GROUP + 2 * SWA_KV_DIM)
D_IN = 6 * D_GROUP + 2 * SWA_KV_DIM

kernel_name = "hymba_pool_conv_swa_memory_decoder_step"


def rmsnorm(x, g):
    x32 = x.astype(jnp.float32)
    y = x32 * lax.rsqrt(jnp.mean(x32 * x32, axis=-1, keepdims=True) + RMS_EPS)
    return (y * g.astype(jnp.float32)).astype(x.dtype)


def multiscale_pool(u_ext, start_pos, t_new):
    b = u_ext.shape[0]
    cs = jnp.cumsum(u_ext.astype(jnp.float32), axis=1)
    cs = jnp.concatenate([jnp.zeros((b, 1, D_GROUP), jnp.float32), cs], axis=1)
    hi = cs[:, POOL_PAD + 1:]
    pos = start_pos + jnp.arange(t_new)
    outs = []
    for gi, w in enumerate(POOL_WINDOWS):
        sl = slice(gi * POOL_CH, (gi + 1) * POOL_CH)
        lo = cs[:, POOL_PAD + 1 - w:POOL_PAD + 1 - w + t_new, sl]
        cnt = jnp.minimum(pos + 1, w).astype(jnp.float32)
        outs.append((hi[:, :, sl] - lo) / cnt[None, :, None])
    return jnp.concatenate(outs, axis=-1).astype(u_ext.dtype)


def pool_mixer(u, u_prev, start_pos, w_pool, pool_scale):
    b, t, _ = u.shape
    u_ext = jnp.concatenate([u_prev, u], axis=1)
    d = (multiscale_pool(u_ext, start_pos, t) - u).reshape(b, t, POOL_GROUPS, POOL_CH)
    y = jnp.einsum('btgc,gcd->btgd', d, w_pool).reshape(b, t, D_GROUP)
    return y * pool_scale, u_ext[:, -POOL_PAD:]


def conv_mixer(hc, gate_b, gate_c, v_prev, conv_w):
    t = hc.shape[1]
    v_ext = jnp.concatenate([v_prev, gate_c * hc], axis=1)
    conv = conv_w[0] * v_ext[:, 0:t]
    for k in range(1, CONV_WIDTH):
        conv = conv + conv_w[k] * v_ext[:, k:k + t]
    return gate_b * conv, v_ext[:, -(CONV_WIDTH - 1):]


def sink_softmax(scores, mask, sink):
    s = jnp.where(mask, scores, -jnp.inf)
    m = jnp.maximum(jnp.max(s, axis=-1, keepdims=True), sink)
    e = jnp.exp(s - m)
    return e / (jnp.sum(e, axis=-1, keepdims=True) + jnp.exp(sink - m))


def swa_banded(q, k, v, sinks):
    b, t = q.shape[0], q.shape[1]
    nb = t // SWA_BLOCK
    qb = q.reshape(b, nb, SWA_BLOCK, SWA_KV_HEADS, SWA_GROUP, SWA_HEAD_DIM)
    def band(a):
        ap = jnp.concatenate([jnp.zeros_like(a[:, :SWA_BLOCK]), a], axis=1)
        ap = ap.reshape(b, nb + 1, SWA_BLOCK, SWA_KV_HEADS, SWA_HEAD_DIM)
        return jnp.concatenate([ap[:, :-1], ap[:, 1:]], axis=2)
    kb, vb = band(k), band(v)
    i = jnp.arange(SWA_BLOCK)[None, :, None]
    j = jnp.arange(2 * SWA_BLOCK)[None, None, :]
    n = jnp.arange(nb)[:, None, None]
    diff = i + SWA_BLOCK - j
    kpos = (n - 1) * SWA_BLOCK + j
    mask = (diff >= 0) & (diff < SWA_WINDOW) & (kpos >= 0)
    scores = jnp.einsum('bnqhgd,bnkhd->bnhgqk', qb, kb,
                        preferred_element_type=jnp.float32) * SWA_SCALE
    sink = sinks.astype(jnp.float32).reshape(SWA_KV_HEADS, SWA_GROUP)[None, None, :, :, None, None]
    p = sink_softmax(scores, mask[None, :, None, None], sink).astype(v.dtype)
    o = jnp.einsum('bnhgqk,bnkhd->bnqhgd', p, vb)
    return o.reshape(b, t, D_GROUP), k[:, -SWA_WINDOW:], v[:, -SWA_WINDOW:]


def swa_buffered(q, k, v, k_prev, v_prev, sinks):
    b, t = q.shape[0], q.shape[1]
    k_all = jnp.concatenate([k_prev, k], axis=1)
    v_all = jnp.concatenate([v_prev, v], axis=1)
    i = jnp.arange(t)[:, None]
    j = jnp.arange(SWA_WINDOW + t)[None, :]
    diff = i + SWA_WINDOW - j
    mask = (diff >= 0) & (diff < SWA_WINDOW)
    scores = jnp.einsum('bqhgd,bkhd->bhgqk', q, k_all,
                        preferred_element_type=jnp.float32) * SWA_SCALE
    sink = sinks.astype(jnp.float32).reshape(SWA_KV_HEADS, SWA_GROUP)[None, :, :, None, None]
    p = sink_softmax(scores, mask[None, None, None], sink).astype(v.dtype)
    o = jnp.einsum('bhgqk,bkhd->bqhgd', p, v_all)
    return o.reshape(b, t, D_GROUP), k_all[:, t:], v_all[:, t:]


def memory_kv(mem, g_mem, w_mem_kv):
    b, m, _ = mem.shape
    kv = rmsnorm(mem, g_mem) @ w_mem_kv
    k, v = jnp.split(kv, 2, axis=-1)
    return (k.reshape(b, m, MEM_HEADS, MEM_HEAD_DIM), v.reshape(b, m, MEM_HEADS, MEM_HEAD_DIM))


def memory_attention(q, mem_k, mem_v):
    b, t = q.shape[0], q.shape[1]
    s = jnp.einsum('bqhd,bkhd->bhqk', q, mem_k, preferred_element_type=jnp.float32) * MEM_SCALE
    p = jax.nn.softmax(s, axis=-1).astype(mem_v.dtype)
    return jnp.einsum('bhqk,bkhd->bqhd', p, mem_v).reshape(b, t, D_GROUP)


def trunk_layer(x, start_pos, pool_prev, conv_prev, swa_k_prev, swa_v_prev, mem_k, mem_v,
                g_mix_pre, w_in, w_pool, pool_scale, conv_w, swa_sinks, w_out, g_mix_post,
                g_mlp_pre, w_up, w_down, g_mlp_post):
    b, t, _ = x.shape
    h = rmsnorm(x, g_mix_pre)
    proj = h @ w_in
    u, hc, gb, gc, q, k, v, qm = jnp.split(proj, SPLIT_OFFSETS, axis=-1)
    y_pool, new_pool = pool_mixer(u, pool_prev, start_pos, w_pool, pool_scale)
    y_conv, new_conv = conv_mixer(hc, gb, gc, conv_prev, conv_w)
    q = q.reshape(b, t, SWA_KV_HEADS, SWA_GROUP, SWA_HEAD_DIM)
    k = k.reshape(b, t, SWA_KV_HEADS, SWA_HEAD_DIM)
    v = v.reshape(b, t, SWA_KV_HEADS, SWA_HEAD_DIM)
    if swa_k_prev is None:
        y_swa, new_k, new_v = swa_banded(q, k, v, swa_sinks)
    else:
        y_swa, new_k, new_v = swa_buffered(q, k, v, swa_k_prev, swa_v_prev, swa_sinks)
    y_mem = memory_attention(qm.reshape(b, t, MEM_HEADS, MEM_HEAD_DIM), mem_k, mem_v)
    mix = jnp.concatenate([y_pool, y_conv, y_swa, y_mem], axis=-1) @ w_out
    x = x + rmsnorm(mix, g_mix_post)
    h2 = rmsnorm(x, g_mlp_pre)
    ff = jnp.square(jax.nn.relu(h2 @ w_up)) @ w_down
    x = x + rmsnorm(ff, g_mlp_post)
    return x, new_pool, new_conv, new_k, new_v


def setup_inputs(seed: int = 0) -> dict:
    key = jax.random.key(seed)
    ks = jax.random.split(key, 24)
    f32 = jnp.float32
    def nrm(k, shape, scale=1.0):
        return jax.random.normal(k, shape, f32) * scale
    def gain(k, shape):
        return 1.0 + 0.05 * jax.random.normal(k, shape, f32)
    return {
        "x_prompt": nrm(ks[0], (BATCH, SEQ, D_MODEL)),
        "x_sample": nrm(ks[1], (DEC_BATCH, DEC_SEQ, D_MODEL)),
        "mem_prompt": nrm(ks[2], (BATCH, MEM_TOKENS, D_MODEL)),
        "state_pool": nrm(ks[3], (DEPTH, DEC_BATCH, POOL_PAD, D_GROUP)),
        "state_conv": nrm(ks[4], (DEPTH, DEC_BATCH, CONV_WIDTH - 1, D_GROUP)),
        "cache_swa_k": nrm(ks[5], (DEPTH, DEC_BATCH, SWA_WINDOW, SWA_KV_HEADS, SWA_HEAD_DIM)),
        "cache_swa_v": nrm(ks[6], (DEPTH, DEC_BATCH, SWA_WINDOW, SWA_KV_HEADS, SWA_HEAD_DIM)),
        "cache_mem_k": nrm(ks[7], (DEPTH, DEC_BATCH, MEM_TOKENS, MEM_HEADS, MEM_HEAD_DIM)),
        "cache_mem_v": nrm(ks[8], (DEPTH, DEC_BATCH, MEM_TOKENS, MEM_HEADS, MEM_HEAD_DIM)),
        "g_mix_pre": gain(ks[9], (DEPTH, D_MODEL)),
        "w_in": nrm(ks[10], (DEPTH, D_MODEL, D_IN), D_MODEL ** -0.5),
        "w_pool": nrm(ks[11], (DEPTH, POOL_GROUPS, POOL_CH, POOL_CH), POOL_CH ** -0.5),
        "pool_scale": gain(ks[12], (DEPTH, D_GROUP)),
        "conv_w": nrm(ks[13], (DEPTH, CONV_WIDTH, D_GROUP), CONV_WIDTH ** -0.5),
        "swa_sinks": nrm(ks[14], (DEPTH, SWA_Q_HEADS), 0.5),
        "g_mem": gain(ks[15], (DEPTH, D_MODEL)),
        "w_mem_kv": nrm(ks[16], (DEPTH, D_MODEL, 2 * D_GROUP), D_MODEL ** -0.5),
        "w_out": nrm(ks[17], (DEPTH, D_MIX, D_MODEL), D_MIX ** -0.5),
        "g_mix_post": gain(ks[18], (DEPTH, D_MODEL)),
        "g_mlp_pre": gain(ks[19], (DEPTH, D_MODEL)),
        "w_up": nrm(ks[20], (DEPTH, D_MODEL, D_FF), D_MODEL ** -0.5),
        "w_down": nrm(ks[21], (DEPTH, D_FF, D_MODEL), D_FF ** -0.5),
        "g_mlp_post": gain(ks[22], (DEPTH, D_MODEL)),
    }


def reference(x_prompt, x_sample, mem_prompt, state_pool, state_conv, cache_swa_k, cache_swa_v,
              cache_mem_k, cache_mem_v, g_mix_pre, w_in, w_pool, pool_scale, conv_w, swa_sinks,
              g_mem, w_mem_kv, w_out, g_mix_post, g_mlp_pre, w_up, w_down, g_mlp_post):
    yp, ys = x_prompt, x_sample
    bp = x_prompt.shape[0]
    pool_p, pool_s, conv_p, conv_s = [], [], [], []
    kp_l, ks_l, vp_l, vs_l, mk_l, mv_l = [], [], [], [], [], []
    for l in range(DEPTH):
        weights = (g_mix_pre[l], w_in[l], w_pool[l], pool_scale[l], conv_w[l], swa_sinks[l],
                   w_out[l], g_mix_post[l], g_mlp_pre[l], w_up[l], w_down[l], g_mlp_post[l])
        mk, mv = memory_kv(mem_prompt, g_mem[l], w_mem_kv[l])
        yp, pp, cp, kp, vp = trunk_layer(
            yp, 0,
            jnp.zeros((bp, POOL_PAD, D_GROUP), x_prompt.dtype),
            jnp.zeros((bp, CONV_WIDTH - 1, D_GROUP), x_prompt.dtype),
            None, None, mk, mv, *weights)
        ys, ps, cs, kss, vss = trunk_layer(
            ys, PAST_LEN, state_pool[l], state_conv[l], cache_swa_k[l], cache_swa_v[l],
            cache_mem_k[l], cache_mem_v[l], *weights)
        pool_p.append(pp); pool_s.append(ps); conv_p.append(cp); conv_s.append(cs)
        kp_l.append(kp); ks_l.append(kss); vp_l.append(vp); vs_l.append(vss)
        mk_l.append(mk); mv_l.append(mv)
    return (yp, ys,
            jnp.stack(pool_p), jnp.stack(pool_s),
            jnp.stack(conv_p), jnp.stack(conv_s),
            jnp.stack(kp_l), jnp.stack(ks_l),
            jnp.stack(vp_l), jnp.stack(vs_l),
            jnp.stack(mk_l), jnp.stack(mv_l))
```

```python
import contextlib
import math
from types import SimpleNamespace
import numpy as np
import concourse.bass as bass
import concourse.mybir as mybir
from concourse.bass_utils import run_bass_kernel_spmd

F32 = mybir.dt.float32
BF16 = mybir.dt.bfloat16
ALU = mybir.AluOpType
AF = mybir.ActivationFunctionType

NCORES = 8
D = 2048
DEPTH = 2
SEQ = 2048
TP = 512
NPT = SEQ // TP
SB = 16
ST = 4
TS = SB * ST
MEMT = 256
DG = 512
D_IN = 3328
DFF = 8192
EPS = 1e-6
SWA_SCALE = 1.0 / 8.0
MEM_SCALE = 1.0 / math.sqrt(128.0)
NEG = -30000.0

ENGS = ("pe", "act", "dve", "pool", "sp")
SEM_LIMIT = 24000
NDMASEM = 12


class Op:
    __slots__ = ("eng", "fn", "deps", "dma", "inc", "cnt", "dsem", "dval", "prewait")

    def __init__(self, eng, fn, deps, dma):
        self.eng = eng
        self.fn = fn
        self.deps = deps
        self.dma = dma
        self.inc = False
        self.cnt = 0
        self.dsem = None
        self.dval = 0
        self.prewait = None


class Prog:
    def __init__(self, nc):
        self.nc = nc
        self.ops = {e: [] for e in ENGS}
        self.lw = {}
        self.rd = {}
        self.ndma = {e: 0 for e in ENGS}

    def add(self, eng, fn, reads=(), writes=(), dma=False, banks=()):
        idx = len(self.ops[eng])
        deps = set()
        for b in banks:
            k = ("__bank", b)
            w = self.lw.get(k)
            if w is not None and w[0] != eng:
                deps.add(w)
            self.lw[k] = (eng, idx)
        for k in reads:
            w = self.lw.get(k)
            if w is not None:
                deps.add(w)
        for k in writes:
            w = self.lw.get(k)
            if w is not None:
                deps.add(w)
            for r in self.rd.get(k, ()):
                deps.add(r)
        me = (eng, idx)
        deps.discard(me)
        if eng == "pe":
            deps = {d for d in deps if d[0] != "pe"}
        op = Op(eng, fn, deps, dma)
        if dma:
            n = self.ndma[eng]
            self.ndma[eng] = n + 1
            op.dsem = (eng, n % NDMASEM)
            op.dval = 16 * (n // NDMASEM + 1)
            if n >= NDMASEM:
                op.prewait = (op.dsem, op.dval - 16)
        self.ops[eng].append(op)
        for k in reads:
            self.rd.setdefault(k, []).append(me)
        for k in writes:
            self.lw[k] = me
            self.rd[k] = []
        return me

    def pe(self, fn, reads=(), writes=(), banks=()):
        return self.add("pe", fn, reads, writes, banks=banks)

    def act(self, fn, reads=(), writes=(), banks=()):
        return self.add("act", fn, reads, writes, banks=banks)

    def dve(self, fn, reads=(), writes=(), banks=()):
        return self.add("dve", fn, reads, writes, banks=banks)

    def pool(self, fn, reads=(), writes=(), banks=()):
        return self.add("pool", fn, reads, writes, banks=banks)

    def dma(self, fn, reads=(), writes=(), eng="sp"):
        return self.add(eng, fn, reads, writes, dma=True)

    def emit(self):
        nc = self.nc
        for e in ENGS:
            for op in self.ops[e]:
                for (de, di) in op.deps:
                    d = self.ops[de][di]
                    if not d.dma:
                        d.inc = True
        nsem = {}
        for e in ENGS:
            c = 0
            for op in self.ops[e]:
                if op.inc and not op.dma:
                    c += 1
                op.cnt = c
            nsem[e] = (c // SEM_LIMIT) + 1
        with contextlib.ExitStack() as st:
            csem = {e: [st.enter_context(nc.semaphore(f"c_{e}_{i}")) for i in range(nsem[e])]
                    for e in ENGS}
            dsem = {}
            for e in ENGS:
                for i in range(min(NDMASEM, self.ndma[e])):
                    dsem[(e, i)] = st.enter_context(nc.semaphore(f"d_{e}_{i}"))
            block = st.enter_context(nc.Block())
            engobj = {"pe": "tensor", "act": "scalar", "dve": "vector", "pool": "gpsimd", "sp": "sync"}

            def body(e):
                def run(eng):
                    waited = {}

                    def do_wait(key, sem, val):
                        if waited.get(key, 0) >= val:
                            return
                        eng.wait_ge(sem, val)
                        waited[key] = val

                    for op in self.ops[e]:
                        for (de, di) in sorted(op.deps):
                            d = self.ops[de][di]
                            if d.dma:
                                do_wait(("d",) + d.dsem, dsem[d.dsem], d.dval)
                            else:
                                si, v = divmod(d.cnt - 1, SEM_LIMIT)
                                do_wait(("c", de, si), csem[de][si], v + 1)
                        if op.prewait is not None:
                            do_wait(("d",) + op.prewait[0], dsem[op.prewait[0]], op.prewait[1])
                        ins = op.fn(eng)
                        if op.dma:
                            ins.then_inc(dsem[op.dsem], 16)
                        elif op.inc:
                            ins.then_inc(csem[e][(op.cnt - 1) // SEM_LIMIT], 1)
                    for (de, i), s in dsem.items():
                        if de == e:
                            n = self.ndma[e]
                            uses = (n - i + NDMASEM - 1) // NDMASEM
                            if uses > 0:
                                do_wait(("d", de, i), s, 16 * uses)
                return run

            for e in ENGS:
                if self.ops[e]:
                    getattr(block, engobj[e])(body(e))


class Region:
    def __init__(self, buf, name, off, C, W, dt):
        esz = 4 if dt == F32 else 2
        nb = C * W * esz
        assert off % 4 == 0
        sl = buf[:, off // 2:(off + nb) // 2]
        if dt == F32:
            sl = sl.bitcast(F32)
        self.ap = sl.rearrange("p (c w) -> p c w", c=C)
        self.name = name
        self.off = off
        self.cb = W * esz
        self.C = C
        self.end = off + nb

    def keys(self, c0=0, c1=None):
        if c1 is None:
            c1 = self.C
        lo = self.off + c0 * self.cb
        hi = self.off + c1 * self.cb
        return [(self.name, b) for b in range(lo // 1024, (hi - 1) // 1024 + 1)]


WIN_NSLAB = 7
SLAB_ELEMS = 8192


DBG = {"mem": True, "ntiles": NPT, "nlayers": DEPTH, "mixers": True, "ffn": True}


def build_program(with_sample=True):
    nc = bass.Bass("TRN2", target_bir_lowering=False)

    def din(name, shape, dt=F32):
        return nc.dram_tensor(name, list(shape), dt, kind="ExternalInput").ap()

    def dout(name, shape, dt=F32):
        return nc.dram_tensor(name, list(shape), dt, kind="ExternalOutput").ap()

    xp = din("xp", [SEQ, D])
    xs = din("xs", [TS, D])
    mem = din("mem", [MEMT, D])
    spool = din("spool", [DEPTH, SB, 15, DG])
    sconv = din("sconv", [DEPTH, SB, 2, DG])
    ck = din("ck", [DEPTH, SB, 128, 128])
    cv = din("cv", [DEPTH, SB, 128, 128])
    cmk = din("cmk", [DEPTH, SB, MEMT, DG])
    cmv = din("cmv", [DEPTH, SB, MEMT, DG])
    g_mix_pre = din("g_mix_pre", [DEPTH, D])
    w_in = din("w_in", [DEPTH, D, D_IN])
    w_pool = din("w_pool", [DEPTH, 4, 128, 128])
    pool_scale = din("pool_scale", [DEPTH, DG])
    conv_w = din("conv_w", [DEPTH, 3, DG])
    swa_sinks = din("swa_sinks", [DEPTH, 8])
    g_mem = din("g_mem", [DEPTH, D])
    w_mem_kv = din("w_mem_kv", [DEPTH, D, 2 * DG])
    w_out = din("w_out", [DEPTH, D, D])
    g_mix_post = din("g_mix_post", [DEPTH, D])
    g_mlp_pre = din("g_mlp_pre", [DEPTH, D])
    w_up = din("w_up", [DEPTH, D, DFF])
    w_down = din("w_down", [DEPTH, DFF, D])
    g_mlp_post = din("g_mlp_post", [DEPTH, D])

    yp = dout("yp", [SEQ, D])
    ys = dout("ys", [TS, D])
    o_pool_p = dout("o_pool_p", [DEPTH, 15, DG])
    o_pool_s = dout("o_pool_s", [DEPTH, SB, 15, DG])
    o_conv_p = dout("o_conv_p", [DEPTH, 2, DG])
    o_conv_s = dout("o_conv_s", [DEPTH, SB, 2, DG])
    o_k_p = dout("o_k_p", [DEPTH, 128, 128])
    o_k_s = dout("o_k_s", [DEPTH, SB, 128, 128])
    o_v_p = dout("o_v_p", [DEPTH, 128, 128])
    o_v_s = dout("o_v_s", [DEPTH, SB, 128, 128])
    o_mk_p = dout("o_mk_p", [DEPTH, MEMT, DG])
    o_mv_p = dout("o_mv_p", [DEPTH, MEMT, DG])

    def scratch(name, nslab):
        return nc.dram_tensor(name, [DEPTH, nslab, 128, SLAB_ELEMS], BF16, kind="Internal").ap()

    s_in = scratch("s_in", WIN_NSLAB)
    s_mem = scratch("s_mem", 2)
    s_out = scratch("s_out", 4)
    s_up = scratch("s_up", 16)
    s_down = scratch("s_down", 16)

    P = Prog(nc)
    st = contextlib.ExitStack()
    with st:
        def sb(name, shape, dt):
            return st.enter_context(nc.sbuf_tensor(name, list(shape), dt))

        xT = sb("xT", [128, 16, TP], F32)
        U1 = sb("U1", [128, 16384], BF16)
        BIG = sb("BIG", [128, 32768], BF16)
        NWB = 2
        wbuf = [sb(f"wbuf{i}", [128, SLAB_ELEMS], BF16) for i in range(NWB)]
        ident = sb("ident", [128, 128], F32)
        identb = sb("identb", [128, 128], BF16)
        onesD = sb("onesD", [128, 128], BF16)
        ones1 = sb("ones1", [128, 128], BF16)
        mask_cat = sb("mask_cat", [128, 512], BF16)
        mask_first = sb("mask_first", [128, 512], BF16)
        mask_sc = sb("mask_sc", [128, 256], BF16)
        mask_sn = sb("mask_sn", [128, 256], BF16)
        esink_s = sb("esink_s", [1, DEPTH, 2, 256], BF16)
        gv = sb("gv", [128, 5, DEPTH, 16], F32)
        pscale = sb("pscale", [128, DEPTH, 4], F32)
        convw = sb("convw", [128, DEPTH, 3, 4], F32)
        wpool = sb("wpool", [128, DEPTH, 4, 128], BF16)
        sinks_sb = sb("sinks_sb", [1, DEPTH * 8], F32)
        esink = sb("esink", [1, DEPTH, 2, 512], BF16)
        invcnt = sb("invcnt", [128, 4, 16], F32)
        rtm = sb("rtm", [128, 8], F32)
        Fs = [sb(f"F{i}", [128, 16 + TP], F32) for i in range(6)]
        Bt = [sb(f"Bt{i}", [128, TP], BF16) for i in range(4)]
        dTp = [sb(f"dTp{i}", [128, TP], BF16) for i in range(2)]
        Vd = sb("Vd", [128, 5, 2, 128], BF16)
        carry_u = sb("carry_u", [128, DEPTH, 4, 16], F32)
        carry_v = sb("carry_v", [128, DEPTH, 4, 2], F32)
        carry_k = sb("carry_k", [128, DEPTH, 2, 128], BF16)
        carry_V = sb("carry_V", [128, DEPTH, 2, 128], BF16)
        mkT = sb("mkT", [128, DEPTH, 4, MEMT], BF16)
        mvv = sb("mvv", [128, DEPTH, 2, DG], BF16)

        ps = [st.enter_context(nc.psum_tensor(f"ps{i}", [128, 512], F32)) for i in range(8)]

        rstd = Fs[0]
        cacc = [Fs[1], Fs[2]]
        rtmp = [Fs[1], Fs[2]]
        rden = [Fs[0], Fs[5]]
        RDK = [[("F", 0)], [("F", 5)]]
        ptmp = [Fs[3], Fs[4]]
        tmst = [Fs[3], Fs[4]]
        hcst = Fs[5]
        mtmp = Fs[5]
        pfix = Fs[5]
        epsq = Fs[5]
        sq = Bt
        Pt = Bt
        dT = [Bt[0], Bt[1]]
        KF = lambda i: [("F", i)]
        KB = lambda i: [("Bt", i)]

        xstage = Region(U1, "U1", 0, 4, D, F32)
        hT = Region(U1, "U1", 0, 16, TP, BF16)
        mixT = Region(U1, "U1", 0, 16, TP, F32)
        off = 0
        u_ext = Region(BIG, "BIG", off, 4, 16 + TP, F32); off = u_ext.end
        hcR = Region(BIG, "BIG", off, 4, TP, F32); off = hcR.end
        gbR = Region(BIG, "BIG", off, 4, TP, F32); off = gbR.end
        v_ext = Region(BIG, "BIG", off, 4, 2 + TP, F32); off = v_ext.end
        qR = Region(BIG, "BIG", off, 4, TP, BF16); off = qR.end
        kext = Region(BIG, "BIG", off, 2, 128 + TP, BF16); off = kext.end
        qmR = Region(BIG, "BIG", off, 4, TP, BF16); off = qmR.end
        yT = Region(BIG, "BIG", off, 16, TP, BF16); off = yT.end
        assert off <= 65536, off
        hidT = Region(BIG, "BIG", 0, 64, TP, BF16)
        RP = SimpleNamespace(hT=hT, mixT=mixT, yT=yT, hidT=hidT)

        state = {"bank": 0, "alt": 0, "use_pool": False}

        def alt():
            state["alt"] ^= 1
            return "act" if state["alt"] else "dve"

        def evac_copy(eng, out_ap, in_ap, reads, writes, banks=()):
            if eng == "act":
                P.act(lambda e: e.activation(out=out_ap, in_=in_ap, func=AF.Copy), reads, writes, banks)
            else:
                P.dve(lambda e: e.tensor_copy(out=out_ap, in_=in_ap), reads, writes, banks)

        def psk(b):
            return [("ps", b)]

        P.pool(lambda e: e.memset(ident[:], 1.0), writes=["ident"])
        P.pool(lambda e: e.affine_select(out=ident[:], in_=ident[:], pattern=[[-1, 128]],
                                         compare_op=ALU.is_equal, fill=0.0, base=0, channel_multiplier=1),
               reads=["ident"], writes=["ident"])
        P.dve(lambda e: e.tensor_copy(out=identb[:], in_=ident[:]), reads=["ident"], writes=["identb"])
        P.pool(lambda e: e.memset(onesD[:], 1.0 / D), writes=["onesD"])
        P.pool(lambda e: e.memset(ones1[:], 1.0), writes=["ones1"])
        P.pool(lambda e: e.memset(mtmp[:, 0:512], 0.0), writes=[("F", 5)])
        P.pool(lambda e: e.affine_select(out=mtmp[:, 0:256].rearrange("p (a q) -> p a q", a=2),
                                         in_=mtmp[:, 0:256].rearrange("p (a q) -> p a q", a=2),
                                         pattern=[[0, 2], [1, 128]], compare_op=ALU.is_ge, fill=NEG,
                                         base=0, channel_multiplier=-1), reads=[("F", 5)], writes=[("F", 5)])
        P.pool(lambda e: e.affine_select(out=mtmp[:, 256:512].rearrange("p (a q) -> p a q", a=2),
                                         in_=mtmp[:, 256:512].rearrange("p (a q) -> p a q", a=2),
                                         pattern=[[0, 2], [-1, 128]], compare_op=ALU.is_ge, fill=NEG,
                                         base=-1, channel_multiplier=1), reads=[("F", 5)], writes=[("F", 5)])
        P.dve(lambda e: e.tensor_copy(out=mask_cat[:], in_=mtmp[:, 0:512]), reads=[("F", 5)], writes=["mask_cat"])
        P.dve(lambda e: e.tensor_copy(out=mask_first[:, 0:256], in_=mtmp[:, 0:256]), reads=[("F", 5)], writes=["mask_first"])
        P.pool(lambda e: e.memset(mask_first[:, 256:512], NEG), reads=["mask_first"], writes=["mask_first"])
        P.pool(lambda e: e.memset(mtmp[:, 0:512], 0.0), reads=[("F", 5)], writes=[("F", 5)])
        P.pool(lambda e: e.affine_select(out=mtmp[:, 0:256].rearrange("p (a t) -> p a t", t=4),
                                         in_=mtmp[:, 0:256].rearrange("p (a t) -> p a t", t=4),
                                         pattern=[[0, 64], [-1, 4]], compare_op=ALU.is_ge, fill=NEG,
                                         base=-1, channel_multiplier=1), reads=[("F", 5)], writes=[("F", 5)])
        P.pool(lambda e: e.affine_select(out=mtmp[:, 256:512].rearrange("p (a t) -> p a t", t=4),
                                         in_=mtmp[:, 256:512].rearrange("p (a t) -> p a t", t=4),
                                         pattern=[[0, 64], [1, 4]], compare_op=ALU.is_ge, fill=NEG,
                                         base=0, channel_multiplier=-1), reads=[("F", 5)], writes=[("F", 5)])
        P.dve(lambda e: e.tensor_copy(out=mask_sc[:], in_=mtmp[:, 0:256]), reads=[("F", 5)], writes=["mask_sc"])
        P.dve(lambda e: e.tensor_copy(out=mask_sn[:], in_=mtmp[:, 256:512]), reads=[("F", 5)], writes=["mask_sn"])
        for gi in range(4):
            w = 2 << gi
            for j in range(16):
                P.pool(lambda e, gi=gi, j=j, w=w: e.memset(invcnt[:, gi, j:j + 1], 1.0 / min(j + 1, w)),
                       writes=[("invcnt", gi, j)])
        INVK = [("invcnt", gi, j) for gi in range(4) for j in range(16)]
        P.pool(lambda e: e.memset(carry_u[:], 0.0), writes=["carry_u"])
        P.pool(lambda e: e.memset(carry_v[:], 0.0), writes=["carry_v"])
        P.pool(lambda e: e.memset(carry_k[:], 0.0), writes=["carry_k"])
        P.pool(lambda e: e.memset(carry_V[:], 0.0), writes=["carry_V"])

        for i, g in enumerate((g_mix_pre, g_mem, g_mix_post, g_mlp_pre, g_mlp_post)):
            for l in range(DEPTH):
                P.dma(lambda e, i=i, l=l, g=g: e.dma_start(out=gv[:, i, l, :],
                                                            in_=g[l].rearrange("(c p) -> p c", p=128),
                                                            allow_slow_non_contiguous=True),
                      writes=["gv"])
        for l in range(DEPTH):
            P.dma(lambda e, l=l: e.dma_start(out=pscale[:, l, :], in_=pool_scale[l].rearrange("(c p) -> p c", p=128),
                                             allow_slow_non_contiguous=True), writes=["pscale"])
            for k in range(3):
                P.dma(lambda e, l=l, k=k: e.dma_start(out=convw[:, l, k, :],
                                                      in_=conv_w[l, k].rearrange("(c p) -> p c", p=128),
                                                      allow_slow_non_contiguous=True), writes=["convw"])
        P.dma(lambda e: e.dma_start(out=sinks_sb[:], in_=swa_sinks.rearrange("l j -> (l j)").rearrange("(o n) -> o n", o=1)),
              writes=["sinks_sb"])
        for l in range(DEPTH):
            P.dma(lambda e, l=l: e.dma_start(out=wpool[:, l, :, :], in_=w_pool[l].rearrange("g c d -> c g d")),
                  writes=["wpool"], eng="pool")
        P.act(lambda e: e.activation(out=sinks_sb[:], in_=sinks_sb[:], func=AF.Exp), reads=["sinks_sb"], writes=["sinks_sb"])
        for l in range(DEPTH):
            for h in range(2):
                for par in range(2):
                    for gg in range(2):
                        j = l * 8 + 4 * h + 2 * gg + par
                        c0 = par * 256 + gg * 128
                        P.dve(lambda e, l=l, h=h, j=j, c0=c0: e.tensor_copy(
                            out=esink[0:1, l, h, c0:c0 + 128], in_=sinks_sb[0:1, j:j + 1].broadcast_to([1, 128])),
                            reads=["sinks_sb"], writes=[("esink", l, h, c0)])
        ESK = lambda l, h: [("esink", l, h, c0) for c0 in (0, 128, 256, 384)]
        for l in range(DEPTH):
            for par in range(2):
                for h in range(2):
                    for gg in range(2):
                        j = l * 8 + 4 * h + 2 * gg + par
                        P.dve(lambda e, l=l, par=par, h=h, gg=gg, j=j: e.tensor_copy(
                            out=esink_s[0:1, l, par, h * 128:(h + 1) * 128].rearrange("o (b g t) -> o b g t", b=SB, g=2)[:, :, gg, :],
                            in_=sinks_sb[0:1, j:j + 1].unsqueeze(2).to_broadcast([1, SB, 4])),
                            reads=["sinks_sb"], writes=[("esink_s", l)])

        def cast(dst, src, key):
            P.dma(lambda e: e.dma_start(out=dst, in_=src), writes=[key], eng="pool")

        def slabview(scr, l, s, KC, NC_):
            return scr[l, s].rearrange("p (k n) -> p k n", k=KC)

        def cast_layer_mem(l):
            src = w_mem_kv[l].rearrange("(k p) n -> p k n", p=128)
            for s in range(2):
                cast(slabview(s_mem, l, s, 16, 512), src[:, :, s * 512:(s + 1) * 512], ("s_mem", l, s))

        def cast_layer_in(l):
            src = w_in[l].rearrange("(k p) n -> p k n", p=128)
            for s in range(5):
                cast(slabview(s_in, l, s, 16, 512), src[:, :, s * 512:(s + 1) * 512], ("s_in", l, s, 0))
            d5 = slabview(s_in, l, 5, 16, 512)
            for h in range(2):
                for r in range(2):
                    cast(d5[:, :, h * 128 + r * 64:h * 128 + r * 64 + 64], src[:, :, 2560 + h * 64:2560 + h * 64 + 64],
                         ("s_in", l, 5, h * 2 + r))
            cast(d5[:, :, 256:512], src[:, :, 2816:3072], ("s_in", l, 5, 4))
            d6 = slabview(s_in, l, 6, 16, 512)
            cast(d6[:, :, 0:256], src[:, :, 3072:3328], ("s_in", l, 6, 0))
            cast(d6[:, :, 256:512], src[:, :, 2560:2816], ("s_in", l, 6, 1))

        S_IN_KEYS = {}
        for l in range(DEPTH):
            for s in range(5):
                S_IN_KEYS[(l, s)] = [("s_in", l, s, 0)]
            S_IN_KEYS[(l, 5)] = [("s_in", l, 5, i) for i in range(5)]
            S_IN_KEYS[(l, 6)] = [("s_in", l, 6, i) for i in range(2)]

        def cast_layer_out(l):
            src = w_out[l].rearrange("(k p) n -> p k n", p=128)
            for s in range(4):
                cast(slabview(s_out, l, s, 16, 512), src[:, :, s * 512:(s + 1) * 512], ("s_out", l, s))

        def cast_layer_up(l):
            src = w_up[l].rearrange("(k p) n -> p k n", p=128)
            for s in range(16):
                cast(slabview(s_up, l, s, 16, 512), src[:, :, s * 512:(s + 1) * 512], ("s_up", l, s))

        def cast_layer_down(l):
            src = w_down[l].rearrange("(k p) n -> p k n", p=128)
            for s in range(16):
                cast(slabview(s_down, l, s, 64, 128), src[:, :, s * 128:(s + 1) * 128], ("s_down", l, s))

        cast_layer_mem(0)
        cast_layer_mem(1)
        for l in range(DEPTH):
            cast_layer_in(l)
            cast_layer_out(l)
            cast_layer_up(l)
            cast_layer_down(l)

        seq = []
        for l in range(DEPTH):
            for s in range(2):
                seq.append((s_mem[l, s], [("s_mem", l, s)], ("mem", l, s)))
        if not DBG["mem"]:
            seq = []
        tiles = [("p", i) for i in range(DBG["ntiles"])] + ([("s", 0)] if with_sample else [])
        for t in tiles:
            for l in range(DBG["nlayers"]):
                for s in range(WIN_NSLAB):
                    seq.append((s_in[l, s], S_IN_KEYS[(l, s)], ("in", l, s)))
                if not DBG["ffn"]:
                    continue
                for s in range(4):
                    seq.append((s_out[l, s], [("s_out", l, s)], ("out", l, s)))
                for s in range(16):
                    seq.append((s_up[l, s], [("s_up", l, s)], ("up", l, s)))
                for s in range(16):
                    seq.append((s_down[l, s], [("s_down", l, s)], ("down", l, s)))
        wst = {"issued": 0, "next": 0}
        PREF = 1

        def w_issue(upto):
            while wst["issued"] < min(upto, len(seq)):
                i = wst["issued"]
                src, skeys, _ = seq[i]
                bi = i % NWB
                P.dma(lambda e, src=src, bi=bi: e.dma_start(out=wbuf[bi][:], in_=src), reads=skeys, writes=[("wbuf", bi)])
                wst["issued"] += 1

        def w_next(tag, KC):
            i = wst["next"]
            assert seq[i][2] == tag, (seq[i][2], tag)
            w_issue(i + 1 + PREF)
            wst["next"] += 1
            bi = i % NWB
            return wbuf[bi][:].rearrange("p (k n) -> p k n", k=KC), [("wbuf", bi)]

        def nbank(lo=0, hi=6):
            b = state["bank"]
            state["bank"] = b + 1
            return lo + b % (hi - lo)

        def load_xT(src, T):
            NG = (T + 127) // 128
            rows = min(128, T)
            for g in range(NG):
                P.dma(lambda e, g=g: e.dma_start(out=xstage.ap[0:rows, g, :], in_=src[g * 128:g * 128 + rows, :]),
                      writes=xstage.keys(g, g + 1))
                for cb in range(4):
                    b = nbank(0, 4)

                    def fn(e, g=g, cb=cb, b=b):
                        ins = None
                        for j in range(4):
                            c = cb * 4 + j
                            ins = e.transpose(out=ps[b][:, j * 128:j * 128 + rows],
                                              in_=xstage.ap[0:rows, g, c * 128:(c + 1) * 128],
                                              identity=ident[0:rows, 0:rows])
                        return ins
                    P.pe(fn, reads=xstage.keys(g, g + 1) + ["ident"], writes=psk(b), banks=[b])
                    evac_copy(alt(), xT[:, cb * 4:cb * 4 + 4, g * 128:g * 128 + rows],
                              ps[b][:].rearrange("p (j t) -> p j t", j=4)[:, :, 0:rows],
                              psk(b), [("xT", cb * 4 + j) for j in range(4)], [b])

        def store_xT(dst, T):
            NG = (T + 127) // 128
            rows = min(128, T)
            for g in range(NG):
                for cb in range(4):
                    b = nbank(0, 4)

                    def fn(e, g=g, cb=cb, b=b):
                        ins = None
                        for j in range(4):
                            c = cb * 4 + j
                            ins = e.transpose(out=ps[b][0:rows, j * 128:(j + 1) * 128],
                                              in_=xT[:, c, g * 128:g * 128 + rows], identity=ident[:])
                        return ins
                    P.pe(fn, reads=[("xT", cb * 4 + j) for j in range(4)] + ["ident"], writes=psk(b), banks=[b])
                    evac_copy(alt(), xstage.ap[0:rows, g, cb * 512:(cb + 1) * 512], ps[b][0:rows, :],
                              psk(b), xstage.keys(g, g + 1), [b])
                P.dma(lambda e, g=g: e.dma_start(out=dst[g * 128:g * 128 + rows, :], in_=xstage.ap[0:rows, g, :]),
                      reads=xstage.keys(g, g + 1), writes=[("ydst", g)], eng="act")

        def norm_stats(src_ap, src_keys, T, nch=16, mode="rstd"):
            b = 7
            for c in range(nch):
                sb_ = sq[c % 4]
                P.act(lambda e, c=c, sb_=sb_: e.activation(out=sb_[:, 0:T], in_=src_ap(c), func=AF.Square),
                      reads=src_keys(c), writes=[("Bt", c % 4)])
                P.pe(lambda e, c=c, sb_=sb_: e.matmul(ps[b][:, 0:T], lhsT=onesD[:], rhs=sb_[:, 0:T],
                                                      start=(c == 0), stop=(c == nch - 1)),
                     reads=[("Bt", c % 4), "onesD"], writes=psk(b), banks=[b])
            if mode == "epsq":
                P.act(lambda e: e.activation(out=epsq[:, 0:T], in_=ps[b][:, 0:T], func=AF.Square,
                                             bias=EPS * math.sqrt(EPS), scale=math.sqrt(EPS)),
                      reads=psk(b), writes=[("F", 5)], banks=[b])
                return
            if mode == "rstd_q":
                P.dve(lambda e: e.tensor_tensor(out=rstd[:, 0:T], in0=ps[b][:, 0:T], in1=epsq[:, 0:T], op=ALU.add),
                      reads=psk(b) + [("F", 5)], writes=[("F", 0)], banks=[b])
                P.act(lambda e: e.activation(out=rstd[:, 0:T], in_=rstd[:, 0:T], func=AF.Sqrt),
                      reads=[("F", 0)], writes=[("F", 0)])
            else:
                P.act(lambda e: e.activation(out=rstd[:, 0:T], in_=ps[b][:, 0:T], func=AF.Sqrt, bias=EPS, scale=1.0),
                      reads=psk(b), writes=[("F", 0)], banks=[b])
            P.dve(lambda e: e.reciprocal(out=rstd[:, 0:T], in_=rstd[:, 0:T]), reads=[("F", 0)], writes=[("F", 0)])

        def xT_ap(T):
            return (lambda c: xT[:, c, 0:T]), (lambda c: [("xT", c)])

        def prenorm_to_hT(R, gidx, l, T):
            a, k = xT_ap(T)
            norm_stats(a, k, T)
            for c in range(16):
                if c % 2 == 1 and state["use_pool"]:
                    tb = Fs[1 + (c // 2) % 2]
                    tk = KF(1 + (c // 2) % 2)
                    P.pool(lambda e, c=c, tb=tb: e.tensor_tensor(out=tb[:, 0:T], in0=xT[:, c, 0:T], in1=rstd[:, 0:T],
                                                                 op=ALU.mult),
                           reads=[("xT", c), ("F", 0)], writes=tk)
                    P.act(lambda e, c=c, tb=tb: e.activation(out=R.hT.ap[:, c, 0:T], in_=tb[:, 0:T], func=AF.Copy,
                                                             scale=gv[:, gidx, l, c:c + 1]),
                          reads=tk + ["gv"], writes=R.hT.keys(c, c + 1))
                else:
                    P.dve(lambda e, c=c: e.scalar_tensor_tensor(out=R.hT.ap[:, c, 0:T], in0=xT[:, c, 0:T],
                                                                scalar=gv[:, gidx, l, c:c + 1], in1=rstd[:, 0:T],
                                                                op0=ALU.mult, op1=ALU.mult),
                          reads=[("xT", c), "gv", ("F", 0)], writes=R.hT.keys(c, c + 1))

        def postnorm_residual(R, gidx, l, T, mode="rstd", after_chunk=None):
            mixT_ = R.mixT
            norm_stats(lambda c: mixT_.ap[:, c, 0:T], lambda c: mixT_.keys(c, c + 1), T, mode=mode)
            def scale_chunk(c):
                P.dve(lambda e, c=c: e.scalar_tensor_tensor(out=mixT_.ap[:, c, 0:T], in0=mixT_.ap[:, c, 0:T],
                                                            scalar=gv[:, gidx, l, c:c + 1], in1=rstd[:, 0:T],
                                                            op0=ALU.mult, op1=ALU.mult),
                      reads=mixT_.keys(c, c + 1) + ["gv", ("F", 0)], writes=mixT_.keys(c, c + 1))

            def add_chunk(c):
                (P.pool if (state["use_pool"] and c % 2 == 1) else P.dve)(
                    lambda e, c=c: e.tensor_tensor(out=xT[:, c, 0:T], in0=xT[:, c, 0:T], in1=mixT_.ap[:, c, 0:T],
                                                   op=ALU.add),
                    reads=mixT_.keys(c, c + 1) + [("xT", c)], writes=[("xT", c)])
                if after_chunk is not None:
                    after_chunk(c)
            for c in range(17):
                if c < 16:
                    scale_chunk(c)
                if c >= 1:
                    add_chunk(c - 1)

        def proj_chunk(sap, skeys, KC, j, inR, T, b, fine=False):
            if fine:
                for kc in range(KC):
                    P.pe(lambda e, kc=kc: e.matmul(ps[b][:, 0:T], lhsT=sap[:, kc, j * 128:(j + 1) * 128],
                                                   rhs=inR.ap[:, kc, 0:T], start=(kc == 0), stop=(kc == KC - 1)),
                         reads=skeys + inR.keys(kc, kc + 1), writes=psk(b), banks=[b])
                return

            def fn(e):
                ins = None
                for kc in range(KC):
                    ins = e.matmul(ps[b][:, 0:T], lhsT=sap[:, kc, j * 128:(j + 1) * 128], rhs=inR.ap[:, kc, 0:T],
                                   start=(kc == 0), stop=(kc == KC - 1))
                return ins
            P.pe(fn, reads=skeys + inR.keys(), writes=psk(b), banks=[b])

        def tm_chunk(sap, skeys, KC, c0, ncols, inR, t0, M, b):
            def fn(e):
                ins = None
                for kc in range(KC):
                    ins = e.matmul(ps[b][0:M, 0:ncols], lhsT=inR.ap[:, kc, t0:t0 + M], rhs=sap[:, kc, c0:c0 + ncols],
                                   start=(kc == 0), stop=(kc == KC - 1))
                return ins
            P.pe(fn, reads=skeys + inR.keys(), writes=psk(b), banks=[b])

        def mem_phase():
            load_xT(mem, MEMT)
            a, k = xT_ap(MEMT)
            norm_stats(a, k, MEMT)
            for l in range(DEPTH):
                for c in range(16):
                    P.dve(lambda e, c=c, l=l: e.scalar_tensor_tensor(out=hT.ap[:, c, 0:MEMT], in0=xT[:, c, 0:MEMT],
                                                                     scalar=gv[:, 1, l, c:c + 1], in1=rstd[:, 0:MEMT],
                                                                     op0=ALU.mult, op1=ALU.mult),
                          reads=[("xT", c), "gv", ("F", 0)], writes=hT.keys(c, c + 1))
                for s in range(2):
                    sap, skeys = w_next(("mem", l, s), 16)
                    if s == 0:
                        for j in range(4):
                            b = nbank()
                            proj_chunk(sap, skeys, 16, j, hT, MEMT, b)
                            evac_copy(alt(), mkT[:, l, j, :], ps[b][:, 0:MEMT], psk(b), [("mkT", l)], [b])
                    for g in range(2):
                        b = nbank()
                        tm_chunk(sap, skeys, 16, 0, 512, hT, g * 128, 128, b)
                        stg = tmst[g % 2]
                        evac_copy("act", stg[:, 0:512], ps[b][:], psk(b), [("F", 3 + g % 2)], [b])
                        if s == 1:
                            P.dve(lambda e, g=g, b=b, l=l: e.tensor_copy(out=mvv[:, l, g, :], in_=ps[b][:]),
                                  reads=psk(b), writes=[("mvv", l)], banks=[b])
                        dst = (o_mk_p if s == 0 else o_mv_p)[l, g * 128:(g + 1) * 128, :]
                        P.dma(lambda e, dst=dst, stg=stg: e.dma_start(out=dst, in_=stg[:, 0:512]),
                              reads=[("F", 3 + g % 2)], writes=[("omem", l, s, g)], eng="act")

        def pool_prompt(l, T, first):
            P.dve(lambda e: e.tensor_copy(out=u_ext.ap[:, :, 0:16], in_=carry_u[:, l, :, :]),
                  reads=["carry_u"], writes=u_ext.keys())
            L = 16 + T
            for gi in range(4):
                w = 2 << gi
                src = u_ext.ap[:, gi, :]
                srck = u_ext.keys(gi, gi + 1)
                cur, curk = src, srck
                for k in range(1, gi + 2):
                    sh = 1 << (k - 1)
                    lo = (1 << k) - 1
                    dst = ptmp[k % 2]
                    P.dve(lambda e, cur=cur, dst=dst, lo=lo, sh=sh: e.tensor_tensor(
                        out=dst[:, lo:L], in0=cur[:, lo:L], in1=cur[:, lo - sh:L - sh], op=ALU.add),
                        reads=curk, writes=[("F", 3 + k % 2)])
                    cur, curk = dst[:], [("F", 3 + k % 2)]
                d = dTp[gi % 2]
                dk = [("dTp", gi % 2)]
                P.dve(lambda e, cur=cur, d=d, src=src, w=w: e.scalar_tensor_tensor(
                    out=d[:, 0:T], in0=cur[:, 16:16 + T], scalar=1.0 / w, in1=src[:, 16:16 + T],
                    op0=ALU.mult, op1=ALU.subtract), reads=curk + srck, writes=dk)
                if first:
                    P.dve(lambda e, cur=cur, gi=gi: e.tensor_tensor(out=pfix[:, 0:16], in0=cur[:, 16:32], in1=invcnt[:, gi, :],
                                                                   op=ALU.mult), reads=curk + INVK, writes=[("F", 5)])
                    P.dve(lambda e, d=d, src=src: e.tensor_tensor(out=d[:, 0:16], in0=pfix[:, 0:16], in1=src[:, 16:32],
                                                                 op=ALU.subtract), reads=[("F", 5)] + srck, writes=dk)
                yield
                b = state.get("fb", 6)
                P.pe(lambda e, b=b, gi=gi, d=d: e.matmul(ps[b][:, 0:T], lhsT=wpool[:, l, gi, :], rhs=d[:, 0:T],
                                                        start=True, stop=True),
                     reads=dk + ["wpool"], writes=psk(b), banks=[b])
                P.act(lambda e, b=b, gi=gi: e.activation(out=yT.ap[:, gi, 0:T], in_=ps[b][:, 0:T], func=AF.Copy,
                                                         scale=pscale[:, l, gi:gi + 1]),
                      reads=psk(b) + ["pscale"], writes=yT.keys(gi, gi + 1), banks=[b])
            P.dve(lambda e: e.tensor_copy(out=carry_u[:, l, :, :], in_=u_ext.ap[:, :, T:T + 16]),
                  reads=u_ext.keys(), writes=["carry_u"])
            yield

        def conv_prompt(l, T):
            P.dve(lambda e: e.tensor_copy(out=v_ext.ap[:, :, 0:2], in_=carry_v[:, l, :, :]),
                  reads=["carry_v"], writes=v_ext.keys())
            for c in range(4):
                vk = v_ext.keys(c, c + 1)
                ca = cacc[c % 2]
                cak = [("F", 1 + c % 2)]
                P.dve(lambda e, c=c: e.tensor_tensor(out=v_ext.ap[:, c, 2:2 + T], in0=v_ext.ap[:, c, 2:2 + T],
                                                     in1=hcR.ap[:, c, 0:T], op=ALU.mult),
                      reads=vk + hcR.keys(c, c + 1), writes=vk)
                P.act(lambda e, c=c, ca=ca: e.activation(out=ca[:, 0:T], in_=v_ext.ap[:, c, 0:T], func=AF.Copy,
                                                         scale=convw[:, l, 0, c:c + 1]),
                      reads=vk + ["convw"], writes=cak)
                for kk in (1, 2):
                    P.dve(lambda e, c=c, ca=ca, kk=kk: e.scalar_tensor_tensor(
                        out=ca[:, 0:T], in0=v_ext.ap[:, c, kk:kk + T], scalar=convw[:, l, kk, c:c + 1], in1=ca[:, 0:T],
                        op0=ALU.mult, op1=ALU.add), reads=vk + ["convw"] + cak, writes=cak)
                P.dve(lambda e, c=c, ca=ca: e.tensor_tensor(out=yT.ap[:, 4 + c, 0:T], in0=ca[:, 0:T],
                                                            in1=gbR.ap[:, c, 0:T], op=ALU.mult),
                      reads=cak + gbR.keys(c, c + 1), writes=yT.keys(4 + c, 5 + c))
                if c < 3:
                    yield
            P.dve(lambda e: e.tensor_copy(out=carry_v[:, l, :, :], in_=v_ext.ap[:, :, T:T + 2]),
                  reads=v_ext.keys(), writes=["carry_v"])
            yield

        def swa_prompt(l, T, first):
            NG = T // 128
            P.dve(lambda e: e.tensor_copy(out=kext.ap[:, :, 0:128], in_=carry_k[:, l, :, :]),
                  reads=["carry_k"], writes=kext.keys())
            it = 0
            for n in range(NG):
                for h in range(2):
                    base_b = 0 if it % 2 == 0 else 4
                    state["fb"] = 4 - base_b
                    it += 1
                    bD, bO = base_b + 2, base_b + 3
                    msk, mskk = (mask_first, "mask_first") if (first and n == 0) else (mask_cat, "mask_cat")
                    pts = []
                    for par in range(2):
                        b = base_b + par
                        p0 = par * 64

                        def fnS(e, b=b, p0=p0, msk=msk, h=h, n=n):
                            e.matmul(ps[b][:, 0:512], lhsT=identb[:], rhs=msk[:], start=True, stop=False)
                            ins = None
                            for part in range(2):
                                kc0 = 128 + n * 128 if part == 0 else n * 128
                                ins = e.matmul(ps[b][:, part * 256:(part + 1) * 256].rearrange("p (g q) -> p g q", g=2),
                                               lhsT=kext.ap[p0:p0 + 64, h, kc0:kc0 + 128],
                                               rhs=qR.ap[p0:p0 + 64, 2 * h:2 * h + 2, n * 128:(n + 1) * 128],
                                               start=False, stop=(part == 1))
                            return ins
                        P.pe(fnS, reads=["identb", mskk] + kext.keys(h, h + 1) + qR.keys(2 * h, 2 * h + 2),
                             writes=psk(b), banks=[b])
                        pi = (it % 2) * 2 + par
                        pt = Pt[pi]
                        P.act(lambda e, b=b, pt=pt: e.activation(out=pt[:], in_=ps[b][:], func=AF.Exp, scale=SWA_SCALE),
                              reads=psk(b), writes=KB(pi), banks=[b])
                        pts.append((pt, KB(pi)))

                    def fnD(e, pts=pts, h=h, bD=bD):
                        ins = None
                        for par in range(2):
                            pt = pts[par][0]
                            o = ps[bD][:, par * 256:(par + 1) * 256]
                            e.matmul(o, lhsT=ones1[:], rhs=pt[:, 0:256], start=True, stop=False)
                            e.matmul(o, lhsT=ones1[:], rhs=pt[:, 256:512], start=False, stop=False)
                            ins = e.matmul(o, lhsT=ones1[0:1, :], rhs=esink[0:1, l, h, par * 256:(par + 1) * 256],
                                           start=False, stop=True)
                        return ins
                    P.pe(fnD, reads=["ones1"] + ESK(l, h) + pts[0][1] + pts[1][1], writes=psk(bD), banks=[bD])

                    def fnO(e, pts=pts, h=h, bO=bO, n=n):
                        ins = None
                        for par in range(2):
                            pt = pts[par][0]
                            o = ps[bO][:, par * 256:(par + 1) * 256]
                            e.matmul(o, lhsT=Vd[:, n + 1, h, :], rhs=pt[:, 0:256], start=True, stop=False)
                            ins = e.matmul(o, lhsT=Vd[:, n, h, :], rhs=pt[:, 256:512], start=False, stop=True)
                        return ins
                    P.pe(fnO, reads=["Vd"] + pts[0][1] + pts[1][1], writes=psk(bO), banks=[bO])
                    yield
                    ri = it % 2
                    rd = rden[ri]
                    rdk = RDK[ri]
                    P.dve(lambda e, rd=rd, bD=bD: e.reciprocal(out=rd[:, 0:512], in_=ps[bD][:]),
                          reads=psk(bD), writes=rdk, banks=[bD])
                    for par in range(2):
                        p0 = par * 64
                        P.dve(lambda e, p0=p0, par=par, rd=rd, h=h, n=n, bO=bO: e.tensor_tensor(
                            out=yT.ap[p0:p0 + 64, 8 + 2 * h:10 + 2 * h, n * 128:(n + 1) * 128],
                            in0=ps[bO][p0:p0 + 64, par * 256:(par + 1) * 256].rearrange("p (g q) -> p g q", g=2),
                            in1=rd[p0:p0 + 64, par * 256:(par + 1) * 256].rearrange("p (g q) -> p g q", g=2),
                            op=ALU.mult),
                            reads=psk(bO) + rdk, writes=yT.keys(8 + 2 * h, 10 + 2 * h), banks=[bO])
                    yield
            P.dve(lambda e: e.tensor_copy(out=carry_k[:, l, :, :], in_=kext.ap[:, :, T:T + 128]),
                  reads=kext.keys(), writes=["carry_k"])
            P.dve(lambda e: e.tensor_copy(out=carry_V[:, l, :, :], in_=Vd[:, NG, :, :]),
                  reads=["Vd"], writes=["carry_V"])

        def mem_prompt(l, T):
            for hd in range(4):
                base_b = 0 if hd % 2 == 0 else 4
                state["fb"] = 4 - base_b
                bD, bO = base_b + 2, base_b + 3
                pts = []
                for kb in range(2):
                    b = base_b + kb
                    P.pe(lambda e, b=b, kb=kb, hd=hd: e.matmul(ps[b][:, 0:T], lhsT=mkT[:, l, hd, kb * 128:(kb + 1) * 128],
                                                              rhs=qmR.ap[:, hd, 0:T], start=True, stop=True),
                         reads=[("mkT", l)] + qmR.keys(hd, hd + 1), writes=psk(b), banks=[b])
                    pt = Pt[(hd % 2) * 2 + kb]
                    ptk = [("Bt", (hd % 2) * 2 + kb)]
                    P.act(lambda e, b=b, pt=pt: e.activation(out=pt[:, 0:T], in_=ps[b][:, 0:T], func=AF.Exp, scale=MEM_SCALE),
                          reads=psk(b), writes=ptk, banks=[b])
                    pts.append((pt, ptk))

                def fnD(e, pts=pts, bD=bD):
                    e.matmul(ps[bD][:, 0:T], lhsT=ones1[:], rhs=pts[0][0][:, 0:T], start=True, stop=False)
                    return e.matmul(ps[bD][:, 0:T], lhsT=ones1[:], rhs=pts[1][0][:, 0:T], start=False, stop=True)
                P.pe(fnD, reads=["ones1"] + pts[0][1] + pts[1][1], writes=psk(bD), banks=[bD])

                def fnO(e, pts=pts, hd=hd, bO=bO):
                    e.matmul(ps[bO][:, 0:T], lhsT=mvv[:, l, 0, hd * 128:(hd + 1) * 128], rhs=pts[0][0][:, 0:T],
                             start=True, stop=False)
                    return e.matmul(ps[bO][:, 0:T], lhsT=mvv[:, l, 1, hd * 128:(hd + 1) * 128], rhs=pts[1][0][:, 0:T],
                                    start=False, stop=True)
                P.pe(fnO, reads=[("mvv", l)] + pts[0][1] + pts[1][1], writes=psk(bO), banks=[bO])
                yield
                rd = rden[hd % 2]
                rdk = RDK[hd % 2]
                P.dve(lambda e, rd=rd, bD=bD: e.reciprocal(out=rd[:, 0:T], in_=ps[bD][:, 0:T]),
                      reads=psk(bD), writes=rdk, banks=[bD])
                P.dve(lambda e, rd=rd, bO=bO, hd=hd: e.tensor_tensor(out=yT.ap[:, 12 + hd, 0:T], in0=ps[bO][:, 0:T],
                                                                     in1=rd[:, 0:T], op=ALU.mult),
                      reads=psk(bO) + rdk, writes=yT.keys(12 + hd, 13 + hd), banks=[bO])
                yield

        def pre1_chunk(l, T):
            def f(c):
                P.act(lambda e, c=c: e.activation(out=hT.ap[:, c, 0:T], in_=xT[:, c, 0:T], func=AF.Copy,
                                                  scale=gv[:, 0, l, c:c + 1]),
                      reads=[("xT", c), "gv"], writes=hT.keys(c, c + 1))
            return f

        def layer_prompt(l, ti):
            T = TP
            first = (ti == 0)
            last = (ti == NPT - 1)
            if l == 0:
                for c in range(16):
                    pre1_chunk(0, T)(c)
            P.dve(lambda e: e.tensor_copy(out=Vd[:, 0, :, :], in_=carry_V[:, l, :, :]), reads=["carry_V"], writes=["Vd"])
            RK = [("F", 0)]

            def evac_scaled(out_ap, b, wkeys):
                P.dve(lambda e: e.tensor_tensor(out=out_ap, in0=ps[b][:, 0:T], in1=rstd[:, 0:T], op=ALU.mult),
                      reads=psk(b) + RK, writes=wkeys, banks=[b])

            def dest(m):
                if m < 4:
                    return u_ext.ap[:, m, 16:16 + T], u_ext.keys(m, m + 1)
                if m < 8:
                    return hcR.ap[:, m - 4, 0:T], hcR.keys(m - 4, m - 3)
                if m < 12:
                    return gbR.ap[:, m - 8, 0:T], gbR.keys(m - 8, m - 7)
                if m < 16:
                    return v_ext.ap[:, m - 12, 2:2 + T], v_ext.keys(m - 12, m - 11)
                if m < 20:
                    return qR.ap[:, m - 16, 0:T], qR.keys(m - 16, m - 15)
                if m < 22:
                    return kext.ap[:, m - 20, 128:128 + T], kext.keys(m - 20, m - 19)
                return qmR.ap[:, m - 22, 0:T], qmR.keys(m - 22, m - 21)
            for s in range(WIN_NSLAB):
                sap, skeys = w_next(("in", l, s), 16)
                pend = []
                for j in range(4):
                    m = s * 4 + j
                    if m >= 26:
                        continue
                    b = nbank()
                    proj_chunk(sap, skeys, 16, j, hT, T, b, fine=(m == 0))
                    if s == 0:
                        pend.append((m, b))
                    else:
                        o_, k_ = dest(m)
                        evac_scaled(o_, b, k_)
                if last and s == 0:
                    tm_chunk(sap, skeys, 16, 0, 512, hT, T - 16, 16, 6)
                if s == 0:
                    a_, k_ = xT_ap(T)
                    norm_stats(a_, k_, T, mode="rstd")
                    bq = nbank()

                    def fnr(e, bq=bq):
                        ins = None
                        for g in range(T // 128):
                            ins = e.matmul(ps[bq][:, g:g + 1], lhsT=rstd[:, g * 128:(g + 1) * 128], rhs=ident[:, 0:1],
                                           start=True, stop=True)
                        if last:
                            ins = e.matmul(ps[bq][0:16, 4:5], lhsT=rstd[:, T - 16:T], rhs=ident[:, 0:1], start=True, stop=True)
                        return ins
                    P.pe(fnr, reads=RK + ["ident"], writes=psk(bq), banks=[bq])
                    P.act(lambda e, bq=bq: e.activation(out=rtm[:, 0:8], in_=ps[bq][:, 0:8], func=AF.Copy),
                          reads=psk(bq), writes=["rtm"], banks=[bq])
                    for (m, b) in pend:
                        o_, k_ = dest(m)
                        evac_scaled(o_, b, k_)
                if last and s == 0:
                    b = 6
                    P.act(lambda e, b=b: e.activation(out=tmst[0][0:16, 0:512], in_=ps[b][0:16, :], func=AF.Copy,
                                                      scale=rtm[0:16, 4:5]),
                          reads=psk(b) + ["rtm"], writes=[("F", 3)], banks=[b])
                    P.dma(lambda e: e.dma_start(out=o_pool_p[l], in_=tmst[0][1:16, 0:512]), reads=[("F", 3)],
                          writes=[("o_pool_p", l)], eng="act")
                if last and s == 1:
                    b = 6
                    tm_chunk(sap, skeys, 16, 0, 512, hT, T - 16, 16, b)
                    P.act(lambda e, b=b: e.activation(out=hcst[0:16, 0:512], in_=ps[b][0:16, :], func=AF.Copy,
                                                      scale=rtm[0:16, 4:5]),
                          reads=psk(b) + ["rtm"], writes=[("F", 5)], banks=[b])
                if last and s == 3:
                    b = 6
                    tm_chunk(sap, skeys, 16, 0, 512, hT, T - 16, 16, b)
                    P.dve(lambda e, b=b: e.scalar_tensor_tensor(out=tmst[1][0:16, 0:512], in0=ps[b][0:16, :],
                                                                scalar=rtm[0:16, 4:5], in1=hcst[0:16, 0:512],
                                                                op0=ALU.mult, op1=ALU.mult),
                          reads=psk(b) + [("F", 5), "rtm"], writes=[("F", 4)], banks=[b])
                    P.dma(lambda e: e.dma_start(out=o_conv_p[l], in_=tmst[1][14:16, 0:512]), reads=[("F", 4)],
                          writes=[("o_conv_p", l)], eng="act")
                if s == 6:
                    for g in range(T // 128):
                        b = 6
                        tm_chunk(sap, skeys, 16, 256, 256, hT, g * 128, 128, b)
                        for h in range(2):
                            P.dve(lambda e, g=g, h=h, b=b: e.tensor_scalar(
                                out=Vd[:, g + 1, h, :].rearrange("p (r d) -> p r d", r=2),
                                in0=ps[b][:, 128 + h * 64:128 + (h + 1) * 64].unsqueeze(1).to_broadcast([128, 2, 64]),
                                scalar1=rtm[:, g:g + 1], scalar2=None, op0=ALU.mult),
                                reads=psk(b) + ["rtm"], writes=["Vd"], banks=[b])
                        if last and g == T // 128 - 1:
                            P.act(lambda e, b=b, g=g: e.activation(out=tmst[0][:, 0:256], in_=ps[b][:, 0:256], func=AF.Copy,
                                                                   scale=rtm[:, g:g + 1]),
                                  reads=psk(b) + ["rtm"], writes=[("F", 3)], banks=[b])
                            P.dma(lambda e: e.dma_start(out=o_k_p[l], in_=tmst[0][:, 0:128]), reads=[("F", 3)],
                                  writes=[("o_k_p", l)], eng="act")
                            P.dma(lambda e: e.dma_start(out=o_v_p[l], in_=tmst[0][:, 128:256]), reads=[("F", 3)],
                                  writes=[("o_v_p", l)], eng="act")
            if DBG["mixers"]:
                import itertools
                A = itertools.chain(swa_prompt(l, T, first), mem_prompt(l, T))
                B = itertools.chain(pool_prompt(l, T, first), conv_prompt(l, T))
                a_alive, b_alive = True, True
                while a_alive or b_alive:
                    if a_alive:
                        a_alive = next(A, "end") != "end"
                    if b_alive:
                        b_alive = next(B, "end") != "end"
                    if a_alive:
                        a_alive = next(A, "end") != "end"
            if DBG["ffn"]:
                ffn_and_out(RP, l, T, post2_hook=(pre1_chunk(l + 1, T) if l + 1 < DBG["nlayers"] else None))

        def ffn_and_out(R, l, T, post2_hook=None):
            for s in range(4):
                sap, skeys = w_next(("out", l, s), 16)
                for j in range(4):
                    m = s * 4 + j
                    b = nbank()
                    proj_chunk(sap, skeys, 16, j, R.yT, T, b)
                    evac_copy(alt(), R.mixT.ap[:, m, 0:T], ps[b][:, 0:T], psk(b), R.mixT.keys(m, m + 1), [b])
            def h2_chunk(c):
                P.act(lambda e, c=c: e.activation(out=R.hT.ap[:, c, 0:T], in_=xT[:, c, 0:T], func=AF.Copy,
                                                  scale=gv[:, 3, l, c:c + 1]),
                      reads=[("xT", c), "gv"], writes=R.hT.keys(c, c + 1))
            postnorm_residual(R, 2, l, T, after_chunk=h2_chunk)
            for s in range(16):
                sap, skeys = w_next(("up", l, s), 16)
                if s == 1:
                    a_, k_ = xT_ap(T)
                    norm_stats(a_, k_, T, mode="epsq")
                for j in range(4):
                    m = s * 4 + j
                    b = nbank()
                    proj_chunk(sap, skeys, 16, j, R.hT, T, b, fine=(m == 0))
                    rt = rtmp[m % 2]
                    rk = [("F", 1 + m % 2)]
                    P.act(lambda e, b=b, rt=rt: e.activation(out=rt[:, 0:T], in_=ps[b][:, 0:T], func=AF.Relu),
                          reads=psk(b), writes=rk, banks=[b])
                    P.dve(lambda e, m=m, rt=rt: e.tensor_tensor(out=R.hidT.ap[:, m, 0:T], in0=rt[:, 0:T], in1=rt[:, 0:T],
                                                                op=ALU.mult),
                          reads=rk, writes=R.hidT.keys(m, m + 1))
            for s in range(16):
                sap, skeys = w_next(("down", l, s), 64)
                b = nbank()
                proj_chunk(sap, skeys, 64, 0, R.hidT, T, b, fine=(s == 0))
                evac_copy(alt(), R.mixT.ap[:, s, 0:T], ps[b][:, 0:T], psk(b), R.mixT.keys(s, s + 1), [b])
            postnorm_residual(R, 4, l, T, mode="rstd_q", after_chunk=post2_hook)

        psb = [p_[:].bitcast(BF16) for p_ in ps]
        s_hidT = Region(BIG, "BIG", 0, 64, 64, BF16)
        s_yT = Region(BIG, "BIG", 8192, 16, 64, BF16)
        s_q = Region(BIG, "BIG", 10240, 4, 64, BF16)
        s_kx = Region(BIG, "BIG", 10752, 2, 64, BF16)
        s_qm = Region(BIG, "BIG", 11008, 4, 64, BF16)
        s_hc = Region(BIG, "BIG", 11520, 4, 64, F32)
        s_gb = Region(BIG, "BIG", 12544, 4, 64, F32)
        s_vx = Region(BIG, "BIG", 13568, 4, 96, F32)
        s_ux = Region(BIG, "BIG", 15104, 4, 304, F32)
        pst = Region(BIG, "BIG", 20480, 2, 512, F32)
        cst = Region(BIG, "BIG", 24576, 1, 512, F32)
        Kd = Region(BIG, "BIG", 26624, 16, 256, BF16)
        KTd = Region(BIG, "BIG", 34816, 32, 128, BF16)
        Vds = Region(BIG, "BIG", 43008, 16, 256, BF16)
        mc = [dict(K=Region(BIG, "BIG", 51200, 4, 512, BF16), KT=Region(BIG, "BIG", 55296, 8, 256, BF16),
                   V=Region(BIG, "BIG", 59392, 4, 512, BF16)),
              dict(K=Region(U1, "U1", 8192, 4, 512, BF16), KT=Region(U1, "U1", 12288, 8, 256, BF16),
                   V=Region(U1, "U1", 16384, 4, 512, BF16))]
        Vn = Region(U1, "U1", 20480, 16, 256, BF16)
        s_hT = Region(U1, "U1", 0, 16, 64, BF16)
        s_mixT = Region(U1, "U1", 2048, 16, 64, F32)
        RS = SimpleNamespace(hT=s_hT, mixT=s_mixT, yT=s_yT, hidT=s_hidT)

        def bt4(ap):
            return ap.rearrange("p (b t) -> p b t", t=4)

        def layer_sample(l):
            T = TS
            prenorm_to_hT(RS, 0, l, T)
            sp2 = spool[l].rearrange("b r f -> (b r) f")
            P.dma(lambda e: e.dma_start(out=pst.ap[0:128, 0, :], in_=sp2[0:128, :]), writes=pst.keys(0, 1))
            P.dma(lambda e: e.dma_start(out=pst.ap[0:112, 1, :], in_=sp2[128:240, :]), writes=pst.keys(1, 2))
            P.dma(lambda e: e.dma_start(out=cst.ap[0:32, 0, :], in_=sconv[l].rearrange("b r f -> (b r) f")), writes=cst.keys())
            Kd5 = Kd.ap.rearrange("p b (h r d) -> p b h r d", h=2, r=2)
            Vd5 = Vds.ap.rearrange("p b (h r d) -> p b h r d", h=2, r=2)
            for r in range(2):
                for h in range(2):
                    P.dma(lambda e, r=r, h=h: e.dma_start(out=Kd5[:, :, h, r, :],
                                                          in_=ck[l][:, :, h * 64:(h + 1) * 64].rearrange("b k d -> k b d")),
                          writes=Kd.keys(), eng="pool")
                    P.dma(lambda e, r=r, h=h: e.dma_start(out=Vd5[:, :, h, r, :],
                                                          in_=cv[l][:, :, h * 64:(h + 1) * 64].rearrange("b k d -> k b d")),
                          writes=Vds.keys(), eng="pool")
            P.dma(lambda e: e.dma_start(out=o_pool_s[l][:, 0:11, :], in_=spool[l][:, 4:15, :]), writes=[("o_pool_s", l, 0)], eng="pool")
            P.dma(lambda e: e.dma_start(out=o_k_s[l][:, 0:124, :], in_=ck[l][:, 4:128, :]), writes=[("o_k_s", l, 0)], eng="pool")
            P.dma(lambda e: e.dma_start(out=o_v_s[l][:, 0:124, :], in_=cv[l][:, 4:128, :]), writes=[("o_v_s", l, 0)], eng="pool")
            for gi in range(4):
                b = nbank(0, 4)

                def fn(e, gi=gi, b=b):
                    e.transpose(out=ps[b][:, 0:128], in_=pst.ap[0:128, 0, gi * 128:(gi + 1) * 128], identity=ident[:])
                    return e.transpose(out=ps[b][:, 128:240], in_=pst.ap[0:112, 1, gi * 128:(gi + 1) * 128],
                                       identity=ident[0:112, 0:112])
                P.pe(fn, reads=pst.keys() + ["ident"], writes=psk(b), banks=[b])
                u3 = s_ux.ap[:, gi, :].rearrange("p (b r) -> p b r", r=19)
                evac_copy(alt(), u3[:, :, 0:15], ps[b][:, 0:240].rearrange("p (b r) -> p b r", r=15), psk(b),
                          s_ux.keys(gi, gi + 1), [b])
            for c in range(4):
                b = nbank(0, 4)
                P.pe(lambda e, c=c, b=b: e.transpose(out=ps[b][:, 0:32], in_=cst.ap[0:32, 0, c * 128:(c + 1) * 128],
                                                     identity=ident[0:32, 0:32]),
                     reads=cst.keys() + ["ident"], writes=psk(b), banks=[b])
                v3 = s_vx.ap[:, c, :].rearrange("p (b r) -> p b r", r=6)
                evac_copy(alt(), v3[:, :, 0:2], ps[b][:, 0:32].rearrange("p (b r) -> p b r", r=2), psk(b),
                          s_vx.keys(c, c + 1), [b])
            for q4 in range(4):
                b = 4 + q4

                def fnk(e, q4=q4, b=b):
                    ins = None
                    for i in range(8):
                        idx = q4 * 8 + i
                        bb, h = idx // 2, idx % 2
                        ins = e.transpose(out=psb[b][:, i * 128:(i + 1) * 128], in_=Kd.ap[:, bb, h * 128:(h + 1) * 128],
                                          identity=identb[:])
                    return ins
                P.pe(fnk, reads=Kd.keys() + ["identb"], writes=psk(b), banks=[b])
                evac_copy(alt(), KTd.ap[:, q4 * 8:(q4 + 1) * 8, :], psb[b][:].rearrange("p (i k) -> p i k", i=8), psk(b),
                          KTd.keys(q4 * 8, q4 * 8 + 8), [b])
            for s in range(WIN_NSLAB):
                sap, skeys = w_next(("in", l, s), 16)
                for j in range(4):
                    m = s * 4 + j
                    if m >= 26:
                        continue
                    b = nbank(0, 4)
                    proj_chunk(sap, skeys, 16, j, s_hT, T, b, fine=(m == 0))
                    eng = alt()
                    src = ps[b][:, 0:T]
                    if m < 4:
                        u3 = s_ux.ap[:, m, :].rearrange("p (b r) -> p b r", r=19)
                        evac_copy(eng, u3[:, :, 15:19], bt4(src), psk(b), s_ux.keys(m, m + 1), [b])
                    elif m < 8:
                        evac_copy(eng, s_hc.ap[:, m - 4, :], src, psk(b), s_hc.keys(m - 4, m - 3), [b])
                    elif m < 12:
                        evac_copy(eng, s_gb.ap[:, m - 8, :], src, psk(b), s_gb.keys(m - 8, m - 7), [b])
                    elif m < 16:
                        v3 = s_vx.ap[:, m - 12, :].rearrange("p (b r) -> p b r", r=6)
                        evac_copy(eng, v3[:, :, 2:6], bt4(src), psk(b), s_vx.keys(m - 12, m - 11), [b])
                    elif m < 20:
                        evac_copy(eng, s_q.ap[:, m - 16, :], src, psk(b), s_q.keys(m - 16, m - 15), [b])
                    elif m < 22:
                        evac_copy(eng, s_kx.ap[:, m - 20, :], src, psk(b), s_kx.keys(m - 20, m - 19), [b])
                    else:
                        evac_copy(eng, s_qm.ap[:, m - 22, :], src, psk(b), s_qm.keys(m - 22, m - 21), [b])
                if s == 0:
                    b = 6
                    tm_chunk(sap, skeys, 16, 0, 512, s_hT, 0, 64, b)
                    evac_copy("act", tmst[0][0:64, 0:512], ps[b][0:64, :], psk(b), KF(3), [b])
                    P.dma(lambda e: e.dma_start(out=o_pool_s[l][:, 11:15, :], in_=tmst[0][0:64, 0:512]), reads=KF(3),
                          writes=[("o_pool_s", l, 1)], eng="pool")
                if s == 1:
                    b = 6
                    tm_chunk(sap, skeys, 16, 0, 512, s_hT, 0, 64, b)
                    evac_copy("act", hcst[0:64, 0:512], ps[b][0:64, :], psk(b), KF(5), [b])
                if s == 3:
                    b = 6
                    tm_chunk(sap, skeys, 16, 0, 512, s_hT, 0, 64, b)
                    P.dve(lambda e, b=b: e.tensor_tensor(out=tmst[1][0:64, 0:512], in0=ps[b][0:64, :], in1=hcst[0:64, 0:512],
                                                         op=ALU.mult), reads=psk(b) + KF(5), writes=KF(4), banks=[b])
                    for t in (2, 3):
                        for bb in range(SB):
                            P.dma(lambda e, t=t, bb=bb: e.dma_start(out=o_conv_s[l][bb, t - 2:t - 1, :],
                                                                    in_=tmst[1][bb * 4 + t:bb * 4 + t + 1, 0:512]),
                                  reads=KF(4), writes=[("o_conv_s", l, t, bb)], eng="pool")
                if s == 6:
                    b = 6
                    tm_chunk(sap, skeys, 16, 256, 256, s_hT, 0, 64, b)
                    evac_copy("act", tmst[0][0:64, 0:256], ps[b][0:64, 0:256], psk(b), KF(3), [b])
                    P.dma(lambda e: e.dma_start(out=o_k_s[l][:, 124:128, :], in_=tmst[0][0:64, 0:128]), reads=KF(3),
                          writes=[("o_k_s", l, 1)], eng="pool")
                    P.dma(lambda e: e.dma_start(out=o_v_s[l][:, 124:128, :], in_=tmst[0][0:64, 128:256]), reads=KF(3),
                          writes=[("o_v_s", l, 1)], eng="pool")
                    for q4 in range(4):
                        bk = nbank(0, 4)

                        def fnv(e, q4=q4, bk=bk, sap=sap):
                            ins = None
                            for i in range(4):
                                bb = q4 * 4 + i
                                for kc in range(16):
                                    ins = e.matmul(ps[bk][0:4, i * 128:(i + 1) * 128], lhsT=s_hT.ap[:, kc, bb * 4:bb * 4 + 4],
                                                   rhs=sap[:, kc, 384:512], start=(kc == 0), stop=(kc == 15))
                            return ins
                        P.pe(fnv, reads=skeys + s_hT.keys(), writes=psk(bk), banks=[bk])
                        for h in range(2):
                            P.dve(lambda e, q4=q4, bk=bk, h=h: e.tensor_copy(
                                out=Vn.ap[0:4, q4 * 4:(q4 + 1) * 4, h * 128:(h + 1) * 128].rearrange("p b (r d) -> p b r d", r=2),
                                in_=ps[bk][0:4, :].rearrange("p (i h d) -> p i h d", i=4, h=2)[:, :, h, :]
                                .unsqueeze(2).to_broadcast([4, 4, 2, 64])),
                                reads=psk(bk), writes=Vn.keys(q4 * 4, q4 * 4 + 4), banks=[bk])
            for gi in range(4):
                w = 2 << gi
                u3 = s_ux.ap[:, gi, :].rearrange("p (b r) -> p b r", r=19)
                uk = s_ux.keys(gi, gi + 1)
                cur, curk = u3, uk
                for k in range(1, gi + 2):
                    sh = 1 << (k - 1)
                    lo = (1 << k) - 1
                    dst = ptmp[k % 2][:, 0:304].rearrange("p (b r) -> p b r", r=19)
                    P.dve(lambda e, cur=cur, dst=dst, lo=lo, sh=sh: e.tensor_tensor(
                        out=dst[:, :, lo:19], in0=cur[:, :, lo:19], in1=cur[:, :, lo - sh:19 - sh], op=ALU.add),
                        reads=curk, writes=KF(3 + k % 2))
                    cur, curk = dst, KF(3 + k % 2)
                d = dT[gi % 2]
                P.dve(lambda e, cur=cur, d=d, u3=u3, w=w: e.scalar_tensor_tensor(
                    out=bt4(d[:, 0:64]), in0=cur[:, :, 15:19], scalar=1.0 / w, in1=u3[:, :, 15:19],
                    op0=ALU.mult, op1=ALU.subtract), reads=curk + uk, writes=KB(gi % 2))
                b = nbank(0, 4)
                P.pe(lambda e, b=b, gi=gi, d=d: e.matmul(ps[b][:, 0:64], lhsT=wpool[:, l, gi, :], rhs=d[:, 0:64],
                                                        start=True, stop=True),
                     reads=KB(gi % 2) + ["wpool"], writes=psk(b), banks=[b])
                P.act(lambda e, b=b, gi=gi: e.activation(out=s_yT.ap[:, gi, :], in_=ps[b][:, 0:64], func=AF.Copy,
                                                         scale=pscale[:, l, gi:gi + 1]),
                      reads=psk(b) + ["pscale"], writes=s_yT.keys(gi, gi + 1), banks=[b])
            for c in range(4):
                v3 = s_vx.ap[:, c, :].rearrange("p (b r) -> p b r", r=6)
                vk = s_vx.keys(c, c + 1)
                ca = bt4(cacc[c % 2][:, 0:64])
                cak = KF(1 + c % 2)
                P.dve(lambda e, v3=v3, c=c: e.tensor_tensor(out=v3[:, :, 2:6], in0=v3[:, :, 2:6], in1=bt4(s_hc.ap[:, c, :]),
                                                           op=ALU.mult), reads=vk + s_hc.keys(c, c + 1), writes=vk)
                P.act(lambda e, v3=v3, c=c, ca=ca: e.activation(out=ca, in_=v3[:, :, 0:4], func=AF.Copy,
                                                               scale=convw[:, l, 0, c:c + 1]),
                      reads=vk + ["convw"], writes=cak)
                for kk in (1, 2):
                    P.dve(lambda e, v3=v3, c=c, ca=ca, kk=kk: e.scalar_tensor_tensor(
                        out=ca, in0=v3[:, :, kk:kk + 4], scalar=convw[:, l, kk, c:c + 1], in1=ca,
                        op0=ALU.mult, op1=ALU.add), reads=vk + ["convw"] + cak, writes=cak)
                P.dve(lambda e, c=c, ca=ca: e.tensor_tensor(out=bt4(s_yT.ap[:, 4 + c, :]), in0=ca, in1=bt4(s_gb.ap[:, c, :]),
                                                            op=ALU.mult),
                      reads=cak + s_gb.keys(c, c + 1), writes=s_yT.keys(4 + c, 5 + c))
            bA, bC, bE, bF = [0, 1], [2, 3], 4, 5
            for par in range(2):
                p0 = par * 64

                def fnS(e, par=par, p0=p0):
                    e.matmul(ps[bA[par]][:, 0:256], lhsT=identb[:], rhs=mask_sc[:], start=True, stop=False)
                    ins = None
                    for bb in range(SB):
                        for h in range(2):
                            c0 = h * 128 + bb * 8
                            ins = e.matmul(ps[bA[par]][:, c0:c0 + 8].rearrange("p (g t) -> p g t", g=2),
                                           lhsT=KTd.ap[p0:p0 + 64, bb * 2 + h, :],
                                           rhs=s_q.ap[p0:p0 + 64, 2 * h:2 * h + 2, bb * 4:bb * 4 + 4],
                                           start=False, stop=(bb == SB - 1 and h == 1))
                    return ins
                P.pe(fnS, reads=["identb", "mask_sc"] + KTd.keys() + s_q.keys(), writes=psk(bA[par]), banks=[bA[par]])

                def fnN(e, par=par, p0=p0):
                    e.matmul(ps[bC[par]][0:4, 0:256], lhsT=identb[0:4, 0:4], rhs=mask_sn[0:4, :], start=True, stop=False)
                    ins = None
                    for bb in range(SB):
                        for h in range(2):
                            c0 = h * 128 + bb * 8
                            ins = e.matmul(ps[bC[par]][0:4, c0:c0 + 8].rearrange("p (g t) -> p g t", g=2),
                                           lhsT=s_kx.ap[p0:p0 + 64, h, bb * 4:bb * 4 + 4],
                                           rhs=s_q.ap[p0:p0 + 64, 2 * h:2 * h + 2, bb * 4:bb * 4 + 4],
                                           start=False, stop=(bb == SB - 1 and h == 1))
                    return ins
                P.pe(fnN, reads=["identb", "mask_sn"] + s_kx.keys() + s_q.keys(), writes=psk(bC[par]), banks=[bC[par]])
                P.act(lambda e, par=par: e.activation(out=Pt[par][:, 0:256], in_=ps[bA[par]][:, 0:256], func=AF.Exp,
                                                      scale=SWA_SCALE), reads=psk(bA[par]), writes=KB(par), banks=[bA[par]])
                P.act(lambda e, par=par: e.activation(out=Pt[2 + par][0:4, 0:256], in_=ps[bC[par]][0:4, 0:256], func=AF.Exp,
                                                      scale=SWA_SCALE), reads=psk(bC[par]), writes=KB(2 + par), banks=[bC[par]])

            def fnDs(e):
                ins = None
                for par in range(2):
                    o = ps[bE][:, par * 256:(par + 1) * 256]
                    e.matmul(o, lhsT=ones1[:], rhs=Pt[par][:, 0:256], start=True, stop=False)
                    e.matmul(o, lhsT=ones1[0:4, :], rhs=Pt[2 + par][0:4, 0:256], start=False, stop=False)
                    ins = e.matmul(o, lhsT=ones1[0:1, :], rhs=esink_s[0:1, l, par, :], start=False, stop=True)
                return ins
            P.pe(fnDs, reads=["ones1", ("esink_s", l)] + KB(0) + KB(1) + KB(2) + KB(3), writes=psk(bE), banks=[bE])

            def fnOs(e):
                ins = None
                for par in range(2):
                    for bb in range(SB):
                        for h in range(2):
                            cc = h * 128 + bb * 8
                            c0 = par * 256 + cc
                            e.matmul(ps[bF][:, c0:c0 + 8], lhsT=Vds.ap[:, bb, h * 128:(h + 1) * 128], rhs=Pt[par][:, cc:cc + 8],
                                     start=True, stop=False)
                            ins = e.matmul(ps[bF][:, c0:c0 + 8], lhsT=Vn.ap[0:4, bb, h * 128:(h + 1) * 128],
                                           rhs=Pt[2 + par][0:4, cc:cc + 8], start=False, stop=True)
                return ins
            P.pe(fnOs, reads=Vds.keys() + Vn.keys() + KB(0) + KB(1) + KB(2) + KB(3), writes=psk(bF), banks=[bF])
            P.dve(lambda e: e.reciprocal(out=rden[0][:, 0:512], in_=ps[bE][:]), reads=psk(bE), writes=RDK[0], banks=[bE])
            for par in range(2):
                for h in range(2):
                    p0 = par * 64
                    c0 = par * 256 + h * 128
                    P.dve(lambda e, p0=p0, c0=c0, h=h: e.tensor_tensor(
                        out=s_yT.ap[p0:p0 + 64, 8 + 2 * h:10 + 2 * h, :].rearrange("p g (b t) -> p b g t", t=4),
                        in0=ps[bF][p0:p0 + 64, c0:c0 + 128].rearrange("p (b g t) -> p b g t", b=SB, g=2),
                        in1=rden[0][p0:p0 + 64, c0:c0 + 128].rearrange("p (b g t) -> p b g t", b=SB, g=2),
                        op=ALU.mult), reads=psk(bF) + RDK[0], writes=s_yT.keys(8 + 2 * h, 10 + 2 * h), banks=[bF])
            bS, bDn, bOm = 2, 3, 6
            Pm = Bt[0]
            for g in range(SB // 2):
                M_ = mc[g % 2]
                b0 = 2 * g
                P.dma(lambda e, M_=M_, b0=b0: e.dma_start(out=M_["K"].ap.rearrange("p (b k) f -> p b k f", b=2),
                                                          in_=cmk[l][b0:b0 + 2].rearrange("b (k p) f -> p b k f", p=128)),
                      writes=M_["K"].keys(), eng="pool")
                P.dma(lambda e, M_=M_, b0=b0: e.dma_start(out=M_["V"].ap.rearrange("p (b k) f -> p b k f", b=2),
                                                          in_=cmv[l][b0:b0 + 2].rearrange("b (k p) f -> p b k f", p=128)),
                      writes=M_["V"].keys(), eng="pool")
                for b2 in range(2):
                    bk = b2

                    def fnT(e, M_=M_, b2=b2, bk=bk):
                        ins = None
                        for hd in range(4):
                            for blk in range(2):
                                i = hd * 2 + blk
                                ins = e.transpose(out=psb[bk][:, i * 128:(i + 1) * 128],
                                                  in_=M_["K"].ap[:, b2 * 2 + blk, hd * 128:(hd + 1) * 128], identity=identb[:])
                        return ins
                    P.pe(fnT, reads=M_["K"].keys() + ["identb"], writes=psk(bk), banks=[bk])
                    evac_copy(alt(), M_["KT"].ap[:, b2 * 4:(b2 + 1) * 4, :], psb[bk][:].rearrange("p (h k) -> p h k", h=4),
                              psk(bk), M_["KT"].keys(b2 * 4, b2 * 4 + 4), [bk])

                def fnSm(e, M_=M_, b0=b0):
                    ins = None
                    for b2 in range(2):
                        bb = b0 + b2
                        for blk in range(2):
                            for hd in range(4):
                                c0 = bb * 32 + blk * 16 + hd * 4
                                ins = e.matmul(ps[bS][:, c0:c0 + 4], lhsT=M_["KT"].ap[:, b2 * 4 + hd, blk * 128:(blk + 1) * 128],
                                               rhs=s_qm.ap[:, hd, bb * 4:bb * 4 + 4], start=True, stop=True)
                    return ins
                P.pe(fnSm, reads=M_["KT"].keys() + s_qm.keys(), writes=psk(bS), banks=[bS])
                P.act(lambda e, b0=b0: e.activation(out=Pm[:, b0 * 32:b0 * 32 + 64], in_=ps[bS][:, b0 * 32:b0 * 32 + 64],
                                                    func=AF.Exp, scale=MEM_SCALE), reads=psk(bS), writes=KB(0), banks=[bS])

                def fnDm(e, b0=b0):
                    v = Pm[:, b0 * 32:b0 * 32 + 64].rearrange("p (b k x) -> p b k x", b=2, k=2)
                    o = ps[bDn][:, b0 * 16:b0 * 16 + 32].rearrange("p (b x) -> p b x", b=2)
                    e.matmul(o, lhsT=ones1[:], rhs=v[:, :, 0, :], start=True, stop=False)
                    return e.matmul(o, lhsT=ones1[:], rhs=v[:, :, 1, :], start=False, stop=True)
                P.pe(fnDm, reads=["ones1"] + KB(0), writes=psk(bDn), banks=[bDn])

                def fnOm(e, M_=M_, b0=b0):
                    ins = None
                    for b2 in range(2):
                        bb = b0 + b2
                        for hd in range(4):
                            oc = bb * 16 + hd * 4
                            for blk in range(2):
                                c0 = bb * 32 + blk * 16 + hd * 4
                                ins = e.matmul(ps[bOm][:, oc:oc + 4], lhsT=M_["V"].ap[:, b2 * 2 + blk, hd * 128:(hd + 1) * 128],
                                               rhs=Pm[:, c0:c0 + 4], start=(blk == 0), stop=(blk == 1))
                    return ins
                P.pe(fnOm, reads=M_["V"].keys() + KB(0), writes=psk(bOm), banks=[bOm])
            P.dve(lambda e: e.reciprocal(out=rden[1][:, 0:256], in_=ps[bDn][:, 0:256]), reads=psk(bDn), writes=RDK[1], banks=[bDn])
            P.dve(lambda e: e.tensor_tensor(
                out=s_yT.ap[:, 12:16, :].rearrange("p h (b t) -> p b h t", t=4),
                in0=ps[bOm][:, 0:256].rearrange("p (b h t) -> p b h t", b=SB, h=4),
                in1=rden[1][:, 0:256].rearrange("p (b h t) -> p b h t", b=SB, h=4), op=ALU.mult),
                reads=psk(bOm) + RDK[1], writes=s_yT.keys(12, 16), banks=[bOm])
            ffn_and_out(RS, l, T)

        if DBG["mem"]:
            mem_phase()
        for ti in range(DBG["ntiles"]):
            state["use_pool"] = ti > 0
            load_xT(xp[ti * TP:(ti + 1) * TP, :], TP)
            for l in range(DBG["nlayers"]):
                layer_prompt(l, ti)
            store_xT(yp[ti * TP:(ti + 1) * TP, :], TP)
        if with_sample:
            load_xT(xs, TS)
            for l in range(DBG["nlayers"]):
                layer_sample(l)
            store_xT(ys, TS)
        assert wst["next"] == len(seq), (wst["next"], len(seq))
        P.emit()
    return nc


_CACHE = {}


def kernel(**inputs):
    f = lambda a: np.ascontiguousarray(np.asarray(a, dtype=np.float32))
    inp = {k: f(v) for k, v in inputs.items()}
    with_sample = True
    if "nc" not in _CACHE:
        _CACHE["nc"] = build_program(with_sample)
    nc = _CACHE["nc"]
    shared = {k: inp[k] for k in ("g_mix_pre", "w_in", "w_pool", "pool_scale", "conv_w", "swa_sinks", "g_mem",
                                  "w_mem_kv", "w_out", "g_mix_post", "g_mlp_pre", "w_up", "w_down", "g_mlp_post")}
    in_maps = []
    for c in range(NCORES):
        m = dict(shared)
        m["xp"] = inp["x_prompt"][c]
        m["xs"] = inp["x_sample"][c * SB:(c + 1) * SB].reshape(TS, D)
        m["mem"] = inp["mem_prompt"][c]
        m["spool"] = np.ascontiguousarray(inp["state_pool"][:, c * SB:(c + 1) * SB])
        m["sconv"] = np.ascontiguousarray(inp["state_conv"][:, c * SB:(c + 1) * SB])
        m["ck"] = np.ascontiguousarray(inp["cache_swa_k"][:, c * SB:(c + 1) * SB].reshape(DEPTH, SB, 128, 128))
        m["cv"] = np.ascontiguousarray(inp["cache_swa_v"][:, c * SB:(c + 1) * SB].reshape(DEPTH, SB, 128, 128))
        m["cmk"] = np.ascontiguousarray(inp["cache_mem_k"][:, c * SB:(c + 1) * SB].reshape(DEPTH, SB, MEMT, DG))
        m["cmv"] = np.ascontiguousarray(inp["cache_mem_v"][:, c * SB:(c + 1) * SB].reshape(DEPTH, SB, MEMT, DG))
        in_maps.append(m)
    res = run_bass_kernel_spmd(nc, in_maps, core_ids=list(range(NCORES)))
    R = res.results
    cat = lambda name, axis: np.concatenate([np.asarray(R[c][name], dtype=np.float32) for c in range(NCORES)], axis=axis)
    stk = lambda name: np.stack([np.asarray(R[c][name], dtype=np.float32) for c in range(NCORES)], axis=1)
    y_prompt = np.stack([np.asarray(R[c]["yp"], dtype=np.float32) for c in range(NCORES)], axis=0)
    y_sample = cat("ys", 0).reshape(NCORES * SB, ST, D)
    return (
        y_prompt,
        y_sample,
        stk("o_pool_p"),
        cat("o_pool_s", 1),
        stk("o_conv_p"),
        cat("o_conv_s", 1),
        stk("o_k_p").reshape(DEPTH, NCORES, 128, 2, 64),
        cat("o_k_s", 1).reshape(DEPTH, NCORES * SB, 128, 2, 64),
        stk("o_v_p").reshape(DEPTH, NCORES, 128, 2, 64),
        cat("o_v_s", 1).reshape(DEPTH, NCORES * SB, 128, 2, 64),
        stk("o_mk_p").reshape(DEPTH, NCORES, MEMT, 4, 128),
        stk("o_mv_p").reshape(DEPTH, NCORES, MEMT, 4, 128),
    )
```

```python
import contextlib
import math
from types import SimpleNamespace
import numpy as np
import concourse.bass as bass
import concourse.mybir as mybir
from concourse.bass_utils import run_bass_kernel_spmd

F32 = mybir.dt.float32
BF16 = mybir.dt.bfloat16
ALU = mybir.AluOpType
AF = mybir.ActivationFunctionType

NCORES = 8
D = 2048
DEPTH = 2
SEQ = 2048
TP = 512
NPT = SEQ // TP
SB = 16
ST = 4
TS = SB * ST
MEMT = 256
DG = 512
D_IN = 3328
DFF = 8192
EPS = 1e-6
SWA_SCALE = 1.0 / 8.0
MEM_SCALE = 1.0 / math.sqrt(128.0)
NEG = -30000.0

ENGS = ("pe", "act", "dve", "pool", "sp")
SEM_LIMIT = 24000
NDMASEM = 12


class Op:
    __slots__ = ("eng", "fn", "deps", "dma", "inc", "cnt", "dsem", "dval", "prewait")

    def __init__(self, eng, fn, deps, dma):
        self.eng = eng
        self.fn = fn
        self.deps = deps
        self.dma = dma
        self.inc = False
        self.cnt = 0
        self.dsem = None
        self.dval = 0
        self.prewait = None


class Prog:
    def __init__(self, nc):
        self.nc = nc
        self.ops = {e: [] for e in ENGS}
        self.lw = {}
        self.rd = {}
        self.ndma = {e: 0 for e in ENGS}

    def add(self, eng, fn, reads=(), writes=(), dma=False, banks=()):
        idx = len(self.ops[eng])
        deps = set()
        for b in banks:
            k = ("__bank", b)
            w = self.lw.get(k)
            if w is not None and w[0] != eng:
                deps.add(w)
            self.lw[k] = (eng, idx)
        for k in reads:
            w = self.lw.get(k)
            if w is not None:
                deps.add(w)
        for k in writes:
            w = self.lw.get(k)
            if w is not None:
                deps.add(w)
            for r in self.rd.get(k, ()):
                deps.add(r)
        me = (eng, idx)
        deps.discard(me)
        if eng == "pe":
            deps = {d for d in deps if d[0] != "pe"}
        op = Op(eng, fn, deps, dma)
        if dma:
            n = self.ndma[eng]
            self.ndma[eng] = n + 1
            op.dsem = (eng, n % NDMASEM)
            op.dval = 16 * (n // NDMASEM + 1)
            if n >= NDMASEM:
                op.prewait = (op.dsem, op.dval - 16)
        self.ops[eng].append(op)
        for k in reads:
            self.rd.setdefault(k, []).append(me)
        for k in writes:
            self.lw[k] = me
            self.rd[k] = []
        return me

    def pe(self, fn, reads=(), writes=(), banks=()):
        return self.add("pe", fn, reads, writes, banks=banks)

    def act(self, fn, reads=(), writes=(), banks=()):
        return self.add("act", fn, reads, writes, banks=banks)

    def dve(self, fn, reads=(), writes=(), banks=()):
        return self.add("dve", fn, reads, writes, banks=banks)

    def pool(self, fn, reads=(), writes=(), banks=()):
        return self.add("pool", fn, reads, writes, banks=banks)

    def dma(self, fn, reads=(), writes=(), eng="sp"):
        return self.add(eng, fn, reads, writes, dma=True)

    def emit(self):
        nc = self.nc
        for e in ENGS:
            for op in self.ops[e]:
                for (de, di) in op.deps:
                    d = self.ops[de][di]
                    if not d.dma:
                        d.inc = True
        nsem = {}
        for e in ENGS:
            c = 0
            for op in self.ops[e]:
                if op.inc and not op.dma:
                    c += 1
                op.cnt = c
            nsem[e] = (c // SEM_LIMIT) + 1
        with contextlib.ExitStack() as st:
            csem = {e: [st.enter_context(nc.semaphore(f"c_{e}_{i}")) for i in range(nsem[e])]
                    for e in ENGS}
            dsem = {}
            for e in ENGS:
                for i in range(min(NDMASEM, self.ndma[e])):
                    dsem[(e, i)] = st.enter_context(nc.semaphore(f"d_{e}_{i}"))
            block = st.enter_context(nc.Block())
            engobj = {"pe": "tensor", "act": "scalar", "dve": "vector", "pool": "gpsimd", "sp": "sync"}

            def body(e):
                def run(eng):
                    waited = {}

                    def do_wait(key, sem, val):
                        if waited.get(key, 0) >= val:
                            return
                        eng.wait_ge(sem, val)
                        waited[key] = val

                    for op in self.ops[e]:
                        for (de, di) in sorted(op.deps):
                            d = self.ops[de][di]
                            if d.dma:
                                do_wait(("d",) + d.dsem, dsem[d.dsem], d.dval)
                            else:
                                si, v = divmod(d.cnt - 1, SEM_LIMIT)
                                do_wait(("c", de, si), csem[de][si], v + 1)
                        if op.prewait is not None:
                            do_wait(("d",) + op.prewait[0], dsem[op.prewait[0]], op.prewait[1])
                        ins = op.fn(eng)
                        if op.dma:
                            ins.then_inc(dsem[op.dsem], 16)
                        elif op.inc:
                            ins.then_inc(csem[e][(op.cnt - 1) // SEM_LIMIT], 1)
                    for (de, i), s in dsem.items():
                        if de == e:
                            n = self.ndma[e]
                            uses = (n - i + NDMASEM - 1) // NDMASEM
                            if uses > 0:
                                do_wait(("d", de, i), s, 16 * uses)
                return run

            for e in ENGS:
                if self.ops[e]:
                    getattr(block, engobj[e])(body(e))


class Region:
    def __init__(self, buf, name, off, C, W, dt):
        esz = 4 if dt == F32 else 2
        nb = C * W * esz
        assert off % 4 == 0
        sl = buf[:, off // 2:(off + nb) // 2]
        if dt == F32:
            sl = sl.bitcast(F32)
        self.ap = sl.rearrange("p (c w) -> p c w", c=C)
        self.name = name
        self.off = off
        self.cb = W * esz
        self.C = C
        self.end = off + nb

    def keys(self, c0=0, c1=None):
        if c1 is None:
            c1 = self.C
        lo = self.off + c0 * self.cb
        hi = self.off + c1 * self.cb
        return [(self.name, b) for b in range(lo // 1024, (hi - 1) // 1024 + 1)]


WIN_NSLAB = 7
SLAB_ELEMS = 8192


DBG = {"mem": True, "ntiles": NPT, "nlayers": DEPTH, "mixers": True, "ffn": True}


def build_program(with_sample=True):
    nc = bass.Bass("TRN2", target_bir_lowering=False)

    def din(name, shape, dt=F32):
        return nc.dram_tensor(name, list(shape), dt, kind="ExternalInput").ap()

    def dout(name, shape, dt=F32):
        return nc.dram_tensor(name, list(shape), dt, kind="ExternalOutput").ap()

    xp = din("xp", [SEQ, D])
    xs = din("xs", [TS, D])
    mem = din("mem", [MEMT, D])
    spool = din("spool", [DEPTH, SB, 15, DG])
    sconv = din("sconv", [DEPTH, SB, 2, DG])
    ck = din("ck", [DEPTH, SB, 128, 128])
    cv = din("cv", [DEPTH, SB, 128, 128])
    cmk = din("cmk", [DEPTH, SB, MEMT, DG])
    cmv = din("cmv", [DEPTH, SB, MEMT, DG])
    g_mix_pre = din("g_mix_pre", [DEPTH, D])
    w_in = din("w_in", [DEPTH, D, D_IN])
    w_pool = din("w_pool", [DEPTH, 4, 128, 128])
    pool_scale = din("pool_scale", [DEPTH, DG])
    conv_w = din("conv_w", [DEPTH, 3, DG])
    swa_sinks = din("swa_sinks", [DEPTH, 8])
    g_mem = din("g_mem", [DEPTH, D])
    w_mem_kv = din("w_mem_kv", [DEPTH, D, 2 * DG])
    w_out = din("w_out", [DEPTH, D, D])
    g_mix_post = din("g_mix_post", [DEPTH, D])
    g_mlp_pre = din("g_mlp_pre", [DEPTH, D])
    w_up = din("w_up", [DEPTH, D, DFF])
    w_down = din("w_down", [DEPTH, DFF, D])
    g_mlp_post = din("g_mlp_post", [DEPTH, D])

    yp = dout("yp", [SEQ, D])
    ys = dout("ys", [TS, D])
    o_pool_p = dout("o_pool_p", [DEPTH, 15, DG])
    o_pool_s = dout("o_pool_s", [DEPTH, SB, 15, DG])
    o_conv_p = dout("o_conv_p", [DEPTH, 2, DG])
    o_conv_s = dout("o_conv_s", [DEPTH, SB, 2, DG])
    o_k_p = dout("o_k_p", [DEPTH, 128, 128])
    o_k_s = dout("o_k_s", [DEPTH, SB, 128, 128])
    o_v_p = dout("o_v_p", [DEPTH, 128, 128])
    o_v_s = dout("o_v_s", [DEPTH, SB, 128, 128])
    o_mk_p = dout("o_mk_p", [DEPTH, MEMT, DG])
    o_mv_p = dout("o_mv_p", [DEPTH, MEMT, DG])

    def scratch(name, nslab):
        return nc.dram_tensor(name, [DEPTH, nslab, 128, SLAB_ELEMS], BF16, kind="Internal").ap()

    s_in = scratch("s_in", WIN_NSLAB)
    s_out = scratch("s_out", 4)
    s_up = scratch("s_up", 16)
    s_down = scratch("s_down", 16)

    P = Prog(nc)
    st = contextlib.ExitStack()
    with st:
        def sb(name, shape, dt):
            return st.enter_context(nc.sbuf_tensor(name, list(shape), dt))

        xT = sb("xT", [128, 16, TP], F32)
        U1 = sb("U1", [128, 16384], BF16)
        BIG = sb("BIG", [128, 32768], BF16)
        NWB = 2
        wbuf = [sb(f"wbuf{i}", [128, SLAB_ELEMS], BF16) for i in range(NWB)]
        ident = sb("ident", [128, 128], F32)
        identb = sb("identb", [128, 128], BF16)
        onesD = sb("onesD", [128, 128], BF16)
        ones1 = sb("ones1", [128, 128], BF16)
        mask_cat = sb("mask_cat", [128, 512], BF16)
        mask_first = sb("mask_first", [128, 512], BF16)
        mask_sc = sb("mask_sc", [128, 256], BF16)
        mask_sn = sb("mask_sn", [128, 256], BF16)
        esink_s = sb("esink_s", [1, DEPTH, 2, 256], BF16)
        gv = sb("gv", [128, 5, DEPTH, 16], F32)
        pscale = sb("pscale", [128, DEPTH, 4], F32)
        convw = sb("convw", [128, DEPTH, 3, 4], F32)
        wpool = sb("wpool", [128, DEPTH, 4, 128], BF16)
        sinks_sb = sb("sinks_sb", [1, DEPTH * 8], F32)
        esink = sb("esink", [1, DEPTH, 2, 512], BF16)
        invcnt = sb("invcnt", [128, 4, 16], F32)
        rtm = sb("rtm", [128, 8], F32)
        Fs = [sb(f"F{i}", [128, 16 + TP], F32) for i in range(6)]
        Bt = [sb(f"Bt{i}", [128, TP], BF16) for i in range(4)]
        dTp = [sb(f"dTp{i}", [128, TP], BF16) for i in range(2)]
        Vd = sb("Vd", [128, 5, 2, 128], BF16)
        carry_u = sb("carry_u", [128, DEPTH, 4, 16], F32)
        carry_v = sb("carry_v", [128, DEPTH, 4, 2], F32)
        carry_k = sb("carry_k", [128, DEPTH, 2, 128], BF16)
        carry_V = sb("carry_V", [128, DEPTH, 2, 128], BF16)
        mkT = sb("mkT", [128, DEPTH, 4, MEMT], BF16)
        mvv = sb("mvv", [128, DEPTH, 2, DG], BF16)

        ps = [st.enter_context(nc.psum_tensor(f"ps{i}", [128, 512], F32)) for i in range(8)]

        rstd = Fs[0]
        cacc = [Fs[1], Fs[2]]
        rtmp = [Fs[1], Fs[2]]
        rden = [Fs[0], Fs[5]]
        RDK = [[("F", 0)], [("F", 5)]]
        ptmp = [Fs[3], Fs[4]]
        tmst = [Fs[3], Fs[4]]
        hcst = Fs[5]
        mtmp = Fs[5]
        pfix = Fs[5]
        epsq = Fs[5]
        sq = Bt
        Pt = Bt
        dT = [Bt[0], Bt[1]]
        KF = lambda i: [("F", i)]
        KB = lambda i: [("Bt", i)]

        xstage = Region(U1, "U1", 0, 4, D, F32)
        hT = Region(U1, "U1", 0, 16, TP, BF16)
        mixT = Region(U1, "U1", 0, 16, TP, F32)
        off = 0
        u_ext = Region(BIG, "BIG", off, 4, 16 + TP, F32); off = u_ext.end
        hcR = Region(BIG, "BIG", off, 4, TP, F32); off = hcR.end
        gbR = Region(BIG, "BIG", off, 4, TP, F32); off = gbR.end
        v_ext = Region(BIG, "BIG", off, 4, 2 + TP, F32); off = v_ext.end
        qR = Region(BIG, "BIG", off, 4, TP, BF16); off = qR.end
        kext = Region(BIG, "BIG", off, 2, 128 + TP, BF16); off = kext.end
        qmR = Region(BIG, "BIG", off, 4, TP, BF16); off = qmR.end
        yT = Region(BIG, "BIG", off, 16, TP, BF16); off = yT.end
        assert off <= 65536, off
        hidT = Region(BIG, "BIG", 0, 64, TP, BF16)
        RP = SimpleNamespace(hT=hT, mixT=mixT, yT=yT, hidT=hidT)

        state = {"bank": 0, "alt": 0, "use_pool": False}

        def alt():
            state["alt"] ^= 1
            return "act" if state["alt"] else "dve"

        def evac_copy(eng, out_ap, in_ap, reads, writes, banks=()):
            if eng == "act":
                P.act(lambda e: e.activation(out=out_ap, in_=in_ap, func=AF.Copy), reads, writes, banks)
            else:
                P.dve(lambda e: e.tensor_copy(out=out_ap, in_=in_ap), reads, writes, banks)

        def psk(b):
            return [("ps", b)]

        P.pool(lambda e: e.memset(ident[:], 1.0), writes=["ident"])
        P.pool(lambda e: e.affine_select(out=ident[:], in_=ident[:], pattern=[[-1, 128]],
                                         compare_op=ALU.is_equal, fill=0.0, base=0, channel_multiplier=1),
               reads=["ident"], writes=["ident"])
        P.dve(lambda e: e.tensor_copy(out=identb[:], in_=ident[:]), reads=["ident"], writes=["identb"])
        P.pool(lambda e: e.memset(onesD[:], 1.0 / D), writes=["onesD"])
        P.pool(lambda e: e.memset(ones1[:], 1.0), writes=["ones1"])
        P.pool(lambda e: e.memset(mtmp[:, 0:512], 0.0), writes=[("F", 5)])
        P.pool(lambda e: e.affine_select(out=mtmp[:, 0:256].rearrange("p (a q) -> p a q", a=2),
                                         in_=mtmp[:, 0:256].rearrange("p (a q) -> p a q", a=2),
                                         pattern=[[0, 2], [1, 128]], compare_op=ALU.is_ge, fill=NEG,
                                         base=0, channel_multiplier=-1), reads=[("F", 5)], writes=[("F", 5)])
        P.pool(lambda e: e.affine_select(out=mtmp[:, 256:512].rearrange("p (a q) -> p a q", a=2),
                                         in_=mtmp[:, 256:512].rearrange("p (a q) -> p a q", a=2),
                                         pattern=[[0, 2], [-1, 128]], compare_op=ALU.is_ge, fill=NEG,
                                         base=-1, channel_multiplier=1), reads=[("F", 5)], writes=[("F", 5)])
        P.dve(lambda e: e.tensor_copy(out=mask_cat[:], in_=mtmp[:, 0:512]), reads=[("F", 5)], writes=["mask_cat"])
        P.dve(lambda e: e.tensor_copy(out=mask_first[:, 0:256], in_=mtmp[:, 0:256]), reads=[("F", 5)], writes=["mask_first"])
        P.pool(lambda e: e.memset(mask_first[:, 256:512], NEG), reads=["mask_first"], writes=["mask_first"])
        P.pool(lambda e: e.memset(mtmp[:, 0:512], 0.0), reads=[("F", 5)], writes=[("F", 5)])
        P.pool(lambda e: e.affine_select(out=mtmp[:, 0:256].rearrange("p (a t) -> p a t", t=4),
                                         in_=mtmp[:, 0:256].rearrange("p (a t) -> p a t", t=4),
                                         pattern=[[0, 64], [-1, 4]], compare_op=ALU.is_ge, fill=NEG,
                                         base=-1, channel_multiplier=1), reads=[("F", 5)], writes=[("F", 5)])
        P.pool(lambda e: e.affine_select(out=mtmp[:, 256:512].rearrange("p (a t) -> p a t", t=4),
                                         in_=mtmp[:, 256:512].rearrange("p (a t) -> p a t", t=4),
                                         pattern=[[0, 64], [1, 4]], compare_op=ALU.is_ge, fill=NEG,
                                         base=0, channel_multiplier=-1), reads=[("F", 5)], writes=[("F", 5)])
        P.dve(lambda e: e.tensor_copy(out=mask_sc[:], in_=mtmp[:, 0:256]), reads=[("F", 5)], writes=["mask_sc"])
        P.dve(lambda e: e.tensor_copy(out=mask_sn[:], in_=mtmp[:, 256:512]), reads=[("F", 5)], writes=["mask_sn"])
        for gi in range(4):
            w = 2 << gi
            for j in range(16):
                P.pool(lambda e, gi=gi, j=j, w=w: e.memset(invcnt[:, gi, j:j + 1], 1.0 / min(j + 1, w)),
                       writes=[("invcnt", gi, j)])
        INVK = [("invcnt", gi, j) for gi in range(4) for j in range(16)]
        P.pool(lambda e: e.memset(carry_u[:], 0.0), writes=["carry_u"])
        P.pool(lambda e: e.memset(carry_v[:], 0.0), writes=["carry_v"])
        P.pool(lambda e: e.memset(carry_k[:], 0.0), writes=["carry_k"])
        P.pool(lambda e: e.memset(carry_V[:], 0.0), writes=["carry_V"])

        for i, g in enumerate((g_mix_pre, g_mem, g_mix_post, g_mlp_pre, g_mlp_post)):
            for l in range(DEPTH):
                P.dma(lambda e, i=i, l=l, g=g: e.dma_start(out=gv[:, i, l, :],
                                                            in_=g[l].rearrange("(c p) -> p c", p=128),
                                                            allow_slow_non_contiguous=True),
                      writes=["gv"])
        for l in range(DEPTH):
            P.dma(lambda e, l=l: e.dma_start(out=pscale[:, l, :], in_=pool_scale[l].rearrange("(c p) -> p c", p=128),
                                             allow_slow_non_contiguous=True), writes=["pscale"])
            for k in range(3):
                P.dma(lambda e, l=l, k=k: e.dma_start(out=convw[:, l, k, :],
                                                      in_=conv_w[l, k].rearrange("(c p) -> p c", p=128),
                                                      allow_slow_non_contiguous=True), writes=["convw"])
        P.dma(lambda e: e.dma_start(out=sinks_sb[:], in_=swa_sinks.rearrange("l j -> (l j)").rearrange("(o n) -> o n", o=1)),
              writes=["sinks_sb"])
        for l in range(DEPTH):
            P.dma(lambda e, l=l: e.dma_start(out=wpool[:, l, :, :], in_=w_pool[l].rearrange("g c d -> c g d")),
                  writes=["wpool"], eng="pool")
        P.act(lambda e: e.activation(out=sinks_sb[:], in_=sinks_sb[:], func=AF.Exp), reads=["sinks_sb"], writes=["sinks_sb"])
        for l in range(DEPTH):
            for h in range(2):
                for par in range(2):
                    for gg in range(2):
                        j = l * 8 + 4 * h + 2 * gg + par
                        c0 = par * 256 + gg * 128
                        P.dve(lambda e, l=l, h=h, j=j, c0=c0: e.tensor_copy(
                            out=esink[0:1, l, h, c0:c0 + 128], in_=sinks_sb[0:1, j:j + 1].broadcast_to([1, 128])),
                            reads=["sinks_sb"], writes=[("esink", l, h, c0)])
        ESK = lambda l, h: [("esink", l, h, c0) for c0 in (0, 128, 256, 384)]
        for l in range(DEPTH):
            for par in range(2):
                for h in range(2):
                    for gg in range(2):
                        j = l * 8 + 4 * h + 2 * gg + par
                        P.dve(lambda e, l=l, par=par, h=h, gg=gg, j=j: e.tensor_copy(
                            out=esink_s[0:1, l, par, h * 128:(h + 1) * 128].rearrange("o (b g t) -> o b g t", b=SB, g=2)[:, :, gg, :],
                            in_=sinks_sb[0:1, j:j + 1].unsqueeze(2).to_broadcast([1, SB, 4])),
                            reads=["sinks_sb"], writes=[("esink_s", l)])

        SCR = {"in": s_in, "out": s_out, "up": s_up, "down": s_down}
        SRC = {"mem": w_mem_kv, "in": w_in, "out": w_out, "up": w_up, "down": w_down}

        def slab_pieces(kind, l, s):
            src = SRC[kind][l].rearrange("(k p) n -> p k n", p=128)
            if kind == "down":
                return 64, [(0, 128, src[:, :, s * 128:(s + 1) * 128])]
            if kind != "in" or s < 5:
                return 16, [(0, 512, src[:, :, s * 512:(s + 1) * 512])]
            if s == 5:
                pcs = []
                for h in range(2):
                    for r in range(2):
                        c0 = h * 128 + r * 64
                        pcs.append((c0, c0 + 64, src[:, :, 2560 + h * 64:2560 + h * 64 + 64]))
                pcs.append((256, 512, src[:, :, 2816:3072]))
                return 16, pcs
            return 16, [(0, 256, src[:, :, 3072:3328]), (256, 512, src[:, :, 2560:2816])]

        seq = []
        seen = set()

        def add_seq(tag):
            seq.append((tag, tag not in seen))
            seen.add(tag)
        if DBG["mem"]:
            for l in range(DEPTH):
                for s in range(2):
                    add_seq(("mem", l, s))
        tiles = [("p", i) for i in range(DBG["ntiles"])] + ([("s", 0)] if with_sample else [])
        for t in tiles:
            for l in range(DBG["nlayers"]):
                for s in range(WIN_NSLAB):
                    add_seq(("in", l, s))
                if not DBG["ffn"]:
                    continue
                for s in range(4):
                    add_seq(("out", l, s))
                for s in range(16):
                    add_seq(("up", l, s))
                for s in range(16):
                    add_seq(("down", l, s))
        wst = {"issued": 0, "next": 0}
        PREF = 1

        def w_issue(upto):
            while wst["issued"] < min(upto, len(seq)):
                i = wst["issued"]
                tag, first_use = seq[i]
                kind, l, s_ = tag
                bi = i % NWB
                if first_use:
                    KC, pcs = slab_pieces(kind, l, s_)
                    wv = wbuf[bi][:].rearrange("p (k n) -> p k n", k=KC)
                    for (c0, c1, src) in pcs:
                        P.dma(lambda e, wv=wv, c0=c0, c1=c1, src=src: e.dma_start(out=wv[:, :, c0:c1], in_=src),
                              writes=[("wbuf", bi)], eng="pool")
                    if kind != "mem":
                        P.dma(lambda e, kind=kind, l=l, s_=s_, bi=bi: e.dma_start(out=SCR[kind][l, s_], in_=wbuf[bi][:]),
                              reads=[("wbuf", bi)], writes=[("scr",) + tag])
                else:
                    P.dma(lambda e, kind=kind, l=l, s_=s_, bi=bi: e.dma_start(out=wbuf[bi][:], in_=SCR[kind][l, s_]),
                          reads=[("scr",) + tag], writes=[("wbuf", bi)])
                wst["issued"] += 1

        def w_next(tag, KC):
            i = wst["next"]
            assert seq[i][0] == tag, (seq[i][0], tag)
            w_issue(i + 1 + PREF)
            wst["next"] += 1
            bi = i % NWB
            return wbuf[bi][:].rearrange("p (k n) -> p k n", k=KC), [("wbuf", bi)]

        def nbank(lo=0, hi=6):
            b = state["bank"]
            state["bank"] = b + 1
            return lo + b % (hi - lo)

        def load_xT(src, T):
            NG = (T + 127) // 128
            rows = min(128, T)
            for g in range(NG):
                P.dma(lambda e, g=g: e.dma_start(out=xstage.ap[0:rows, g, :], in_=src[g * 128:g * 128 + rows, :]),
                      writes=xstage.keys(g, g + 1))
                for cb in range(4):
                    b = nbank(0, 4)

                    def fn(e, g=g, cb=cb, b=b):
                        ins = None
                        for j in range(4):
                            c = cb * 4 + j
                            ins = e.transpose(out=ps[b][:, j * 128:j * 128 + rows],
                                              in_=xstage.ap[0:rows, g, c * 128:(c + 1) * 128],
                                              identity=ident[0:rows, 0:rows])
                        return ins
                    P.pe(fn, reads=xstage.keys(g, g + 1) + ["ident"], writes=psk(b), banks=[b])
                    evac_copy(alt(), xT[:, cb * 4:cb * 4 + 4, g * 128:g * 128 + rows],
                              ps[b][:].rearrange("p (j t) -> p j t", j=4)[:, :, 0:rows],
                              psk(b), [("xT", cb * 4 + j) for j in range(4)], [b])

        def store_xT(dst, T):
            NG = (T + 127) // 128
            rows = min(128, T)
            for g in range(NG):
                for cb in range(4):
                    b = nbank(0, 4)

                    def fn(e, g=g, cb=cb, b=b):
                        ins = None
                        for j in range(4):
                            c = cb * 4 + j
                            ins = e.transpose(out=ps[b][0:rows, j * 128:(j + 1) * 128],
                                              in_=xT[:, c, g * 128:g * 128 + rows], identity=ident[:])
                        return ins
                    P.pe(fn, reads=[("xT", cb * 4 + j) for j in range(4)] + ["ident"], writes=psk(b), banks=[b])
                    evac_copy(alt(), xstage.ap[0:rows, g, cb * 512:(cb + 1) * 512], ps[b][0:rows, :],
                              psk(b), xstage.keys(g, g + 1), [b])
                P.dma(lambda e, g=g: e.dma_start(out=dst[g * 128:g * 128 + rows, :], in_=xstage.ap[0:rows, g, :]),
                      reads=xstage.keys(g, g + 1), writes=[("ydst", g)], eng="act")

        def norm_stats(src_ap, src_keys, T, nch=16, mode="rstd"):
            b = 7
            for c in range(nch):
                sb_ = sq[c % 4]
                P.act(lambda e, c=c, sb_=sb_: e.activation(out=sb_[:, 0:T], in_=src_ap(c), func=AF.Square),
                      reads=src_keys(c), writes=[("Bt", c % 4)])
                P.pe(lambda e, c=c, sb_=sb_: e.matmul(ps[b][:, 0:T], lhsT=onesD[:], rhs=sb_[:, 0:T],
                                                      start=(c == 0), stop=(c == nch - 1)),
                     reads=[("Bt", c % 4), "onesD"], writes=psk(b), banks=[b])
            if mode == "epsq":
                P.act(lambda e: e.activation(out=epsq[:, 0:T], in_=ps[b][:, 0:T], func=AF.Square,
                                             bias=EPS * math.sqrt(EPS), scale=math.sqrt(EPS)),
                      reads=psk(b), writes=[("F", 5)], banks=[b])
                return
            if mode == "rstd_q":
                P.dve(lambda e: e.tensor_tensor(out=rstd[:, 0:T], in0=ps[b][:, 0:T], in1=epsq[:, 0:T], op=ALU.add),
                      reads=psk(b) + [("F", 5)], writes=[("F", 0)], banks=[b])
                P.act(lambda e: e.activation(out=rstd[:, 0:T], in_=rstd[:, 0:T], func=AF.Sqrt),
                      reads=[("F", 0)], writes=[("F", 0)])
            else:
                P.act(lambda e: e.activation(out=rstd[:, 0:T], in_=ps[b][:, 0:T], func=AF.Sqrt, bias=EPS, scale=1.0),
                      reads=psk(b), writes=[("F", 0)], banks=[b])
            P.dve(lambda e: e.reciprocal(out=rstd[:, 0:T], in_=rstd[:, 0:T]), reads=[("F", 0)], writes=[("F", 0)])

        def xT_ap(T):
            return (lambda c: xT[:, c, 0:T]), (lambda c: [("xT", c)])

        def prenorm_to_hT(R, gidx, l, T):
            a, k = xT_ap(T)
            norm_stats(a, k, T)
            for c in range(16):
                if c % 2 == 1 and state["use_pool"]:
                    tb = Fs[1 + (c // 2) % 2]
                    tk = KF(1 + (c // 2) % 2)
                    P.pool(lambda e, c=c, tb=tb: e.tensor_tensor(out=tb[:, 0:T], in0=xT[:, c, 0:T], in1=rstd[:, 0:T],
                                                                 op=ALU.mult),
                           reads=[("xT", c), ("F", 0)], writes=tk)
                    P.act(lambda e, c=c, tb=tb: e.activation(out=R.hT.ap[:, c, 0:T], in_=tb[:, 0:T], func=AF.Copy,
                                                             scale=gv[:, gidx, l, c:c + 1]),
                          reads=tk + ["gv"], writes=R.hT.keys(c, c + 1))
                else:
                    P.dve(lambda e, c=c: e.scalar_tensor_tensor(out=R.hT.ap[:, c, 0:T], in0=xT[:, c, 0:T],
                                                                scalar=gv[:, gidx, l, c:c + 1], in1=rstd[:, 0:T],
                                                                op0=ALU.mult, op1=ALU.mult),
                          reads=[("xT", c), "gv", ("F", 0)], writes=R.hT.keys(c, c + 1))

        def postnorm_residual(R, gidx, l, T, mode="rstd", after_chunk=None):
            mixT_ = R.mixT
            norm_stats(lambda c: mixT_.ap[:, c, 0:T], lambda c: mixT_.keys(c, c + 1), T, mode=mode)
            def scale_chunk(c):
                P.dve(lambda e, c=c: e.scalar_tensor_tensor(out=mixT_.ap[:, c, 0:T], in0=mixT_.ap[:, c, 0:T],
                                                            scalar=gv[:, gidx, l, c:c + 1], in1=rstd[:, 0:T],
                                                            op0=ALU.mult, op1=ALU.mult),
                      reads=mixT_.keys(c, c + 1) + ["gv", ("F", 0)], writes=mixT_.keys(c, c + 1))

            def add_chunk(c):
                (P.pool if (state["use_pool"] and c % 2 == 1) else P.dve)(
                    lambda e, c=c: e.tensor_tensor(out=xT[:, c, 0:T], in0=xT[:, c, 0:T], in1=mixT_.ap[:, c, 0:T],
                                                   op=ALU.add),
                    reads=mixT_.keys(c, c + 1) + [("xT", c)], writes=[("xT", c)])
                if after_chunk is not None:
                    after_chunk(c)
            for c in range(17):
                if c < 16:
                    scale_chunk(c)
                if c >= 1:
                    add_chunk(c - 1)

        def proj_chunk(sap, skeys, KC, j, inR, T, b, fine=False):
            if fine:
                for kc in range(KC):
                    P.pe(lambda e, kc=kc: e.matmul(ps[b][:, 0:T], lhsT=sap[:, kc, j * 128:(j + 1) * 128],
                                                   rhs=inR.ap[:, kc, 0:T], start=(kc == 0), stop=(kc == KC - 1)),
                         reads=skeys + inR.keys(kc, kc + 1), writes=psk(b), banks=[b])
                return

            def fn(e):
                ins = None
                for kc in range(KC):
                    ins = e.matmul(ps[b][:, 0:T], lhsT=sap[:, kc, j * 128:(j + 1) * 128], rhs=inR.ap[:, kc, 0:T],
                                   start=(kc == 0), stop=(kc == KC - 1))
                return ins
            P.pe(fn, reads=skeys + inR.keys(), writes=psk(b), banks=[b])

        def tm_chunk(sap, skeys, KC, c0, ncols, inR, t0, M, b):
            def fn(e):
                ins = None
                for kc in range(KC):
                    ins = e.matmul(ps[b][0:M, 0:ncols], lhsT=inR.ap[:, kc, t0:t0 + M], rhs=sap[:, kc, c0:c0 + ncols],
                                   start=(kc == 0), stop=(kc == KC - 1))
                return ins
            P.pe(fn, reads=skeys + inR.keys(), writes=psk(b), banks=[b])

        def mem_phase():
            load_xT(mem, MEMT)
            a, k = xT_ap(MEMT)
            norm_stats(a, k, MEMT)
            for l in range(DEPTH):
                for c in range(16):
                    P.dve(lambda e, c=c, l=l: e.scalar_tensor_tensor(out=hT.ap[:, c, 0:MEMT], in0=xT[:, c, 0:MEMT],
                                                                     scalar=gv[:, 1, l, c:c + 1], in1=rstd[:, 0:MEMT],
                                                                     op0=ALU.mult, op1=ALU.mult),
                          reads=[("xT", c), "gv", ("F", 0)], writes=hT.keys(c, c + 1))
                for s in range(2):
                    sap, skeys = w_next(("mem", l, s), 16)
                    if s == 0:
                        for j in range(4):
                            b = nbank()
                            proj_chunk(sap, skeys, 16, j, hT, MEMT, b)
                            evac_copy(alt(), mkT[:, l, j, :], ps[b][:, 0:MEMT], psk(b), [("mkT", l)], [b])
                    for g in range(2):
                        b = nbank()
                        tm_chunk(sap, skeys, 16, 0, 512, hT, g * 128, 128, b)
                        stg = tmst[g % 2]
                        evac_copy("act", stg[:, 0:512], ps[b][:], psk(b), [("F", 3 + g % 2)], [b])
                        if s == 1:
                            P.dve(lambda e, g=g, b=b, l=l: e.tensor_copy(out=mvv[:, l, g, :], in_=ps[b][:]),
                                  reads=psk(b), writes=[("mvv", l)], banks=[b])
                        dst = (o_mk_p if s == 0 else o_mv_p)[l, g * 128:(g + 1) * 128, :]
                        P.dma(lambda e, dst=dst, stg=stg: e.dma_start(out=dst, in_=stg[:, 0:512]),
                              reads=[("F", 3 + g % 2)], writes=[("omem", l, s, g)], eng="act")

        def pool_prompt(l, T, first):
            P.dve(lambda e: e.tensor_copy(out=u_ext.ap[:, :, 0:16], in_=carry_u[:, l, :, :]),
                  reads=["carry_u"], writes=u_ext.keys())
            L = 16 + T
            for gi in range(4):
                w = 2 << gi
                src = u_ext.ap[:, gi, :]
                srck = u_ext.keys(gi, gi + 1)
                cur, curk = src, srck
                for k in range(1, gi + 2):
                    sh = 1 << (k - 1)
                    lo = (1 << k) - 1
                    dst = ptmp[k % 2]
                    P.dve(lambda e, cur=cur, dst=dst, lo=lo, sh=sh: e.tensor_tensor(
                        out=dst[:, lo:L], in0=cur[:, lo:L], in1=cur[:, lo - sh:L - sh], op=ALU.add),
                        reads=curk, writes=[("F", 3 + k % 2)])
                    cur, curk = dst[:], [("F", 3 + k % 2)]
                d = dTp[gi % 2]
                dk = [("dTp", gi % 2)]
                P.dve(lambda e, cur=cur, d=d, src=src, w=w: e.scalar_tensor_tensor(
                    out=d[:, 0:T], in0=cur[:, 16:16 + T], scalar=1.0 / w, in1=src[:, 16:16 + T],
                    op0=ALU.mult, op1=ALU.subtract), reads=curk + srck, writes=dk)
                if first:
                    P.dve(lambda e, cur=cur, gi=gi: e.tensor_tensor(out=pfix[:, 0:16], in0=cur[:, 16:32], in1=invcnt[:, gi, :],
                                                                   op=ALU.mult), reads=curk + INVK, writes=[("F", 5)])
                    P.dve(lambda e, d=d, src=src: e.tensor_tensor(out=d[:, 0:16], in0=pfix[:, 0:16], in1=src[:, 16:32],
                                                                 op=ALU.subtract), reads=[("F", 5)] + srck, writes=dk)
                yield
                b = state.get("fb", 6)
                P.pe(lambda e, b=b, gi=gi, d=d: e.matmul(ps[b][:, 0:T], lhsT=wpool[:, l, gi, :], rhs=d[:, 0:T],
                                                        start=True, stop=True),
                     reads=dk + ["wpool"], writes=psk(b), banks=[b])
                P.act(lambda e, b=b, gi=gi: e.activation(out=yT.ap[:, gi, 0:T], in_=ps[b][:, 0:T], func=AF.Copy,
                                                         scale=pscale[:, l, gi:gi + 1]),
                      reads=psk(b) + ["pscale"], writes=yT.keys(gi, gi + 1), banks=[b])
            P.dve(lambda e: e.tensor_copy(out=carry_u[:, l, :, :], in_=u_ext.ap[:, :, T:T + 16]),
                  reads=u_ext.keys(), writes=["carry_u"])
            yield

        def conv_prompt(l, T):
            P.dve(lambda e: e.tensor_copy(out=v_ext.ap[:, :, 0:2], in_=carry_v[:, l, :, :]),
                  reads=["carry_v"], writes=v_ext.keys())
            for c in range(4):
                vk = v_ext.keys(c, c + 1)
                ca = cacc[c % 2]
                cak = [("F", 1 + c % 2)]
                P.dve(lambda e, c=c: e.tensor_tensor(out=v_ext.ap[:, c, 2:2 + T], in0=v_ext.ap[:, c, 2:2 + T],
                                                     in1=hcR.ap[:, c, 0:T], op=ALU.mult),
                      reads=vk + hcR.keys(c, c + 1), writes=vk)
                P.act(lambda e, c=c, ca=ca: e.activation(out=ca[:, 0:T], in_=v_ext.ap[:, c, 0:T], func=AF.Copy,
                                                         scale=convw[:, l, 0, c:c + 1]),
                      reads=vk + ["convw"], writes=cak)
                for kk in (1, 2):
                    P.dve(lambda e, c=c, ca=ca, kk=kk: e.scalar_tensor_tensor(
                        out=ca[:, 0:T], in0=v_ext.ap[:, c, kk:kk + T], scalar=convw[:, l, kk, c:c + 1], in1=ca[:, 0:T],
                        op0=ALU.mult, op1=ALU.add), reads=vk + ["convw"] + cak, writes=cak)
                P.dve(lambda e, c=c, ca=ca: e.tensor_tensor(out=yT.ap[:, 4 + c, 0:T], in0=ca[:, 0:T],
                                                            in1=gbR.ap[:, c, 0:T], op=ALU.mult),
                      reads=cak + gbR.keys(c, c + 1), writes=yT.keys(4 + c, 5 + c))
                if c < 3:
                    yield
            P.dve(lambda e: e.tensor_copy(out=carry_v[:, l, :, :], in_=v_ext.ap[:, :, T:T + 2]),
                  reads=v_ext.keys(), writes=["carry_v"])
            yield

        def swa_prompt(l, T, first):
            NG = T // 128
            P.dve(lambda e: e.tensor_copy(out=kext.ap[:, :, 0:128], in_=carry_k[:, l, :, :]),
                  reads=["carry_k"], writes=kext.keys())
            it = 0
            for n in range(NG):
                for h in range(2):
                    base_b = 0 if it % 2 == 0 else 4
                    state["fb"] = 4 - base_b
                    it += 1
                    bD, bO = base_b + 2, base_b + 3
                    msk, mskk = (mask_first, "mask_first") if (first and n == 0) else (mask_cat, "mask_cat")
                    pts = []
                    for par in range(2):
                        b = base_b + par
                        p0 = par * 64

                        def fnS(e, b=b, p0=p0, msk=msk, h=h, n=n):
                            e.matmul(ps[b][:, 0:512], lhsT=identb[:], rhs=msk[:], start=True, stop=False)
                            ins = None
                            for part in range(2):
                                kc0 = 128 + n * 128 if part == 0 else n * 128
                                ins = e.matmul(ps[b][:, part * 256:(part + 1) * 256].rearrange("p (g q) -> p g q", g=2),
                                               lhsT=kext.ap[p0:p0 + 64, h, kc0:kc0 + 128],
                                               rhs=qR.ap[p0:p0 + 64, 2 * h:2 * h + 2, n * 128:(n + 1) * 128],
                                               start=False, stop=(part == 1))
                            return ins
                        P.pe(fnS, reads=["identb", mskk] + kext.keys(h, h + 1) + qR.keys(2 * h, 2 * h + 2),
                             writes=psk(b), banks=[b])
                        pi = (it % 2) * 2 + par
                        pt = Pt[pi]
                        P.act(lambda e, b=b, pt=pt: e.activation(out=pt[:], in_=ps[b][:], func=AF.Exp, scale=SWA_SCALE),
                              reads=psk(b), writes=KB(pi), banks=[b])
                        pts.append((pt, KB(pi)))

                    def fnD(e, pts=pts, h=h, bD=bD):
                        ins = None
                        for par in range(2):
                            pt = pts[par][0]
                            o = ps[bD][:, par * 256:(par + 1) * 256]
                            e.matmul(o, lhsT=ones1[:], rhs=pt[:, 0:256], start=True, stop=False)
                            e.matmul(o, lhsT=ones1[:], rhs=pt[:, 256:512], start=False, stop=False)
                            ins = e.matmul(o, lhsT=ones1[0:1, :], rhs=esink[0:1, l, h, par * 256:(par + 1) * 256],
                                           start=False, stop=True)
                        return ins
                    P.pe(fnD, reads=["ones1"] + ESK(l, h) + pts[0][1] + pts[1][1], writes=psk(bD), banks=[bD])

                    def fnO(e, pts=pts, h=h, bO=bO, n=n):
                        ins = None
                        for par in range(2):
                            pt = pts[par][0]
                            o = ps[bO][:, par * 256:(par + 1) * 256]
                            e.matmul(o, lhsT=Vd[:, n + 1, h, :], rhs=pt[:, 0:256], start=True, stop=False)
                            ins = e.matmul(o, lhsT=Vd[:, n, h, :], rhs=pt[:, 256:512], start=False, stop=True)
                        return ins
                    P.pe(fnO, reads=["Vd"] + pts[0][1] + pts[1][1], writes=psk(bO), banks=[bO])
                    yield
                    ri = it % 2
                    rd = rden[ri]
                    rdk = RDK[ri]
                    P.dve(lambda e, rd=rd, bD=bD: e.reciprocal(out=rd[:, 0:512], in_=ps[bD][:]),
                          reads=psk(bD), writes=rdk, banks=[bD])
                    for par in range(2):
                        p0 = par * 64
                        P.dve(lambda e, p0=p0, par=par, rd=rd, h=h, n=n, bO=bO: e.tensor_tensor(
                            out=yT.ap[p0:p0 + 64, 8 + 2 * h:10 + 2 * h, n * 128:(n + 1) * 128],
                            in0=ps[bO][p0:p0 + 64, par * 256:(par + 1) * 256].rearrange("p (g q) -> p g q", g=2),
                            in1=rd[p0:p0 + 64, par * 256:(par + 1) * 256].rearrange("p (g q) -> p g q", g=2),
                            op=ALU.mult),
                            reads=psk(bO) + rdk, writes=yT.keys(8 + 2 * h, 10 + 2 * h), banks=[bO])
                    yield
            P.dve(lambda e: e.tensor_copy(out=carry_k[:, l, :, :], in_=kext.ap[:, :, T:T + 128]),
                  reads=kext.keys(), writes=["carry_k"])
            P.dve(lambda e: e.tensor_copy(out=carry_V[:, l, :, :], in_=Vd[:, NG, :, :]),
                  reads=["Vd"], writes=["carry_V"])

        def mem_prompt(l, T):
            for hd in range(4):
                base_b = 0 if hd % 2 == 0 else 4
                state["fb"] = 4 - base_b
                bD, bO = base_b + 2, base_b + 3
                pts = []
                for kb in range(2):
                    b = base_b + kb
                    P.pe(lambda e, b=b, kb=kb, hd=hd: e.matmul(ps[b][:, 0:T], lhsT=mkT[:, l, hd, kb * 128:(kb + 1) * 128],
                                                              rhs=qmR.ap[:, hd, 0:T], start=True, stop=True),
                         reads=[("mkT", l)] + qmR.keys(hd, hd + 1), writes=psk(b), banks=[b])
                    pt = Pt[(hd % 2) * 2 + kb]
                    ptk = [("Bt", (hd % 2) * 2 + kb)]
                    P.act(lambda e, b=b, pt=pt: e.activation(out=pt[:, 0:T], in_=ps[b][:, 0:T], func=AF.Exp, scale=MEM_SCALE),
                          reads=psk(b), writes=ptk, banks=[b])
                    pts.append((pt, ptk))

                def fnD(e, pts=pts, bD=bD):
                    e.matmul(ps[bD][:, 0:T], lhsT=ones1[:], rhs=pts[0][0][:, 0:T], start=True, stop=False)
                    return e.matmul(ps[bD][:, 0:T], lhsT=ones1[:], rhs=pts[1][0][:, 0:T], start=False, stop=True)
                P.pe(fnD, reads=["ones1"] + pts[0][1] + pts[1][1], writes=psk(bD), banks=[bD])

                def fnO(e, pts=pts, hd=hd, bO=bO):
                    e.matmul(ps[bO][:, 0:T], lhsT=mvv[:, l, 0, hd * 128:(hd + 1) * 128], rhs=pts[0][0][:, 0:T],
                             start=True, stop=False)
                    return e.matmul(ps[bO][:, 0:T], lhsT=mvv[:, l, 1, hd * 128:(hd + 1) * 128], rhs=pts[1][0][:, 0:T],
                                    start=False, stop=True)
                P.pe(fnO, reads=[("mvv", l)] + pts[0][1] + pts[1][1], writes=psk(bO), banks=[bO])
                yield
                rd = rden[hd % 2]
                rdk = RDK[hd % 2]
                P.dve(lambda e, rd=rd, bD=bD: e.reciprocal(out=rd[:, 0:T], in_=ps[bD][:, 0:T]),
                      reads=psk(bD), writes=rdk, banks=[bD])
                P.dve(lambda e, rd=rd, bO=bO, hd=hd: e.tensor_tensor(out=yT.ap[:, 12 + hd, 0:T], in0=ps[bO][:, 0:T],
                                                                     in1=rd[:, 0:T], op=ALU.mult),
                      reads=psk(bO) + rdk, writes=yT.keys(12 + hd, 13 + hd), banks=[bO])
                yield

        def pre1_chunk(l, T):
            def f(c):
                P.act(lambda e, c=c: e.activation(out=hT.ap[:, c, 0:T], in_=xT[:, c, 0:T], func=AF.Copy,
                                                  scale=gv[:, 0, l, c:c + 1]),
                      reads=[("xT", c), "gv"], writes=hT.keys(c, c + 1))
            return f

        def layer_prompt(l, ti):
            T = TP
            first = (ti == 0)
            last = (ti == NPT - 1)
            if l == 0:
                for c in range(16):
                    pre1_chunk(0, T)(c)
            P.dve(lambda e: e.tensor_copy(out=Vd[:, 0, :, :], in_=carry_V[:, l, :, :]), reads=["carry_V"], writes=["Vd"])
            RK = [("F", 0)]

            def evac_scaled(out_ap, b, wkeys):
                P.dve(lambda e: e.tensor_tensor(out=out_ap, in0=ps[b][:, 0:T], in1=rstd[:, 0:T], op=ALU.mult),
                      reads=psk(b) + RK, writes=wkeys, banks=[b])

            def dest(m):
                if m < 4:
                    return u_ext.ap[:, m, 16:16 + T], u_ext.keys(m, m + 1)
                if m < 8:
                    return hcR.ap[:, m - 4, 0:T], hcR.keys(m - 4, m - 3)
                if m < 12:
                    return gbR.ap[:, m - 8, 0:T], gbR.keys(m - 8, m - 7)
                if m < 16:
                    return v_ext.ap[:, m - 12, 2:2 + T], v_ext.keys(m - 12, m - 11)
                if m < 20:
                    return qR.ap[:, m - 16, 0:T], qR.keys(m - 16, m - 15)
                if m < 22:
                    return kext.ap[:, m - 20, 128:128 + T], kext.keys(m - 20, m - 19)
                return qmR.ap[:, m - 22, 0:T], qmR.keys(m - 22, m - 21)
            for s in range(WIN_NSLAB):
                sap, skeys = w_next(("in", l, s), 16)
                pend = []
                for j in range(4):
                    m = s * 4 + j
                    if m >= 26:
                        continue
                    b = nbank()
                    proj_chunk(sap, skeys, 16, j, hT, T, b, fine=(m == 0))
                    if s == 0:
                        pend.append((m, b))
                    else:
                        o_, k_ = dest(m)
                        evac_scaled(o_, b, k_)
                if last and s == 0:
                    tm_chunk(sap, skeys, 16, 0, 512, hT, T - 16, 16, 6)
                if s == 0:
                    a_, k_ = xT_ap(T)
                    norm_stats(a_, k_, T, mode="rstd")
                    bq = nbank()

                    def fnr(e, bq=bq):
                        ins = None
                        for g in range(T // 128):
                            ins = e.matmul(ps[bq][:, g:g + 1], lhsT=rstd[:, g * 128:(g + 1) * 128], rhs=ident[:, 0:1],
                                           start=True, stop=True)
                        if last:
                            ins = e.matmul(ps[bq][0:16, 4:5], lhsT=rstd[:, T - 16:T], rhs=ident[:, 0:1], start=True, stop=True)
                        return ins
                    P.pe(fnr, reads=RK + ["ident"], writes=psk(bq), banks=[bq])
                    P.act(lambda e, bq=bq: e.activation(out=rtm[:, 0:8], in_=ps[bq][:, 0:8], func=AF.Copy),
                          reads=psk(bq), writes=["rtm"], banks=[bq])
                    for (m, b) in pend:
                        o_, k_ = dest(m)
                        evac_scaled(o_, b, k_)
                if last and s == 0:
                    b = 6
                    P.act(lambda e, b=b: e.activation(out=tmst[0][0:16, 0:512], in_=ps[b][0:16, :], func=AF.Copy,
                                                      scale=rtm[0:16, 4:5]),
                          reads=psk(b) + ["rtm"], writes=[("F", 3)], banks=[b])
                    P.dma(lambda e: e.dma_start(out=o_pool_p[l], in_=tmst[0][1:16, 0:512]), reads=[("F", 3)],
                          writes=[("o_pool_p", l)], eng="act")
                if last and s == 1:
                    b = 6
                    tm_chunk(sap, skeys, 16, 0, 512, hT, T - 16, 16, b)
                    P.act(lambda e, b=b: e.activation(out=hcst[0:16, 0:512], in_=ps[b][0:16, :], func=AF.Copy,
                                                      scale=rtm[0:16, 4:5]),
                          reads=psk(b) + ["rtm"], writes=[("F", 5)], banks=[b])
                if last and s == 3:
                    b = 6
                    tm_chunk(sap, skeys, 16, 0, 512, hT, T - 16, 16, b)
                    P.dve(lambda e, b=b: e.scalar_tensor_tensor(out=tmst[1][0:16, 0:512], in0=ps[b][0:16, :],
                                                                scalar=rtm[0:16, 4:5], in1=hcst[0:16, 0:512],
                                                                op0=ALU.mult, op1=ALU.mult),
                          reads=psk(b) + [("F", 5), "rtm"], writes=[("F", 4)], banks=[b])
                    P.dma(lambda e: e.dma_start(out=o_conv_p[l], in_=tmst[1][14:16, 0:512]), reads=[("F", 4)],
                          writes=[("o_conv_p", l)], eng="act")
                if s == 6:
                    for g in range(T // 128):
                        b = 6
                        tm_chunk(sap, skeys, 16, 256, 256, hT, g * 128, 128, b)
                        for h in range(2):
                            P.dve(lambda e, g=g, h=h, b=b: e.tensor_scalar(
                                out=Vd[:, g + 1, h, :].rearrange("p (r d) -> p r d", r=2),
                                in0=ps[b][:, 128 + h * 64:128 + (h + 1) * 64].unsqueeze(1).to_broadcast([128, 2, 64]),
                                scalar1=rtm[:, g:g + 1], scalar2=None, op0=ALU.mult),
                                reads=psk(b) + ["rtm"], writes=["Vd"], banks=[b])
                        if last and g == T // 128 - 1:
                            P.act(lambda e, b=b, g=g: e.activation(out=tmst[0][:, 0:256], in_=ps[b][:, 0:256], func=AF.Copy,
                                                                   scale=rtm[:, g:g + 1]),
                                  reads=psk(b) + ["rtm"], writes=[("F", 3)], banks=[b])
                            P.dma(lambda e: e.dma_start(out=o_k_p[l], in_=tmst[0][:, 0:128]), reads=[("F", 3)],
                                  writes=[("o_k_p", l)], eng="act")
                            P.dma(lambda e: e.dma_start(out=o_v_p[l], in_=tmst[0][:, 128:256]), reads=[("F", 3)],
                                  writes=[("o_v_p", l)], eng="act")
            if DBG["mixers"]:
                import itertools
                A = itertools.chain(swa_prompt(l, T, first), mem_prompt(l, T))
                B = itertools.chain(pool_prompt(l, T, first), conv_prompt(l, T))
                a_alive, b_alive = True, True
                while a_alive or b_alive:
                    if a_alive:
                        a_alive = next(A, "end") != "end"
                    if b_alive:
                        b_alive = next(B, "end") != "end"
                    if a_alive:
                        a_alive = next(A, "end") != "end"
            if DBG["ffn"]:
                ffn_and_out(RP, l, T, post2_hook=(pre1_chunk(l + 1, T) if l + 1 < DBG["nlayers"] else None))

        def ffn_and_out(R, l, T, post2_hook=None):
            for s in range(4):
                sap, skeys = w_next(("out", l, s), 16)
                for j in range(4):
                    m = s * 4 + j
                    b = nbank()
                    proj_chunk(sap, skeys, 16, j, R.yT, T, b)
                    evac_copy(alt(), R.mixT.ap[:, m, 0:T], ps[b][:, 0:T], psk(b), R.mixT.keys(m, m + 1), [b])
            def h2_chunk(c):
                P.act(lambda e, c=c: e.activation(out=R.hT.ap[:, c, 0:T], in_=xT[:, c, 0:T], func=AF.Copy,
                                                  scale=gv[:, 3, l, c:c + 1]),
                      reads=[("xT", c), "gv"], writes=R.hT.keys(c, c + 1))
            postnorm_residual(R, 2, l, T, after_chunk=h2_chunk)
            for s in range(16):
                sap, skeys = w_next(("up", l, s), 16)
                if s == 1:
                    a_, k_ = xT_ap(T)
                    norm_stats(a_, k_, T, mode="epsq")
                for j in range(4):
                    m = s * 4 + j
                    b = nbank()
                    proj_chunk(sap, skeys, 16, j, R.hT, T, b, fine=(m == 0))
                    rt = rtmp[m % 2]
                    rk = [("F", 1 + m % 2)]
                    P.act(lambda e, b=b, rt=rt: e.activation(out=rt[:, 0:T], in_=ps[b][:, 0:T], func=AF.Relu),
                          reads=psk(b), writes=rk, banks=[b])
                    P.dve(lambda e, m=m, rt=rt: e.tensor_tensor(out=R.hidT.ap[:, m, 0:T], in0=rt[:, 0:T], in1=rt[:, 0:T],
                                                                op=ALU.mult),
                          reads=rk, writes=R.hidT.keys(m, m + 1))
            for s in range(16):
                sap, skeys = w_next(("down", l, s), 64)
                b = nbank()
                proj_chunk(sap, skeys, 64, 0, R.hidT, T, b, fine=(s == 0))
                evac_copy(alt(), R.mixT.ap[:, s, 0:T], ps[b][:, 0:T], psk(b), R.mixT.keys(s, s + 1), [b])
            postnorm_residual(R, 4, l, T, mode="rstd_q", after_chunk=post2_hook)

        psb = [p_[:].bitcast(BF16) for p_ in ps]
        s_hidT = Region(BIG, "BIG", 0, 64, 64, BF16)
        s_yT = Region(BIG, "BIG", 8192, 16, 64, BF16)
        s_q = Region(BIG, "BIG", 10240, 4, 64, BF16)
        s_kx = Region(BIG, "BIG", 10752, 2, 64, BF16)
        s_qm = Region(BIG, "BIG", 11008, 4, 64, BF16)
        s_hc = Region(BIG, "BIG", 11520, 4, 64, F32)
        s_gb = Region(BIG, "BIG", 12544, 4, 64, F32)
        s_vx = Region(BIG, "BIG", 13568, 4, 96, F32)
        s_ux = Region(BIG, "BIG", 15104, 4, 304, F32)
        pst = Region(BIG, "BIG", 20480, 2, 512, F32)
        cst = Region(BIG, "BIG", 24576, 1, 512, F32)
        Kd = Region(BIG, "BIG", 26624, 16, 256, BF16)
        KTd = Region(BIG, "BIG", 34816, 32, 128, BF16)
        Vds = Region(BIG, "BIG", 43008, 16, 256, BF16)
        mc = [dict(K=Region(BIG, "BIG", 51200, 4, 512, BF16), KT=Region(BIG, "BIG", 55296, 8, 256, BF16),
                   V=Region(BIG, "BIG", 59392, 4, 512, BF16)),
              dict(K=Region(U1, "U1", 8192, 4, 512, BF16), KT=Region(U1, "U1", 12288, 8, 256, BF16),
                   V=Region(U1, "U1", 16384, 4, 512, BF16))]
        Vn = Region(U1, "U1", 20480, 16, 256, BF16)
        s_hT = Region(U1, "U1", 0, 16, 64, BF16)
        s_mixT = Region(U1, "U1", 2048, 16, 64, F32)
        RS = SimpleNamespace(hT=s_hT, mixT=s_mixT, yT=s_yT, hidT=s_hidT)

        def bt4(ap):
            return ap.rearrange("p (b t) -> p b t", t=4)

        def layer_sample(l):
            T = TS
            prenorm_to_hT(RS, 0, l, T)
            sp2 = spool[l].rearrange("b r f -> (b r) f")
            P.dma(lambda e: e.dma_start(out=pst.ap[0:128, 0, :], in_=sp2[0:128, :]), writes=pst.keys(0, 1))
            P.dma(lambda e: e.dma_start(out=pst.ap[0:112, 1, :], in_=sp2[128:240, :]), writes=pst.keys(1, 2))
            P.dma(lambda e: e.dma_start(out=cst.ap[0:32, 0, :], in_=sconv[l].rearrange("b r f -> (b r) f")), writes=cst.keys())
            Kd5 = Kd.ap.rearrange("p b (h r d) -> p b h r d", h=2, r=2)
            Vd5 = Vds.ap.rearrange("p b (h r d) -> p b h r d", h=2, r=2)
            for r in range(2):
                for h in range(2):
                    P.dma(lambda e, r=r, h=h: e.dma_start(out=Kd5[:, :, h, r, :],
                                                          in_=ck[l][:, :, h * 64:(h + 1) * 64].rearrange("b k d -> k b d")),
                          writes=Kd.keys(), eng="pool")
                    P.dma(lambda e, r=r, h=h: e.dma_start(out=Vd5[:, :, h, r, :],
                                                          in_=cv[l][:, :, h * 64:(h + 1) * 64].rearrange("b k d -> k b d")),
                          writes=Vds.keys(), eng="pool")
            P.dma(lambda e: e.dma_start(out=o_pool_s[l][:, 0:11, :], in_=spool[l][:, 4:15, :]), writes=[("o_pool_s", l, 0)], eng="pool")
            P.dma(lambda e: e.dma_start(out=o_k_s[l][:, 0:124, :], in_=ck[l][:, 4:128, :]), writes=[("o_k_s", l, 0)], eng="pool")
            P.dma(lambda e: e.dma_start(out=o_v_s[l][:, 0:124, :], in_=cv[l][:, 4:128, :]), writes=[("o_v_s", l, 0)], eng="pool")
            for gi in range(4):
                b = nbank(0, 4)

                def fn(e, gi=gi, b=b):
                    e.transpose(out=ps[b][:, 0:128], in_=pst.ap[0:128, 0, gi * 128:(gi + 1) * 128], identity=ident[:])
                    return e.transpose(out=ps[b][:, 128:240], in_=pst.ap[0:112, 1, gi * 128:(gi + 1) * 128],
                                       identity=ident[0:112, 0:112])
                P.pe(fn, reads=pst.keys() + ["ident"], writes=psk(b), banks=[b])
                u3 = s_ux.ap[:, gi, :].rearrange("p (b r) -> p b r", r=19)
                evac_copy(alt(), u3[:, :, 0:15], ps[b][:, 0:240].rearrange("p (b r) -> p b r", r=15), psk(b),
                          s_ux.keys(gi, gi + 1), [b])
            for c in range(4):
                b = nbank(0, 4)
                P.pe(lambda e, c=c, b=b: e.transpose(out=ps[b][:, 0:32], in_=cst.ap[0:32, 0, c * 128:(c + 1) * 128],
                                                     identity=ident[0:32, 0:32]),
                     reads=cst.keys() + ["ident"], writes=psk(b), banks=[b])
                v3 = s_vx.ap[:, c, :].rearrange("p (b r) -> p b r", r=6)
                evac_copy(alt(), v3[:, :, 0:2], ps[b][:, 0:32].rearrange("p (b r) -> p b r", r=2), psk(b),
                          s_vx.keys(c, c + 1), [b])
            for q4 in range(4):
                b = 4 + q4

                def fnk(e, q4=q4, b=b):
                    ins = None
                    for i in range(8):
                        idx = q4 * 8 + i
                        bb, h = idx // 2, idx % 2
                        ins = e.transpose(out=psb[b][:, i * 128:(i + 1) * 128], in_=Kd.ap[:, bb, h * 128:(h + 1) * 128],
                                          identity=identb[:])
                    return ins
                P.pe(fnk, reads=Kd.keys() + ["identb"], writes=psk(b), banks=[b])
                evac_copy(alt(), KTd.ap[:, q4 * 8:(q4 + 1) * 8, :], psb[b][:].rearrange("p (i k) -> p i k", i=8), psk(b),
                          KTd.keys(q4 * 8, q4 * 8 + 8), [b])
            for s in range(WIN_NSLAB):
                sap, skeys = w_next(("in", l, s), 16)
                for j in range(4):
                    m = s * 4 + j
                    if m >= 26:
                        continue
                    b = nbank(0, 4)
                    proj_chunk(sap, skeys, 16, j, s_hT, T, b, fine=(m == 0))
                    eng = alt()
                    src = ps[b][:, 0:T]
                    if m < 4:
                        u3 = s_ux.ap[:, m, :].rearrange("p (b r) -> p b r", r=19)
                        evac_copy(eng, u3[:, :, 15:19], bt4(src), psk(b), s_ux.keys(m, m + 1), [b])
                    elif m < 8:
                        evac_copy(eng, s_hc.ap[:, m - 4, :], src, psk(b), s_hc.keys(m - 4, m - 3), [b])
                    elif m < 12:
                        evac_copy(eng, s_gb.ap[:, m - 8, :], src, psk(b), s_gb.keys(m - 8, m - 7), [b])
                    elif m < 16:
                        v3 = s_vx.ap[:, m - 12, :].rearrange("p (b r) -> p b r", r=6)
                        evac_copy(eng, v3[:, :, 2:6], bt4(src), psk(b), s_vx.keys(m - 12, m - 11), [b])
                    elif m < 20:
                        evac_copy(eng, s_q.ap[:, m - 16, :], src, psk(b), s_q.keys(m - 16, m - 15), [b])
                    elif m < 22:
                        evac_copy(eng, s_kx.ap[:, m - 20, :], src, psk(b), s_kx.keys(m - 20, m - 19), [b])
                    else:
                        evac_copy(eng, s_qm.ap[:, m - 22, :], src, psk(b), s_qm.keys(m - 22, m - 21), [b])
                if s == 0:
                    b = 6
                    tm_chunk(sap, skeys, 16, 0, 512, s_hT, 0, 64, b)
                    evac_copy("act", tmst[0][0:64, 0:512], ps[b][0:64, :], psk(b), KF(3), [b])
                    P.dma(lambda e: e.dma_start(out=o_pool_s[l][:, 11:15, :], in_=tmst[0][0:64, 0:512]), reads=KF(3),
                          writes=[("o_pool_s", l, 1)], eng="pool")
                if s == 1:
                    b = 6
                    tm_chunk(sap, skeys, 16, 0, 512, s_hT, 0, 64, b)
                    evac_copy("act", hcst[0:64, 0:512], ps[b][0:64, :], psk(b), KF(5), [b])
                if s == 3:
                    b = 6
                    tm_chunk(sap, skeys, 16, 0, 512, s_hT, 0, 64, b)
                    P.dve(lambda e, b=b: e.tensor_tensor(out=tmst[1][0:64, 0:512], in0=ps[b][0:64, :], in1=hcst[0:64, 0:512],
                                                         op=ALU.mult), reads=psk(b) + KF(5), writes=KF(4), banks=[b])
                    for t in (2, 3):
                        for bb in range(SB):
                            P.dma(lambda e, t=t, bb=bb: e.dma_start(out=o_conv_s[l][bb, t - 2:t - 1, :],
                                                                    in_=tmst[1][bb * 4 + t:bb * 4 + t + 1, 0:512]),
                                  reads=KF(4), writes=[("o_conv_s", l, t, bb)], eng="pool")
                if s == 6:
                    b = 6
                    tm_chunk(sap, skeys, 16, 256, 256, s_hT, 0, 64, b)
                    evac_copy("act", tmst[0][0:64, 0:256], ps[b][0:64, 0:256], psk(b), KF(3), [b])
                    P.dma(lambda e: e.dma_start(out=o_k_s[l][:, 124:128, :], in_=tmst[0][0:64, 0:128]), reads=KF(3),
                          writes=[("o_k_s", l, 1)], eng="pool")
                    P.dma(lambda e: e.dma_start(out=o_v_s[l][:, 124:128, :], in_=tmst[0][0:64, 128:256]), reads=KF(3),
                          writes=[("o_v_s", l, 1)], eng="pool")
                    for q4 in range(4):
                        bk = nbank(0, 4)

                        def fnv(e, q4=q4, bk=bk, sap=sap):
                            ins = None
                            for i in range(4):
                                bb = q4 * 4 + i
                                for kc in range(16):
                                    ins = e.matmul(ps[bk][0:4, i * 128:(i + 1) * 128], lhsT=s_hT.ap[:, kc, bb * 4:bb * 4 + 4],
                                                   rhs=sap[:, kc, 384:512], start=(kc == 0), stop=(kc == 15))
                            return ins
                        P.pe(fnv, reads=skeys + s_hT.keys(), writes=psk(bk), banks=[bk])
                        for h in range(2):
                            P.dve(lambda e, q4=q4, bk=bk, h=h: e.tensor_copy(
                                out=Vn.ap[0:4, q4 * 4:(q4 + 1) * 4, h * 128:(h + 1) * 128].rearrange("p b (r d) -> p b r d", r=2),
                                in_=ps[bk][0:4, :].rearrange("p (i h d) -> p i h d", i=4, h=2)[:, :, h, :]
                                .unsqueeze(2).to_broadcast([4, 4, 2, 64])),
                                reads=psk(bk), writes=Vn.keys(q4 * 4, q4 * 4 + 4), banks=[bk])
            for gi in range(4):
                w = 2 << gi
                u3 = s_ux.ap[:, gi, :].rearrange("p (b r) -> p b r", r=19)
                uk = s_ux.keys(gi, gi + 1)
                cur, curk = u3, uk
                for k in range(1, gi + 2):
                    sh = 1 << (k - 1)
                    lo = (1 << k) - 1
                    dst = ptmp[k % 2][:, 0:304].rearrange("p (b r) -> p b r", r=19)
                    P.dve(lambda e, cur=cur, dst=dst, lo=lo, sh=sh: e.tensor_tensor(
                        out=dst[:, :, lo:19], in0=cur[:, :, lo:19], in1=cur[:, :, lo - sh:19 - sh], op=ALU.add),
                        reads=curk, writes=KF(3 + k % 2))
                    cur, curk = dst, KF(3 + k % 2)
                d = dT[gi % 2]
                P.dve(lambda e, cur=cur, d=d, u3=u3, w=w: e.scalar_tensor_tensor(
                    out=bt4(d[:, 0:64]), in0=cur[:, :, 15:19], scalar=1.0 / w, in1=u3[:, :, 15:19],
                    op0=ALU.mult, op1=ALU.subtract), reads=curk + uk, writes=KB(gi % 2))
                b = nbank(0, 4)
                P.pe(lambda e, b=b, gi=gi, d=d: e.matmul(ps[b][:, 0:64], lhsT=wpool[:, l, gi, :], rhs=d[:, 0:64],
                                                        start=True, stop=True),
                     reads=KB(gi % 2) + ["wpool"], writes=psk(b), banks=[b])
                P.act(lambda e, b=b, gi=gi: e.activation(out=s_yT.ap[:, gi, :], in_=ps[b][:, 0:64], func=AF.Copy,
                                                         scale=pscale[:, l, gi:gi + 1]),
                      reads=psk(b) + ["pscale"], writes=s_yT.keys(gi, gi + 1), banks=[b])
            for c in range(4):
                v3 = s_vx.ap[:, c, :].rearrange("p (b r) -> p b r", r=6)
                vk = s_vx.keys(c, c + 1)
                ca = bt4(cacc[c % 2][:, 0:64])
                cak = KF(1 + c % 2)
                P.dve(lambda e, v3=v3, c=c: e.tensor_tensor(out=v3[:, :, 2:6], in0=v3[:, :, 2:6], in1=bt4(s_hc.ap[:, c, :]),
                                                           op=ALU.mult), reads=vk + s_hc.keys(c, c + 1), writes=vk)
                P.act(lambda e, v3=v3, c=c, ca=ca: e.activation(out=ca, in_=v3[:, :, 0:4], func=AF.Copy,
                                                               scale=convw[:, l, 0, c:c + 1]),
                      reads=vk + ["convw"], writes=cak)
                for kk in (1, 2):
                    P.dve(lambda e, v3=v3, c=c, ca=ca, kk=kk: e.scalar_tensor_tensor(
                        out=ca, in0=v3[:, :, kk:kk + 4], scalar=convw[:, l, kk, c:c + 1], in1=ca,
                        op0=ALU.mult, op1=ALU.add), reads=vk + ["convw"] + cak, writes=cak)
                P.dve(lambda e, c=c, ca=ca: e.tensor_tensor(out=bt4(s_yT.ap[:, 4 + c, :]), in0=ca, in1=bt4(s_gb.ap[:, c, :]),
                                                            op=ALU.mult),
                      reads=cak + s_gb.keys(c, c + 1), writes=s_yT.keys(4 + c, 5 + c))
            bA, bC, bE, bF = [0, 1], [2, 3], 4, 5
            for par in range(2):
                p0 = par * 64

                def fnS(e, par=par, p0=p0):
                    e.matmul(ps[bA[par]][:, 0:256], lhsT=identb[:], rhs=mask_sc[:], start=True, stop=False)
                    ins = None
                    for bb in range(SB):
                        for h in range(2):
                            c0 = h * 128 + bb * 8
                            ins = e.matmul(ps[bA[par]][:, c0:c0 + 8].rearrange("p (g t) -> p g t", g=2),
                                           lhsT=KTd.ap[p0:p0 + 64, bb * 2 + h, :],
                                           rhs=s_q.ap[p0:p0 + 64, 2 * h:2 * h + 2, bb * 4:bb * 4 + 4],
                                           start=False, stop=(bb == SB - 1 and h == 1))
                    return ins
                P.pe(fnS, reads=["identb", "mask_sc"] + KTd.keys() + s_q.keys(), writes=psk(bA[par]), banks=[bA[par]])

                def fnN(e, par=par, p0=p0):
                    e.matmul(ps[bC[par]][0:4, 0:256], lhsT=identb[0:4, 0:4], rhs=mask_sn[0:4, :], start=True, stop=False)
                    ins = None
                    for bb in range(SB):
                        for h in range(2):
                            c0 = h * 128 + bb * 8
                            ins = e.matmul(ps[bC[par]][0:4, c0:c0 + 8].rearrange("p (g t) -> p g t", g=2),
                                           lhsT=s_kx.ap[p0:p0 + 64, h, bb * 4:bb * 4 + 4],
                                           rhs=s_q.ap[p0:p0 + 64, 2 * h:2 * h + 2, bb * 4:bb * 4 + 4],
                                           start=False, stop=(bb == SB - 1 and h == 1))
                    return ins
                P.pe(fnN, reads=["identb", "mask_sn"] + s_kx.keys() + s_q.keys(), writes=psk(bC[par]), banks=[bC[par]])
                P.act(lambda e, par=par: e.activation(out=Pt[par][:, 0:256], in_=ps[bA[par]][:, 0:256], func=AF.Exp,
                                                      scale=SWA_SCALE), reads=psk(bA[par]), writes=KB(par), banks=[bA[par]])
                P.act(lambda e, par=par: e.activation(out=Pt[2 + par][0:4, 0:256], in_=ps[bC[par]][0:4, 0:256], func=AF.Exp,
                                                      scale=SWA_SCALE), reads=psk(bC[par]), writes=KB(2 + par), banks=[bC[par]])

            def fnDs(e):
                ins = None
                for par in range(2):
                    o = ps[bE][:, par * 256:(par + 1) * 256]
                    e.matmul(o, lhsT=ones1[:], rhs=Pt[par][:, 0:256], start=True, stop=False)
                    e.matmul(o, lhsT=ones1[0:4, :], rhs=Pt[2 + par][0:4, 0:256], start=False, stop=False)
                    ins = e.matmul(o, lhsT=ones1[0:1, :], rhs=esink_s[0:1, l, par, :], start=False, stop=True)
                return ins
            P.pe(fnDs, reads=["ones1", ("esink_s", l)] + KB(0) + KB(1) + KB(2) + KB(3), writes=psk(bE), banks=[bE])

            def fnOs(e):
                ins = None
                for par in range(2):
                    for bb in range(SB):
                        for h in range(2):
                            cc = h * 128 + bb * 8
                            c0 = par * 256 + cc
                            e.matmul(ps[bF][:, c0:c0 + 8], lhsT=Vds.ap[:, bb, h * 128:(h + 1) * 128], rhs=Pt[par][:, cc:cc + 8],
                                     start=True, stop=False)
                            ins = e.matmul(ps[bF][:, c0:c0 + 8], lhsT=Vn.ap[0:4, bb, h * 128:(h + 1) * 128],
                                           rhs=Pt[2 + par][0:4, cc:cc + 8], start=False, stop=True)
                return ins
            P.pe(fnOs, reads=Vds.keys() + Vn.keys() + KB(0) + KB(1) + KB(2) + KB(3), writes=psk(bF), banks=[bF])
            P.dve(lambda e: e.reciprocal(out=rden[0][:, 0:512], in_=ps[bE][:]), reads=psk(bE), writes=RDK[0], banks=[bE])
            for par in range(2):
                for h in range(2):
                    p0 = par * 64
                    c0 = par * 256 + h * 128
                    P.dve(lambda e, p0=p0, c0=c0, h=h: e.tensor_tensor(
                        out=s_yT.ap[p0:p0 + 64, 8 + 2 * h:10 + 2 * h, :].rearrange("p g (b t) -> p b g t", t=4),
                        in0=ps[bF][p0:p0 + 64, c0:c0 + 128].rearrange("p (b g t) -> p b g t", b=SB, g=2),
                        in1=rden[0][p0:p0 + 64, c0:c0 + 128].rearrange("p (b g t) -> p b g t", b=SB, g=2),
                        op=ALU.mult), reads=psk(bF) + RDK[0], writes=s_yT.keys(8 + 2 * h, 10 + 2 * h), banks=[bF])
            bS, bDn, bOm = 2, 3, 6
            Pm = Bt[0]
            for g in range(SB // 2):
                M_ = mc[g % 2]
                b0 = 2 * g
                P.dma(lambda e, M_=M_, b0=b0: e.dma_start(out=M_["K"].ap.rearrange("p (b k) f -> p b k f", b=2),
                                                          in_=cmk[l][b0:b0 + 2].rearrange("b (k p) f -> p b k f", p=128)),
                      writes=M_["K"].keys(), eng="pool")
                P.dma(lambda e, M_=M_, b0=b0: e.dma_start(out=M_["V"].ap.rearrange("p (b k) f -> p b k f", b=2),
                                                          in_=cmv[l][b0:b0 + 2].rearrange("b (k p) f -> p b k f", p=128)),
                      writes=M_["V"].keys(), eng="pool")
                for b2 in range(2):
                    bk = b2

                    def fnT(e, M_=M_, b2=b2, bk=bk):
                        ins = None
                        for hd in range(4):
                            for blk in range(2):
                                i = hd * 2 + blk
                                ins = e.transpose(out=psb[bk][:, i * 128:(i + 1) * 128],
                                                  in_=M_["K"].ap[:, b2 * 2 + blk, hd * 128:(hd + 1) * 128], identity=identb[:])
                        return ins
                    P.pe(fnT, reads=M_["K"].keys() + ["identb"], writes=psk(bk), banks=[bk])
                    evac_copy(alt(), M_["KT"].ap[:, b2 * 4:(b2 + 1) * 4, :], psb[bk][:].rearrange("p (h k) -> p h k", h=4),
                              psk(bk), M_["KT"].keys(b2 * 4, b2 * 4 + 4), [bk])

                def fnSm(e, M_=M_, b0=b0):
                    ins = None
                    for b2 in range(2):
                        bb = b0 + b2
                        for blk in range(2):
                            for hd in range(4):
                                c0 = bb * 32 + blk * 16 + hd * 4
                                ins = e.matmul(ps[bS][:, c0:c0 + 4], lhsT=M_["KT"].ap[:, b2 * 4 + hd, blk * 128:(blk + 1) * 128],
                                               rhs=s_qm.ap[:, hd, bb * 4:bb * 4 + 4], start=True, stop=True)
                    return ins
                P.pe(fnSm, reads=M_["KT"].keys() + s_qm.keys(), writes=psk(bS), banks=[bS])
                P.act(lambda e, b0=b0: e.activation(out=Pm[:, b0 * 32:b0 * 32 + 64], in_=ps[bS][:, b0 * 32:b0 * 32 + 64],
                                                    func=AF.Exp, scale=MEM_SCALE), reads=psk(bS), writes=KB(0), banks=[bS])

                def fnDm(e, b0=b0):
                    v = Pm[:, b0 * 32:b0 * 32 + 64].rearrange("p (b k x) -> p b k x", b=2, k=2)
                    o = ps[bDn][:, b0 * 16:b0 * 16 + 32].rearrange("p (b x) -> p b x", b=2)
                    e.matmul(o, lhsT=ones1[:], rhs=v[:, :, 0, :], start=True, stop=False)
                    return e.matmul(o, lhsT=ones1[:], rhs=v[:, :, 1, :], start=False, stop=True)
                P.pe(fnDm, reads=["ones1"] + KB(0), writes=psk(bDn), banks=[bDn])

                def fnOm(e, M_=M_, b0=b0):
                    ins = None
                    for b2 in range(2):
                        bb = b0 + b2
                        for hd in range(4):
                            oc = bb * 16 + hd * 4
                            for blk in range(2):
                                c0 = bb * 32 + blk * 16 + hd * 4
                                ins = e.matmul(ps[bOm][:, oc:oc + 4], lhsT=M_["V"].ap[:, b2 * 2 + blk, hd * 128:(hd + 1) * 128],
                                               rhs=Pm[:, c0:c0 + 4], start=(blk == 0), stop=(blk == 1))
                    return ins
                P.pe(fnOm, reads=M_["V"].keys() + KB(0), writes=psk(bOm), banks=[bOm])
            P.dve(lambda e: e.reciprocal(out=rden[1][:, 0:256], in_=ps[bDn][:, 0:256]), reads=psk(bDn), writes=RDK[1], banks=[bDn])
            P.dve(lambda e: e.tensor_tensor(
                out=s_yT.ap[:, 12:16, :].rearrange("p h (b t) -> p b h t", t=4),
                in0=ps[bOm][:, 0:256].rearrange("p (b h t) -> p b h t", b=SB, h=4),
                in1=rden[1][:, 0:256].rearrange("p (b h t) -> p b h t", b=SB, h=4), op=ALU.mult),
                reads=psk(bOm) + RDK[1], writes=s_yT.keys(12, 16), banks=[bOm])
            ffn_and_out(RS, l, T)

        if DBG["mem"]:
            mem_phase()
        for ti in range(DBG["ntiles"]):
            state["use_pool"] = ti > 0
            load_xT(xp[ti * TP:(ti + 1) * TP, :], TP)
            for l in range(DBG["nlayers"]):
                layer_prompt(l, ti)
            store_xT(yp[ti * TP:(ti + 1) * TP, :], TP)
        if with_sample:
            load_xT(xs, TS)
            for l in range(DBG["nlayers"]):
                layer_sample(l)
            store_xT(ys, TS)
        assert wst["next"] == len(seq), (wst["next"], len(seq))
        P.emit()
    return nc


_CACHE = {}


def kernel(**inputs):
    f = lambda a: np.ascontiguousarray(np.asarray(a, dtype=np.float32))
    inp = {k: f(v) for k, v in inputs.items()}
    with_sample = True
    if "nc" not in _CACHE:
        _CACHE["nc"] = build_program(with_sample)
    nc = _CACHE["nc"]
    shared = {k: inp[k] for k in ("g_mix_pre", "w_in", "w_pool", "pool_scale", "conv_w", "swa_sinks", "g_mem",
                                  "w_mem_kv", "w_out", "g_mix_post", "g_mlp_pre", "w_up", "w_down", "g_mlp_post")}
    in_maps = []
    for c in range(NCORES):
        m = dict(shared)
        m["xp"] = inp["x_prompt"][c]
        m["xs"] = inp["x_sample"][c * SB:(c + 1) * SB].reshape(TS, D)
        m["mem"] = inp["mem_prompt"][c]
        m["spool"] = np.ascontiguousarray(inp["state_pool"][:, c * SB:(c + 1) * SB])
        m["sconv"] = np.ascontiguousarray(inp["state_conv"][:, c * SB:(c + 1) * SB])
        m["ck"] = np.ascontiguousarray(inp["cache_swa_k"][:, c * SB:(c + 1) * SB].reshape(DEPTH, SB, 128, 128))
        m["cv"] = np.ascontiguousarray(inp["cache_swa_v"][:, c * SB:(c + 1) * SB].reshape(DEPTH, SB, 128, 128))
        m["cmk"] = np.ascontiguousarray(inp["cache_mem_k"][:, c * SB:(c + 1) * SB].reshape(DEPTH, SB, MEMT, DG))
        m["cmv"] = np.ascontiguousarray(inp["cache_mem_v"][:, c * SB:(c + 1) * SB].reshape(DEPTH, SB, MEMT, DG))
        in_maps.append(m)
    res = run_bass_kernel_spmd(nc, in_maps, core_ids=list(range(NCORES)))
    R = res.results
    cat = lambda name, axis: np.concatenate([np.asarray(R[c][name], dtype=np.float32) for c in range(NCORES)], axis=axis)
    stk = lambda name: np.stack([np.asarray(R[c][name], dtype=np.float32) for c in range(NCORES)], axis=1)
    y_prompt = np.stack([np.asarray(R[c]["yp"], dtype=np.float32) for c in range(NCORES)], axis=0)
    y_sample = cat("ys", 0).reshape(NCORES * SB, ST, D)
    return (
        y_prompt,
        y_sample,
        stk("o_pool_p"),
        cat("o_pool_s", 1),
        stk("o_conv_p"),
        cat("o_conv_s", 1),
        stk("o_k_p").reshape(DEPTH, NCORES, 128, 2, 64),
        cat("o_k_s", 1).reshape(DEPTH, NCORES * SB, 128, 2, 64),
        stk("o_v_p").reshape(DEPTH, NCORES, 128, 2, 64),
        cat("o_v_s", 1).reshape(DEPTH, NCORES * SB, 128, 2, 64),
        stk("o_mk_p").reshape(DEPTH, NCORES, MEMT, 4, 128),
        stk("o_mv_p").reshape(DEPTH, NCORES, MEMT, 4, 128),
    )
```

```python
import contextlib
import math
from types import SimpleNamespace
import numpy as np
import concourse.bass as bass
import concourse.mybir as mybir
from concourse.bass_utils import run_bass_kernel_spmd

F32 = mybir.dt.float32
BF16 = mybir.dt.bfloat16
ALU = mybir.AluOpType
AF = mybir.ActivationFunctionType

NCORES = 8
D = 2048
DEPTH = 2
SEQ = 2048
TP = 512
NPT = SEQ // TP
SB = 16
ST = 4
TS = SB * ST
MEMT = 256
DG = 512
D_IN = 3328
DFF = 8192
EPS = 1e-6
SWA_SCALE = 1.0 / 8.0
MEM_SCALE = 1.0 / math.sqrt(128.0)
NEG = -30000.0

ENGS = ("pe", "act", "dve", "pool", "sp")
SEM_LIMIT = 24000
NDMASEM = 12


class Op:
    __slots__ = ("eng", "fn", "deps", "dma", "inc", "cnt", "dsem", "dval", "prewait")

    def __init__(self, eng, fn, deps, dma):
        self.eng = eng
        self.fn = fn
        self.deps = deps
        self.dma = dma
        self.inc = False
        self.cnt = 0
        self.dsem = None
        self.dval = 0
        self.prewait = None


class Prog:
    def __init__(self, nc):
        self.nc = nc
        self.ops = {e: [] for e in ENGS}
        self.lw = {}
        self.rd = {}
        self.ndma = {e: 0 for e in ENGS}

    def add(self, eng, fn, reads=(), writes=(), dma=False, banks=()):
        idx = len(self.ops[eng])
        deps = set()
        for b in banks:
            k = ("__bank", b)
            w = self.lw.get(k)
            if w is not None and w[0] != eng:
                deps.add(w)
            self.lw[k] = (eng, idx)
        for k in reads:
            w = self.lw.get(k)
            if w is not None:
                deps.add(w)
        for k in writes:
            w = self.lw.get(k)
            if w is not None:
                deps.add(w)
            for r in self.rd.get(k, ()):
                deps.add(r)
        me = (eng, idx)
        deps.discard(me)
        if eng == "pe":
            deps = {d for d in deps if d[0] != "pe"}
        op = Op(eng, fn, deps, dma)
        if dma:
            n = self.ndma[eng]
            self.ndma[eng] = n + 1
            op.dsem = (eng, n % NDMASEM)
            op.dval = 16 * (n // NDMASEM + 1)
            if n >= NDMASEM:
                op.prewait = (op.dsem, op.dval - 16)
        self.ops[eng].append(op)
        for k in reads:
            self.rd.setdefault(k, []).append(me)
        for k in writes:
            self.lw[k] = me
            self.rd[k] = []
        return me

    def pe(self, fn, reads=(), writes=(), banks=()):
        return self.add("pe", fn, reads, writes, banks=banks)

    def act(self, fn, reads=(), writes=(), banks=()):
        return self.add("act", fn, reads, writes, banks=banks)

    def dve(self, fn, reads=(), writes=(), banks=()):
        return self.add("dve", fn, reads, writes, banks=banks)

    def pool(self, fn, reads=(), writes=(), banks=()):
        return self.add("pool", fn, reads, writes, banks=banks)

    def dma(self, fn, reads=(), writes=(), eng="sp"):
        return self.add(eng, fn, reads, writes, dma=True)

    def emit(self):
        nc = self.nc
        for e in ENGS:
            for op in self.ops[e]:
                for (de, di) in op.deps:
                    d = self.ops[de][di]
                    if not d.dma:
                        d.inc = True
        nsem = {}
        for e in ENGS:
            c = 0
            for op in self.ops[e]:
                if op.inc and not op.dma:
                    c += 1
                op.cnt = c
            nsem[e] = (c // SEM_LIMIT) + 1
        with contextlib.ExitStack() as st:
            csem = {e: [st.enter_context(nc.semaphore(f"c_{e}_{i}")) for i in range(nsem[e])]
                    for e in ENGS}
            dsem = {}
            for e in ENGS:
                for i in range(min(NDMASEM, self.ndma[e])):
                    dsem[(e, i)] = st.enter_context(nc.semaphore(f"d_{e}_{i}"))
            block = st.enter_context(nc.Block())
            engobj = {"pe": "tensor", "act": "scalar", "dve": "vector", "pool": "gpsimd", "sp": "sync"}

            def body(e):
                def run(eng):
                    waited = {}

                    def do_wait(key, sem, val):
                        if waited.get(key, 0) >= val:
                            return
                        eng.wait_ge(sem, val)
                        waited[key] = val

                    for op in self.ops[e]:
                        for (de, di) in sorted(op.deps):
                            d = self.ops[de][di]
                            if d.dma:
                                do_wait(("d",) + d.dsem, dsem[d.dsem], d.dval)
                            else:
                                si, v = divmod(d.cnt - 1, SEM_LIMIT)
                                do_wait(("c", de, si), csem[de][si], v + 1)
                        if op.prewait is not None:
                            do_wait(("d",) + op.prewait[0], dsem[op.prewait[0]], op.prewait[1])
                        ins = op.fn(eng)
                        if op.dma:
                            ins.then_inc(dsem[op.dsem], 16)
                        elif op.inc:
                            ins.then_inc(csem[e][(op.cnt - 1) // SEM_LIMIT], 1)
                    for (de, i), s in dsem.items():
                        if de == e:
                            n = self.ndma[e]
                            uses = (n - i + NDMASEM - 1) // NDMASEM
                            if uses > 0:
                                do_wait(("d", de, i), s, 16 * uses)
                return run

            for e in ENGS:
                if self.ops[e]:
                    getattr(block, engobj[e])(body(e))


class Region:
    def __init__(self, buf, name, off, C, W, dt):
        esz = 4 if dt == F32 else 2
        nb = C * W * esz
        assert off % 4 == 0
        sl = buf[:, off // 2:(off + nb) // 2]
        if dt == F32:
            sl = sl.bitcast(F32)
        self.ap = sl.rearrange("p (c w) -> p c w", c=C)
        self.name = name
        self.off = off
        self.cb = W * esz
        self.C = C
        self.end = off + nb

    def keys(self, c0=0, c1=None):
        if c1 is None:
            c1 = self.C
        lo = self.off + c0 * self.cb
        hi = self.off + c1 * self.cb
        return [(self.name, b) for b in range(lo // 1024, (hi - 1) // 1024 + 1)]


WIN_NSLAB = 7
SLAB_ELEMS = 8192


DBG = {"mem": True, "ntiles": NPT, "nlayers": DEPTH, "mixers": True, "ffn": True}


def build_program(with_sample=True):
    nc = bass.Bass("TRN2", target_bir_lowering=False)

    def din(name, shape, dt=F32):
        return nc.dram_tensor(name, list(shape), dt, kind="ExternalInput").ap()

    def dout(name, shape, dt=F32):
        return nc.dram_tensor(name, list(shape), dt, kind="ExternalOutput").ap()

    xp = din("xp", [SEQ, D])
    xs = din("xs", [TS, D])
    mem = din("mem", [MEMT, D])
    spool = din("spool", [DEPTH, SB, 15, DG])
    sconv = din("sconv", [DEPTH, SB, 2, DG])
    ck = din("ck", [DEPTH, SB, 128, 128])
    cv = din("cv", [DEPTH, SB, 128, 128])
    cmk = din("cmk", [DEPTH, SB, MEMT, DG])
    cmv = din("cmv", [DEPTH, SB, MEMT, DG])
    g_mix_pre = din("g_mix_pre", [DEPTH, D])
    w_in = din("w_in", [DEPTH, D, D_IN])
    w_pool = din("w_pool", [DEPTH, 4, 128, 128])
    pool_scale = din("pool_scale", [DEPTH, DG])
    conv_w = din("conv_w", [DEPTH, 3, DG])
    swa_sinks = din("swa_sinks", [DEPTH, 8])
    g_mem = din("g_mem", [DEPTH, D])
    w_mem_kv = din("w_mem_kv", [DEPTH, D, 2 * DG])
    w_out = din("w_out", [DEPTH, D, D])
    g_mix_post = din("g_mix_post", [DEPTH, D])
    g_mlp_pre = din("g_mlp_pre", [DEPTH, D])
    w_up = din("w_up", [DEPTH, D, DFF])
    w_down = din("w_down", [DEPTH, DFF, D])
    g_mlp_post = din("g_mlp_post", [DEPTH, D])

    yp = dout("yp", [SEQ, D])
    ys = dout("ys", [TS, D])
    o_pool_p = dout("o_pool_p", [DEPTH, 15, DG])
    o_pool_s = dout("o_pool_s", [DEPTH, SB, 15, DG])
    o_conv_p = dout("o_conv_p", [DEPTH, 2, DG])
    o_conv_s = dout("o_conv_s", [DEPTH, SB, 2, DG])
    o_k_p = dout("o_k_p", [DEPTH, 128, 128])
    o_k_s = dout("o_k_s", [DEPTH, SB, 128, 128])
    o_v_p = dout("o_v_p", [DEPTH, 128, 128])
    o_v_s = dout("o_v_s", [DEPTH, SB, 128, 128])
    o_mk_p = dout("o_mk_p", [DEPTH, MEMT, DG])
    o_mv_p = dout("o_mv_p", [DEPTH, MEMT, DG])

    def scratch(name, nslab):
        return nc.dram_tensor(name, [DEPTH, nslab, 128, SLAB_ELEMS], BF16, kind="Internal").ap()

    s_in = scratch("s_in", WIN_NSLAB)
    s_out = scratch("s_out", 4)
    s_up = scratch("s_up", 16)
    s_down = scratch("s_down", 16)

    P = Prog(nc)
    st = contextlib.ExitStack()
    with st:
        def sb(name, shape, dt):
            return st.enter_context(nc.sbuf_tensor(name, list(shape), dt))

        xT = sb("xT", [128, 16, TP], F32)
        U1 = sb("U1", [128, 16384], BF16)
        BIG = sb("BIG", [128, 32768], BF16)
        NWB = 2
        wbuf = [sb(f"wbuf{i}", [128, SLAB_ELEMS], BF16) for i in range(NWB)]
        ident = sb("ident", [128, 128], F32)
        identb = sb("identb", [128, 128], BF16)
        onesD = sb("onesD", [128, 128], BF16)
        ones1 = sb("ones1", [128, 128], BF16)
        mask_cat = sb("mask_cat", [128, 512], BF16)
        mask_first = sb("mask_first", [128, 512], BF16)
        mask_sc = sb("mask_sc", [128, 256], BF16)
        mask_sn = sb("mask_sn", [128, 256], BF16)
        esink_s = sb("esink_s", [1, DEPTH, 2, 256], BF16)
        gv = sb("gv", [128, 5, DEPTH, 16], F32)
        pscale = sb("pscale", [128, DEPTH, 4], F32)
        convw = sb("convw", [128, DEPTH, 3, 4], F32)
        wpool = sb("wpool", [128, DEPTH, 4, 128], BF16)
        sinks_sb = sb("sinks_sb", [1, DEPTH * 8], F32)
        esink = sb("esink", [1, DEPTH, 2, 512], BF16)
        invcnt = sb("invcnt", [128, 4, 16], F32)
        rtm = sb("rtm", [128, 8], F32)
        Fs = [sb(f"F{i}", [128, 16 + TP], F32) for i in range(6)]
        Bt = [sb(f"Bt{i}", [128, TP], BF16) for i in range(4)]
        dTp = [sb(f"dTp{i}", [128, TP], BF16) for i in range(2)]
        Vd = sb("Vd", [128, 5, 2, 128], BF16)
        carry_u = sb("carry_u", [128, DEPTH, 4, 16], F32)
        carry_v = sb("carry_v", [128, DEPTH, 4, 2], F32)
        carry_k = sb("carry_k", [128, DEPTH, 2, 128], BF16)
        carry_V = sb("carry_V", [128, DEPTH, 2, 128], BF16)
        mkT = sb("mkT", [128, DEPTH, 4, MEMT], BF16)
        mvv = sb("mvv", [128, DEPTH, 2, DG], BF16)

        ps = [st.enter_context(nc.psum_tensor(f"ps{i}", [128, 512], F32)) for i in range(8)]

        rstd = Fs[0]
        cacc = [Fs[1], Fs[2]]
        rtmp = [Fs[1], Fs[2]]
        rden = [Fs[0], Fs[5]]
        RDK = [[("F", 0)], [("F", 5)]]
        ptmp = [Fs[3], Fs[4]]
        tmst = [Fs[3], Fs[4]]
        hcst = Fs[5]
        mtmp = Fs[5]
        pfix = Fs[5]
        epsq = Fs[5]
        sq = Bt
        Pt = Bt
        dT = [Bt[0], Bt[1]]
        KF = lambda i: [("F", i)]
        KB = lambda i: [("Bt", i)]

        xstage = Region(U1, "U1", 0, 4, D, F32)
        hT = Region(U1, "U1", 0, 16, TP, BF16)
        mixT = Region(U1, "U1", 0, 16, TP, F32)
        off = 0
        u_ext = Region(BIG, "BIG", off, 4, 16 + TP, F32); off = u_ext.end
        hcR = Region(BIG, "BIG", off, 4, TP, F32); off = hcR.end
        gbR = Region(BIG, "BIG", off, 4, TP, F32); off = gbR.end
        v_ext = Region(BIG, "BIG", off, 4, 2 + TP, F32); off = v_ext.end
        qR = Region(BIG, "BIG", off, 4, TP, BF16); off = qR.end
        kext = Region(BIG, "BIG", off, 2, 128 + TP, BF16); off = kext.end
        qmR = Region(BIG, "BIG", off, 4, TP, BF16); off = qmR.end
        yT = Region(BIG, "BIG", off, 16, TP, BF16); off = yT.end
        assert off <= 65536, off
        hidT = Region(BIG, "BIG", 0, 64, TP, BF16)
        RP = SimpleNamespace(hT=hT, mixT=mixT, yT=yT, hidT=hidT)

        state = {"bank": 0, "alt": 0, "use_pool": False}

        def alt():
            state["alt"] ^= 1
            return "act" if state["alt"] else "dve"

        def evac_copy(eng, out_ap, in_ap, reads, writes, banks=()):
            if eng == "act":
                P.act(lambda e: e.activation(out=out_ap, in_=in_ap, func=AF.Copy), reads, writes, banks)
            else:
                P.dve(lambda e: e.tensor_copy(out=out_ap, in_=in_ap), reads, writes, banks)

        def psk(b):
            return [("ps", b)]

        P.pool(lambda e: e.memset(ident[:], 1.0), writes=["ident"])
        P.pool(lambda e: e.affine_select(out=ident[:], in_=ident[:], pattern=[[-1, 128]],
                                         compare_op=ALU.is_equal, fill=0.0, base=0, channel_multiplier=1),
               reads=["ident"], writes=["ident"])
        P.dve(lambda e: e.tensor_copy(out=identb[:], in_=ident[:]), reads=["ident"], writes=["identb"])
        P.pool(lambda e: e.memset(onesD[:], 1.0 / D), writes=["onesD"])
        P.pool(lambda e: e.memset(ones1[:], 1.0), writes=["ones1"])
        P.pool(lambda e: e.memset(mtmp[:, 0:512], 0.0), writes=[("F", 5)])
        P.pool(lambda e: e.affine_select(out=mtmp[:, 0:256].rearrange("p (a q) -> p a q", a=2),
                                         in_=mtmp[:, 0:256].rearrange("p (a q) -> p a q", a=2),
                                         pattern=[[0, 2], [1, 128]], compare_op=ALU.is_ge, fill=NEG,
                                         base=0, channel_multiplier=-1), reads=[("F", 5)], writes=[("F", 5)])
        P.pool(lambda e: e.affine_select(out=mtmp[:, 256:512].rearrange("p (a q) -> p a q", a=2),
                                         in_=mtmp[:, 256:512].rearrange("p (a q) -> p a q", a=2),
                                         pattern=[[0, 2], [-1, 128]], compare_op=ALU.is_ge, fill=NEG,
                                         base=-1, channel_multiplier=1), reads=[("F", 5)], writes=[("F", 5)])
        P.dve(lambda e: e.tensor_copy(out=mask_cat[:], in_=mtmp[:, 0:512]), reads=[("F", 5)], writes=["mask_cat"])
        P.dve(lambda e: e.tensor_copy(out=mask_first[:, 0:256], in_=mtmp[:, 0:256]), reads=[("F", 5)], writes=["mask_first"])
        P.pool(lambda e: e.memset(mask_first[:, 256:512], NEG), reads=["mask_first"], writes=["mask_first"])
        P.pool(lambda e: e.memset(mtmp[:, 0:512], 0.0), reads=[("F", 5)], writes=[("F", 5)])
        P.pool(lambda e: e.affine_select(out=mtmp[:, 0:256].rearrange("p (a t) -> p a t", t=4),
                                         in_=mtmp[:, 0:256].rearrange("p (a t) -> p a t", t=4),
                                         pattern=[[0, 64], [-1, 4]], compare_op=ALU.is_ge, fill=NEG,
                                         base=-1, channel_multiplier=1), reads=[("F", 5)], writes=[("F", 5)])
        P.pool(lambda e: e.affine_select(out=mtmp[:, 256:512].rearrange("p (a t) -> p a t", t=4),
                                         in_=mtmp[:, 256:512].rearrange("p (a t) -> p a t", t=4),
                                         pattern=[[0, 64], [1, 4]], compare_op=ALU.is_ge, fill=NEG,
                                         base=0, channel_multiplier=-1), reads=[("F", 5)], writes=[("F", 5)])
        P.dve(lambda e: e.tensor_copy(out=mask_sc[:], in_=mtmp[:, 0:256]), reads=[("F", 5)], writes=["mask_sc"])
        P.dve(lambda e: e.tensor_copy(out=mask_sn[:], in_=mtmp[:, 256:512]), reads=[("F", 5)], writes=["mask_sn"])
        for gi in range(4):
            w = 2 << gi
            for j in range(16):
                P.pool(lambda e, gi=gi, j=j, w=w: e.memset(invcnt[:, gi, j:j + 1], 1.0 / min(j + 1, w)),
                       writes=[("invcnt", gi, j)])
        INVK = [("invcnt", gi, j) for gi in range(4) for j in range(16)]
        P.pool(lambda e: e.memset(carry_u[:], 0.0), writes=["carry_u"])
        P.pool(lambda e: e.memset(carry_v[:], 0.0), writes=["carry_v"])
        P.pool(lambda e: e.memset(carry_k[:], 0.0), writes=["carry_k"])
        P.pool(lambda e: e.memset(carry_V[:], 0.0), writes=["carry_V"])

        for i, g in enumerate((g_mix_pre, g_mem, g_mix_post, g_mlp_pre, g_mlp_post)):
            for l in range(DEPTH):
                P.dma(lambda e, i=i, l=l, g=g: e.dma_start(out=gv[:, i, l, :],
                                                            in_=g[l].rearrange("(c p) -> p c", p=128),
                                                            allow_slow_non_contiguous=True),
                      writes=["gv"])
        for l in range(DEPTH):
            P.dma(lambda e, l=l: e.dma_start(out=pscale[:, l, :], in_=pool_scale[l].rearrange("(c p) -> p c", p=128),
                                             allow_slow_non_contiguous=True), writes=["pscale"])
            for k in range(3):
                P.dma(lambda e, l=l, k=k: e.dma_start(out=convw[:, l, k, :],
                                                      in_=conv_w[l, k].rearrange("(c p) -> p c", p=128),
                                                      allow_slow_non_contiguous=True), writes=["convw"])
        P.dma(lambda e: e.dma_start(out=sinks_sb[:], in_=swa_sinks.rearrange("l j -> (l j)").rearrange("(o n) -> o n", o=1)),
              writes=["sinks_sb"])
        for l in range(DEPTH):
            P.dma(lambda e, l=l: e.dma_start(out=wpool[:, l, :, :], in_=w_pool[l].rearrange("g c d -> c g d")),
                  writes=["wpool"], eng="pool")
        P.act(lambda e: e.activation(out=sinks_sb[:], in_=sinks_sb[:], func=AF.Exp), reads=["sinks_sb"], writes=["sinks_sb"])
        for l in range(DEPTH):
            for h in range(2):
                for par in range(2):
                    for gg in range(2):
                        j = l * 8 + 4 * h + 2 * gg + par
                        c0 = par * 256 + gg * 128
                        P.dve(lambda e, l=l, h=h, j=j, c0=c0: e.tensor_copy(
                            out=esink[0:1, l, h, c0:c0 + 128], in_=sinks_sb[0:1, j:j + 1].broadcast_to([1, 128])),
                            reads=["sinks_sb"], writes=[("esink", l, h, c0)])
        ESK = lambda l, h: [("esink", l, h, c0) for c0 in (0, 128, 256, 384)]
        for l in range(DEPTH):
            for par in range(2):
                for h in range(2):
                    for gg in range(2):
                        j = l * 8 + 4 * h + 2 * gg + par
                        P.dve(lambda e, l=l, par=par, h=h, gg=gg, j=j: e.tensor_copy(
                            out=esink_s[0:1, l, par, h * 128:(h + 1) * 128].rearrange("o (b g t) -> o b g t", b=SB, g=2)[:, :, gg, :],
                            in_=sinks_sb[0:1, j:j + 1].unsqueeze(2).to_broadcast([1, SB, 4])),
                            reads=["sinks_sb"], writes=[("esink_s", l)])

        SCR = {"in": s_in, "out": s_out, "up": s_up, "down": s_down}
        SRC = {"mem": w_mem_kv, "in": w_in, "out": w_out, "up": w_up, "down": w_down}

        def slab_pieces(kind, l, s):
            src = SRC[kind][l].rearrange("(k p) n -> p k n", p=128)
            if kind == "down":
                G, q = s // 4, s % 4
                return 16, [(0, 512, src[:, q * 16:(q + 1) * 16, G * 512:(G + 1) * 512])]
            if kind != "in" or s < 5:
                return 16, [(0, 512, src[:, :, s * 512:(s + 1) * 512])]
            if s == 5:
                pcs = []
                for h in range(2):
                    for r in range(2):
                        c0 = h * 128 + r * 64
                        pcs.append((c0, c0 + 64, src[:, :, 2560 + h * 64:2560 + h * 64 + 64]))
                pcs.append((256, 512, src[:, :, 2816:3072]))
                return 16, pcs
            return 16, [(0, 256, src[:, :, 3072:3328]), (256, 512, src[:, :, 2560:2816])]

        seq = []
        seen = set()

        def add_seq(tag):
            seq.append((tag, tag not in seen))
            seen.add(tag)
        if DBG["mem"]:
            for l in range(DEPTH):
                for s in range(2):
                    add_seq(("mem", l, s))
        tiles = [("p", i) for i in range(DBG["ntiles"])] + ([("s", 0)] if with_sample else [])
        for t in tiles:
            for l in range(DBG["nlayers"]):
                for s in range(WIN_NSLAB):
                    add_seq(("in", l, s))
                if not DBG["ffn"]:
                    continue
                for s in range(4):
                    add_seq(("out", l, s))
                for s in range(16):
                    add_seq(("up", l, s))
                for s in range(16):
                    add_seq(("down", l, s))
        wst = {"issued": 0, "next": 0}
        PREF = 1

        def w_issue(upto):
            while wst["issued"] < min(upto, len(seq)):
                i = wst["issued"]
                tag, first_use = seq[i]
                kind, l, s_ = tag
                bi = i % NWB
                if first_use:
                    KC, pcs = slab_pieces(kind, l, s_)
                    wv = wbuf[bi][:].rearrange("p (k n) -> p k n", k=KC)
                    for (c0, c1, src) in pcs:
                        P.dma(lambda e, wv=wv, c0=c0, c1=c1, src=src: e.dma_start(out=wv[:, :, c0:c1], in_=src),
                              writes=[("wbuf", bi)], eng="pool")
                    if kind != "mem":
                        P.dma(lambda e, kind=kind, l=l, s_=s_, bi=bi: e.dma_start(out=SCR[kind][l, s_], in_=wbuf[bi][:]),
                              reads=[("wbuf", bi)], writes=[("scr",) + tag])
                else:
                    P.dma(lambda e, kind=kind, l=l, s_=s_, bi=bi: e.dma_start(out=wbuf[bi][:], in_=SCR[kind][l, s_]),
                          reads=[("scr",) + tag], writes=[("wbuf", bi)])
                wst["issued"] += 1

        def w_next(tag, KC):
            i = wst["next"]
            assert seq[i][0] == tag, (seq[i][0], tag)
            w_issue(i + 1 + PREF)
            wst["next"] += 1
            bi = i % NWB
            return wbuf[bi][:].rearrange("p (k n) -> p k n", k=KC), [("wbuf", bi)]

        def nbank(lo=0, hi=6):
            b = state["bank"]
            state["bank"] = b + 1
            return lo + b % (hi - lo)

        def load_xT(src, T):
            NG = (T + 127) // 128
            rows = min(128, T)
            for g in range(NG):
                P.dma(lambda e, g=g: e.dma_start(out=xstage.ap[0:rows, g, :], in_=src[g * 128:g * 128 + rows, :]),
                      writes=xstage.keys(g, g + 1))
                for cb in range(4):
                    b = nbank(0, 4)

                    def fn(e, g=g, cb=cb, b=b):
                        ins = None
                        for j in range(4):
                            c = cb * 4 + j
                            ins = e.transpose(out=ps[b][:, j * 128:j * 128 + rows],
                                              in_=xstage.ap[0:rows, g, c * 128:(c + 1) * 128],
                                              identity=ident[0:rows, 0:rows])
                        return ins
                    P.pe(fn, reads=xstage.keys(g, g + 1) + ["ident"], writes=psk(b), banks=[b])
                    evac_copy(alt(), xT[:, cb * 4:cb * 4 + 4, g * 128:g * 128 + rows],
                              ps[b][:].rearrange("p (j t) -> p j t", j=4)[:, :, 0:rows],
                              psk(b), [("xT", cb * 4 + j) for j in range(4)], [b])

        def store_xT(dst, T):
            NG = (T + 127) // 128
            rows = min(128, T)
            for g in range(NG):
                for cb in range(4):
                    b = nbank(0, 4)

                    def fn(e, g=g, cb=cb, b=b):
                        ins = None
                        for j in range(4):
                            c = cb * 4 + j
                            ins = e.transpose(out=ps[b][0:rows, j * 128:(j + 1) * 128],
                                              in_=xT[:, c, g * 128:g * 128 + rows], identity=ident[:])
                        return ins
                    P.pe(fn, reads=[("xT", cb * 4 + j) for j in range(4)] + ["ident"], writes=psk(b), banks=[b])
                    evac_copy(alt(), xstage.ap[0:rows, g, cb * 512:(cb + 1) * 512], ps[b][0:rows, :],
                              psk(b), xstage.keys(g, g + 1), [b])
                P.dma(lambda e, g=g: e.dma_start(out=dst[g * 128:g * 128 + rows, :], in_=xstage.ap[0:rows, g, :]),
                      reads=xstage.keys(g, g + 1), writes=[("ydst", g)], eng="act")

        def norm_stats(src_ap, src_keys, T, nch=16, mode="rstd"):
            b = 7
            for c in range(nch):
                sb_ = sq[c % 4]
                P.act(lambda e, c=c, sb_=sb_: e.activation(out=sb_[:, 0:T], in_=src_ap(c), func=AF.Square),
                      reads=src_keys(c), writes=[("Bt", c % 4)])
                P.pe(lambda e, c=c, sb_=sb_: e.matmul(ps[b][:, 0:T], lhsT=onesD[:], rhs=sb_[:, 0:T],
                                                      start=(c == 0), stop=(c == nch - 1)),
                     reads=[("Bt", c % 4), "onesD"], writes=psk(b), banks=[b])
            if mode == "epsq":
                P.act(lambda e: e.activation(out=epsq[:, 0:T], in_=ps[b][:, 0:T], func=AF.Square,
                                             bias=EPS * math.sqrt(EPS), scale=math.sqrt(EPS)),
                      reads=psk(b), writes=[("F", 5)], banks=[b])
                return
            if mode == "rstd_q":
                P.dve(lambda e: e.tensor_tensor(out=rstd[:, 0:T], in0=ps[b][:, 0:T], in1=epsq[:, 0:T], op=ALU.add),
                      reads=psk(b) + [("F", 5)], writes=[("F", 0)], banks=[b])
                P.act(lambda e: e.activation(out=rstd[:, 0:T], in_=rstd[:, 0:T], func=AF.Sqrt),
                      reads=[("F", 0)], writes=[("F", 0)])
            else:
                P.act(lambda e: e.activation(out=rstd[:, 0:T], in_=ps[b][:, 0:T], func=AF.Sqrt, bias=EPS, scale=1.0),
                      reads=psk(b), writes=[("F", 0)], banks=[b])
            P.dve(lambda e: e.reciprocal(out=rstd[:, 0:T], in_=rstd[:, 0:T]), reads=[("F", 0)], writes=[("F", 0)])

        def xT_ap(T):
            return (lambda c: xT[:, c, 0:T]), (lambda c: [("xT", c)])

        def prenorm_to_hT(R, gidx, l, T):
            a, k = xT_ap(T)
            norm_stats(a, k, T)
            for c in range(16):
                if c % 2 == 1 and state["use_pool"]:
                    tb = Fs[1 + (c // 2) % 2]
                    tk = KF(1 + (c // 2) % 2)
                    P.pool(lambda e, c=c, tb=tb: e.tensor_tensor(out=tb[:, 0:T], in0=xT[:, c, 0:T], in1=rstd[:, 0:T],
                                                                 op=ALU.mult),
                           reads=[("xT", c), ("F", 0)], writes=tk)
                    P.act(lambda e, c=c, tb=tb: e.activation(out=R.hT.ap[:, c, 0:T], in_=tb[:, 0:T], func=AF.Copy,
                                                             scale=gv[:, gidx, l, c:c + 1]),
                          reads=tk + ["gv"], writes=R.hT.keys(c, c + 1))
                else:
                    P.dve(lambda e, c=c: e.scalar_tensor_tensor(out=R.hT.ap[:, c, 0:T], in0=xT[:, c, 0:T],
                                                                scalar=gv[:, gidx, l, c:c + 1], in1=rstd[:, 0:T],
                                                                op0=ALU.mult, op1=ALU.mult),
                          reads=[("xT", c), "gv", ("F", 0)], writes=R.hT.keys(c, c + 1))

        def postnorm_residual(R, gidx, l, T, mode="rstd", after_chunk=None):
            mixT_ = R.mixT
            norm_stats(lambda c: mixT_.ap[:, c, 0:T], lambda c: mixT_.keys(c, c + 1), T, mode=mode)
            def scale_chunk(c):
                P.dve(lambda e, c=c: e.scalar_tensor_tensor(out=mixT_.ap[:, c, 0:T], in0=mixT_.ap[:, c, 0:T],
                                                            scalar=gv[:, gidx, l, c:c + 1], in1=rstd[:, 0:T],
                                                            op0=ALU.mult, op1=ALU.mult),
                      reads=mixT_.keys(c, c + 1) + ["gv", ("F", 0)], writes=mixT_.keys(c, c + 1))

            def add_chunk(c):
                (P.pool if (state["use_pool"] and c % 2 == 1) else P.dve)(
                    lambda e, c=c: e.tensor_tensor(out=xT[:, c, 0:T], in0=xT[:, c, 0:T], in1=mixT_.ap[:, c, 0:T],
                                                   op=ALU.add),
                    reads=mixT_.keys(c, c + 1) + [("xT", c)], writes=[("xT", c)])
                if after_chunk is not None:
                    after_chunk(c)
            for c in range(17):
                if c < 16:
                    scale_chunk(c)
                if c >= 1:
                    add_chunk(c - 1)

        def proj_chunk(sap, skeys, KC, j, inR, T, b, fine=False):
            if fine:
                for kc in range(KC):
                    P.pe(lambda e, kc=kc: e.matmul(ps[b][:, 0:T], lhsT=sap[:, kc, j * 128:(j + 1) * 128],
                                                   rhs=inR.ap[:, kc, 0:T], start=(kc == 0), stop=(kc == KC - 1)),
                         reads=skeys + inR.keys(kc, kc + 1), writes=psk(b), banks=[b])
                return

            def fn(e):
                ins = None
                for kc in range(KC):
                    ins = e.matmul(ps[b][:, 0:T], lhsT=sap[:, kc, j * 128:(j + 1) * 128], rhs=inR.ap[:, kc, 0:T],
                                   start=(kc == 0), stop=(kc == KC - 1))
                return ins
            P.pe(fn, reads=skeys + inR.keys(), writes=psk(b), banks=[b])

        def tm_chunk(sap, skeys, KC, c0, ncols, inR, t0, M, b):
            def fn(e):
                ins = None
                for kc in range(KC):
                    ins = e.matmul(ps[b][0:M, 0:ncols], lhsT=inR.ap[:, kc, t0:t0 + M], rhs=sap[:, kc, c0:c0 + ncols],
                                   start=(kc == 0), stop=(kc == KC - 1))
                return ins
            P.pe(fn, reads=skeys + inR.keys(), writes=psk(b), banks=[b])

        def mem_phase():
            load_xT(mem, MEMT)
            a, k = xT_ap(MEMT)
            norm_stats(a, k, MEMT)
            for l in range(DEPTH):
                for c in range(16):
                    P.dve(lambda e, c=c, l=l: e.scalar_tensor_tensor(out=hT.ap[:, c, 0:MEMT], in0=xT[:, c, 0:MEMT],
                                                                     scalar=gv[:, 1, l, c:c + 1], in1=rstd[:, 0:MEMT],
                                                                     op0=ALU.mult, op1=ALU.mult),
                          reads=[("xT", c), "gv", ("F", 0)], writes=hT.keys(c, c + 1))
                for s in range(2):
                    sap, skeys = w_next(("mem", l, s), 16)
                    if s == 0:
                        for j in range(4):
                            b = nbank()
                            proj_chunk(sap, skeys, 16, j, hT, MEMT, b)
                            evac_copy(alt(), mkT[:, l, j, :], ps[b][:, 0:MEMT], psk(b), [("mkT", l)], [b])
                    for g in range(2):
                        b = nbank()
                        tm_chunk(sap, skeys, 16, 0, 512, hT, g * 128, 128, b)
                        stg = tmst[g % 2]
                        evac_copy("act", stg[:, 0:512], ps[b][:], psk(b), [("F", 3 + g % 2)], [b])
                        if s == 1:
                            P.dve(lambda e, g=g, b=b, l=l: e.tensor_copy(out=mvv[:, l, g, :], in_=ps[b][:]),
                                  reads=psk(b), writes=[("mvv", l)], banks=[b])
                        dst = (o_mk_p if s == 0 else o_mv_p)[l, g * 128:(g + 1) * 128, :]
                        P.dma(lambda e, dst=dst, stg=stg: e.dma_start(out=dst, in_=stg[:, 0:512]),
                              reads=[("F", 3 + g % 2)], writes=[("omem", l, s, g)], eng="act")

        def pool_prompt(l, T, first):
            P.dve(lambda e: e.tensor_copy(out=u_ext.ap[:, :, 0:16], in_=carry_u[:, l, :, :]),
                  reads=["carry_u"], writes=u_ext.keys())
            L = 16 + T
            for gi in range(4):
                w = 2 << gi
                src = u_ext.ap[:, gi, :]
                srck = u_ext.keys(gi, gi + 1)
                cur, curk = src, srck
                for k in range(1, gi + 2):
                    sh = 1 << (k - 1)
                    lo = (1 << k) - 1
                    dst = ptmp[k % 2]
                    P.dve(lambda e, cur=cur, dst=dst, lo=lo, sh=sh: e.tensor_tensor(
                        out=dst[:, lo:L], in0=cur[:, lo:L], in1=cur[:, lo - sh:L - sh], op=ALU.add),
                        reads=curk, writes=[("F", 3 + k % 2)])
                    cur, curk = dst[:], [("F", 3 + k % 2)]
                d = dTp[gi % 2]
                dk = [("dTp", gi % 2)]
                P.dve(lambda e, cur=cur, d=d, src=src, w=w: e.scalar_tensor_tensor(
                    out=d[:, 0:T], in0=cur[:, 16:16 + T], scalar=1.0 / w, in1=src[:, 16:16 + T],
                    op0=ALU.mult, op1=ALU.subtract), reads=curk + srck, writes=dk)
                if first:
                    P.dve(lambda e, cur=cur, gi=gi: e.tensor_tensor(out=pfix[:, 0:16], in0=cur[:, 16:32], in1=invcnt[:, gi, :],
                                                                   op=ALU.mult), reads=curk + INVK, writes=[("F", 5)])
                    P.dve(lambda e, d=d, src=src: e.tensor_tensor(out=d[:, 0:16], in0=pfix[:, 0:16], in1=src[:, 16:32],
                                                                 op=ALU.subtract), reads=[("F", 5)] + srck, writes=dk)
                yield
                b = state.get("fb", 6)
                P.pe(lambda e, b=b, gi=gi, d=d: e.matmul(ps[b][:, 0:T], lhsT=wpool[:, l, gi, :], rhs=d[:, 0:T],
                                                        start=True, stop=True),
                     reads=dk + ["wpool"], writes=psk(b), banks=[b])
                P.act(lambda e, b=b, gi=gi: e.activation(out=yT.ap[:, gi, 0:T], in_=ps[b][:, 0:T], func=AF.Copy,
                                                         scale=pscale[:, l, gi:gi + 1]),
                      reads=psk(b) + ["pscale"], writes=yT.keys(gi, gi + 1), banks=[b])
            P.dve(lambda e: e.tensor_copy(out=carry_u[:, l, :, :], in_=u_ext.ap[:, :, T:T + 16]),
                  reads=u_ext.keys(), writes=["carry_u"])
            yield

        def conv_prompt(l, T):
            P.dve(lambda e: e.tensor_copy(out=v_ext.ap[:, :, 0:2], in_=carry_v[:, l, :, :]),
                  reads=["carry_v"], writes=v_ext.keys())
            for c in range(4):
                vk = v_ext.keys(c, c + 1)
                ca = cacc[c % 2]
                cak = [("F", 1 + c % 2)]
                P.dve(lambda e, c=c: e.tensor_tensor(out=v_ext.ap[:, c, 2:2 + T], in0=v_ext.ap[:, c, 2:2 + T],
                                                     in1=hcR.ap[:, c, 0:T], op=ALU.mult),
                      reads=vk + hcR.keys(c, c + 1), writes=vk)
                P.act(lambda e, c=c, ca=ca: e.activation(out=ca[:, 0:T], in_=v_ext.ap[:, c, 0:T], func=AF.Copy,
                                                         scale=convw[:, l, 0, c:c + 1]),
                      reads=vk + ["convw"], writes=cak)
                for kk in (1, 2):
                    P.dve(lambda e, c=c, ca=ca, kk=kk: e.scalar_tensor_tensor(
                        out=ca[:, 0:T], in0=v_ext.ap[:, c, kk:kk + T], scalar=convw[:, l, kk, c:c + 1], in1=ca[:, 0:T],
                        op0=ALU.mult, op1=ALU.add), reads=vk + ["convw"] + cak, writes=cak)
                P.dve(lambda e, c=c, ca=ca: e.tensor_tensor(out=yT.ap[:, 4 + c, 0:T], in0=ca[:, 0:T],
                                                            in1=gbR.ap[:, c, 0:T], op=ALU.mult),
                      reads=cak + gbR.keys(c, c + 1), writes=yT.keys(4 + c, 5 + c))
                if c < 3:
                    yield
            P.dve(lambda e: e.tensor_copy(out=carry_v[:, l, :, :], in_=v_ext.ap[:, :, T:T + 2]),
                  reads=v_ext.keys(), writes=["carry_v"])
            yield

        def swa_prompt(l, T, first):
            NG = T // 128
            P.dve(lambda e: e.tensor_copy(out=kext.ap[:, :, 0:128], in_=carry_k[:, l, :, :]),
                  reads=["carry_k"], writes=kext.keys())
            it = 0
            for n in range(NG):
                for h in range(2):
                    base_b = 0 if it % 2 == 0 else 4
                    state["fb"] = 4 - base_b
                    it += 1
                    bD, bO = base_b + 2, base_b + 3
                    msk, mskk = (mask_first, "mask_first") if (first and n == 0) else (mask_cat, "mask_cat")
                    pts = []
                    for par in range(2):
                        b = base_b + par
                        p0 = par * 64

                        def fnS(e, b=b, p0=p0, msk=msk, h=h, n=n):
                            e.matmul(ps[b][:, 0:512], lhsT=identb[:], rhs=msk[:], start=True, stop=False)
                            ins = None
                            for part in range(2):
                                kc0 = 128 + n * 128 if part == 0 else n * 128
                                ins = e.matmul(ps[b][:, part * 256:(part + 1) * 256].rearrange("p (g q) -> p g q", g=2),
                                               lhsT=kext.ap[p0:p0 + 64, h, kc0:kc0 + 128],
                                               rhs=qR.ap[p0:p0 + 64, 2 * h:2 * h + 2, n * 128:(n + 1) * 128],
                                               start=False, stop=(part == 1))
                            return ins
                        P.pe(fnS, reads=["identb", mskk] + kext.keys(h, h + 1) + qR.keys(2 * h, 2 * h + 2),
                             writes=psk(b), banks=[b])
                        pi = (it % 2) * 2 + par
                        pt = Pt[pi]
                        P.act(lambda e, b=b, pt=pt: e.activation(out=pt[:], in_=ps[b][:], func=AF.Exp, scale=SWA_SCALE),
                              reads=psk(b), writes=KB(pi), banks=[b])
                        pts.append((pt, KB(pi)))

                    def fnD(e, pts=pts, h=h, bD=bD):
                        ins = None
                        for par in range(2):
                            pt = pts[par][0]
                            o = ps[bD][:, par * 256:(par + 1) * 256]
                            e.matmul(o, lhsT=ones1[:], rhs=pt[:, 0:256], start=True, stop=False)
                            e.matmul(o, lhsT=ones1[:], rhs=pt[:, 256:512], start=False, stop=False)
                            ins = e.matmul(o, lhsT=ones1[0:1, :], rhs=esink[0:1, l, h, par * 256:(par + 1) * 256],
                                           start=False, stop=True)
                        return ins
                    P.pe(fnD, reads=["ones1"] + ESK(l, h) + pts[0][1] + pts[1][1], writes=psk(bD), banks=[bD])

                    def fnO(e, pts=pts, h=h, bO=bO, n=n):
                        ins = None
                        for par in range(2):
                            pt = pts[par][0]
                            o = ps[bO][:, par * 256:(par + 1) * 256]
                            e.matmul(o, lhsT=Vd[:, n + 1, h, :], rhs=pt[:, 0:256], start=True, stop=False)
                            ins = e.matmul(o, lhsT=Vd[:, n, h, :], rhs=pt[:, 256:512], start=False, stop=True)
                        return ins
                    P.pe(fnO, reads=["Vd"] + pts[0][1] + pts[1][1], writes=psk(bO), banks=[bO])
                    yield
                    ri = it % 2
                    rd = rden[ri]
                    rdk = RDK[ri]
                    P.dve(lambda e, rd=rd, bD=bD: e.reciprocal(out=rd[:, 0:512], in_=ps[bD][:]),
                          reads=psk(bD), writes=rdk, banks=[bD])
                    for par in range(2):
                        p0 = par * 64
                        P.dve(lambda e, p0=p0, par=par, rd=rd, h=h, n=n, bO=bO: e.tensor_tensor(
                            out=yT.ap[p0:p0 + 64, 8 + 2 * h:10 + 2 * h, n * 128:(n + 1) * 128],
                            in0=ps[bO][p0:p0 + 64, par * 256:(par + 1) * 256].rearrange("p (g q) -> p g q", g=2),
                            in1=rd[p0:p0 + 64, par * 256:(par + 1) * 256].rearrange("p (g q) -> p g q", g=2),
                            op=ALU.mult),
                            reads=psk(bO) + rdk, writes=yT.keys(8 + 2 * h, 10 + 2 * h), banks=[bO])
                    yield
            P.dve(lambda e: e.tensor_copy(out=carry_k[:, l, :, :], in_=kext.ap[:, :, T:T + 128]),
                  reads=kext.keys(), writes=["carry_k"])
            P.dve(lambda e: e.tensor_copy(out=carry_V[:, l, :, :], in_=Vd[:, NG, :, :]),
                  reads=["Vd"], writes=["carry_V"])

        def mem_prompt(l, T):
            for hd in range(4):
                base_b = 0 if hd % 2 == 0 else 4
                state["fb"] = 4 - base_b
                bD, bO = base_b + 2, base_b + 3
                pts = []
                for kb in range(2):
                    b = base_b + kb
                    P.pe(lambda e, b=b, kb=kb, hd=hd: e.matmul(ps[b][:, 0:T], lhsT=mkT[:, l, hd, kb * 128:(kb + 1) * 128],
                                                              rhs=qmR.ap[:, hd, 0:T], start=True, stop=True),
                         reads=[("mkT", l)] + qmR.keys(hd, hd + 1), writes=psk(b), banks=[b])
                    pt = Pt[(hd % 2) * 2 + kb]
                    ptk = [("Bt", (hd % 2) * 2 + kb)]
                    P.act(lambda e, b=b, pt=pt: e.activation(out=pt[:, 0:T], in_=ps[b][:, 0:T], func=AF.Exp, scale=MEM_SCALE),
                          reads=psk(b), writes=ptk, banks=[b])
                    pts.append((pt, ptk))

                def fnD(e, pts=pts, bD=bD):
                    e.matmul(ps[bD][:, 0:T], lhsT=ones1[:], rhs=pts[0][0][:, 0:T], start=True, stop=False)
                    return e.matmul(ps[bD][:, 0:T], lhsT=ones1[:], rhs=pts[1][0][:, 0:T], start=False, stop=True)
                P.pe(fnD, reads=["ones1"] + pts[0][1] + pts[1][1], writes=psk(bD), banks=[bD])

                def fnO(e, pts=pts, hd=hd, bO=bO):
                    e.matmul(ps[bO][:, 0:T], lhsT=mvv[:, l, 0, hd * 128:(hd + 1) * 128], rhs=pts[0][0][:, 0:T],
                             start=True, stop=False)
                    return e.matmul(ps[bO][:, 0:T], lhsT=mvv[:, l, 1, hd * 128:(hd + 1) * 128], rhs=pts[1][0][:, 0:T],
                                    start=False, stop=True)
                P.pe(fnO, reads=[("mvv", l)] + pts[0][1] + pts[1][1], writes=psk(bO), banks=[bO])
                yield
                rd = rden[hd % 2]
                rdk = RDK[hd % 2]
                P.dve(lambda e, rd=rd, bD=bD: e.reciprocal(out=rd[:, 0:T], in_=ps[bD][:, 0:T]),
                      reads=psk(bD), writes=rdk, banks=[bD])
                P.dve(lambda e, rd=rd, bO=bO, hd=hd: e.tensor_tensor(out=yT.ap[:, 12 + hd, 0:T], in0=ps[bO][:, 0:T],
                                                                     in1=rd[:, 0:T], op=ALU.mult),
                      reads=psk(bO) + rdk, writes=yT.keys(12 + hd, 13 + hd), banks=[bO])
                yield

        def pre1_chunk(l, T):
            def f(c):
                P.act(lambda e, c=c: e.activation(out=hT.ap[:, c, 0:T], in_=xT[:, c, 0:T], func=AF.Copy,
                                                  scale=gv[:, 0, l, c:c + 1]),
                      reads=[("xT", c), "gv"], writes=hT.keys(c, c + 1))
            return f

        def layer_prompt(l, ti):
            T = TP
            first = (ti == 0)
            last = (ti == NPT - 1)
            if l == 0:
                for c in range(16):
                    pre1_chunk(0, T)(c)
            P.dve(lambda e: e.tensor_copy(out=Vd[:, 0, :, :], in_=carry_V[:, l, :, :]), reads=["carry_V"], writes=["Vd"])
            RK = [("F", 0)]

            def evac_scaled(out_ap, b, wkeys):
                P.dve(lambda e: e.tensor_tensor(out=out_ap, in0=ps[b][:, 0:T], in1=rstd[:, 0:T], op=ALU.mult),
                      reads=psk(b) + RK, writes=wkeys, banks=[b])

            def dest(m):
                if m < 4:
                    return u_ext.ap[:, m, 16:16 + T], u_ext.keys(m, m + 1)
                if m < 8:
                    return hcR.ap[:, m - 4, 0:T], hcR.keys(m - 4, m - 3)
                if m < 12:
                    return gbR.ap[:, m - 8, 0:T], gbR.keys(m - 8, m - 7)
                if m < 16:
                    return v_ext.ap[:, m - 12, 2:2 + T], v_ext.keys(m - 12, m - 11)
                if m < 20:
                    return qR.ap[:, m - 16, 0:T], qR.keys(m - 16, m - 15)
                if m < 22:
                    return kext.ap[:, m - 20, 128:128 + T], kext.keys(m - 20, m - 19)
                return qmR.ap[:, m - 22, 0:T], qmR.keys(m - 22, m - 21)
            for s in range(WIN_NSLAB):
                sap, skeys = w_next(("in", l, s), 16)
                pend = []
                for j in range(4):
                    m = s * 4 + j
                    if m >= 26:
                        continue
                    b = nbank()
                    proj_chunk(sap, skeys, 16, j, hT, T, b, fine=(m == 0))
                    if s == 0:
                        pend.append((m, b))
                    else:
                        o_, k_ = dest(m)
                        evac_scaled(o_, b, k_)
                if last and s == 0:
                    tm_chunk(sap, skeys, 16, 0, 512, hT, T - 16, 16, 6)
                if s == 0:
                    a_, k_ = xT_ap(T)
                    norm_stats(a_, k_, T, mode="rstd")
                    bq = nbank()

                    def fnr(e, bq=bq):
                        ins = None
                        for g in range(T // 128):
                            ins = e.matmul(ps[bq][:, g:g + 1], lhsT=rstd[:, g * 128:(g + 1) * 128], rhs=ident[:, 0:1],
                                           start=True, stop=True)
                        if last:
                            ins = e.matmul(ps[bq][0:16, 4:5], lhsT=rstd[:, T - 16:T], rhs=ident[:, 0:1], start=True, stop=True)
                        return ins
                    P.pe(fnr, reads=RK + ["ident"], writes=psk(bq), banks=[bq])
                    P.act(lambda e, bq=bq: e.activation(out=rtm[:, 0:8], in_=ps[bq][:, 0:8], func=AF.Copy),
                          reads=psk(bq), writes=["rtm"], banks=[bq])
                    for (m, b) in pend:
                        o_, k_ = dest(m)
                        evac_scaled(o_, b, k_)
                if last and s == 0:
                    b = 6
                    P.act(lambda e, b=b: e.activation(out=tmst[0][0:16, 0:512], in_=ps[b][0:16, :], func=AF.Copy,
                                                      scale=rtm[0:16, 4:5]),
                          reads=psk(b) + ["rtm"], writes=[("F", 3)], banks=[b])
                    P.dma(lambda e: e.dma_start(out=o_pool_p[l], in_=tmst[0][1:16, 0:512]), reads=[("F", 3)],
                          writes=[("o_pool_p", l)], eng="act")
                if last and s == 1:
                    b = 6
                    tm_chunk(sap, skeys, 16, 0, 512, hT, T - 16, 16, b)
                    P.act(lambda e, b=b: e.activation(out=hcst[0:16, 0:512], in_=ps[b][0:16, :], func=AF.Copy,
                                                      scale=rtm[0:16, 4:5]),
                          reads=psk(b) + ["rtm"], writes=[("F", 5)], banks=[b])
                if last and s == 3:
                    b = 6
                    tm_chunk(sap, skeys, 16, 0, 512, hT, T - 16, 16, b)
                    P.dve(lambda e, b=b: e.scalar_tensor_tensor(out=tmst[1][0:16, 0:512], in0=ps[b][0:16, :],
                                                                scalar=rtm[0:16, 4:5], in1=hcst[0:16, 0:512],
                                                                op0=ALU.mult, op1=ALU.mult),
                          reads=psk(b) + [("F", 5), "rtm"], writes=[("F", 4)], banks=[b])
                    P.dma(lambda e: e.dma_start(out=o_conv_p[l], in_=tmst[1][14:16, 0:512]), reads=[("F", 4)],
                          writes=[("o_conv_p", l)], eng="act")
                if s == 6:
                    for g in range(T // 128):
                        b = 6
                        tm_chunk(sap, skeys, 16, 256, 256, hT, g * 128, 128, b)
                        for h in range(2):
                            P.dve(lambda e, g=g, h=h, b=b: e.tensor_scalar(
                                out=Vd[:, g + 1, h, :].rearrange("p (r d) -> p r d", r=2),
                                in0=ps[b][:, 128 + h * 64:128 + (h + 1) * 64].unsqueeze(1).to_broadcast([128, 2, 64]),
                                scalar1=rtm[:, g:g + 1], scalar2=None, op0=ALU.mult),
                                reads=psk(b) + ["rtm"], writes=["Vd"], banks=[b])
                        if last and g == T // 128 - 1:
                            P.act(lambda e, b=b, g=g: e.activation(out=tmst[0][:, 0:256], in_=ps[b][:, 0:256], func=AF.Copy,
                                                                   scale=rtm[:, g:g + 1]),
                                  reads=psk(b) + ["rtm"], writes=[("F", 3)], banks=[b])
                            P.dma(lambda e: e.dma_start(out=o_k_p[l], in_=tmst[0][:, 0:128]), reads=[("F", 3)],
                                  writes=[("o_k_p", l)], eng="act")
                            P.dma(lambda e: e.dma_start(out=o_v_p[l], in_=tmst[0][:, 128:256]), reads=[("F", 3)],
                                  writes=[("o_v_p", l)], eng="act")
            if DBG["mixers"]:
                import itertools
                A = itertools.chain(swa_prompt(l, T, first), mem_prompt(l, T))
                B = itertools.chain(pool_prompt(l, T, first), conv_prompt(l, T))
                a_alive, b_alive = True, True
                while a_alive or b_alive:
                    if a_alive:
                        a_alive = next(A, "end") != "end"
                    if b_alive:
                        b_alive = next(B, "end") != "end"
                    if a_alive:
                        a_alive = next(A, "end") != "end"
            if DBG["ffn"]:
                ffn_and_out(RP, l, T, post2_hook=(pre1_chunk(l + 1, T) if l + 1 < DBG["nlayers"] else None))

        def ffn_and_out(R, l, T, post2_hook=None):
            for s in range(4):
                sap, skeys = w_next(("out", l, s), 16)
                for j in range(4):
                    m = s * 4 + j
                    b = nbank()
                    proj_chunk(sap, skeys, 16, j, R.yT, T, b)
                    evac_copy(alt(), R.mixT.ap[:, m, 0:T], ps[b][:, 0:T], psk(b), R.mixT.keys(m, m + 1), [b])
            def h2_chunk(c):
                P.act(lambda e, c=c: e.activation(out=R.hT.ap[:, c, 0:T], in_=xT[:, c, 0:T], func=AF.Copy,
                                                  scale=gv[:, 3, l, c:c + 1]),
                      reads=[("xT", c), "gv"], writes=R.hT.keys(c, c + 1))
            postnorm_residual(R, 2, l, T, after_chunk=h2_chunk)
            for s in range(16):
                sap, skeys = w_next(("up", l, s), 16)
                if s == 1:
                    a_, k_ = xT_ap(T)
                    norm_stats(a_, k_, T, mode="epsq")
                for j in range(4):
                    m = s * 4 + j
                    b = nbank()
                    proj_chunk(sap, skeys, 16, j, R.hT, T, b, fine=(m == 0))
                    rt = rtmp[m % 2]
                    rk = [("F", 1 + m % 2)]
                    P.act(lambda e, b=b, rt=rt: e.activation(out=rt[:, 0:T], in_=ps[b][:, 0:T], func=AF.Relu),
                          reads=psk(b), writes=rk, banks=[b])
                    P.dve(lambda e, m=m, rt=rt: e.tensor_tensor(out=R.hidT.ap[:, m, 0:T], in0=rt[:, 0:T], in1=rt[:, 0:T],
                                                                op=ALU.mult),
                          reads=rk, writes=R.hidT.keys(m, m + 1))
            for G in range(4):
                base = 0 if G % 2 == 0 else 4
                for q in range(4):
                    sap, skeys = w_next(("down", l, G * 4 + q), 16)
                    for j in range(4):
                        b = base + j
                        if G == 0 and q == 0 and j == 0:
                            for kc in range(16):
                                P.pe(lambda e, kc=kc, sap=sap, b=b: e.matmul(
                                    ps[b][:, 0:T], lhsT=sap[:, kc, 0:128], rhs=R.hidT.ap[:, kc, 0:T],
                                    start=(kc == 0), stop=False),
                                    reads=skeys + R.hidT.keys(kc, kc + 1), writes=psk(b), banks=[b])
                            continue

                        def fn(e, sap=sap, q=q, j=j, b=b):
                            ins = None
                            for kc in range(16):
                                ins = e.matmul(ps[b][:, 0:T], lhsT=sap[:, kc, j * 128:(j + 1) * 128],
                                               rhs=R.hidT.ap[:, q * 16 + kc, 0:T],
                                               start=(q == 0 and kc == 0), stop=(q == 3 and kc == 15))
                            return ins
                        P.pe(fn, reads=skeys + R.hidT.keys(q * 16, q * 16 + 16), writes=psk(b), banks=[b])
                for j in range(4):
                    b = base + j
                    m = G * 4 + j
                    evac_copy(alt(), R.mixT.ap[:, m, 0:T], ps[b][:, 0:T], psk(b), R.mixT.keys(m, m + 1), [b])
            postnorm_residual(R, 4, l, T, mode="rstd_q", after_chunk=post2_hook)

        psb = [p_[:].bitcast(BF16) for p_ in ps]
        s_hidT = Region(BIG, "BIG", 0, 64, 64, BF16)
        s_yT = Region(BIG, "BIG", 8192, 16, 64, BF16)
        s_q = Region(BIG, "BIG", 10240, 4, 64, BF16)
        s_kx = Region(BIG, "BIG", 10752, 2, 64, BF16)
        s_qm = Region(BIG, "BIG", 11008, 4, 64, BF16)
        s_hc = Region(BIG, "BIG", 11520, 4, 64, F32)
        s_gb = Region(BIG, "BIG", 12544, 4, 64, F32)
        s_vx = Region(BIG, "BIG", 13568, 4, 96, F32)
        s_ux = Region(BIG, "BIG", 15104, 4, 304, F32)
        pst = Region(BIG, "BIG", 20480, 2, 512, F32)
        cst = Region(BIG, "BIG", 24576, 1, 512, F32)
        Kd = Region(BIG, "BIG", 26624, 16, 256, BF16)
        KTd = Region(BIG, "BIG", 34816, 32, 128, BF16)
        Vds = Region(BIG, "BIG", 43008, 16, 256, BF16)
        mc = [dict(K=Region(BIG, "BIG", 51200, 4, 512, BF16), KT=Region(BIG, "BIG", 55296, 8, 256, BF16),
                   V=Region(BIG, "BIG", 59392, 4, 512, BF16)),
              dict(K=Region(U1, "U1", 8192, 4, 512, BF16), KT=Region(U1, "U1", 12288, 8, 256, BF16),
                   V=Region(U1, "U1", 16384, 4, 512, BF16))]
        Vn = Region(U1, "U1", 20480, 16, 256, BF16)
        s_hT = Region(U1, "U1", 0, 16, 64, BF16)
        s_mixT = Region(U1, "U1", 2048, 16, 64, F32)
        RS = SimpleNamespace(hT=s_hT, mixT=s_mixT, yT=s_yT, hidT=s_hidT)

        def bt4(ap):
            return ap.rearrange("p (b t) -> p b t", t=4)

        def layer_sample(l):
            T = TS
            prenorm_to_hT(RS, 0, l, T)
            sp2 = spool[l].rearrange("b r f -> (b r) f")
            P.dma(lambda e: e.dma_start(out=pst.ap[0:128, 0, :], in_=sp2[0:128, :]), writes=pst.keys(0, 1))
            P.dma(lambda e: e.dma_start(out=pst.ap[0:112, 1, :], in_=sp2[128:240, :]), writes=pst.keys(1, 2))
            P.dma(lambda e: e.dma_start(out=cst.ap[0:32, 0, :], in_=sconv[l].rearrange("b r f -> (b r) f")), writes=cst.keys())
            Kd5 = Kd.ap.rearrange("p b (h r d) -> p b h r d", h=2, r=2)
            Vd5 = Vds.ap.rearrange("p b (h r d) -> p b h r d", h=2, r=2)
            for r in range(2):
                for h in range(2):
                    P.dma(lambda e, r=r, h=h: e.dma_start(out=Kd5[:, :, h, r, :],
                                                          in_=ck[l][:, :, h * 64:(h + 1) * 64].rearrange("b k d -> k b d")),
                          writes=Kd.keys(), eng="pool")
                    P.dma(lambda e, r=r, h=h: e.dma_start(out=Vd5[:, :, h, r, :],
                                                          in_=cv[l][:, :, h * 64:(h + 1) * 64].rearrange("b k d -> k b d")),
                          writes=Vds.keys(), eng="pool")
            P.dma(lambda e: e.dma_start(out=o_pool_s[l][:, 0:11, :], in_=spool[l][:, 4:15, :]), writes=[("o_pool_s", l, 0)], eng="pool")
            P.dma(lambda e: e.dma_start(out=o_k_s[l][:, 0:124, :], in_=ck[l][:, 4:128, :]), writes=[("o_k_s", l, 0)], eng="pool")
            P.dma(lambda e: e.dma_start(out=o_v_s[l][:, 0:124, :], in_=cv[l][:, 4:128, :]), writes=[("o_v_s", l, 0)], eng="pool")
            for gi in range(4):
                b = nbank(0, 4)

                def fn(e, gi=gi, b=b):
                    e.transpose(out=ps[b][:, 0:128], in_=pst.ap[0:128, 0, gi * 128:(gi + 1) * 128], identity=ident[:])
                    return e.transpose(out=ps[b][:, 128:240], in_=pst.ap[0:112, 1, gi * 128:(gi + 1) * 128],
                                       identity=ident[0:112, 0:112])
                P.pe(fn, reads=pst.keys() + ["ident"], writes=psk(b), banks=[b])
                u3 = s_ux.ap[:, gi, :].rearrange("p (b r) -> p b r", r=19)
                evac_copy(alt(), u3[:, :, 0:15], ps[b][:, 0:240].rearrange("p (b r) -> p b r", r=15), psk(b),
                          s_ux.keys(gi, gi + 1), [b])
            for c in range(4):
                b = nbank(0, 4)
                P.pe(lambda e, c=c, b=b: e.transpose(out=ps[b][:, 0:32], in_=cst.ap[0:32, 0, c * 128:(c + 1) * 128],
                                                     identity=ident[0:32, 0:32]),
                     reads=cst.keys() + ["ident"], writes=psk(b), banks=[b])
                v3 = s_vx.ap[:, c, :].rearrange("p (b r) -> p b r", r=6)
                evac_copy(alt(), v3[:, :, 0:2], ps[b][:, 0:32].rearrange("p (b r) -> p b r", r=2), psk(b),
                          s_vx.keys(c, c + 1), [b])
            for q4 in range(4):
                b = 4 + q4

                def fnk(e, q4=q4, b=b):
                    ins = None
                    for i in range(8):
                        idx = q4 * 8 + i
                        bb, h = idx // 2, idx % 2
                        ins = e.transpose(out=psb[b][:, i * 128:(i + 1) * 128], in_=Kd.ap[:, bb, h * 128:(h + 1) * 128],
                                          identity=identb[:])
                    return ins
                P.pe(fnk, reads=Kd.keys() + ["identb"], writes=psk(b), banks=[b])
                evac_copy(alt(), KTd.ap[:, q4 * 8:(q4 + 1) * 8, :], psb[b][:].rearrange("p (i k) -> p i k", i=8), psk(b),
                          KTd.keys(q4 * 8, q4 * 8 + 8), [b])
            for s in range(WIN_NSLAB):
                sap, skeys = w_next(("in", l, s), 16)
                for j in range(4):
                    m = s * 4 + j
                    if m >= 26:
                        continue
                    b = nbank(0, 4)
                    proj_chunk(sap, skeys, 16, j, s_hT, T, b, fine=(m == 0))
                    eng = alt()
                    src = ps[b][:, 0:T]
                    if m < 4:
                        u3 = s_ux.ap[:, m, :].rearrange("p (b r) -> p b r", r=19)
                        evac_copy(eng, u3[:, :, 15:19], bt4(src), psk(b), s_ux.keys(m, m + 1), [b])
                    elif m < 8:
                        evac_copy(eng, s_hc.ap[:, m - 4, :], src, psk(b), s_hc.keys(m - 4, m - 3), [b])
                    elif m < 12:
                        evac_copy(eng, s_gb.ap[:, m - 8, :], src, psk(b), s_gb.keys(m - 8, m - 7), [b])
                    elif m < 16:
                        v3 = s_vx.ap[:, m - 12, :].rearrange("p (b r) -> p b r", r=6)
                        evac_copy(eng, v3[:, :, 2:6], bt4(src), psk(b), s_vx.keys(m - 12, m - 11), [b])
                    elif m < 20:
                        evac_copy(eng, s_q.ap[:, m - 16, :], src, psk(b), s_q.keys(m - 16, m - 15), [b])
                    elif m < 22:
                        evac_copy(eng, s_kx.ap[:, m - 20, :], src, psk(b), s_kx.keys(m - 20, m - 19), [b])
                    else:
                        evac_copy(eng, s_qm.ap[:, m - 22, :], src, psk(b), s_qm.keys(m - 22, m - 21), [b])
                if s == 0:
                    b = 6
                    tm_chunk(sap, skeys, 16, 0, 512, s_hT, 0, 64, b)
                    evac_copy("act", tmst[0][0:64, 0:512], ps[b][0:64, :], psk(b), KF(3), [b])
                    P.dma(lambda e: e.dma_start(out=o_pool_s[l][:, 11:15, :], in_=tmst[0][0:64, 0:512]), reads=KF(3),
                          writes=[("o_pool_s", l, 1)], eng="pool")
                if s == 1:
                    b = 6
                    tm_chunk(sap, skeys, 16, 0, 512, s_hT, 0, 64, b)
                    evac_copy("act", hcst[0:64, 0:512], ps[b][0:64, :], psk(b), KF(5), [b])
                if s == 3:
                    b = 6
                    tm_chunk(sap, skeys, 16, 0, 512, s_hT, 0, 64, b)
                    P.dve(lambda e, b=b: e.tensor_tensor(out=tmst[1][0:64, 0:512], in0=ps[b][0:64, :], in1=hcst[0:64, 0:512],
                                                         op=ALU.mult), reads=psk(b) + KF(5), writes=KF(4), banks=[b])
                    for t in (2, 3):
                        for bb in range(SB):
                            P.dma(lambda e, t=t, bb=bb: e.dma_start(out=o_conv_s[l][bb, t - 2:t - 1, :],
                                                                    in_=tmst[1][bb * 4 + t:bb * 4 + t + 1, 0:512]),
                                  reads=KF(4), writes=[("o_conv_s", l, t, bb)], eng="pool")
                if s == 6:
                    b = 6
                    tm_chunk(sap, skeys, 16, 256, 256, s_hT, 0, 64, b)
                    evac_copy("act", tmst[0][0:64, 0:256], ps[b][0:64, 0:256], psk(b), KF(3), [b])
                    P.dma(lambda e: e.dma_start(out=o_k_s[l][:, 124:128, :], in_=tmst[0][0:64, 0:128]), reads=KF(3),
                          writes=[("o_k_s", l, 1)], eng="pool")
                    P.dma(lambda e: e.dma_start(out=o_v_s[l][:, 124:128, :], in_=tmst[0][0:64, 128:256]), reads=KF(3),
                          writes=[("o_v_s", l, 1)], eng="pool")
                    for q4 in range(4):
                        bk = nbank(0, 4)

                        def fnv(e, q4=q4, bk=bk, sap=sap):
                            ins = None
                            for i in range(4):
                                bb = q4 * 4 + i
                                for kc in range(16):
                                    ins = e.matmul(ps[bk][0:4, i * 128:(i + 1) * 128], lhsT=s_hT.ap[:, kc, bb * 4:bb * 4 + 4],
                                                   rhs=sap[:, kc, 384:512], start=(kc == 0), stop=(kc == 15))
                            return ins
                        P.pe(fnv, reads=skeys + s_hT.keys(), writes=psk(bk), banks=[bk])
                        for h in range(2):
                            P.dve(lambda e, q4=q4, bk=bk, h=h: e.tensor_copy(
                                out=Vn.ap[0:4, q4 * 4:(q4 + 1) * 4, h * 128:(h + 1) * 128].rearrange("p b (r d) -> p b r d", r=2),
                                in_=ps[bk][0:4, :].rearrange("p (i h d) -> p i h d", i=4, h=2)[:, :, h, :]
                                .unsqueeze(2).to_broadcast([4, 4, 2, 64])),
                                reads=psk(bk), writes=Vn.keys(q4 * 4, q4 * 4 + 4), banks=[bk])
            for gi in range(4):
                w = 2 << gi
                u3 = s_ux.ap[:, gi, :].rearrange("p (b r) -> p b r", r=19)
                uk = s_ux.keys(gi, gi + 1)
                cur, curk = u3, uk
                for k in range(1, gi + 2):
                    sh = 1 << (k - 1)
                    lo = (1 << k) - 1
                    dst = ptmp[k % 2][:, 0:304].rearrange("p (b r) -> p b r", r=19)
                    P.dve(lambda e, cur=cur, dst=dst, lo=lo, sh=sh: e.tensor_tensor(
                        out=dst[:, :, lo:19], in0=cur[:, :, lo:19], in1=cur[:, :, lo - sh:19 - sh], op=ALU.add),
                        reads=curk, writes=KF(3 + k % 2))
                    cur, curk = dst, KF(3 + k % 2)
                d = dT[gi % 2]
                P.dve(lambda e, cur=cur, d=d, u3=u3, w=w: e.scalar_tensor_tensor(
                    out=bt4(d[:, 0:64]), in0=cur[:, :, 15:19], scalar=1.0 / w, in1=u3[:, :, 15:19],
                    op0=ALU.mult, op1=ALU.subtract), reads=curk + uk, writes=KB(gi % 2))
                b = nbank(0, 4)
                P.pe(lambda e, b=b, gi=gi, d=d: e.matmul(ps[b][:, 0:64], lhsT=wpool[:, l, gi, :], rhs=d[:, 0:64],
                                                        start=True, stop=True),
                     reads=KB(gi % 2) + ["wpool"], writes=psk(b), banks=[b])
                P.act(lambda e, b=b, gi=gi: e.activation(out=s_yT.ap[:, gi, :], in_=ps[b][:, 0:64], func=AF.Copy,
                                                         scale=pscale[:, l, gi:gi + 1]),
                      reads=psk(b) + ["pscale"], writes=s_yT.keys(gi, gi + 1), banks=[b])
            for c in range(4):
                v3 = s_vx.ap[:, c, :].rearrange("p (b r) -> p b r", r=6)
                vk = s_vx.keys(c, c + 1)
                ca = bt4(cacc[c % 2][:, 0:64])
                cak = KF(1 + c % 2)
                P.dve(lambda e, v3=v3, c=c: e.tensor_tensor(out=v3[:, :, 2:6], in0=v3[:, :, 2:6], in1=bt4(s_hc.ap[:, c, :]),
                                                           op=ALU.mult), reads=vk + s_hc.keys(c, c + 1), writes=vk)
                P.act(lambda e, v3=v3, c=c, ca=ca: e.activation(out=ca, in_=v3[:, :, 0:4], func=AF.Copy,
                                                               scale=convw[:, l, 0, c:c + 1]),
                      reads=vk + ["convw"], writes=cak)
                for kk in (1, 2):
                    P.dve(lambda e, v3=v3, c=c, ca=ca, kk=kk: e.scalar_tensor_tensor(
                        out=ca, in0=v3[:, :, kk:kk + 4], scalar=convw[:, l, kk, c:c + 1], in1=ca,
                        op0=ALU.mult, op1=ALU.add), reads=vk + ["convw"] + cak, writes=cak)
                P.dve(lambda e, c=c, ca=ca: e.tensor_tensor(out=bt4(s_yT.ap[:, 4 + c, :]), in0=ca, in1=bt4(s_gb.ap[:, c, :]),
                                                            op=ALU.mult),
                      reads=cak + s_gb.keys(c, c + 1), writes=s_yT.keys(4 + c, 5 + c))
            bA, bC, bE, bF = [0, 1], [2, 3], 4, 5
            for par in range(2):
                p0 = par * 64

                def fnS(e, par=par, p0=p0):
                    e.matmul(ps[bA[par]][:, 0:256], lhsT=identb[:], rhs=mask_sc[:], start=True, stop=False)
                    ins = None
                    for bb in range(SB):
                        for h in range(2):
                            c0 = h * 128 + bb * 8
                            ins = e.matmul(ps[bA[par]][:, c0:c0 + 8].rearrange("p (g t) -> p g t", g=2),
                                           lhsT=KTd.ap[p0:p0 + 64, bb * 2 + h, :],
                                           rhs=s_q.ap[p0:p0 + 64, 2 * h:2 * h + 2, bb * 4:bb * 4 + 4],
                                           start=False, stop=(bb == SB - 1 and h == 1))
                    return ins
                P.pe(fnS, reads=["identb", "mask_sc"] + KTd.keys() + s_q.keys(), writes=psk(bA[par]), banks=[bA[par]])

                def fnN(e, par=par, p0=p0):
                    e.matmul(ps[bC[par]][0:4, 0:256], lhsT=identb[0:4, 0:4], rhs=mask_sn[0:4, :], start=True, stop=False)
                    ins = None
                    for bb in range(SB):
                        for h in range(2):
                            c0 = h * 128 + bb * 8
                            ins = e.matmul(ps[bC[par]][0:4, c0:c0 + 8].rearrange("p (g t) -> p g t", g=2),
                                           lhsT=s_kx.ap[p0:p0 + 64, h, bb * 4:bb * 4 + 4],
                                           rhs=s_q.ap[p0:p0 + 64, 2 * h:2 * h + 2, bb * 4:bb * 4 + 4],
                                           start=False, stop=(bb == SB - 1 and h == 1))
                    return ins
                P.pe(fnN, reads=["identb", "mask_sn"] + s_kx.keys() + s_q.keys(), writes=psk(bC[par]), banks=[bC[par]])
                P.act(lambda e, par=par: e.activation(out=Pt[par][:, 0:256], in_=ps[bA[par]][:, 0:256], func=AF.Exp,
                                                      scale=SWA_SCALE), reads=psk(bA[par]), writes=KB(par), banks=[bA[par]])
                P.act(lambda e, par=par: e.activation(out=Pt[2 + par][0:4, 0:256], in_=ps[bC[par]][0:4, 0:256], func=AF.Exp,
                                                      scale=SWA_SCALE), reads=psk(bC[par]), writes=KB(2 + par), banks=[bC[par]])

            def fnDs(e):
                ins = None
                for par in range(2):
                    o = ps[bE][:, par * 256:(par + 1) * 256]
                    e.matmul(o, lhsT=ones1[:], rhs=Pt[par][:, 0:256], start=True, stop=False)
                    e.matmul(o, lhsT=ones1[0:4, :], rhs=Pt[2 + par][0:4, 0:256], start=False, stop=False)
                    ins = e.matmul(o, lhsT=ones1[0:1, :], rhs=esink_s[0:1, l, par, :], start=False, stop=True)
                return ins
            P.pe(fnDs, reads=["ones1", ("esink_s", l)] + KB(0) + KB(1) + KB(2) + KB(3), writes=psk(bE), banks=[bE])

            def fnOs(e):
                ins = None
                for par in range(2):
                    for bb in range(SB):
                        for h in range(2):
                            cc = h * 128 + bb * 8
                            c0 = par * 256 + cc
                            e.matmul(ps[bF][:, c0:c0 + 8], lhsT=Vds.ap[:, bb, h * 128:(h + 1) * 128], rhs=Pt[par][:, cc:cc + 8],
                                     start=True, stop=False)
                            ins = e.matmul(ps[bF][:, c0:c0 + 8], lhsT=Vn.ap[0:4, bb, h * 128:(h + 1) * 128],
                                           rhs=Pt[2 + par][0:4, cc:cc + 8], start=False, stop=True)
                return ins
            P.pe(fnOs, reads=Vds.keys() + Vn.keys() + KB(0) + KB(1) + KB(2) + KB(3), writes=psk(bF), banks=[bF])
            P.dve(lambda e: e.reciprocal(out=rden[0][:, 0:512], in_=ps[bE][:]), reads=psk(bE), writes=RDK[0], banks=[bE])
            for par in range(2):
                for h in range(2):
                    p0 = par * 64
                    c0 = par * 256 + h * 128
                    P.dve(lambda e, p0=p0, c0=c0, h=h: e.tensor_tensor(
                        out=s_yT.ap[p0:p0 + 64, 8 + 2 * h:10 + 2 * h, :].rearrange("p g (b t) -> p b g t", t=4),
                        in0=ps[bF][p0:p0 + 64, c0:c0 + 128].rearrange("p (b g t) -> p b g t", b=SB, g=2),
                        in1=rden[0][p0:p0 + 64, c0:c0 + 128].rearrange("p (b g t) -> p b g t", b=SB, g=2),
                        op=ALU.mult), reads=psk(bF) + RDK[0], writes=s_yT.keys(8 + 2 * h, 10 + 2 * h), banks=[bF])
            bS, bDn, bOm = 2, 3, 6
            Pm = Bt[0]
            for g in range(SB // 2):
                M_ = mc[g % 2]
                b0 = 2 * g
                P.dma(lambda e, M_=M_, b0=b0: e.dma_start(out=M_["K"].ap.rearrange("p (b k) f -> p b k f", b=2),
                                                          in_=cmk[l][b0:b0 + 2].rearrange("b (k p) f -> p b k f", p=128)),
                      writes=M_["K"].keys(), eng="pool")
                P.dma(lambda e, M_=M_, b0=b0: e.dma_start(out=M_["V"].ap.rearrange("p (b k) f -> p b k f", b=2),
                                                          in_=cmv[l][b0:b0 + 2].rearrange("b (k p) f -> p b k f", p=128)),
                      writes=M_["V"].keys(), eng="pool")
                for b2 in range(2):
                    bk = b2

                    def fnT(e, M_=M_, b2=b2, bk=bk):
                        ins = None
                        for hd in range(4):
                            for blk in range(2):
                                i = hd * 2 + blk
                                ins = e.transpose(out=psb[bk][:, i * 128:(i + 1) * 128],
                                                  in_=M_["K"].ap[:, b2 * 2 + blk, hd * 128:(hd + 1) * 128], identity=identb[:])
                        return ins
                    P.pe(fnT, reads=M_["K"].keys() + ["identb"], writes=psk(bk), banks=[bk])
                    evac_copy(alt(), M_["KT"].ap[:, b2 * 4:(b2 + 1) * 4, :], psb[bk][:].rearrange("p (h k) -> p h k", h=4),
                              psk(bk), M_["KT"].keys(b2 * 4, b2 * 4 + 4), [bk])

                def fnSm(e, M_=M_, b0=b0):
                    ins = None
                    for b2 in range(2):
                        bb = b0 + b2
                        for blk in range(2):
                            for hd in range(4):
                                c0 = bb * 32 + blk * 16 + hd * 4
                                ins = e.matmul(ps[bS][:, c0:c0 + 4], lhsT=M_["KT"].ap[:, b2 * 4 + hd, blk * 128:(blk + 1) * 128],
                                               rhs=s_qm.ap[:, hd, bb * 4:bb * 4 + 4], start=True, stop=True)
                    return ins
                P.pe(fnSm, reads=M_["KT"].keys() + s_qm.keys(), writes=psk(bS), banks=[bS])
                P.act(lambda e, b0=b0: e.activation(out=Pm[:, b0 * 32:b0 * 32 + 64], in_=ps[bS][:, b0 * 32:b0 * 32 + 64],
                                                    func=AF.Exp, scale=MEM_SCALE), reads=psk(bS), writes=KB(0), banks=[bS])

                def fnDm(e, b0=b0):
                    v = Pm[:, b0 * 32:b0 * 32 + 64].rearrange("p (b k x) -> p b k x", b=2, k=2)
                    o = ps[bDn][:, b0 * 16:b0 * 16 + 32].rearrange("p (b x) -> p b x", b=2)
                    e.matmul(o, lhsT=ones1[:], rhs=v[:, :, 0, :], start=True, stop=False)
                    return e.matmul(o, lhsT=ones1[:], rhs=v[:, :, 1, :], start=False, stop=True)
                P.pe(fnDm, reads=["ones1"] + KB(0), writes=psk(bDn), banks=[bDn])

                def fnOm(e, M_=M_, b0=b0):
                    ins = None
                    for b2 in range(2):
                        bb = b0 + b2
                        for hd in range(4):
                            oc = bb * 16 + hd * 4
                            for blk in range(2):
                                c0 = bb * 32 + blk * 16 + hd * 4
                                ins = e.matmul(ps[bOm][:, oc:oc + 4], lhsT=M_["V"].ap[:, b2 * 2 + blk, hd * 128:(hd + 1) * 128],
                                               rhs=Pm[:, c0:c0 + 4], start=(blk == 0), stop=(blk == 1))
                    return ins
                P.pe(fnOm, reads=M_["V"].keys() + KB(0), writes=psk(bOm), banks=[bOm])
            P.dve(lambda e: e.reciprocal(out=rden[1][:, 0:256], in_=ps[bDn][:, 0:256]), reads=psk(bDn), writes=RDK[1], banks=[bDn])
            P.dve(lambda e: e.tensor_tensor(
                out=s_yT.ap[:, 12:16, :].rearrange("p h (b t) -> p b h t", t=4),
                in0=ps[bOm][:, 0:256].rearrange("p (b h t) -> p b h t", b=SB, h=4),
                in1=rden[1][:, 0:256].rearrange("p (b h t) -> p b h t", b=SB, h=4), op=ALU.mult),
                reads=psk(bOm) + RDK[1], writes=s_yT.keys(12, 16), banks=[bOm])
            ffn_and_out(RS, l, T)

        if DBG["mem"]:
            mem_phase()
        for ti in range(DBG["ntiles"]):
            state["use_pool"] = ti > 0
            load_xT(xp[ti * TP:(ti + 1) * TP, :], TP)
            for l in range(DBG["nlayers"]):
                layer_prompt(l, ti)
            store_xT(yp[ti * TP:(ti + 1) * TP, :], TP)
        if with_sample:
            load_xT(xs, TS)
            for l in range(DBG["nlayers"]):
                layer_sample(l)
            store_xT(ys, TS)
        assert wst["next"] == len(seq), (wst["next"], len(seq))
        P.emit()
    return nc


_CACHE = {}


def kernel(**inputs):
    f = lambda a: np.ascontiguousarray(np.asarray(a, dtype=np.float32))
    inp = {k: f(v) for k, v in inputs.items()}
    with_sample = True
    if "nc" not in _CACHE:
        _CACHE["nc"] = build_program(with_sample)
    nc = _CACHE["nc"]
    shared = {k: inp[k] for k in ("g_mix_pre", "w_in", "w_pool", "pool_scale", "conv_w", "swa_sinks", "g_mem",
                                  "w_mem_kv", "w_out", "g_mix_post", "g_mlp_pre", "w_up", "w_down", "g_mlp_post")}
    in_maps = []
    for c in range(NCORES):
        m = dict(shared)
        m["xp"] = inp["x_prompt"][c]
        m["xs"] = inp["x_sample"][c * SB:(c + 1) * SB].reshape(TS, D)
        m["mem"] = inp["mem_prompt"][c]
        m["spool"] = np.ascontiguousarray(inp["state_pool"][:, c * SB:(c + 1) * SB])
        m["sconv"] = np.ascontiguousarray(inp["state_conv"][:, c * SB:(c + 1) * SB])
        m["ck"] = np.ascontiguousarray(inp["cache_swa_k"][:, c * SB:(c + 1) * SB].reshape(DEPTH, SB, 128, 128))
        m["cv"] = np.ascontiguousarray(inp["cache_swa_v"][:, c * SB:(c + 1) * SB].reshape(DEPTH, SB, 128, 128))
        m["cmk"] = np.ascontiguousarray(inp["cache_mem_k"][:, c * SB:(c + 1) * SB].reshape(DEPTH, SB, MEMT, DG))
        m["cmv"] = np.ascontiguousarray(inp["cache_mem_v"][:, c * SB:(c + 1) * SB].reshape(DEPTH, SB, MEMT, DG))
        in_maps.append(m)
    res = run_bass_kernel_spmd(nc, in_maps, core_ids=list(range(NCORES)))
    R = res.results
    cat = lambda name, axis: np.concatenate([np.asarray(R[c][name], dtype=np.float32) for c in range(NCORES)], axis=axis)
    stk = lambda name: np.stack([np.asarray(R[c][name], dtype=np.float32) for c in range(NCORES)], axis=1)
    y_prompt = np.stack([np.asarray(R[c]["yp"], dtype=np.float32) for c in range(NCORES)], axis=0)
    y_sample = cat("ys", 0).reshape(NCORES * SB, ST, D)
    return (
        y_prompt,
        y_sample,
        stk("o_pool_p"),
        cat("o_pool_s", 1),
        stk("o_conv_p"),
        cat("o_conv_s", 1),
        stk("o_k_p").reshape(DEPTH, NCORES, 128, 2, 64),
        cat("o_k_s", 1).reshape(DEPTH, NCORES * SB, 128, 2, 64),
        stk("o_v_p").reshape(DEPTH, NCORES, 128, 2, 64),
        cat("o_v_s", 1).reshape(DEPTH, NCORES * SB, 128, 2, 64),
        stk("o_mk_p").reshape(DEPTH, NCORES, MEMT, 4, 128),
        stk("o_mv_p").reshape(DEPTH, NCORES, MEMT, 4, 128),
    )
```

```python
import contextlib
import math
from types import SimpleNamespace
import numpy as np
import concourse.bass as bass
import concourse.mybir as mybir
from concourse.bass_utils import run_bass_kernel_spmd

F32 = mybir.dt.float32
BF16 = mybir.dt.bfloat16
ALU = mybir.AluOpType
AF = mybir.ActivationFunctionType

NCORES = 8
D = 2048
DEPTH = 2
SEQ = 2048
TP = 512
NPT = SEQ // TP
SB = 16
ST = 4
TS = SB * ST
MEMT = 256
DG = 512
D_IN = 3328
DFF = 8192
EPS = 1e-6
SWA_SCALE = 1.0 / 8.0
MEM_SCALE = 1.0 / math.sqrt(128.0)
NEG = -30000.0

ENGS = ("pe", "act", "dve", "pool", "sp")
SEM_LIMIT = 24000
NDMASEM = 12


class Op:
    __slots__ = ("eng", "fn", "deps", "dma", "inc", "cnt", "dsem", "dval", "prewait")

    def __init__(self, eng, fn, deps, dma):
        self.eng = eng
        self.fn = fn
        self.deps = deps
        self.dma = dma
        self.inc = False
        self.cnt = 0
        self.dsem = None
        self.dval = 0
        self.prewait = None


class Prog:
    def __init__(self, nc):
        self.nc = nc
        self.ops = {e: [] for e in ENGS}
        self.lw = {}
        self.rd = {}
        self.ndma = {e: 0 for e in ENGS}

    def add(self, eng, fn, reads=(), writes=(), dma=False, banks=()):
        idx = len(self.ops[eng])
        deps = set()
        for b in banks:
            k = ("__bank", b)
            w = self.lw.get(k)
            if w is not None and w[0] != eng:
                deps.add(w)
            self.lw[k] = (eng, idx)
        for k in reads:
            w = self.lw.get(k)
            if w is not None:
                deps.add(w)
        for k in writes:
            w = self.lw.get(k)
            if w is not None:
                deps.add(w)
            for r in self.rd.get(k, ()):
                deps.add(r)
        me = (eng, idx)
        deps.discard(me)
        if eng == "pe":
            deps = {d for d in deps if d[0] != "pe"}
        op = Op(eng, fn, deps, dma)
        if dma:
            n = self.ndma[eng]
            self.ndma[eng] = n + 1
            op.dsem = (eng, n % NDMASEM)
            op.dval = 16 * (n // NDMASEM + 1)
            if n >= NDMASEM:
                op.prewait = (op.dsem, op.dval - 16)
        self.ops[eng].append(op)
        for k in reads:
            self.rd.setdefault(k, []).append(me)
        for k in writes:
            self.lw[k] = me
            self.rd[k] = []
        return me

    def pe(self, fn, reads=(), writes=(), banks=()):
        return self.add("pe", fn, reads, writes, banks=banks)

    def act(self, fn, reads=(), writes=(), banks=()):
        return self.add("act", fn, reads, writes, banks=banks)

    def dve(self, fn, reads=(), writes=(), banks=()):
        return self.add("dve", fn, reads, writes, banks=banks)

    def pool(self, fn, reads=(), writes=(), banks=()):
        return self.add("pool", fn, reads, writes, banks=banks)

    def dma(self, fn, reads=(), writes=(), eng="sp"):
        return self.add(eng, fn, reads, writes, dma=True)

    def emit(self):
        nc = self.nc
        for e in ENGS:
            for op in self.ops[e]:
                for (de, di) in op.deps:
                    d = self.ops[de][di]
                    if not d.dma:
                        d.inc = True
        nsem = {}
        for e in ENGS:
            c = 0
            for op in self.ops[e]:
                if op.inc and not op.dma:
                    c += 1
                op.cnt = c
            nsem[e] = (c // SEM_LIMIT) + 1
        with contextlib.ExitStack() as st:
            csem = {e: [st.enter_context(nc.semaphore(f"c_{e}_{i}")) for i in range(nsem[e])]
                    for e in ENGS}
            dsem = {}
            for e in ENGS:
                for i in range(min(NDMASEM, self.ndma[e])):
                    dsem[(e, i)] = st.enter_context(nc.semaphore(f"d_{e}_{i}"))
            block = st.enter_context(nc.Block())
            engobj = {"pe": "tensor", "act": "scalar", "dve": "vector", "pool": "gpsimd", "sp": "sync"}

            def body(e):
                def run(eng):
                    waited = {}

                    def do_wait(key, sem, val):
                        if waited.get(key, 0) >= val:
                            return
                        eng.wait_ge(sem, val)
                        waited[key] = val

                    for op in self.ops[e]:
                        for (de, di) in sorted(op.deps):
                            d = self.ops[de][di]
                            if d.dma:
                                do_wait(("d",) + d.dsem, dsem[d.dsem], d.dval)
                            else:
                                si, v = divmod(d.cnt - 1, SEM_LIMIT)
                                do_wait(("c", de, si), csem[de][si], v + 1)
                        if op.prewait is not None:
                            do_wait(("d",) + op.prewait[0], dsem[op.prewait[0]], op.prewait[1])
                        ins = op.fn(eng)
                        if op.dma:
                            ins.then_inc(dsem[op.dsem], 16)
                        elif op.inc:
                            ins.then_inc(csem[e][(op.cnt - 1) // SEM_LIMIT], 1)
                    for (de, i), s in dsem.items():
                        if de == e:
                            n = self.ndma[e]
                            uses = (n - i + NDMASEM - 1) // NDMASEM
                            if uses > 0:
                                do_wait(("d", de, i), s, 16 * uses)
                return run

            for e in ENGS:
                if self.ops[e]:
                    getattr(block, engobj[e])(body(e))


class Region:
    def __init__(self, buf, name, off, C, W, dt):
        esz = 4 if dt == F32 else 2
        nb = C * W * esz
        assert off % 4 == 0
        sl = buf[:, off // 2:(off + nb) // 2]
        if dt == F32:
            sl = sl.bitcast(F32)
        self.ap = sl.rearrange("p (c w) -> p c w", c=C)
        self.name = name
        self.off = off
        self.cb = W * esz
        self.C = C
        self.end = off + nb

    def keys(self, c0=0, c1=None):
        if c1 is None:
            c1 = self.C
        lo = self.off + c0 * self.cb
        hi = self.off + c1 * self.cb
        return [(self.name, b) for b in range(lo // 1024, (hi - 1) // 1024 + 1)]


WIN_NSLAB = 7
SLAB_ELEMS = 8192


DBG = {"mem": True, "ntiles": NPT, "nlayers": DEPTH, "mixers": True, "ffn": True}


def build_program(with_sample=True):
    nc = bass.Bass("TRN2", target_bir_lowering=False)

    def din(name, shape, dt=F32):
        return nc.dram_tensor(name, list(shape), dt, kind="ExternalInput").ap()

    def dout(name, shape, dt=F32):
        return nc.dram_tensor(name, list(shape), dt, kind="ExternalOutput").ap()

    xp = din("xp", [SEQ, D])
    xs = din("xs", [TS, D])
    mem = din("mem", [MEMT, D])
    spool = din("spool", [DEPTH, SB, 15, DG])
    sconv = din("sconv", [DEPTH, SB, 2, DG])
    ck = din("ck", [DEPTH, SB, 128, 128])
    cv = din("cv", [DEPTH, SB, 128, 128])
    cmk = din("cmk", [DEPTH, SB, MEMT, DG])
    cmv = din("cmv", [DEPTH, SB, MEMT, DG])
    g_mix_pre = din("g_mix_pre", [DEPTH, D])
    w_in = din("w_in", [DEPTH, D, D_IN])
    w_pool = din("w_pool", [DEPTH, 4, 128, 128])
    pool_scale = din("pool_scale", [DEPTH, DG])
    conv_w = din("conv_w", [DEPTH, 3, DG])
    swa_sinks = din("swa_sinks", [DEPTH, 8])
    g_mem = din("g_mem", [DEPTH, D])
    w_mem_kv = din("w_mem_kv", [DEPTH, D, 2 * DG])
    w_out = din("w_out", [DEPTH, D, D])
    g_mix_post = din("g_mix_post", [DEPTH, D])
    g_mlp_pre = din("g_mlp_pre", [DEPTH, D])
    w_up = din("w_up", [DEPTH, D, DFF])
    w_down = din("w_down", [DEPTH, DFF, D])
    g_mlp_post = din("g_mlp_post", [DEPTH, D])

    yp = dout("yp", [SEQ, D])
    ys = dout("ys", [TS, D])
    o_pool_p = dout("o_pool_p", [DEPTH, 15, DG])
    o_pool_s = dout("o_pool_s", [DEPTH, SB, 15, DG])
    o_conv_p = dout("o_conv_p", [DEPTH, 2, DG])
    o_conv_s = dout("o_conv_s", [DEPTH, SB, 2, DG])
    o_k_p = dout("o_k_p", [DEPTH, 128, 128])
    o_k_s = dout("o_k_s", [DEPTH, SB, 128, 128])
    o_v_p = dout("o_v_p", [DEPTH, 128, 128])
    o_v_s = dout("o_v_s", [DEPTH, SB, 128, 128])
    o_mk_p = dout("o_mk_p", [DEPTH, MEMT, DG])
    o_mv_p = dout("o_mv_p", [DEPTH, MEMT, DG])

    def scratch(name, nslab):
        return nc.dram_tensor(name, [DEPTH, nslab, 128, SLAB_ELEMS], BF16, kind="Internal").ap()

    s_in = scratch("s_in", WIN_NSLAB)
    s_out = scratch("s_out", 4)
    s_up = scratch("s_up", 16)
    s_down = scratch("s_down", 16)

    P = Prog(nc)
    st = contextlib.ExitStack()
    with st:
        def sb(name, shape, dt):
            return st.enter_context(nc.sbuf_tensor(name, list(shape), dt))

        xT = sb("xT", [128, 16, TP], F32)
        U1 = sb("U1", [128, 16384], BF16)
        BIG = sb("BIG", [128, 32768], BF16)
        NWB = 2
        wbuf = [sb(f"wbuf{i}", [128, SLAB_ELEMS], BF16) for i in range(NWB)]
        ident = sb("ident", [128, 128], F32)
        identb = sb("identb", [128, 128], BF16)
        onesD = sb("onesD", [128, 128], BF16)
        ones1 = sb("ones1", [128, 128], BF16)
        mask_cat = sb("mask_cat", [128, 512], BF16)
        mask_first = sb("mask_first", [128, 512], BF16)
        mask_sc = sb("mask_sc", [128, 256], BF16)
        mask_sn = sb("mask_sn", [128, 256], BF16)
        esink_s = sb("esink_s", [1, DEPTH, 2, 256], BF16)
        gv = sb("gv", [128, 5, DEPTH, 16], F32)
        pscale = sb("pscale", [128, DEPTH, 4], F32)
        convw = sb("convw", [128, DEPTH, 3, 4], F32)
        wpool = sb("wpool", [128, DEPTH, 4, 128], BF16)
        sinks_sb = sb("sinks_sb", [1, DEPTH * 8], F32)
        esink = sb("esink", [1, DEPTH, 2, 512], BF16)
        invcnt = sb("invcnt", [128, 4, 16], F32)
        rtm = sb("rtm", [128, 8], F32)
        Fs = [sb(f"F{i}", [128, 16 + TP], F32) for i in range(6)]
        Bt = [sb(f"Bt{i}", [128, TP], BF16) for i in range(4)]
        dTp = [sb(f"dTp{i}", [128, TP], BF16) for i in range(2)]
        Vd = sb("Vd", [128, 5, 2, 128], BF16)
        carry_u = sb("carry_u", [128, DEPTH, 4, 16], F32)
        carry_v = sb("carry_v", [128, DEPTH, 4, 2], F32)
        carry_k = sb("carry_k", [128, DEPTH, 2, 128], BF16)
        carry_V = sb("carry_V", [128, DEPTH, 2, 128], BF16)
        mkT = sb("mkT", [128, DEPTH, 4, MEMT], BF16)
        mvv = sb("mvv", [128, DEPTH, 2, DG], BF16)

        ps = [st.enter_context(nc.psum_tensor(f"ps{i}", [128, 512], F32)) for i in range(8)]

        rstd = Fs[0]
        cacc = [Fs[1], Fs[2]]
        rtmp = [Fs[1], Fs[2]]
        rden = [Fs[0], Fs[5]]
        RDK = [[("F", 0)], [("F", 5)]]
        ptmp = [Fs[3], Fs[4]]
        tmst = [Fs[3], Fs[4]]
        hcst = Fs[5]
        mtmp = Fs[5]
        pfix = Fs[5]
        epsq = Fs[5]
        sq = Bt
        Pt = Bt
        dT = [Bt[0], Bt[1]]
        KF = lambda i: [("F", i)]
        KB = lambda i: [("Bt", i)]

        xstage = Region(U1, "U1", 0, 4, D, F32)
        xstage_ld = Region(BIG, "BIG", 0, 4, D, F32)
        hT = Region(U1, "U1", 0, 16, TP, BF16)
        mixT = Region(U1, "U1", 0, 16, TP, F32)
        off = 0
        u_ext = Region(BIG, "BIG", off, 4, 16 + TP, F32); off = u_ext.end
        hcR = Region(BIG, "BIG", off, 4, TP, F32); off = hcR.end
        gbR = Region(BIG, "BIG", off, 4, TP, F32); off = gbR.end
        v_ext = Region(BIG, "BIG", off, 4, 2 + TP, F32); off = v_ext.end
        qR = Region(BIG, "BIG", off, 4, TP, BF16); off = qR.end
        kext = Region(BIG, "BIG", off, 2, 128 + TP, BF16); off = kext.end
        qmR = Region(BIG, "BIG", off, 4, TP, BF16); off = qmR.end
        yT = Region(BIG, "BIG", off, 16, TP, BF16); off = yT.end
        assert off <= 65536, off
        hidT = Region(BIG, "BIG", 0, 64, TP, BF16)
        RP = SimpleNamespace(hT=hT, mixT=mixT, yT=yT, hidT=hidT)

        state = {"bank": 0, "alt": 0, "use_pool": False}

        def alt():
            state["alt"] ^= 1
            return "act" if state["alt"] else "dve"

        def evac_copy(eng, out_ap, in_ap, reads, writes, banks=()):
            if eng == "act":
                P.act(lambda e: e.activation(out=out_ap, in_=in_ap, func=AF.Copy), reads, writes, banks)
            else:
                P.dve(lambda e: e.tensor_copy(out=out_ap, in_=in_ap), reads, writes, banks)

        def psk(b):
            return [("ps", b)]

        P.pool(lambda e: e.memset(ident[:], 1.0), writes=["ident"])
        P.pool(lambda e: e.affine_select(out=ident[:], in_=ident[:], pattern=[[-1, 128]],
                                         compare_op=ALU.is_equal, fill=0.0, base=0, channel_multiplier=1),
               reads=["ident"], writes=["ident"])
        P.dve(lambda e: e.tensor_copy(out=identb[:], in_=ident[:]), reads=["ident"], writes=["identb"])
        P.pool(lambda e: e.memset(onesD[:], 1.0 / D), writes=["onesD"])
        P.pool(lambda e: e.memset(ones1[:], 1.0), writes=["ones1"])
        P.pool(lambda e: e.memset(mtmp[:, 0:512], 0.0), writes=[("F", 5)])
        P.pool(lambda e: e.affine_select(out=mtmp[:, 0:256].rearrange("p (a q) -> p a q", a=2),
                                         in_=mtmp[:, 0:256].rearrange("p (a q) -> p a q", a=2),
                                         pattern=[[0, 2], [1, 128]], compare_op=ALU.is_ge, fill=NEG,
                                         base=0, channel_multiplier=-1), reads=[("F", 5)], writes=[("F", 5)])
        P.pool(lambda e: e.affine_select(out=mtmp[:, 256:512].rearrange("p (a q) -> p a q", a=2),
                                         in_=mtmp[:, 256:512].rearrange("p (a q) -> p a q", a=2),
                                         pattern=[[0, 2], [-1, 128]], compare_op=ALU.is_ge, fill=NEG,
                                         base=-1, channel_multiplier=1), reads=[("F", 5)], writes=[("F", 5)])
        P.dve(lambda e: e.tensor_copy(out=mask_cat[:], in_=mtmp[:, 0:512]), reads=[("F", 5)], writes=["mask_cat"])
        P.dve(lambda e: e.tensor_copy(out=mask_first[:, 0:256], in_=mtmp[:, 0:256]), reads=[("F", 5)], writes=["mask_first"])
        P.pool(lambda e: e.memset(mask_first[:, 256:512], NEG), reads=["mask_first"], writes=["mask_first"])
        P.pool(lambda e: e.memset(mtmp[:, 0:512], 0.0), reads=[("F", 5)], writes=[("F", 5)])
        P.pool(lambda e: e.affine_select(out=mtmp[:, 0:256].rearrange("p (a t) -> p a t", t=4),
                                         in_=mtmp[:, 0:256].rearrange("p (a t) -> p a t", t=4),
                                         pattern=[[0, 64], [-1, 4]], compare_op=ALU.is_ge, fill=NEG,
                                         base=-1, channel_multiplier=1), reads=[("F", 5)], writes=[("F", 5)])
        P.pool(lambda e: e.affine_select(out=mtmp[:, 256:512].rearrange("p (a t) -> p a t", t=4),
                                         in_=mtmp[:, 256:512].rearrange("p (a t) -> p a t", t=4),
                                         pattern=[[0, 64], [1, 4]], compare_op=ALU.is_ge, fill=NEG,
                                         base=0, channel_multiplier=-1), reads=[("F", 5)], writes=[("F", 5)])
        P.dve(lambda e: e.tensor_copy(out=mask_sc[:], in_=mtmp[:, 0:256]), reads=[("F", 5)], writes=["mask_sc"])
        P.dve(lambda e: e.tensor_copy(out=mask_sn[:], in_=mtmp[:, 256:512]), reads=[("F", 5)], writes=["mask_sn"])
        for gi in range(4):
            w = 2 << gi
            for j in range(16):
                P.pool(lambda e, gi=gi, j=j, w=w: e.memset(invcnt[:, gi, j:j + 1], 1.0 / min(j + 1, w)),
                       writes=[("invcnt", gi, j)])
        INVK = [("invcnt", gi, j) for gi in range(4) for j in range(16)]
        P.pool(lambda e: e.memset(carry_u[:], 0.0), writes=["carry_u"])
        P.pool(lambda e: e.memset(carry_v[:], 0.0), writes=["carry_v"])
        P.pool(lambda e: e.memset(carry_k[:], 0.0), writes=["carry_k"])
        P.pool(lambda e: e.memset(carry_V[:], 0.0), writes=["carry_V"])

        for i, g in enumerate((g_mix_pre, g_mem, g_mix_post, g_mlp_pre, g_mlp_post)):
            for l in range(DEPTH):
                P.dma(lambda e, i=i, l=l, g=g: e.dma_start(out=gv[:, i, l, :],
                                                            in_=g[l].rearrange("(c p) -> p c", p=128),
                                                            allow_slow_non_contiguous=True),
                      writes=["gv"])
        for l in range(DEPTH):
            P.dma(lambda e, l=l: e.dma_start(out=pscale[:, l, :], in_=pool_scale[l].rearrange("(c p) -> p c", p=128),
                                             allow_slow_non_contiguous=True), writes=["pscale"])
            for k in range(3):
                P.dma(lambda e, l=l, k=k: e.dma_start(out=convw[:, l, k, :],
                                                      in_=conv_w[l, k].rearrange("(c p) -> p c", p=128),
                                                      allow_slow_non_contiguous=True), writes=["convw"])
        P.dma(lambda e: e.dma_start(out=sinks_sb[:], in_=swa_sinks.rearrange("l j -> (l j)").rearrange("(o n) -> o n", o=1)),
              writes=["sinks_sb"])
        for l in range(DEPTH):
            P.dma(lambda e, l=l: e.dma_start(out=wpool[:, l, :, :], in_=w_pool[l].rearrange("g c d -> c g d")),
                  writes=["wpool"], eng="pool")
        P.act(lambda e: e.activation(out=sinks_sb[:], in_=sinks_sb[:], func=AF.Exp), reads=["sinks_sb"], writes=["sinks_sb"])
        for l in range(DEPTH):
            for h in range(2):
                for par in range(2):
                    for gg in range(2):
                        j = l * 8 + 4 * h + 2 * gg + par
                        c0 = par * 256 + gg * 128
                        P.dve(lambda e, l=l, h=h, j=j, c0=c0: e.tensor_copy(
                            out=esink[0:1, l, h, c0:c0 + 128], in_=sinks_sb[0:1, j:j + 1].broadcast_to([1, 128])),
                            reads=["sinks_sb"], writes=[("esink", l, h, c0)])
        ESK = lambda l, h: [("esink", l, h, c0) for c0 in (0, 128, 256, 384)]
        for l in range(DEPTH):
            for par in range(2):
                for h in range(2):
                    for gg in range(2):
                        j = l * 8 + 4 * h + 2 * gg + par
                        P.dve(lambda e, l=l, par=par, h=h, gg=gg, j=j: e.tensor_copy(
                            out=esink_s[0:1, l, par, h * 128:(h + 1) * 128].rearrange("o (b g t) -> o b g t", b=SB, g=2)[:, :, gg, :],
                            in_=sinks_sb[0:1, j:j + 1].unsqueeze(2).to_broadcast([1, SB, 4])),
                            reads=["sinks_sb"], writes=[("esink_s", l)])

        SCR = {"in": s_in, "out": s_out, "up": s_up, "down": s_down}
        SRC = {"mem": w_mem_kv, "in": w_in, "out": w_out, "up": w_up, "down": w_down}

        def slab_pieces(kind, l, s):
            src = SRC[kind][l].rearrange("(k p) n -> p k n", p=128)
            if kind == "down":
                G, q = s // 4, s % 4
                return 16, [(0, 512, src[:, q * 16:(q + 1) * 16, G * 512:(G + 1) * 512])]
            if kind != "in" or s < 5:
                return 16, [(0, 512, src[:, :, s * 512:(s + 1) * 512])]
            if s == 5:
                pcs = []
                for h in range(2):
                    for r in range(2):
                        c0 = h * 128 + r * 64
                        pcs.append((c0, c0 + 64, src[:, :, 2560 + h * 64:2560 + h * 64 + 64]))
                pcs.append((256, 512, src[:, :, 2816:3072]))
                return 16, pcs
            return 16, [(0, 256, src[:, :, 3072:3328]), (256, 512, src[:, :, 2560:2816])]

        seq = []
        seen = set()

        def add_seq(tag):
            seq.append((tag, tag not in seen))
            seen.add(tag)
        if DBG["mem"]:
            for l in range(DEPTH):
                for s in range(2):
                    add_seq(("mem", l, s))
        tiles = [("p", i) for i in range(DBG["ntiles"])] + ([("s", 0)] if with_sample else [])
        for t in tiles:
            for l in range(DBG["nlayers"]):
                for s in range(WIN_NSLAB):
                    add_seq(("in", l, s))
                if not DBG["ffn"]:
                    continue
                for s in range(4):
                    add_seq(("out", l, s))
                for s in range(16):
                    add_seq(("up", l, s))
                for s in range(16):
                    add_seq(("down", l, s))
        wst = {"issued": 0, "next": 0}
        PREF = 1
        WK = lambda bi: [("wbuf", bi, i) for i in range(5)]

        def w_issue(upto):
            while wst["issued"] < min(upto, len(seq)):
                i = wst["issued"]
                tag, first_use = seq[i]
                kind, l, s_ = tag
                bi = i % NWB
                if first_use:
                    KC, pcs = slab_pieces(kind, l, s_)
                    wv = wbuf[bi][:].rearrange("p (k n) -> p k n", k=KC)
                    for pi, (c0, c1, src) in enumerate(pcs):
                        wk = WK(bi) if len(pcs) == 1 else [("wbuf", bi, pi)]
                        P.dma(lambda e, wv=wv, c0=c0, c1=c1, src=src: e.dma_start(out=wv[:, :, c0:c1], in_=src),
                              writes=wk, eng="pool")
                    if kind != "mem":
                        P.dma(lambda e, kind=kind, l=l, s_=s_, bi=bi: e.dma_start(out=SCR[kind][l, s_], in_=wbuf[bi][:]),
                              reads=WK(bi), writes=[("scr",) + tag])
                else:
                    P.dma(lambda e, kind=kind, l=l, s_=s_, bi=bi: e.dma_start(out=wbuf[bi][:], in_=SCR[kind][l, s_]),
                          reads=[("scr",) + tag], writes=WK(bi))
                wst["issued"] += 1

        def w_next(tag, KC):
            i = wst["next"]
            assert seq[i][0] == tag, (seq[i][0], tag)
            w_issue(i + 1 + PREF)
            wst["next"] += 1
            bi = i % NWB
            return wbuf[bi][:].rearrange("p (k n) -> p k n", k=KC), WK(bi)

        def nbank(lo=0, hi=6):
            b = state["bank"]
            state["bank"] = b + 1
            return lo + b % (hi - lo)

        def load_xT(src, T, xstage=None):
            xstage = xstage if xstage is not None else xstage_ld
            NG = (T + 127) // 128
            rows = min(128, T)
            for g in range(NG):
                P.dma(lambda e, g=g: e.dma_start(out=xstage.ap[0:rows, g, :], in_=src[g * 128:g * 128 + rows, :]),
                      writes=xstage.keys(g, g + 1))
                for cb in range(4):
                    b = nbank(0, 4)

                    def fn(e, g=g, cb=cb, b=b):
                        ins = None
                        for j in range(4):
                            c = cb * 4 + j
                            ins = e.transpose(out=ps[b][:, j * 128:j * 128 + rows],
                                              in_=xstage.ap[0:rows, g, c * 128:(c + 1) * 128],
                                              identity=ident[0:rows, 0:rows])
                        return ins
                    P.pe(fn, reads=xstage.keys(g, g + 1) + ["ident"], writes=psk(b), banks=[b])
                    evac_copy(alt(), xT[:, cb * 4:cb * 4 + 4, g * 128:g * 128 + rows],
                              ps[b][:].rearrange("p (j t) -> p j t", j=4)[:, :, 0:rows],
                              psk(b), [("xT", cb * 4 + j) for j in range(4)], [b])

        def store_xT(dst, T):
            NG = (T + 127) // 128
            rows = min(128, T)
            for g in range(NG):
                for cb in range(4):
                    b = nbank(0, 4)

                    def fn(e, g=g, cb=cb, b=b):
                        ins = None
                        for j in range(4):
                            c = cb * 4 + j
                            ins = e.transpose(out=ps[b][0:rows, j * 128:(j + 1) * 128],
                                              in_=xT[:, c, g * 128:g * 128 + rows], identity=ident[:])
                        return ins
                    P.pe(fn, reads=[("xT", cb * 4 + j) for j in range(4)] + ["ident"], writes=psk(b), banks=[b])
                    evac_copy(alt(), xstage.ap[0:rows, g, cb * 512:(cb + 1) * 512], ps[b][0:rows, :],
                              psk(b), xstage.keys(g, g + 1), [b])
                P.dma(lambda e, g=g: e.dma_start(out=dst[g * 128:g * 128 + rows, :], in_=xstage.ap[0:rows, g, :]),
                      reads=xstage.keys(g, g + 1), writes=[("ydst", g)], eng="act")

        def norm_stats(src_ap, src_keys, T, nch=16, mode="rstd"):
            b = 7
            for c in range(nch):
                sb_ = sq[c % 4]
                P.act(lambda e, c=c, sb_=sb_: e.activation(out=sb_[:, 0:T], in_=src_ap(c), func=AF.Square),
                      reads=src_keys(c), writes=[("Bt", c % 4)])
                P.pe(lambda e, c=c, sb_=sb_: e.matmul(ps[b][:, 0:T], lhsT=onesD[:], rhs=sb_[:, 0:T],
                                                      start=(c == 0), stop=(c == nch - 1)),
                     reads=[("Bt", c % 4), "onesD"], writes=psk(b), banks=[b])
            if mode == "epsq":
                P.act(lambda e: e.activation(out=epsq[:, 0:T], in_=ps[b][:, 0:T], func=AF.Square,
                                             bias=EPS * math.sqrt(EPS), scale=math.sqrt(EPS)),
                      reads=psk(b), writes=[("F", 5)], banks=[b])
                return
            if mode == "rstd_q":
                P.dve(lambda e: e.tensor_tensor(out=rstd[:, 0:T], in0=ps[b][:, 0:T], in1=epsq[:, 0:T], op=ALU.add),
                      reads=psk(b) + [("F", 5)], writes=[("F", 0)], banks=[b])
                P.act(lambda e: e.activation(out=rstd[:, 0:T], in_=rstd[:, 0:T], func=AF.Sqrt),
                      reads=[("F", 0)], writes=[("F", 0)])
            else:
                P.act(lambda e: e.activation(out=rstd[:, 0:T], in_=ps[b][:, 0:T], func=AF.Sqrt, bias=EPS, scale=1.0),
                      reads=psk(b), writes=[("F", 0)], banks=[b])
            P.dve(lambda e: e.reciprocal(out=rstd[:, 0:T], in_=rstd[:, 0:T]), reads=[("F", 0)], writes=[("F", 0)])

        def xT_ap(T):
            return (lambda c: xT[:, c, 0:T]), (lambda c: [("xT", c)])

        def prenorm_to_hT(R, gidx, l, T):
            a, k = xT_ap(T)
            norm_stats(a, k, T)
            for c in range(16):
                if c % 2 == 1 and state["use_pool"]:
                    tb = Fs[1 + (c // 2) % 2]
                    tk = KF(1 + (c // 2) % 2)
                    P.pool(lambda e, c=c, tb=tb: e.tensor_tensor(out=tb[:, 0:T], in0=xT[:, c, 0:T], in1=rstd[:, 0:T],
                                                                 op=ALU.mult),
                           reads=[("xT", c), ("F", 0)], writes=tk)
                    P.act(lambda e, c=c, tb=tb: e.activation(out=R.hT.ap[:, c, 0:T], in_=tb[:, 0:T], func=AF.Copy,
                                                             scale=gv[:, gidx, l, c:c + 1]),
                          reads=tk + ["gv"], writes=R.hT.keys(c, c + 1))
                else:
                    P.dve(lambda e, c=c: e.scalar_tensor_tensor(out=R.hT.ap[:, c, 0:T], in0=xT[:, c, 0:T],
                                                                scalar=gv[:, gidx, l, c:c + 1], in1=rstd[:, 0:T],
                                                                op0=ALU.mult, op1=ALU.mult),
                          reads=[("xT", c), "gv", ("F", 0)], writes=R.hT.keys(c, c + 1))

        def postnorm_residual(R, gidx, l, T, mode="rstd", after_chunk=None):
            mixT_ = R.mixT
            norm_stats(lambda c: mixT_.ap[:, c, 0:T], lambda c: mixT_.keys(c, c + 1), T, mode=mode)
            def scale_chunk(c):
                P.dve(lambda e, c=c: e.scalar_tensor_tensor(out=mixT_.ap[:, c, 0:T], in0=mixT_.ap[:, c, 0:T],
                                                            scalar=gv[:, gidx, l, c:c + 1], in1=rstd[:, 0:T],
                                                            op0=ALU.mult, op1=ALU.mult),
                      reads=mixT_.keys(c, c + 1) + ["gv", ("F", 0)], writes=mixT_.keys(c, c + 1))

            def add_chunk(c):
                (P.pool if (state["use_pool"] and c % 2 == 1) else P.dve)(
                    lambda e, c=c: e.tensor_tensor(out=xT[:, c, 0:T], in0=xT[:, c, 0:T], in1=mixT_.ap[:, c, 0:T],
                                                   op=ALU.add),
                    reads=mixT_.keys(c, c + 1) + [("xT", c)], writes=[("xT", c)])
                if after_chunk is not None:
                    after_chunk(c)
            for c in range(17):
                if c < 16:
                    scale_chunk(c)
                if c >= 1:
                    add_chunk(c - 1)

        def proj_chunk(sap, skeys, KC, j, inR, T, b, fine=False):
            if fine:
                for kc in range(KC):
                    P.pe(lambda e, kc=kc: e.matmul(ps[b][:, 0:T], lhsT=sap[:, kc, j * 128:(j + 1) * 128],
                                                   rhs=inR.ap[:, kc, 0:T], start=(kc == 0), stop=(kc == KC - 1)),
                         reads=skeys + inR.keys(kc, kc + 1), writes=psk(b), banks=[b])
                return

            def fn(e):
                ins = None
                for kc in range(KC):
                    ins = e.matmul(ps[b][:, 0:T], lhsT=sap[:, kc, j * 128:(j + 1) * 128], rhs=inR.ap[:, kc, 0:T],
                                   start=(kc == 0), stop=(kc == KC - 1))
                return ins
            P.pe(fn, reads=skeys + inR.keys(), writes=psk(b), banks=[b])

        def tm_chunk(sap, skeys, KC, c0, ncols, inR, t0, M, b):
            def fn(e):
                ins = None
                for kc in range(KC):
                    ins = e.matmul(ps[b][0:M, 0:ncols], lhsT=inR.ap[:, kc, t0:t0 + M], rhs=sap[:, kc, c0:c0 + ncols],
                                   start=(kc == 0), stop=(kc == KC - 1))
                return ins
            P.pe(fn, reads=skeys + inR.keys(), writes=psk(b), banks=[b])

        def mem_phase():
            load_xT(mem, MEMT)
            a, k = xT_ap(MEMT)
            norm_stats(a, k, MEMT)
            for l in range(DEPTH):
                for c in range(16):
                    P.dve(lambda e, c=c, l=l: e.scalar_tensor_tensor(out=hT.ap[:, c, 0:MEMT], in0=xT[:, c, 0:MEMT],
                                                                     scalar=gv[:, 1, l, c:c + 1], in1=rstd[:, 0:MEMT],
                                                                     op0=ALU.mult, op1=ALU.mult),
                          reads=[("xT", c), "gv", ("F", 0)], writes=hT.keys(c, c + 1))
                for s in range(2):
                    sap, skeys = w_next(("mem", l, s), 16)
                    if s == 0:
                        for j in range(4):
                            b = nbank()
                            proj_chunk(sap, skeys, 16, j, hT, MEMT, b)
                            evac_copy(alt(), mkT[:, l, j, :], ps[b][:, 0:MEMT], psk(b), [("mkT", l)], [b])
                    for g in range(2):
                        b = nbank()
                        tm_chunk(sap, skeys, 16, 0, 512, hT, g * 128, 128, b)
                        stg = tmst[g % 2]
                        evac_copy("act", stg[:, 0:512], ps[b][:], psk(b), [("F", 3 + g % 2)], [b])
                        if s == 1:
                            P.dve(lambda e, g=g, b=b, l=l: e.tensor_copy(out=mvv[:, l, g, :], in_=ps[b][:]),
                                  reads=psk(b), writes=[("mvv", l)], banks=[b])
                        dst = (o_mk_p if s == 0 else o_mv_p)[l, g * 128:(g + 1) * 128, :]
                        P.dma(lambda e, dst=dst, stg=stg: e.dma_start(out=dst, in_=stg[:, 0:512]),
                              reads=[("F", 3 + g % 2)], writes=[("omem", l, s, g)], eng="act")

        def pool_prompt(l, T, first):
            P.dve(lambda e: e.tensor_copy(out=u_ext.ap[:, :, 0:16], in_=carry_u[:, l, :, :]),
                  reads=["carry_u"], writes=u_ext.keys())
            L = 16 + T
            for gi in range(4):
                w = 2 << gi
                src = u_ext.ap[:, gi, :]
                srck = u_ext.keys(gi, gi + 1)
                cur, curk = src, srck
                for k in range(1, gi + 2):
                    sh = 1 << (k - 1)
                    lo = (1 << k) - 1
                    dst = ptmp[k % 2]
                    P.dve(lambda e, cur=cur, dst=dst, lo=lo, sh=sh: e.tensor_tensor(
                        out=dst[:, lo:L], in0=cur[:, lo:L], in1=cur[:, lo - sh:L - sh], op=ALU.add),
                        reads=curk, writes=[("F", 3 + k % 2)])
                    cur, curk = dst[:], [("F", 3 + k % 2)]
                d = dTp[gi % 2]
                dk = [("dTp", gi % 2)]
                P.dve(lambda e, cur=cur, d=d, src=src, w=w: e.scalar_tensor_tensor(
                    out=d[:, 0:T], in0=cur[:, 16:16 + T], scalar=1.0 / w, in1=src[:, 16:16 + T],
                    op0=ALU.mult, op1=ALU.subtract), reads=curk + srck, writes=dk)
                if first:
                    P.dve(lambda e, cur=cur, gi=gi: e.tensor_tensor(out=pfix[:, 0:16], in0=cur[:, 16:32], in1=invcnt[:, gi, :],
                                                                   op=ALU.mult), reads=curk + INVK, writes=[("F", 5)])
                    P.dve(lambda e, d=d, src=src: e.tensor_tensor(out=d[:, 0:16], in0=pfix[:, 0:16], in1=src[:, 16:32],
                                                                 op=ALU.subtract), reads=[("F", 5)] + srck, writes=dk)
                yield
                b = state.get("fb", 6)
                P.pe(lambda e, b=b, gi=gi, d=d: e.matmul(ps[b][:, 0:T], lhsT=wpool[:, l, gi, :], rhs=d[:, 0:T],
                                                        start=True, stop=True),
                     reads=dk + ["wpool"], writes=psk(b), banks=[b])
                P.act(lambda e, b=b, gi=gi: e.activation(out=yT.ap[:, gi, 0:T], in_=ps[b][:, 0:T], func=AF.Copy,
                                                         scale=pscale[:, l, gi:gi + 1]),
                      reads=psk(b) + ["pscale"], writes=yT.keys(gi, gi + 1), banks=[b])
            P.dve(lambda e: e.tensor_copy(out=carry_u[:, l, :, :], in_=u_ext.ap[:, :, T:T + 16]),
                  reads=u_ext.keys(), writes=["carry_u"])
            yield

        def conv_prompt(l, T):
            P.dve(lambda e: e.tensor_copy(out=v_ext.ap[:, :, 0:2], in_=carry_v[:, l, :, :]),
                  reads=["carry_v"], writes=v_ext.keys())
            for c in range(4):
                vk = v_ext.keys(c, c + 1)
                ca = cacc[c % 2]
                cak = [("F", 1 + c % 2)]
                P.dve(lambda e, c=c: e.tensor_tensor(out=v_ext.ap[:, c, 2:2 + T], in0=v_ext.ap[:, c, 2:2 + T],
                                                     in1=hcR.ap[:, c, 0:T], op=ALU.mult),
                      reads=vk + hcR.keys(c, c + 1), writes=vk)
                P.act(lambda e, c=c, ca=ca: e.activation(out=ca[:, 0:T], in_=v_ext.ap[:, c, 0:T], func=AF.Copy,
                                                         scale=convw[:, l, 0, c:c + 1]),
                      reads=vk + ["convw"], writes=cak)
                for kk in (1, 2):
                    P.dve(lambda e, c=c, ca=ca, kk=kk: e.scalar_tensor_tensor(
                        out=ca[:, 0:T], in0=v_ext.ap[:, c, kk:kk + T], scalar=convw[:, l, kk, c:c + 1], in1=ca[:, 0:T],
                        op0=ALU.mult, op1=ALU.add), reads=vk + ["convw"] + cak, writes=cak)
                P.dve(lambda e, c=c, ca=ca: e.tensor_tensor(out=yT.ap[:, 4 + c, 0:T], in0=ca[:, 0:T],
                                                            in1=gbR.ap[:, c, 0:T], op=ALU.mult),
                      reads=cak + gbR.keys(c, c + 1), writes=yT.keys(4 + c, 5 + c))
                if c < 3:
                    yield
            P.dve(lambda e: e.tensor_copy(out=carry_v[:, l, :, :], in_=v_ext.ap[:, :, T:T + 2]),
                  reads=v_ext.keys(), writes=["carry_v"])
            yield

        def swa_prompt(l, T, first):
            NG = T // 128
            P.dve(lambda e: e.tensor_copy(out=kext.ap[:, :, 0:128], in_=carry_k[:, l, :, :]),
                  reads=["carry_k"], writes=kext.keys())
            it = 0
            for n in range(NG):
                for h in range(2):
                    base_b = 0 if it % 2 == 0 else 4
                    state["fb"] = 4 - base_b
                    it += 1
                    bD, bO = base_b + 2, base_b + 3
                    msk, mskk = (mask_first, "mask_first") if (first and n == 0) else (mask_cat, "mask_cat")
                    pts = []
                    for par in range(2):
                        b = base_b + par
                        p0 = par * 64

                        def fnS(e, b=b, p0=p0, msk=msk, h=h, n=n):
                            e.matmul(ps[b][:, 0:512], lhsT=identb[:], rhs=msk[:], start=True, stop=False)
                            ins = None
                            for part in range(2):
                                kc0 = 128 + n * 128 if part == 0 else n * 128
                                ins = e.matmul(ps[b][:, part * 256:(part + 1) * 256].rearrange("p (g q) -> p g q", g=2),
                                               lhsT=kext.ap[p0:p0 + 64, h, kc0:kc0 + 128],
                                               rhs=qR.ap[p0:p0 + 64, 2 * h:2 * h + 2, n * 128:(n + 1) * 128],
                                               start=False, stop=(part == 1))
                            return ins
                        P.pe(fnS, reads=["identb", mskk] + kext.keys(h, h + 1) + qR.keys(2 * h, 2 * h + 2),
                             writes=psk(b), banks=[b])
                        pi = (it % 2) * 2 + par
                        pt = Pt[pi]
                        P.act(lambda e, b=b, pt=pt: e.activation(out=pt[:], in_=ps[b][:], func=AF.Exp, scale=SWA_SCALE),
                              reads=psk(b), writes=KB(pi), banks=[b])
                        pts.append((pt, KB(pi)))

                    def fnD(e, pts=pts, h=h, bD=bD):
                        ins = None
                        for par in range(2):
                            pt = pts[par][0]
                            o = ps[bD][:, par * 256:(par + 1) * 256]
                            e.matmul(o, lhsT=ones1[:], rhs=pt[:, 0:256], start=True, stop=False)
                            e.matmul(o, lhsT=ones1[:], rhs=pt[:, 256:512], start=False, stop=False)
                            ins = e.matmul(o, lhsT=ones1[0:1, :], rhs=esink[0:1, l, h, par * 256:(par + 1) * 256],
                                           start=False, stop=True)
                        return ins
                    P.pe(fnD, reads=["ones1"] + ESK(l, h) + pts[0][1] + pts[1][1], writes=psk(bD), banks=[bD])

                    def fnO(e, pts=pts, h=h, bO=bO, n=n):
                        ins = None
                        for par in range(2):
                            pt = pts[par][0]
                            o = ps[bO][:, par * 256:(par + 1) * 256]
                            e.matmul(o, lhsT=Vd[:, n + 1, h, :], rhs=pt[:, 0:256], start=True, stop=False)
                            ins = e.matmul(o, lhsT=Vd[:, n, h, :], rhs=pt[:, 256:512], start=False, stop=True)
                        return ins
                    P.pe(fnO, reads=["Vd"] + pts[0][1] + pts[1][1], writes=psk(bO), banks=[bO])
                    yield
                    ri = it % 2
                    rd = rden[ri]
                    rdk = RDK[ri]
                    P.dve(lambda e, rd=rd, bD=bD: e.reciprocal(out=rd[:, 0:512], in_=ps[bD][:]),
                          reads=psk(bD), writes=rdk, banks=[bD])
                    for par in range(2):
                        p0 = par * 64
                        P.dve(lambda e, p0=p0, par=par, rd=rd, h=h, n=n, bO=bO: e.tensor_tensor(
                            out=yT.ap[p0:p0 + 64, 8 + 2 * h:10 + 2 * h, n * 128:(n + 1) * 128],
                            in0=ps[bO][p0:p0 + 64, par * 256:(par + 1) * 256].rearrange("p (g q) -> p g q", g=2),
                            in1=rd[p0:p0 + 64, par * 256:(par + 1) * 256].rearrange("p (g q) -> p g q", g=2),
                            op=ALU.mult),
                            reads=psk(bO) + rdk, writes=yT.keys(8 + 2 * h, 10 + 2 * h), banks=[bO])
                    yield
            P.dve(lambda e: e.tensor_copy(out=carry_k[:, l, :, :], in_=kext.ap[:, :, T:T + 128]),
                  reads=kext.keys(), writes=["carry_k"])
            P.dve(lambda e: e.tensor_copy(out=carry_V[:, l, :, :], in_=Vd[:, NG, :, :]),
                  reads=["Vd"], writes=["carry_V"])

        def mem_prompt(l, T):
            for hd in range(4):
                base_b = 0 if hd % 2 == 0 else 4
                state["fb"] = 4 - base_b
                bD, bO = base_b + 2, base_b + 3
                pts = []
                for kb in range(2):
                    b = base_b + kb
                    P.pe(lambda e, b=b, kb=kb, hd=hd: e.matmul(ps[b][:, 0:T], lhsT=mkT[:, l, hd, kb * 128:(kb + 1) * 128],
                                                              rhs=qmR.ap[:, hd, 0:T], start=True, stop=True),
                         reads=[("mkT", l)] + qmR.keys(hd, hd + 1), writes=psk(b), banks=[b])
                    pt = Pt[(hd % 2) * 2 + kb]
                    ptk = [("Bt", (hd % 2) * 2 + kb)]
                    P.act(lambda e, b=b, pt=pt: e.activation(out=pt[:, 0:T], in_=ps[b][:, 0:T], func=AF.Exp, scale=MEM_SCALE),
                          reads=psk(b), writes=ptk, banks=[b])
                    pts.append((pt, ptk))

                def fnD(e, pts=pts, bD=bD):
                    e.matmul(ps[bD][:, 0:T], lhsT=ones1[:], rhs=pts[0][0][:, 0:T], start=True, stop=False)
                    return e.matmul(ps[bD][:, 0:T], lhsT=ones1[:], rhs=pts[1][0][:, 0:T], start=False, stop=True)
                P.pe(fnD, reads=["ones1"] + pts[0][1] + pts[1][1], writes=psk(bD), banks=[bD])

                def fnO(e, pts=pts, hd=hd, bO=bO):
                    e.matmul(ps[bO][:, 0:T], lhsT=mvv[:, l, 0, hd * 128:(hd + 1) * 128], rhs=pts[0][0][:, 0:T],
                             start=True, stop=False)
                    return e.matmul(ps[bO][:, 0:T], lhsT=mvv[:, l, 1, hd * 128:(hd + 1) * 128], rhs=pts[1][0][:, 0:T],
                                    start=False, stop=True)
                P.pe(fnO, reads=[("mvv", l)] + pts[0][1] + pts[1][1], writes=psk(bO), banks=[bO])
                yield
                rd = rden[hd % 2]
                rdk = RDK[hd % 2]
                P.dve(lambda e, rd=rd, bD=bD: e.reciprocal(out=rd[:, 0:T], in_=ps[bD][:, 0:T]),
                      reads=psk(bD), writes=rdk, banks=[bD])
                P.dve(lambda e, rd=rd, bO=bO, hd=hd: e.tensor_tensor(out=yT.ap[:, 12 + hd, 0:T], in0=ps[bO][:, 0:T],
                                                                     in1=rd[:, 0:T], op=ALU.mult),
                      reads=psk(bO) + rdk, writes=yT.keys(12 + hd, 13 + hd), banks=[bO])
                yield

        def pre1_chunk(l, T):
            def f(c):
                P.act(lambda e, c=c: e.activation(out=hT.ap[:, c, 0:T], in_=xT[:, c, 0:T], func=AF.Copy,
                                                  scale=gv[:, 0, l, c:c + 1]),
                      reads=[("xT", c), "gv"], writes=hT.keys(c, c + 1))
            return f

        def layer_prompt(l, ti):
            T = TP
            first = (ti == 0)
            last = (ti == NPT - 1)
            if l == 0:
                for c in range(16):
                    pre1_chunk(0, T)(c)
            P.dve(lambda e: e.tensor_copy(out=Vd[:, 0, :, :], in_=carry_V[:, l, :, :]), reads=["carry_V"], writes=["Vd"])
            RK = [("F", 0)]

            def evac_scaled(out_ap, b, wkeys):
                P.dve(lambda e: e.tensor_tensor(out=out_ap, in0=ps[b][:, 0:T], in1=rstd[:, 0:T], op=ALU.mult),
                      reads=psk(b) + RK, writes=wkeys, banks=[b])

            def dest(m):
                if m < 4:
                    return u_ext.ap[:, m, 16:16 + T], u_ext.keys(m, m + 1)
                if m < 8:
                    return hcR.ap[:, m - 4, 0:T], hcR.keys(m - 4, m - 3)
                if m < 12:
                    return gbR.ap[:, m - 8, 0:T], gbR.keys(m - 8, m - 7)
                if m < 16:
                    return v_ext.ap[:, m - 12, 2:2 + T], v_ext.keys(m - 12, m - 11)
                if m < 20:
                    return qR.ap[:, m - 16, 0:T], qR.keys(m - 16, m - 15)
                if m < 22:
                    return kext.ap[:, m - 20, 128:128 + T], kext.keys(m - 20, m - 19)
                return qmR.ap[:, m - 22, 0:T], qmR.keys(m - 22, m - 21)
            for s in range(WIN_NSLAB):
                sap, skeys = w_next(("in", l, s), 16)
                pend = []
                for j in range(4):
                    m = s * 4 + j
                    if m >= 26:
                        continue
                    b = nbank()
                    proj_chunk(sap, skeys, 16, j, hT, T, b, fine=(m == 0))
                    if s == 0:
                        pend.append((m, b))
                    else:
                        o_, k_ = dest(m)
                        evac_scaled(o_, b, k_)
                if last and s == 0:
                    tm_chunk(sap, skeys, 16, 0, 512, hT, T - 16, 16, 6)
                if s == 0:
                    a_, k_ = xT_ap(T)
                    norm_stats(a_, k_, T, mode="rstd")
                    bq = nbank()

                    def fnr(e, bq=bq):
                        ins = None
                        for g in range(T // 128):
                            ins = e.matmul(ps[bq][:, g:g + 1], lhsT=rstd[:, g * 128:(g + 1) * 128], rhs=ident[:, 0:1],
                                           start=True, stop=True)
                        if last:
                            ins = e.matmul(ps[bq][0:16, 4:5], lhsT=rstd[:, T - 16:T], rhs=ident[:, 0:1], start=True, stop=True)
                        return ins
                    P.pe(fnr, reads=RK + ["ident"], writes=psk(bq), banks=[bq])
                    P.act(lambda e, bq=bq: e.activation(out=rtm[:, 0:8], in_=ps[bq][:, 0:8], func=AF.Copy),
                          reads=psk(bq), writes=["rtm"], banks=[bq])
                    for (m, b) in pend:
                        o_, k_ = dest(m)
                        evac_scaled(o_, b, k_)
                if last and s == 0:
                    b = 6
                    P.act(lambda e, b=b: e.activation(out=tmst[0][0:16, 0:512], in_=ps[b][0:16, :], func=AF.Copy,
                                                      scale=rtm[0:16, 4:5]),
                          reads=psk(b) + ["rtm"], writes=[("F", 3)], banks=[b])
                    P.dma(lambda e: e.dma_start(out=o_pool_p[l], in_=tmst[0][1:16, 0:512]), reads=[("F", 3)],
                          writes=[("o_pool_p", l)], eng="act")
                if last and s == 1:
                    b = 6
                    tm_chunk(sap, skeys, 16, 0, 512, hT, T - 16, 16, b)
                    P.act(lambda e, b=b: e.activation(out=hcst[0:16, 0:512], in_=ps[b][0:16, :], func=AF.Copy,
                                                      scale=rtm[0:16, 4:5]),
                          reads=psk(b) + ["rtm"], writes=[("F", 5)], banks=[b])
                if last and s == 3:
                    b = 6
                    tm_chunk(sap, skeys, 16, 0, 512, hT, T - 16, 16, b)
                    P.dve(lambda e, b=b: e.scalar_tensor_tensor(out=tmst[1][0:16, 0:512], in0=ps[b][0:16, :],
                                                                scalar=rtm[0:16, 4:5], in1=hcst[0:16, 0:512],
                                                                op0=ALU.mult, op1=ALU.mult),
                          reads=psk(b) + [("F", 5), "rtm"], writes=[("F", 4)], banks=[b])
                    P.dma(lambda e: e.dma_start(out=o_conv_p[l], in_=tmst[1][14:16, 0:512]), reads=[("F", 4)],
                          writes=[("o_conv_p", l)], eng="act")
                if s == 6:
                    for g in range(T // 128):
                        b = 6
                        tm_chunk(sap, skeys, 16, 256, 256, hT, g * 128, 128, b)
                        for h in range(2):
                            P.dve(lambda e, g=g, h=h, b=b: e.tensor_scalar(
                                out=Vd[:, g + 1, h, :].rearrange("p (r d) -> p r d", r=2),
                                in0=ps[b][:, 128 + h * 64:128 + (h + 1) * 64].unsqueeze(1).to_broadcast([128, 2, 64]),
                                scalar1=rtm[:, g:g + 1], scalar2=None, op0=ALU.mult),
                                reads=psk(b) + ["rtm"], writes=["Vd"], banks=[b])
                        if last and g == T // 128 - 1:
                            P.act(lambda e, b=b, g=g: e.activation(out=tmst[0][:, 0:256], in_=ps[b][:, 0:256], func=AF.Copy,
                                                                   scale=rtm[:, g:g + 1]),
                                  reads=psk(b) + ["rtm"], writes=[("F", 3)], banks=[b])
                            P.dma(lambda e: e.dma_start(out=o_k_p[l], in_=tmst[0][:, 0:128]), reads=[("F", 3)],
                                  writes=[("o_k_p", l)], eng="act")
                            P.dma(lambda e: e.dma_start(out=o_v_p[l], in_=tmst[0][:, 128:256]), reads=[("F", 3)],
                                  writes=[("o_v_p", l)], eng="act")
            if DBG["mixers"]:
                import itertools
                A = itertools.chain(swa_prompt(l, T, first), mem_prompt(l, T))
                B = itertools.chain(pool_prompt(l, T, first), conv_prompt(l, T))
                a_alive, b_alive = True, True
                while a_alive or b_alive:
                    if a_alive:
                        a_alive = next(A, "end") != "end"
                    if b_alive:
                        b_alive = next(B, "end") != "end"
                    if a_alive:
                        a_alive = next(A, "end") != "end"
            if DBG["ffn"]:
                ffn_and_out(RP, l, T, post2_hook=(pre1_chunk(l + 1, T) if l + 1 < DBG["nlayers"] else None))

        def ffn_and_out(R, l, T, post2_hook=None):
            for s in range(4):
                sap, skeys = w_next(("out", l, s), 16)
                for j in range(4):
                    m = s * 4 + j
                    b = nbank()
                    proj_chunk(sap, skeys, 16, j, R.yT, T, b)
                    evac_copy(alt(), R.mixT.ap[:, m, 0:T], ps[b][:, 0:T], psk(b), R.mixT.keys(m, m + 1), [b])
            def h2_chunk(c):
                P.act(lambda e, c=c: e.activation(out=R.hT.ap[:, c, 0:T], in_=xT[:, c, 0:T], func=AF.Copy,
                                                  scale=gv[:, 3, l, c:c + 1]),
                      reads=[("xT", c), "gv"], writes=R.hT.keys(c, c + 1))
            postnorm_residual(R, 2, l, T, after_chunk=h2_chunk)
            for s in range(16):
                sap, skeys = w_next(("up", l, s), 16)
                if s == 1:
                    a_, k_ = xT_ap(T)
                    norm_stats(a_, k_, T, mode="epsq")
                for j in range(4):
                    m = s * 4 + j
                    b = nbank()
                    proj_chunk(sap, skeys, 16, j, R.hT, T, b, fine=(m == 0))
                    rt = rtmp[m % 2]
                    rk = [("F", 1 + m % 2)]
                    P.act(lambda e, b=b, rt=rt: e.activation(out=rt[:, 0:T], in_=ps[b][:, 0:T], func=AF.Relu),
                          reads=psk(b), writes=rk, banks=[b])
                    P.dve(lambda e, m=m, rt=rt: e.tensor_tensor(out=R.hidT.ap[:, m, 0:T], in0=rt[:, 0:T], in1=rt[:, 0:T],
                                                                op=ALU.mult),
                          reads=rk, writes=R.hidT.keys(m, m + 1))
            for G in range(4):
                base = 0 if G % 2 == 0 else 4
                for q in range(4):
                    sap, skeys = w_next(("down", l, G * 4 + q), 16)
                    for j in range(4):
                        b = base + j
                        if G == 0 and q == 0 and j == 0:
                            for kc in range(16):
                                P.pe(lambda e, kc=kc, sap=sap, b=b: e.matmul(
                                    ps[b][:, 0:T], lhsT=sap[:, kc, 0:128], rhs=R.hidT.ap[:, kc, 0:T],
                                    start=(kc == 0), stop=False),
                                    reads=skeys + R.hidT.keys(kc, kc + 1), writes=psk(b), banks=[b])
                            continue

                        def fn(e, sap=sap, q=q, j=j, b=b):
                            ins = None
                            for kc in range(16):
                                ins = e.matmul(ps[b][:, 0:T], lhsT=sap[:, kc, j * 128:(j + 1) * 128],
                                               rhs=R.hidT.ap[:, q * 16 + kc, 0:T],
                                               start=(q == 0 and kc == 0), stop=(q == 3 and kc == 15))
                            return ins
                        P.pe(fn, reads=skeys + R.hidT.keys(q * 16, q * 16 + 16), writes=psk(b), banks=[b])
                for j in range(4):
                    b = base + j
                    m = G * 4 + j
                    evac_copy(alt(), R.mixT.ap[:, m, 0:T], ps[b][:, 0:T], psk(b), R.mixT.keys(m, m + 1), [b])
            postnorm_residual(R, 4, l, T, mode="rstd_q", after_chunk=post2_hook)

        psb = [p_[:].bitcast(BF16) for p_ in ps]
        s_hidT = Region(BIG, "BIG", 0, 64, 64, BF16)
        s_yT = Region(BIG, "BIG", 8192, 16, 64, BF16)
        s_q = Region(BIG, "BIG", 10240, 4, 64, BF16)
        s_kx = Region(BIG, "BIG", 10752, 2, 64, BF16)
        s_qm = Region(BIG, "BIG", 11008, 4, 64, BF16)
        s_hc = Region(BIG, "BIG", 11520, 4, 64, F32)
        s_gb = Region(BIG, "BIG", 12544, 4, 64, F32)
        s_vx = Region(BIG, "BIG", 13568, 4, 96, F32)
        s_ux = Region(BIG, "BIG", 15104, 4, 304, F32)
        pst = Region(BIG, "BIG", 20480, 2, 512, F32)
        cst = Region(BIG, "BIG", 24576, 1, 512, F32)
        Kd = Region(BIG, "BIG", 26624, 16, 256, BF16)
        KTd = Region(BIG, "BIG", 34816, 32, 128, BF16)
        Vds = Region(BIG, "BIG", 43008, 16, 256, BF16)
        mc = [dict(K=Region(BIG, "BIG", 51200, 4, 512, BF16), KT=Region(BIG, "BIG", 55296, 8, 256, BF16),
                   V=Region(BIG, "BIG", 59392, 4, 512, BF16)),
              dict(K=Region(U1, "U1", 8192, 4, 512, BF16), KT=Region(U1, "U1", 12288, 8, 256, BF16),
                   V=Region(U1, "U1", 16384, 4, 512, BF16))]
        Vn = Region(U1, "U1", 20480, 16, 256, BF16)
        s_hT = Region(U1, "U1", 0, 16, 64, BF16)
        s_mixT = Region(U1, "U1", 2048, 16, 64, F32)
        RS = SimpleNamespace(hT=s_hT, mixT=s_mixT, yT=s_yT, hidT=s_hidT)

        def bt4(ap):
            return ap.rearrange("p (b t) -> p b t", t=4)

        def layer_sample(l):
            T = TS
            prenorm_to_hT(RS, 0, l, T)
            sp2 = spool[l].rearrange("b r f -> (b r) f")
            P.dma(lambda e: e.dma_start(out=pst.ap[0:128, 0, :], in_=sp2[0:128, :]), writes=pst.keys(0, 1))
            P.dma(lambda e: e.dma_start(out=pst.ap[0:112, 1, :], in_=sp2[128:240, :]), writes=pst.keys(1, 2))
            P.dma(lambda e: e.dma_start(out=cst.ap[0:32, 0, :], in_=sconv[l].rearrange("b r f -> (b r) f")), writes=cst.keys())
            Kd5 = Kd.ap.rearrange("p b (h r d) -> p b h r d", h=2, r=2)
            Vd5 = Vds.ap.rearrange("p b (h r d) -> p b h r d", h=2, r=2)
            for r in range(2):
                for h in range(2):
                    P.dma(lambda e, r=r, h=h: e.dma_start(out=Kd5[:, :, h, r, :],
                                                          in_=ck[l][:, :, h * 64:(h + 1) * 64].rearrange("b k d -> k b d")),
                          writes=Kd.keys(), eng="pool")
                    P.dma(lambda e, r=r, h=h: e.dma_start(out=Vd5[:, :, h, r, :],
                                                          in_=cv[l][:, :, h * 64:(h + 1) * 64].rearrange("b k d -> k b d")),
                          writes=Vds.keys(), eng="pool")
            P.dma(lambda e: e.dma_start(out=o_pool_s[l][:, 0:11, :], in_=spool[l][:, 4:15, :]), writes=[("o_pool_s", l, 0)], eng="pool")
            P.dma(lambda e: e.dma_start(out=o_k_s[l][:, 0:124, :], in_=ck[l][:, 4:128, :]), writes=[("o_k_s", l, 0)], eng="pool")
            P.dma(lambda e: e.dma_start(out=o_v_s[l][:, 0:124, :], in_=cv[l][:, 4:128, :]), writes=[("o_v_s", l, 0)], eng="pool")
            for gi in range(4):
                b = nbank(0, 4)

                def fn(e, gi=gi, b=b):
                    e.transpose(out=ps[b][:, 0:128], in_=pst.ap[0:128, 0, gi * 128:(gi + 1) * 128], identity=ident[:])
                    return e.transpose(out=ps[b][:, 128:240], in_=pst.ap[0:112, 1, gi * 128:(gi + 1) * 128],
                                       identity=ident[0:112, 0:112])
                P.pe(fn, reads=pst.keys() + ["ident"], writes=psk(b), banks=[b])
                u3 = s_ux.ap[:, gi, :].rearrange("p (b r) -> p b r", r=19)
                evac_copy(alt(), u3[:, :, 0:15], ps[b][:, 0:240].rearrange("p (b r) -> p b r", r=15), psk(b),
                          s_ux.keys(gi, gi + 1), [b])
            for c in range(4):
                b = nbank(0, 4)
                P.pe(lambda e, c=c, b=b: e.transpose(out=ps[b][:, 0:32], in_=cst.ap[0:32, 0, c * 128:(c + 1) * 128],
                                                     identity=ident[0:32, 0:32]),
                     reads=cst.keys() + ["ident"], writes=psk(b), banks=[b])
                v3 = s_vx.ap[:, c, :].rearrange("p (b r) -> p b r", r=6)
                evac_copy(alt(), v3[:, :, 0:2], ps[b][:, 0:32].rearrange("p (b r) -> p b r", r=2), psk(b),
                          s_vx.keys(c, c + 1), [b])
            for q4 in range(4):
                b = 4 + q4

                def fnk(e, q4=q4, b=b):
                    ins = None
                    for i in range(8):
                        idx = q4 * 8 + i
                        bb, h = idx // 2, idx % 2
                        ins = e.transpose(out=psb[b][:, i * 128:(i + 1) * 128], in_=Kd.ap[:, bb, h * 128:(h + 1) * 128],
                                          identity=identb[:])
                    return ins
                P.pe(fnk, reads=Kd.keys() + ["identb"], writes=psk(b), banks=[b])
                evac_copy(alt(), KTd.ap[:, q4 * 8:(q4 + 1) * 8, :], psb[b][:].rearrange("p (i k) -> p i k", i=8), psk(b),
                          KTd.keys(q4 * 8, q4 * 8 + 8), [b])
            for s in range(WIN_NSLAB):
                sap, skeys = w_next(("in", l, s), 16)
                for j in range(4):
                    m = s * 4 + j
                    if m >= 26:
                        continue
                    b = nbank(0, 4)
                    proj_chunk(sap, skeys, 16, j, s_hT, T, b, fine=(m == 0))
                    eng = alt()
                    src = ps[b][:, 0:T]
                    if m < 4:
                        u3 = s_ux.ap[:, m, :].rearrange("p (b r) -> p b r", r=19)
                        evac_copy(eng, u3[:, :, 15:19], bt4(src), psk(b), s_ux.keys(m, m + 1), [b])
                    elif m < 8:
                        evac_copy(eng, s_hc.ap[:, m - 4, :], src, psk(b), s_hc.keys(m - 4, m - 3), [b])
                    elif m < 12:
                        evac_copy(eng, s_gb.ap[:, m - 8, :], src, psk(b), s_gb.keys(m - 8, m - 7), [b])
                    elif m < 16:
                        v3 = s_vx.ap[:, m - 12, :].rearrange("p (b r) -> p b r", r=6)
                        evac_copy(eng, v3[:, :, 2:6], bt4(src), psk(b), s_vx.keys(m - 12, m - 11), [b])
                    elif m < 20:
                        evac_copy(eng, s_q.ap[:, m - 16, :], src, psk(b), s_q.keys(m - 16, m - 15), [b])
                    elif m < 22:
                        evac_copy(eng, s_kx.ap[:, m - 20, :], src, psk(b), s_kx.keys(m - 20, m - 19), [b])
                    else:
                        evac_copy(eng, s_qm.ap[:, m - 22, :], src, psk(b), s_qm.keys(m - 22, m - 21), [b])
                if s == 0:
                    b = 6
                    tm_chunk(sap, skeys, 16, 0, 512, s_hT, 0, 64, b)
                    evac_copy("act", tmst[0][0:64, 0:512], ps[b][0:64, :], psk(b), KF(3), [b])
                    P.dma(lambda e: e.dma_start(out=o_pool_s[l][:, 11:15, :], in_=tmst[0][0:64, 0:512]), reads=KF(3),
                          writes=[("o_pool_s", l, 1)], eng="pool")
                if s == 1:
                    b = 6
                    tm_chunk(sap, skeys, 16, 0, 512, s_hT, 0, 64, b)
                    evac_copy("act", hcst[0:64, 0:512], ps[b][0:64, :], psk(b), KF(5), [b])
                if s == 3:
                    b = 6
                    tm_chunk(sap, skeys, 16, 0, 512, s_hT, 0, 64, b)
                    P.dve(lambda e, b=b: e.tensor_tensor(out=tmst[1][0:64, 0:512], in0=ps[b][0:64, :], in1=hcst[0:64, 0:512],
                                                         op=ALU.mult), reads=psk(b) + KF(5), writes=KF(4), banks=[b])
                    for t in (2, 3):
                        for bb in range(SB):
                            P.dma(lambda e, t=t, bb=bb: e.dma_start(out=o_conv_s[l][bb, t - 2:t - 1, :],
                                                                    in_=tmst[1][bb * 4 + t:bb * 4 + t + 1, 0:512]),
                                  reads=KF(4), writes=[("o_conv_s", l, t, bb)], eng="pool")
                if s == 6:
                    b = 6
                    tm_chunk(sap, skeys, 16, 256, 256, s_hT, 0, 64, b)
                    evac_copy("act", tmst[0][0:64, 0:256], ps[b][0:64, 0:256], psk(b), KF(3), [b])
                    P.dma(lambda e: e.dma_start(out=o_k_s[l][:, 124:128, :], in_=tmst[0][0:64, 0:128]), reads=KF(3),
                          writes=[("o_k_s", l, 1)], eng="pool")
                    P.dma(lambda e: e.dma_start(out=o_v_s[l][:, 124:128, :], in_=tmst[0][0:64, 128:256]), reads=KF(3),
                          writes=[("o_v_s", l, 1)], eng="pool")
                    for q4 in range(4):
                        bk = nbank(0, 4)

                        def fnv(e, q4=q4, bk=bk, sap=sap):
                            ins = None
                            for i in range(4):
                                bb = q4 * 4 + i
                                for kc in range(16):
                                    ins = e.matmul(ps[bk][0:4, i * 128:(i + 1) * 128], lhsT=s_hT.ap[:, kc, bb * 4:bb * 4 + 4],
                                                   rhs=sap[:, kc, 384:512], start=(kc == 0), stop=(kc == 15))
                            return ins
                        P.pe(fnv, reads=skeys + s_hT.keys(), writes=psk(bk), banks=[bk])
                        for h in range(2):
                            P.dve(lambda e, q4=q4, bk=bk, h=h: e.tensor_copy(
                                out=Vn.ap[0:4, q4 * 4:(q4 + 1) * 4, h * 128:(h + 1) * 128].rearrange("p b (r d) -> p b r d", r=2),
                                in_=ps[bk][0:4, :].rearrange("p (i h d) -> p i h d", i=4, h=2)[:, :, h, :]
                                .unsqueeze(2).to_broadcast([4, 4, 2, 64])),
                                reads=psk(bk), writes=Vn.keys(q4 * 4, q4 * 4 + 4), banks=[bk])
            for gi in range(4):
                w = 2 << gi
                u3 = s_ux.ap[:, gi, :].rearrange("p (b r) -> p b r", r=19)
                uk = s_ux.keys(gi, gi + 1)
                cur, curk = u3, uk
                for k in range(1, gi + 2):
                    sh = 1 << (k - 1)
                    lo = (1 << k) - 1
                    dst = ptmp[k % 2][:, 0:304].rearrange("p (b r) -> p b r", r=19)
                    P.dve(lambda e, cur=cur, dst=dst, lo=lo, sh=sh: e.tensor_tensor(
                        out=dst[:, :, lo:19], in0=cur[:, :, lo:19], in1=cur[:, :, lo - sh:19 - sh], op=ALU.add),
                        reads=curk, writes=KF(3 + k % 2))
                    cur, curk = dst, KF(3 + k % 2)
                d = dT[gi % 2]
                P.dve(lambda e, cur=cur, d=d, u3=u3, w=w: e.scalar_tensor_tensor(
                    out=bt4(d[:, 0:64]), in0=cur[:, :, 15:19], scalar=1.0 / w, in1=u3[:, :, 15:19],
                    op0=ALU.mult, op1=ALU.subtract), reads=curk + uk, writes=KB(gi % 2))
                b = nbank(0, 4)
                P.pe(lambda e, b=b, gi=gi, d=d: e.matmul(ps[b][:, 0:64], lhsT=wpool[:, l, gi, :], rhs=d[:, 0:64],
                                                        start=True, stop=True),
                     reads=KB(gi % 2) + ["wpool"], writes=psk(b), banks=[b])
                P.act(lambda e, b=b, gi=gi: e.activation(out=s_yT.ap[:, gi, :], in_=ps[b][:, 0:64], func=AF.Copy,
                                                         scale=pscale[:, l, gi:gi + 1]),
                      reads=psk(b) + ["pscale"], writes=s_yT.keys(gi, gi + 1), banks=[b])
            for c in range(4):
                v3 = s_vx.ap[:, c, :].rearrange("p (b r) -> p b r", r=6)
                vk = s_vx.keys(c, c + 1)
                ca = bt4(cacc[c % 2][:, 0:64])
                cak = KF(1 + c % 2)
                P.dve(lambda e, v3=v3, c=c: e.tensor_tensor(out=v3[:, :, 2:6], in0=v3[:, :, 2:6], in1=bt4(s_hc.ap[:, c, :]),
                                                           op=ALU.mult), reads=vk + s_hc.keys(c, c + 1), writes=vk)
                P.act(lambda e, v3=v3, c=c, ca=ca: e.activation(out=ca, in_=v3[:, :, 0:4], func=AF.Copy,
                                                               scale=convw[:, l, 0, c:c + 1]),
                      reads=vk + ["convw"], writes=cak)
                for kk in (1, 2):
                    P.dve(lambda e, v3=v3, c=c, ca=ca, kk=kk: e.scalar_tensor_tensor(
                        out=ca, in0=v3[:, :, kk:kk + 4], scalar=convw[:, l, kk, c:c + 1], in1=ca,
                        op0=ALU.mult, op1=ALU.add), reads=vk + ["convw"] + cak, writes=cak)
                P.dve(lambda e, c=c, ca=ca: e.tensor_tensor(out=bt4(s_yT.ap[:, 4 + c, :]), in0=ca, in1=bt4(s_gb.ap[:, c, :]),
                                                            op=ALU.mult),
                      reads=cak + s_gb.keys(c, c + 1), writes=s_yT.keys(4 + c, 5 + c))
            bA, bC, bE, bF = [0, 1], [2, 3], 4, 5
            for par in range(2):
                p0 = par * 64

                def fnS(e, par=par, p0=p0):
                    e.matmul(ps[bA[par]][:, 0:256], lhsT=identb[:], rhs=mask_sc[:], start=True, stop=False)
                    ins = None
                    for bb in range(SB):
                        for h in range(2):
                            c0 = h * 128 + bb * 8
                            ins = e.matmul(ps[bA[par]][:, c0:c0 + 8].rearrange("p (g t) -> p g t", g=2),
                                           lhsT=KTd.ap[p0:p0 + 64, bb * 2 + h, :],
                                           rhs=s_q.ap[p0:p0 + 64, 2 * h:2 * h + 2, bb * 4:bb * 4 + 4],
                                           start=False, stop=(bb == SB - 1 and h == 1))
                    return ins
                P.pe(fnS, reads=["identb", "mask_sc"] + KTd.keys() + s_q.keys(), writes=psk(bA[par]), banks=[bA[par]])

                def fnN(e, par=par, p0=p0):
                    e.matmul(ps[bC[par]][0:4, 0:256], lhsT=identb[0:4, 0:4], rhs=mask_sn[0:4, :], start=True, stop=False)
                    ins = None
                    for bb in range(SB):
                        for h in range(2):
                            c0 = h * 128 + bb * 8
                            ins = e.matmul(ps[bC[par]][0:4, c0:c0 + 8].rearrange("p (g t) -> p g t", g=2),
                                           lhsT=s_kx.ap[p0:p0 + 64, h, bb * 4:bb * 4 + 4],
                                           rhs=s_q.ap[p0:p0 + 64, 2 * h:2 * h + 2, bb * 4:bb * 4 + 4],
                                           start=False, stop=(bb == SB - 1 and h == 1))
                    return ins
                P.pe(fnN, reads=["identb", "mask_sn"] + s_kx.keys() + s_q.keys(), writes=psk(bC[par]), banks=[bC[par]])
                P.act(lambda e, par=par: e.activation(out=Pt[par][:, 0:256], in_=ps[bA[par]][:, 0:256], func=AF.Exp,
                                                      scale=SWA_SCALE), reads=psk(bA[par]), writes=KB(par), banks=[bA[par]])
                P.act(lambda e, par=par: e.activation(out=Pt[2 + par][0:4, 0:256], in_=ps[bC[par]][0:4, 0:256], func=AF.Exp,
                                                      scale=SWA_SCALE), reads=psk(bC[par]), writes=KB(2 + par), banks=[bC[par]])

            def fnDs(e):
                ins = None
                for par in range(2):
                    o = ps[bE][:, par * 256:(par + 1) * 256]
                    e.matmul(o, lhsT=ones1[:], rhs=Pt[par][:, 0:256], start=True, stop=False)
                    e.matmul(o, lhsT=ones1[0:4, :], rhs=Pt[2 + par][0:4, 0:256], start=False, stop=False)
                    ins = e.matmul(o, lhsT=ones1[0:1, :], rhs=esink_s[0:1, l, par, :], start=False, stop=True)
                return ins
            P.pe(fnDs, reads=["ones1", ("esink_s", l)] + KB(0) + KB(1) + KB(2) + KB(3), writes=psk(bE), banks=[bE])

            def fnOs(e):
                ins = None
                for par in range(2):
                    for bb in range(SB):
                        for h in range(2):
                            cc = h * 128 + bb * 8
                            c0 = par * 256 + cc
                            e.matmul(ps[bF][:, c0:c0 + 8], lhsT=Vds.ap[:, bb, h * 128:(h + 1) * 128], rhs=Pt[par][:, cc:cc + 8],
                                     start=True, stop=False)
                            ins = e.matmul(ps[bF][:, c0:c0 + 8], lhsT=Vn.ap[0:4, bb, h * 128:(h + 1) * 128],
                                           rhs=Pt[2 + par][0:4, cc:cc + 8], start=False, stop=True)
                return ins
            P.pe(fnOs, reads=Vds.keys() + Vn.keys() + KB(0) + KB(1) + KB(2) + KB(3), writes=psk(bF), banks=[bF])
            P.dve(lambda e: e.reciprocal(out=rden[0][:, 0:512], in_=ps[bE][:]), reads=psk(bE), writes=RDK[0], banks=[bE])
            for par in range(2):
                for h in range(2):
                    p0 = par * 64
                    c0 = par * 256 + h * 128
                    P.dve(lambda e, p0=p0, c0=c0, h=h: e.tensor_tensor(
                        out=s_yT.ap[p0:p0 + 64, 8 + 2 * h:10 + 2 * h, :].rearrange("p g (b t) -> p b g t", t=4),
                        in0=ps[bF][p0:p0 + 64, c0:c0 + 128].rearrange("p (b g t) -> p b g t", b=SB, g=2),
                        in1=rden[0][p0:p0 + 64, c0:c0 + 128].rearrange("p (b g t) -> p b g t", b=SB, g=2),
                        op=ALU.mult), reads=psk(bF) + RDK[0], writes=s_yT.keys(8 + 2 * h, 10 + 2 * h), banks=[bF])
            bS, bDn, bOm = 2, 3, 6
            Pm = Bt[0]
            for g in range(SB // 2):
                M_ = mc[g % 2]
                b0 = 2 * g
                P.dma(lambda e, M_=M_, b0=b0: e.dma_start(out=M_["K"].ap.rearrange("p (b k) f -> p b k f", b=2),
                                                          in_=cmk[l][b0:b0 + 2].rearrange("b (k p) f -> p b k f", p=128)),
                      writes=M_["K"].keys(), eng="pool")
                P.dma(lambda e, M_=M_, b0=b0: e.dma_start(out=M_["V"].ap.rearrange("p (b k) f -> p b k f", b=2),
                                                          in_=cmv[l][b0:b0 + 2].rearrange("b (k p) f -> p b k f", p=128)),
                      writes=M_["V"].keys(), eng="pool")
                for b2 in range(2):
                    bk = b2

                    def fnT(e, M_=M_, b2=b2, bk=bk):
                        ins = None
                        for hd in range(4):
                            for blk in range(2):
                                i = hd * 2 + blk
                                ins = e.transpose(out=psb[bk][:, i * 128:(i + 1) * 128],
                                                  in_=M_["K"].ap[:, b2 * 2 + blk, hd * 128:(hd + 1) * 128], identity=identb[:])
                        return ins
                    P.pe(fnT, reads=M_["K"].keys() + ["identb"], writes=psk(bk), banks=[bk])
                    evac_copy(alt(), M_["KT"].ap[:, b2 * 4:(b2 + 1) * 4, :], psb[bk][:].rearrange("p (h k) -> p h k", h=4),
                              psk(bk), M_["KT"].keys(b2 * 4, b2 * 4 + 4), [bk])

                def fnSm(e, M_=M_, b0=b0):
                    ins = None
                    for b2 in range(2):
                        bb = b0 + b2
                        for blk in range(2):
                            for hd in range(4):
                                c0 = bb * 32 + blk * 16 + hd * 4
                                ins = e.matmul(ps[bS][:, c0:c0 + 4], lhsT=M_["KT"].ap[:, b2 * 4 + hd, blk * 128:(blk + 1) * 128],
                                               rhs=s_qm.ap[:, hd, bb * 4:bb * 4 + 4], start=True, stop=True)
                    return ins
                P.pe(fnSm, reads=M_["KT"].keys() + s_qm.keys(), writes=psk(bS), banks=[bS])
                P.act(lambda e, b0=b0: e.activation(out=Pm[:, b0 * 32:b0 * 32 + 64], in_=ps[bS][:, b0 * 32:b0 * 32 + 64],
                                                    func=AF.Exp, scale=MEM_SCALE), reads=psk(bS), writes=KB(0), banks=[bS])

                def fnDm(e, b0=b0):
                    v = Pm[:, b0 * 32:b0 * 32 + 64].rearrange("p (b k x) -> p b k x", b=2, k=2)
                    o = ps[bDn][:, b0 * 16:b0 * 16 + 32].rearrange("p (b x) -> p b x", b=2)
                    e.matmul(o, lhsT=ones1[:], rhs=v[:, :, 0, :], start=True, stop=False)
                    return e.matmul(o, lhsT=ones1[:], rhs=v[:, :, 1, :], start=False, stop=True)
                P.pe(fnDm, reads=["ones1"] + KB(0), writes=psk(bDn), banks=[bDn])

                def fnOm(e, M_=M_, b0=b0):
                    ins = None
                    for b2 in range(2):
                        bb = b0 + b2
                        for hd in range(4):
                            oc = bb * 16 + hd * 4
                            for blk in range(2):
                                c0 = bb * 32 + blk * 16 + hd * 4
                                ins = e.matmul(ps[bOm][:, oc:oc + 4], lhsT=M_["V"].ap[:, b2 * 2 + blk, hd * 128:(hd + 1) * 128],
                                               rhs=Pm[:, c0:c0 + 4], start=(blk == 0), stop=(blk == 1))
                    return ins
                P.pe(fnOm, reads=M_["V"].keys() + KB(0), writes=psk(bOm), banks=[bOm])
            P.dve(lambda e: e.reciprocal(out=rden[1][:, 0:256], in_=ps[bDn][:, 0:256]), reads=psk(bDn), writes=RDK[1], banks=[bDn])
            P.dve(lambda e: e.tensor_tensor(
                out=s_yT.ap[:, 12:16, :].rearrange("p h (b t) -> p b h t", t=4),
                in0=ps[bOm][:, 0:256].rearrange("p (b h t) -> p b h t", b=SB, h=4),
                in1=rden[1][:, 0:256].rearrange("p (b h t) -> p b h t", b=SB, h=4), op=ALU.mult),
                reads=psk(bOm) + RDK[1], writes=s_yT.keys(12, 16), banks=[bOm])
            ffn_and_out(RS, l, T)

        if DBG["mem"]:
            mem_phase()
        for ti in range(DBG["ntiles"]):
            state["use_pool"] = ti > 0
            load_xT(xp[ti * TP:(ti + 1) * TP, :], TP)
            for l in range(DBG["nlayers"]):
                layer_prompt(l, ti)
            store_xT(yp[ti * TP:(ti + 1) * TP, :], TP)
        if with_sample:
            load_xT(xs, TS)
            for l in range(DBG["nlayers"]):
                layer_sample(l)
            store_xT(ys, TS)
        assert wst["next"] == len(seq), (wst["next"], len(seq))
        P.emit()
    return nc


_CACHE = {}


def kernel(**inputs):
    f = lambda a: np.ascontiguousarray(np.asarray(a, dtype=np.float32))
    inp = {k: f(v) for k, v in inputs.items()}
    with_sample = True
    if "nc" not in _CACHE:
        _CACHE["nc"] = build_program(with_sample)
    nc = _CACHE["nc"]
    shared = {k: inp[k] for k in ("g_mix_pre", "w_in", "w_pool", "pool_scale", "conv_w", "swa_sinks", "g_mem",
                                  "w_mem_kv", "w_out", "g_mix_post", "g_mlp_pre", "w_up", "w_down", "g_mlp_post")}
    in_maps = []
    for c in range(NCORES):
        m = dict(shared)
        m["xp"] = inp["x_prompt"][c]
        m["xs"] = inp["x_sample"][c * SB:(c + 1) * SB].reshape(TS, D)
        m["mem"] = inp["mem_prompt"][c]
        m["spool"] = np.ascontiguousarray(inp["state_pool"][:, c * SB:(c + 1) * SB])
        m["sconv"] = np.ascontiguousarray(inp["state_conv"][:, c * SB:(c + 1) * SB])
        m["ck"] = np.ascontiguousarray(inp["cache_swa_k"][:, c * SB:(c + 1) * SB].reshape(DEPTH, SB, 128, 128))
        m["cv"] = np.ascontiguousarray(inp["cache_swa_v"][:, c * SB:(c + 1) * SB].reshape(DEPTH, SB, 128, 128))
        m["cmk"] = np.ascontiguousarray(inp["cache_mem_k"][:, c * SB:(c + 1) * SB].reshape(DEPTH, SB, MEMT, DG))
        m["cmv"] = np.ascontiguousarray(inp["cache_mem_v"][:, c * SB:(c + 1) * SB].reshape(DEPTH, SB, MEMT, DG))
        in_maps.append(m)
    res = run_bass_kernel_spmd(nc, in_maps, core_ids=list(range(NCORES)))
    R = res.results
    cat = lambda name, axis: np.concatenate([np.asarray(R[c][name], dtype=np.float32) for c in range(NCORES)], axis=axis)
    stk = lambda name: np.stack([np.asarray(R[c][name], dtype=np.float32) for c in range(NCORES)], axis=1)
    y_prompt = np.stack([np.asarray(R[c]["yp"], dtype=np.float32) for c in range(NCORES)], axis=0)
    y_sample = cat("ys", 0).reshape(NCORES * SB, ST, D)
    return (
        y_prompt,
        y_sample,
        stk("o_pool_p"),
        cat("o_pool_s", 1),
        stk("o_conv_p"),
        cat("o_conv_s", 1),
        stk("o_k_p").reshape(DEPTH, NCORES, 128, 2, 64),
        cat("o_k_s", 1).reshape(DEPTH, NCORES * SB, 128, 2, 64),
        stk("o_v_p").reshape(DEPTH, NCORES, 128, 2, 64),
        cat("o_v_s", 1).reshape(DEPTH, NCORES * SB, 128, 2, 64),
        stk("o_mk_p").reshape(DEPTH, NCORES, MEMT, 4, 128),
        stk("o_mv_p").reshape(DEPTH, NCORES, MEMT, 4, 128),
    )
```

```python
import contextlib
import math
from types import SimpleNamespace
import numpy as np
import concourse.bass as bass
import concourse.mybir as mybir
from concourse.bass_utils import run_bass_kernel_spmd

F32 = mybir.dt.float32
BF16 = mybir.dt.bfloat16
ALU = mybir.AluOpType
AF = mybir.ActivationFunctionType

NCORES = 8
D = 2048
DEPTH = 2
SEQ = 2048
TP = 512
NPT = SEQ // TP
SB = 16
ST = 4
TS = SB * ST
MEMT = 256
DG = 512
D_IN = 3328
DFF = 8192
EPS = 1e-6
SWA_SCALE = 1.0 / 8.0
MEM_SCALE = 1.0 / math.sqrt(128.0)
NEG = -30000.0

ENGS = ("pe", "act", "dve", "pool", "sp")
SEM_LIMIT = 24000
NDMASEM = 12


class Op:
    __slots__ = ("eng", "fn", "deps", "dma", "inc", "cnt", "dsem", "dval", "prewait")

    def __init__(self, eng, fn, deps, dma):
        self.eng = eng
        self.fn = fn
        self.deps = deps
        self.dma = dma
        self.inc = False
        self.cnt = 0
        self.dsem = None
        self.dval = 0
        self.prewait = None


class Prog:
    def __init__(self, nc):
        self.nc = nc
        self.ops = {e: [] for e in ENGS}
        self.lw = {}
        self.rd = {}
        self.ndma = {e: 0 for e in ENGS}

    def add(self, eng, fn, reads=(), writes=(), dma=False, banks=()):
        idx = len(self.ops[eng])
        deps = set()
        for b in banks:
            k = ("__bank", b)
            w = self.lw.get(k)
            if w is not None and w[0] != eng:
                deps.add(w)
            self.lw[k] = (eng, idx)
        for k in reads:
            w = self.lw.get(k)
            if w is not None:
                deps.add(w)
        for k in writes:
            w = self.lw.get(k)
            if w is not None:
                deps.add(w)
            for r in self.rd.get(k, ()):
                deps.add(r)
        me = (eng, idx)
        deps.discard(me)
        if eng == "pe":
            deps = {d for d in deps if d[0] != "pe"}
        op = Op(eng, fn, deps, dma)
        if dma:
            n = self.ndma[eng]
            self.ndma[eng] = n + 1
            op.dsem = (eng, n % NDMASEM)
            op.dval = 16 * (n // NDMASEM + 1)
            if n >= NDMASEM:
                op.prewait = (op.dsem, op.dval - 16)
        self.ops[eng].append(op)
        for k in reads:
            self.rd.setdefault(k, []).append(me)
        for k in writes:
            self.lw[k] = me
            self.rd[k] = []
        return me

    def pe(self, fn, reads=(), writes=(), banks=()):
        return self.add("pe", fn, reads, writes, banks=banks)

    def act(self, fn, reads=(), writes=(), banks=()):
        return self.add("act", fn, reads, writes, banks=banks)

    def dve(self, fn, reads=(), writes=(), banks=()):
        return self.add("dve", fn, reads, writes, banks=banks)

    def pool(self, fn, reads=(), writes=(), banks=()):
        return self.add("pool", fn, reads, writes, banks=banks)

    def dma(self, fn, reads=(), writes=(), eng="sp"):
        return self.add(eng, fn, reads, writes, dma=True)

    def emit(self):
        nc = self.nc
        for e in ENGS:
            for op in self.ops[e]:
                for (de, di) in op.deps:
                    d = self.ops[de][di]
                    if not d.dma:
                        d.inc = True
        nsem = {}
        for e in ENGS:
            c = 0
            for op in self.ops[e]:
                if op.inc and not op.dma:
                    c += 1
                op.cnt = c
            nsem[e] = (c // SEM_LIMIT) + 1
        with contextlib.ExitStack() as st:
            csem = {e: [st.enter_context(nc.semaphore(f"c_{e}_{i}")) for i in range(nsem[e])]
                    for e in ENGS}
            dsem = {}
            for e in ENGS:
                for i in range(min(NDMASEM, self.ndma[e])):
                    dsem[(e, i)] = st.enter_context(nc.semaphore(f"d_{e}_{i}"))
            block = st.enter_context(nc.Block())
            engobj = {"pe": "tensor", "act": "scalar", "dve": "vector", "pool": "gpsimd", "sp": "sync"}

            def body(e):
                def run(eng):
                    waited = {}

                    def do_wait(key, sem, val):
                        if waited.get(key, 0) >= val:
                            return
                        eng.wait_ge(sem, val)
                        waited[key] = val

                    for op in self.ops[e]:
                        for (de, di) in sorted(op.deps):
                            d = self.ops[de][di]
                            if d.dma:
                                do_wait(("d",) + d.dsem, dsem[d.dsem], d.dval)
                            else:
                                si, v = divmod(d.cnt - 1, SEM_LIMIT)
                                do_wait(("c", de, si), csem[de][si], v + 1)
                        if op.prewait is not None:
                            do_wait(("d",) + op.prewait[0], dsem[op.prewait[0]], op.prewait[1])
                        ins = op.fn(eng)
                        if op.dma:
                            ins.then_inc(dsem[op.dsem], 16)
                        elif op.inc:
                            ins.then_inc(csem[e][(op.cnt - 1) // SEM_LIMIT], 1)
                    for (de, i), s in dsem.items():
                        if de == e:
                            n = self.ndma[e]
                            uses = (n - i + NDMASEM - 1) // NDMASEM
                            if uses > 0:
                                do_wait(("d", de, i), s, 16 * uses)
                return run

            for e in ENGS:
                if self.ops[e]:
                    getattr(block, engobj[e])(body(e))


class Region:
    def __init__(self, buf, name, off, C, W, dt):
        esz = 4 if dt == F32 else 2
        nb = C * W * esz
        assert off % 4 == 0
        sl = buf[:, off // 2:(off + nb) // 2]
        if dt == F32:
            sl = sl.bitcast(F32)
        self.ap = sl.rearrange("p (c w) -> p c w", c=C)
        self.name = name
        self.off = off
        self.cb = W * esz
        self.C = C
        self.end = off + nb

    def keys(self, c0=0, c1=None):
        if c1 is None:
            c1 = self.C
        lo = self.off + c0 * self.cb
        hi = self.off + c1 * self.cb
        return [(self.name, b) for b in range(lo // 1024, (hi - 1) // 1024 + 1)]


WIN_NSLAB = 7
SLAB_ELEMS = 8192


DBG = {"mem": True, "ntiles": NPT, "nlayers": DEPTH, "mixers": True, "ffn": True}


def build_program(with_sample=True):
    nc = bass.Bass("TRN2", target_bir_lowering=False)

    def din(name, shape, dt=F32):
        return nc.dram_tensor(name, list(shape), dt, kind="ExternalInput").ap()

    def dout(name, shape, dt=F32):
        return nc.dram_tensor(name, list(shape), dt, kind="ExternalOutput").ap()

    xp = din("xp", [SEQ, D])
    xs = din("xs", [TS, D])
    mem = din("mem", [MEMT, D])
    spool = din("spool", [DEPTH, SB, 15, DG])
    sconv = din("sconv", [DEPTH, SB, 2, DG])
    ck = din("ck", [DEPTH, SB, 128, 128])
    cv = din("cv", [DEPTH, SB, 128, 128])
    cmk = din("cmk", [DEPTH, SB, MEMT, DG])
    cmv = din("cmv", [DEPTH, SB, MEMT, DG])
    g_mix_pre = din("g_mix_pre", [DEPTH, D])
    w_in = din("w_in", [DEPTH, D, D_IN])
    w_pool = din("w_pool", [DEPTH, 4, 128, 128])
    pool_scale = din("pool_scale", [DEPTH, DG])
    conv_w = din("conv_w", [DEPTH, 3, DG])
    swa_sinks = din("swa_sinks", [DEPTH, 8])
    g_mem = din("g_mem", [DEPTH, D])
    w_mem_kv = din("w_mem_kv", [DEPTH, D, 2 * DG])
    w_out = din("w_out", [DEPTH, D, D])
    g_mix_post = din("g_mix_post", [DEPTH, D])
    g_mlp_pre = din("g_mlp_pre", [DEPTH, D])
    w_up = din("w_up", [DEPTH, D, DFF])
    w_down = din("w_down", [DEPTH, DFF, D])
    g_mlp_post = din("g_mlp_post", [DEPTH, D])

    yp = dout("yp", [SEQ, D])
    ys = dout("ys", [TS, D])
    o_pool_p = dout("o_pool_p", [DEPTH, 15, DG])
    o_pool_s = dout("o_pool_s", [DEPTH, SB, 15, DG])
    o_conv_p = dout("o_conv_p", [DEPTH, 2, DG])
    o_conv_s = dout("o_conv_s", [DEPTH, SB, 2, DG])
    o_k_p = dout("o_k_p", [DEPTH, 128, 128])
    o_k_s = dout("o_k_s", [DEPTH, SB, 128, 128])
    o_v_p = dout("o_v_p", [DEPTH, 128, 128])
    o_v_s = dout("o_v_s", [DEPTH, SB, 128, 128])
    o_mk_p = dout("o_mk_p", [DEPTH, MEMT, DG])
    o_mv_p = dout("o_mv_p", [DEPTH, MEMT, DG])

    def scratch(name, nslab):
        return nc.dram_tensor(name, [DEPTH, nslab, 128, SLAB_ELEMS], BF16, kind="Internal").ap()

    s_in = scratch("s_in", WIN_NSLAB)
    s_out = scratch("s_out", 4)
    s_up = scratch("s_up", 16)
    s_down = scratch("s_down", 16)

    P = Prog(nc)
    st = contextlib.ExitStack()
    with st:
        def sb(name, shape, dt):
            return st.enter_context(nc.sbuf_tensor(name, list(shape), dt))

        xT = sb("xT", [128, 16, TP], F32)
        U1 = sb("U1", [128, 16384], BF16)
        BIG = sb("BIG", [128, 32768], BF16)
        NWB = 2
        wbuf = [sb(f"wbuf{i}", [128, SLAB_ELEMS], BF16) for i in range(NWB)]
        ident = sb("ident", [128, 128], F32)
        identb = sb("identb", [128, 128], BF16)
        onesD = sb("onesD", [128, 128], BF16)
        ones1 = sb("ones1", [128, 128], BF16)
        mask_cat = sb("mask_cat", [128, 512], BF16)
        mask_first = sb("mask_first", [128, 512], BF16)
        mask_sc = sb("mask_sc", [128, 256], BF16)
        mask_sn = sb("mask_sn", [128, 256], BF16)
        esink_s = sb("esink_s", [1, DEPTH, 2, 256], BF16)
        gv = sb("gv", [128, 5, DEPTH, 16], F32)
        pscale = sb("pscale", [128, DEPTH, 4], F32)
        convw = sb("convw", [128, DEPTH, 3, 4], F32)
        wpool = sb("wpool", [128, DEPTH, 4, 128], BF16)
        sinks_sb = sb("sinks_sb", [1, DEPTH * 8], F32)
        esink = sb("esink", [1, DEPTH, 2, 512], BF16)
        invcnt = sb("invcnt", [128, 4, 16], F32)
        rtm = sb("rtm", [128, 8], F32)
        Fs = [sb(f"F{i}", [128, 16 + TP], F32) for i in range(6)]
        Bt = [sb(f"Bt{i}", [128, TP], BF16) for i in range(4)]
        dTp = [sb(f"dTp{i}", [128, TP], BF16) for i in range(2)]
        Vd = sb("Vd", [128, 5, 2, 128], BF16)
        carry_u = sb("carry_u", [128, DEPTH, 4, 16], F32)
        carry_v = sb("carry_v", [128, DEPTH, 4, 2], F32)
        carry_k = sb("carry_k", [128, DEPTH, 2, 128], BF16)
        carry_V = sb("carry_V", [128, DEPTH, 2, 128], BF16)
        mkT = sb("mkT", [128, DEPTH, 4, MEMT], BF16)
        mvv = sb("mvv", [128, DEPTH, 2, DG], BF16)

        ps = [st.enter_context(nc.psum_tensor(f"ps{i}", [128, 512], F32)) for i in range(8)]

        rstd = Fs[0]
        cacc = [Fs[1], Fs[2]]
        rtmp = [Fs[1], Fs[2]]
        rden = [Fs[0], Fs[5]]
        RDK = [[("F", 0)], [("F", 5)]]
        ptmp = [Fs[3], Fs[4]]
        tmst = [Fs[3], Fs[4]]
        hcst = Fs[5]
        mtmp = Fs[5]
        pfix = Fs[5]
        epsq = Fs[5]
        sq = Bt
        Pt = Bt
        dT = [Bt[0], Bt[1]]
        KF = lambda i: [("F", i)]
        KB = lambda i: [("Bt", i)]

        xstage = Region(U1, "U1", 0, 4, D, F32)
        xstage_ld = Region(BIG, "BIG", 0, 4, D, F32)
        hT = Region(U1, "U1", 0, 16, TP, BF16)
        mixT = Region(U1, "U1", 0, 16, TP, F32)
        off = 0
        u_ext = Region(BIG, "BIG", off, 4, 16 + TP, F32); off = u_ext.end
        hcR = Region(BIG, "BIG", off, 4, TP, F32); off = hcR.end
        gbR = Region(BIG, "BIG", off, 4, TP, F32); off = gbR.end
        v_ext = Region(BIG, "BIG", off, 4, 2 + TP, F32); off = v_ext.end
        qR = Region(BIG, "BIG", off, 4, TP, BF16); off = qR.end
        kext = Region(BIG, "BIG", off, 2, 128 + TP, BF16); off = kext.end
        qmR = Region(BIG, "BIG", off, 4, TP, BF16); off = qmR.end
        yT = Region(BIG, "BIG", off, 16, TP, BF16); off = yT.end
        assert off <= 65536, off
        hidT = Region(BIG, "BIG", 0, 64, TP, BF16)
        RP = SimpleNamespace(hT=hT, mixT=mixT, yT=yT, hidT=hidT)

        state = {"bank": 0, "alt": 0, "use_pool": False}

        def alt():
            state["alt"] ^= 1
            return "act" if state["alt"] else "dve"

        def evac_copy(eng, out_ap, in_ap, reads, writes, banks=()):
            if eng == "act":
                P.act(lambda e: e.activation(out=out_ap, in_=in_ap, func=AF.Copy), reads, writes, banks)
            else:
                P.dve(lambda e: e.tensor_copy(out=out_ap, in_=in_ap), reads, writes, banks)

        def psk(b):
            return [("ps", b)]

        P.pool(lambda e: e.memset(ident[:], 1.0), writes=["ident"])
        P.pool(lambda e: e.affine_select(out=ident[:], in_=ident[:], pattern=[[-1, 128]],
                                         compare_op=ALU.is_equal, fill=0.0, base=0, channel_multiplier=1),
               reads=["ident"], writes=["ident"])
        P.dve(lambda e: e.tensor_copy(out=identb[:], in_=ident[:]), reads=["ident"], writes=["identb"])
        P.pool(lambda e: e.memset(onesD[:], 1.0 / D), writes=["onesD"])
        P.pool(lambda e: e.memset(ones1[:], 1.0), writes=["ones1"])
        P.pool(lambda e: e.memset(mtmp[:, 0:512], 0.0), writes=[("F", 5)])
        P.pool(lambda e: e.affine_select(out=mtmp[:, 0:256].rearrange("p (a q) -> p a q", a=2),
                                         in_=mtmp[:, 0:256].rearrange("p (a q) -> p a q", a=2),
                                         pattern=[[0, 2], [1, 128]], compare_op=ALU.is_ge, fill=NEG,
                                         base=0, channel_multiplier=-1), reads=[("F", 5)], writes=[("F", 5)])
        P.pool(lambda e: e.affine_select(out=mtmp[:, 256:512].rearrange("p (a q) -> p a q", a=2),
                                         in_=mtmp[:, 256:512].rearrange("p (a q) -> p a q", a=2),
                                         pattern=[[0, 2], [-1, 128]], compare_op=ALU.is_ge, fill=NEG,
                                         base=-1, channel_multiplier=1), reads=[("F", 5)], writes=[("F", 5)])
        P.dve(lambda e: e.tensor_copy(out=mask_cat[:], in_=mtmp[:, 0:512]), reads=[("F", 5)], writes=["mask_cat"])
        P.dve(lambda e: e.tensor_copy(out=mask_first[:, 0:256], in_=mtmp[:, 0:256]), reads=[("F", 5)], writes=["mask_first"])
        P.pool(lambda e: e.memset(mask_first[:, 256:512], NEG), reads=["mask_first"], writes=["mask_first"])
        P.pool(lambda e: e.memset(mtmp[:, 0:512], 0.0), reads=[("F", 5)], writes=[("F", 5)])
        P.pool(lambda e: e.affine_select(out=mtmp[:, 0:256].rearrange("p (a t) -> p a t", t=4),
                                         in_=mtmp[:, 0:256].rearrange("p (a t) -> p a t", t=4),
                                         pattern=[[0, 64], [-1, 4]], compare_op=ALU.is_ge, fill=NEG,
                                         base=-1, channel_multiplier=1), reads=[("F", 5)], writes=[("F", 5)])
        P.pool(lambda e: e.affine_select(out=mtmp[:, 256:512].rearrange("p (a t) -> p a t", t=4),
                                         in_=mtmp[:, 256:512].rearrange("p (a t) -> p a t", t=4),
                                         pattern=[[0, 64], [1, 4]], compare_op=ALU.is_ge, fill=NEG,
                                         base=0, channel_multiplier=-1), reads=[("F", 5)], writes=[("F", 5)])
        P.dve(lambda e: e.tensor_copy(out=mask_sc[:], in_=mtmp[:, 0:256]), reads=[("F", 5)], writes=["mask_sc"])
        P.dve(lambda e: e.tensor_copy(out=mask_sn[:], in_=mtmp[:, 256:512]), reads=[("F", 5)], writes=["mask_sn"])
        for gi in range(4):
            w = 2 << gi
            for j in range(16):
                P.pool(lambda e, gi=gi, j=j, w=w: e.memset(invcnt[:, gi, j:j + 1], 1.0 / min(j + 1, w)),
                       writes=[("invcnt", gi, j)])
        INVK = [("invcnt", gi, j) for gi in range(4) for j in range(16)]
        P.pool(lambda e: e.memset(carry_u[:], 0.0), writes=["carry_u"])
        P.pool(lambda e: e.memset(carry_v[:], 0.0), writes=["carry_v"])
        P.pool(lambda e: e.memset(carry_k[:], 0.0), writes=["carry_k"])
        P.pool(lambda e: e.memset(carry_V[:], 0.0), writes=["carry_V"])

        for i, g in enumerate((g_mix_pre, g_mem, g_mix_post, g_mlp_pre, g_mlp_post)):
            for l in range(DEPTH):
                P.dma(lambda e, i=i, l=l, g=g: e.dma_start(out=gv[:, i, l, :],
                                                            in_=g[l].rearrange("(c p) -> p c", p=128),
                                                            allow_slow_non_contiguous=True),
                      writes=["gv"])
        for l in range(DEPTH):
            P.dma(lambda e, l=l: e.dma_start(out=pscale[:, l, :], in_=pool_scale[l].rearrange("(c p) -> p c", p=128),
                                             allow_slow_non_contiguous=True), writes=["pscale"])
            for k in range(3):
                P.dma(lambda e, l=l, k=k: e.dma_start(out=convw[:, l, k, :],
                                                      in_=conv_w[l, k].rearrange("(c p) -> p c", p=128),
                                                      allow_slow_non_contiguous=True), writes=["convw"])
        P.dma(lambda e: e.dma_start(out=sinks_sb[:], in_=swa_sinks.rearrange("l j -> (l j)").rearrange("(o n) -> o n", o=1)),
              writes=["sinks_sb"])
        for l in range(DEPTH):
            P.dma(lambda e, l=l: e.dma_start(out=wpool[:, l, :, :], in_=w_pool[l].rearrange("g c d -> c g d")),
                  writes=["wpool"], eng="pool")
        P.act(lambda e: e.activation(out=sinks_sb[:], in_=sinks_sb[:], func=AF.Exp), reads=["sinks_sb"], writes=["sinks_sb"])
        for l in range(DEPTH):
            for h in range(2):
                for par in range(2):
                    for gg in range(2):
                        j = l * 8 + 4 * h + 2 * gg + par
                        c0 = par * 256 + gg * 128
                        P.dve(lambda e, l=l, h=h, j=j, c0=c0: e.tensor_copy(
                            out=esink[0:1, l, h, c0:c0 + 128], in_=sinks_sb[0:1, j:j + 1].broadcast_to([1, 128])),
                            reads=["sinks_sb"], writes=[("esink", l, h, c0)])
        ESK = lambda l, h: [("esink", l, h, c0) for c0 in (0, 128, 256, 384)]
        for l in range(DEPTH):
            for par in range(2):
                for h in range(2):
                    for gg in range(2):
                        j = l * 8 + 4 * h + 2 * gg + par
                        P.dve(lambda e, l=l, par=par, h=h, gg=gg, j=j: e.tensor_copy(
                            out=esink_s[0:1, l, par, h * 128:(h + 1) * 128].rearrange("o (b g t) -> o b g t", b=SB, g=2)[:, :, gg, :],
                            in_=sinks_sb[0:1, j:j + 1].unsqueeze(2).to_broadcast([1, SB, 4])),
                            reads=["sinks_sb"], writes=[("esink_s", l)])

        SCR = {"in": s_in, "out": s_out, "up": s_up, "down": s_down}
        SRC = {"mem": w_mem_kv, "in": w_in, "out": w_out, "up": w_up, "down": w_down}

        def slab_pieces(kind, l, s):
            src = SRC[kind][l].rearrange("(k p) n -> p k n", p=128)
            if kind == "down":
                G, q = s // 4, s % 4
                return 16, [(0, 512, src[:, q * 16:(q + 1) * 16, G * 512:(G + 1) * 512])]
            if kind != "in" or s < 5:
                return 16, [(0, 512, src[:, :, s * 512:(s + 1) * 512])]
            if s == 5:
                pcs = []
                for h in range(2):
                    for r in range(2):
                        c0 = h * 128 + r * 64
                        pcs.append((c0, c0 + 64, src[:, :, 2560 + h * 64:2560 + h * 64 + 64]))
                pcs.append((256, 512, src[:, :, 2816:3072]))
                return 16, pcs
            return 16, [(0, 256, src[:, :, 3072:3328]), (256, 512, src[:, :, 2560:2816])]

        seq = []
        seen = set()

        def add_seq(tag):
            seq.append((tag, tag not in seen))
            seen.add(tag)
        if DBG["mem"]:
            for l in range(DEPTH):
                for s in range(2):
                    add_seq(("mem", l, s))
        tiles = [("p", i) for i in range(DBG["ntiles"])] + ([("s", 0)] if with_sample else [])
        for t in tiles:
            for l in range(DBG["nlayers"]):
                for s in range(WIN_NSLAB):
                    add_seq(("in", l, s))
                if not DBG["ffn"]:
                    continue
                for s in range(4):
                    add_seq(("out", l, s))
                for s in range(16):
                    add_seq(("up", l, s))
                for s in range(16):
                    add_seq(("down", l, s))
        wst = {"issued": 0, "next": 0}
        PREF = 1
        WK = lambda bi: [("wbuf", bi, i) for i in range(5)]

        per_tile = (len(seq) - (2 * DEPTH if DBG["mem"] else 0)) // max(1, len(tiles))
        i0_s = (len(seq) - per_tile) if with_sample else len(seq) + 10
        xT_bf = xT[:].rearrange("p c t -> p (c t)").bitcast(BF16).rearrange("p (c w) -> p c w", c=16)
        wbuf2 = xT_bf[:, :, 128:640]

        def bufidx(i):
            if i < i0_s:
                return i % NWB
            order = [i0_s % 2, 1 - i0_s % 2, 2]
            return order[(i - i0_s) % 3]

        def wview(bi, KC):
            if bi == 2:
                assert KC == 16
                return wbuf2
            return wbuf[bi][:].rearrange("p (k n) -> p k n", k=KC)

        def w_issue(upto):
            while wst["issued"] < min(upto, len(seq)):
                i = wst["issued"]
                tag, first_use = seq[i]
                kind, l, s_ = tag
                bi = bufidx(i)
                if first_use:
                    assert bi < 2
                    KC, pcs = slab_pieces(kind, l, s_)
                    wv = wview(bi, KC)
                    for pi, (c0, c1, src) in enumerate(pcs):
                        wk = WK(bi) if len(pcs) == 1 else [("wbuf", bi, pi)]
                        P.dma(lambda e, wv=wv, c0=c0, c1=c1, src=src: e.dma_start(out=wv[:, :, c0:c1], in_=src),
                              writes=wk, eng="pool")
                    if kind != "mem":
                        P.dma(lambda e, kind=kind, l=l, s_=s_, bi=bi: e.dma_start(out=SCR[kind][l, s_], in_=wbuf[bi][:]),
                              reads=WK(bi), writes=[("scr",) + tag])
                else:
                    extra = []
                    if bi == 2 and not wst.get("b2_used"):
                        wst["b2_used"] = True
                        extra = [("xT", c) for c in range(16)]
                    P.dma(lambda e, kind=kind, l=l, s_=s_, bi=bi: e.dma_start(
                        out=wview(bi, 16), in_=SCR[kind][l, s_].rearrange("p (k n) -> p k n", k=16)),
                        reads=[("scr",) + tag], writes=WK(bi) + extra)
                wst["issued"] += 1

        def w_next(tag, KC):
            i = wst["next"]
            assert seq[i][0] == tag, (seq[i][0], tag)
            w_issue(i + 1 + (2 if i >= i0_s else PREF))
            wst["next"] += 1
            bi = bufidx(i)
            return wview(bi, KC), WK(bi)

        def nbank(lo=0, hi=6):
            b = state["bank"]
            state["bank"] = b + 1
            return lo + b % (hi - lo)

        def load_xT(src, T, xstage=None):
            xstage = xstage if xstage is not None else xstage_ld
            NG = (T + 127) // 128
            rows = min(128, T)
            for g in range(NG):
                P.dma(lambda e, g=g: e.dma_start(out=xstage.ap[0:rows, g, :], in_=src[g * 128:g * 128 + rows, :]),
                      writes=xstage.keys(g, g + 1))
                for cb in range(4):
                    b = nbank(0, 4)

                    def fn(e, g=g, cb=cb, b=b):
                        ins = None
                        for j in range(4):
                            c = cb * 4 + j
                            ins = e.transpose(out=ps[b][:, j * 128:j * 128 + rows],
                                              in_=xstage.ap[0:rows, g, c * 128:(c + 1) * 128],
                                              identity=ident[0:rows, 0:rows])
                        return ins
                    P.pe(fn, reads=xstage.keys(g, g + 1) + ["ident"], writes=psk(b), banks=[b])
                    evac_copy(alt(), xT[:, cb * 4:cb * 4 + 4, g * 128:g * 128 + rows],
                              ps[b][:].rearrange("p (j t) -> p j t", j=4)[:, :, 0:rows],
                              psk(b), [("xT", cb * 4 + j) for j in range(4)], [b])

        def store_xT(dst, T):
            NG = (T + 127) // 128
            rows = min(128, T)
            for g in range(NG):
                for cb in range(4):
                    b = nbank(0, 4)

                    def fn(e, g=g, cb=cb, b=b):
                        ins = None
                        for j in range(4):
                            c = cb * 4 + j
                            ins = e.transpose(out=ps[b][0:rows, j * 128:(j + 1) * 128],
                                              in_=xT[:, c, g * 128:g * 128 + rows], identity=ident[:])
                        return ins
                    P.pe(fn, reads=[("xT", cb * 4 + j) for j in range(4)] + ["ident"], writes=psk(b), banks=[b])
                    evac_copy(alt(), xstage.ap[0:rows, g, cb * 512:(cb + 1) * 512], ps[b][0:rows, :],
                              psk(b), xstage.keys(g, g + 1), [b])
                P.dma(lambda e, g=g: e.dma_start(out=dst[g * 128:g * 128 + rows, :], in_=xstage.ap[0:rows, g, :]),
                      reads=xstage.keys(g, g + 1), writes=[("ydst", g)], eng="act")

        def norm_stats(src_ap, src_keys, T, nch=16, mode="rstd"):
            b = 7
            for c in range(nch):
                sb_ = sq[c % 4]
                P.act(lambda e, c=c, sb_=sb_: e.activation(out=sb_[:, 0:T], in_=src_ap(c), func=AF.Square),
                      reads=src_keys(c), writes=[("Bt", c % 4)])
                P.pe(lambda e, c=c, sb_=sb_: e.matmul(ps[b][:, 0:T], lhsT=onesD[:], rhs=sb_[:, 0:T],
                                                      start=(c == 0), stop=(c == nch - 1)),
                     reads=[("Bt", c % 4), "onesD"], writes=psk(b), banks=[b])
            if mode == "epsq":
                P.act(lambda e: e.activation(out=epsq[:, 0:T], in_=ps[b][:, 0:T], func=AF.Square,
                                             bias=EPS * math.sqrt(EPS), scale=math.sqrt(EPS)),
                      reads=psk(b), writes=[("F", 5)], banks=[b])
                return
            if mode == "rstd_q":
                P.dve(lambda e: e.tensor_tensor(out=rstd[:, 0:T], in0=ps[b][:, 0:T], in1=epsq[:, 0:T], op=ALU.add),
                      reads=psk(b) + [("F", 5)], writes=[("F", 0)], banks=[b])
                P.act(lambda e: e.activation(out=rstd[:, 0:T], in_=rstd[:, 0:T], func=AF.Sqrt),
                      reads=[("F", 0)], writes=[("F", 0)])
            else:
                P.act(lambda e: e.activation(out=rstd[:, 0:T], in_=ps[b][:, 0:T], func=AF.Sqrt, bias=EPS, scale=1.0),
                      reads=psk(b), writes=[("F", 0)], banks=[b])
            P.dve(lambda e: e.reciprocal(out=rstd[:, 0:T], in_=rstd[:, 0:T]), reads=[("F", 0)], writes=[("F", 0)])

        def xT_ap(T):
            return (lambda c: xT[:, c, 0:T]), (lambda c: [("xT", c)])

        def prenorm_to_hT(R, gidx, l, T):
            a, k = xT_ap(T)
            norm_stats(a, k, T)
            for c in range(16):
                if c % 2 == 1 and state["use_pool"]:
                    tb = Fs[1 + (c // 2) % 2]
                    tk = KF(1 + (c // 2) % 2)
                    P.pool(lambda e, c=c, tb=tb: e.tensor_tensor(out=tb[:, 0:T], in0=xT[:, c, 0:T], in1=rstd[:, 0:T],
                                                                 op=ALU.mult),
                           reads=[("xT", c), ("F", 0)], writes=tk)
                    P.act(lambda e, c=c, tb=tb: e.activation(out=R.hT.ap[:, c, 0:T], in_=tb[:, 0:T], func=AF.Copy,
                                                             scale=gv[:, gidx, l, c:c + 1]),
                          reads=tk + ["gv"], writes=R.hT.keys(c, c + 1))
                else:
                    P.dve(lambda e, c=c: e.scalar_tensor_tensor(out=R.hT.ap[:, c, 0:T], in0=xT[:, c, 0:T],
                                                                scalar=gv[:, gidx, l, c:c + 1], in1=rstd[:, 0:T],
                                                                op0=ALU.mult, op1=ALU.mult),
                          reads=[("xT", c), "gv", ("F", 0)], writes=R.hT.keys(c, c + 1))

        def postnorm_residual(R, gidx, l, T, mode="rstd", after_chunk=None):
            mixT_ = R.mixT
            norm_stats(lambda c: mixT_.ap[:, c, 0:T], lambda c: mixT_.keys(c, c + 1), T, mode=mode)
            def scale_chunk(c):
                P.dve(lambda e, c=c: e.scalar_tensor_tensor(out=mixT_.ap[:, c, 0:T], in0=mixT_.ap[:, c, 0:T],
                                                            scalar=gv[:, gidx, l, c:c + 1], in1=rstd[:, 0:T],
                                                            op0=ALU.mult, op1=ALU.mult),
                      reads=mixT_.keys(c, c + 1) + ["gv", ("F", 0)], writes=mixT_.keys(c, c + 1))

            def add_chunk(c):
                (P.pool if (state["use_pool"] and c % 2 == 1) else P.dve)(
                    lambda e, c=c: e.tensor_tensor(out=xT[:, c, 0:T], in0=xT[:, c, 0:T], in1=mixT_.ap[:, c, 0:T],
                                                   op=ALU.add),
                    reads=mixT_.keys(c, c + 1) + [("xT", c)], writes=[("xT", c)])
                if after_chunk is not None:
                    after_chunk(c)
            for c in range(17):
                if c < 16:
                    scale_chunk(c)
                if c >= 1:
                    add_chunk(c - 1)

        def proj_chunk(sap, skeys, KC, j, inR, T, b, fine=False):
            if fine:
                for kc in range(KC):
                    P.pe(lambda e, kc=kc: e.matmul(ps[b][:, 0:T], lhsT=sap[:, kc, j * 128:(j + 1) * 128],
                                                   rhs=inR.ap[:, kc, 0:T], start=(kc == 0), stop=(kc == KC - 1)),
                         reads=skeys + inR.keys(kc, kc + 1), writes=psk(b), banks=[b])
                return

            def fn(e):
                ins = None
                for kc in range(KC):
                    ins = e.matmul(ps[b][:, 0:T], lhsT=sap[:, kc, j * 128:(j + 1) * 128], rhs=inR.ap[:, kc, 0:T],
                                   start=(kc == 0), stop=(kc == KC - 1))
                return ins
            P.pe(fn, reads=skeys + inR.keys(), writes=psk(b), banks=[b])

        def tm_chunk(sap, skeys, KC, c0, ncols, inR, t0, M, b):
            def fn(e):
                ins = None
                for kc in range(KC):
                    ins = e.matmul(ps[b][0:M, 0:ncols], lhsT=inR.ap[:, kc, t0:t0 + M], rhs=sap[:, kc, c0:c0 + ncols],
                                   start=(kc == 0), stop=(kc == KC - 1))
                return ins
            P.pe(fn, reads=skeys + inR.keys(), writes=psk(b), banks=[b])

        def mem_phase():
            load_xT(mem, MEMT)
            a, k = xT_ap(MEMT)
            norm_stats(a, k, MEMT)
            for l in range(DEPTH):
                for c in range(16):
                    P.dve(lambda e, c=c, l=l: e.scalar_tensor_tensor(out=hT.ap[:, c, 0:MEMT], in0=xT[:, c, 0:MEMT],
                                                                     scalar=gv[:, 1, l, c:c + 1], in1=rstd[:, 0:MEMT],
                                                                     op0=ALU.mult, op1=ALU.mult),
                          reads=[("xT", c), "gv", ("F", 0)], writes=hT.keys(c, c + 1))
                for s in range(2):
                    sap, skeys = w_next(("mem", l, s), 16)
                    if s == 0:
                        for j in range(4):
                            b = nbank()
                            proj_chunk(sap, skeys, 16, j, hT, MEMT, b)
                            evac_copy(alt(), mkT[:, l, j, :], ps[b][:, 0:MEMT], psk(b), [("mkT", l)], [b])
                    for g in range(2):
                        b = nbank()
                        tm_chunk(sap, skeys, 16, 0, 512, hT, g * 128, 128, b)
                        stg = tmst[g % 2]
                        evac_copy("act", stg[:, 0:512], ps[b][:], psk(b), [("F", 3 + g % 2)], [b])
                        if s == 1:
                            P.dve(lambda e, g=g, b=b, l=l: e.tensor_copy(out=mvv[:, l, g, :], in_=ps[b][:]),
                                  reads=psk(b), writes=[("mvv", l)], banks=[b])
                        dst = (o_mk_p if s == 0 else o_mv_p)[l, g * 128:(g + 1) * 128, :]
                        P.dma(lambda e, dst=dst, stg=stg: e.dma_start(out=dst, in_=stg[:, 0:512]),
                              reads=[("F", 3 + g % 2)], writes=[("omem", l, s, g)], eng="act")

        def pool_prompt(l, T, first):
            P.dve(lambda e: e.tensor_copy(out=u_ext.ap[:, :, 0:16], in_=carry_u[:, l, :, :]),
                  reads=["carry_u"], writes=u_ext.keys())
            L = 16 + T
            for gi in range(4):
                w = 2 << gi
                src = u_ext.ap[:, gi, :]
                srck = u_ext.keys(gi, gi + 1)
                cur, curk = src, srck
                for k in range(1, gi + 2):
                    sh = 1 << (k - 1)
                    lo = (1 << k) - 1
                    dst = ptmp[k % 2]
                    P.dve(lambda e, cur=cur, dst=dst, lo=lo, sh=sh: e.tensor_tensor(
                        out=dst[:, lo:L], in0=cur[:, lo:L], in1=cur[:, lo - sh:L - sh], op=ALU.add),
                        reads=curk, writes=[("F", 3 + k % 2)])
                    cur, curk = dst[:], [("F", 3 + k % 2)]
                d = dTp[gi % 2]
                dk = [("dTp", gi % 2)]
                P.dve(lambda e, cur=cur, d=d, src=src, w=w: e.scalar_tensor_tensor(
                    out=d[:, 0:T], in0=cur[:, 16:16 + T], scalar=1.0 / w, in1=src[:, 16:16 + T],
                    op0=ALU.mult, op1=ALU.subtract), reads=curk + srck, writes=dk)
                if first:
                    P.dve(lambda e, cur=cur, gi=gi: e.tensor_tensor(out=pfix[:, 0:16], in0=cur[:, 16:32], in1=invcnt[:, gi, :],
                                                                   op=ALU.mult), reads=curk + INVK, writes=[("F", 5)])
                    P.dve(lambda e, d=d, src=src: e.tensor_tensor(out=d[:, 0:16], in0=pfix[:, 0:16], in1=src[:, 16:32],
                                                                 op=ALU.subtract), reads=[("F", 5)] + srck, writes=dk)
                yield
                b = state.get("fb", 6)
                P.pe(lambda e, b=b, gi=gi, d=d: e.matmul(ps[b][:, 0:T], lhsT=wpool[:, l, gi, :], rhs=d[:, 0:T],
                                                        start=True, stop=True),
                     reads=dk + ["wpool"], writes=psk(b), banks=[b])
                P.act(lambda e, b=b, gi=gi: e.activation(out=yT.ap[:, gi, 0:T], in_=ps[b][:, 0:T], func=AF.Copy,
                                                         scale=pscale[:, l, gi:gi + 1]),
                      reads=psk(b) + ["pscale"], writes=yT.keys(gi, gi + 1), banks=[b])
            P.dve(lambda e: e.tensor_copy(out=carry_u[:, l, :, :], in_=u_ext.ap[:, :, T:T + 16]),
                  reads=u_ext.keys(), writes=["carry_u"])
            yield

        def conv_prompt(l, T):
            P.dve(lambda e: e.tensor_copy(out=v_ext.ap[:, :, 0:2], in_=carry_v[:, l, :, :]),
                  reads=["carry_v"], writes=v_ext.keys())
            for c in range(4):
                vk = v_ext.keys(c, c + 1)
                ca = cacc[c % 2]
                cak = [("F", 1 + c % 2)]
                P.dve(lambda e, c=c: e.tensor_tensor(out=v_ext.ap[:, c, 2:2 + T], in0=v_ext.ap[:, c, 2:2 + T],
                                                     in1=hcR.ap[:, c, 0:T], op=ALU.mult),
                      reads=vk + hcR.keys(c, c + 1), writes=vk)
                P.act(lambda e, c=c, ca=ca: e.activation(out=ca[:, 0:T], in_=v_ext.ap[:, c, 0:T], func=AF.Copy,
                                                         scale=convw[:, l, 0, c:c + 1]),
                      reads=vk + ["convw"], writes=cak)
                for kk in (1, 2):
                    P.dve(lambda e, c=c, ca=ca, kk=kk: e.scalar_tensor_tensor(
                        out=ca[:, 0:T], in0=v_ext.ap[:, c, kk:kk + T], scalar=convw[:, l, kk, c:c + 1], in1=ca[:, 0:T],
                        op0=ALU.mult, op1=ALU.add), reads=vk + ["convw"] + cak, writes=cak)
                P.dve(lambda e, c=c, ca=ca: e.tensor_tensor(out=yT.ap[:, 4 + c, 0:T], in0=ca[:, 0:T],
                                                            in1=gbR.ap[:, c, 0:T], op=ALU.mult),
                      reads=cak + gbR.keys(c, c + 1), writes=yT.keys(4 + c, 5 + c))
                if c < 3:
                    yield
            P.dve(lambda e: e.tensor_copy(out=carry_v[:, l, :, :], in_=v_ext.ap[:, :, T:T + 2]),
                  reads=v_ext.keys(), writes=["carry_v"])
            yield

        def swa_prompt(l, T, first):
            NG = T // 128
            P.dve(lambda e: e.tensor_copy(out=kext.ap[:, :, 0:128], in_=carry_k[:, l, :, :]),
                  reads=["carry_k"], writes=kext.keys())
            it = 0
            for n in range(NG):
                for h in range(2):
                    base_b = 0 if it % 2 == 0 else 4
                    state["fb"] = 4 - base_b
                    it += 1
                    bD, bO = base_b + 2, base_b + 3
                    msk, mskk = (mask_first, "mask_first") if (first and n == 0) else (mask_cat, "mask_cat")
                    pts = []
                    for par in range(2):
                        b = base_b + par
                        p0 = par * 64

                        def fnS(e, b=b, p0=p0, msk=msk, h=h, n=n):
                            e.matmul(ps[b][:, 0:512], lhsT=identb[:], rhs=msk[:], start=True, stop=False)
                            ins = None
                            for part in range(2):
                                kc0 = 128 + n * 128 if part == 0 else n * 128
                                ins = e.matmul(ps[b][:, part * 256:(part + 1) * 256].rearrange("p (g q) -> p g q", g=2),
                                               lhsT=kext.ap[p0:p0 + 64, h, kc0:kc0 + 128],
                                               rhs=qR.ap[p0:p0 + 64, 2 * h:2 * h + 2, n * 128:(n + 1) * 128],
                                               start=False, stop=(part == 1))
                            return ins
                        P.pe(fnS, reads=["identb", mskk] + kext.keys(h, h + 1) + qR.keys(2 * h, 2 * h + 2),
                             writes=psk(b), banks=[b])
                        pi = (it % 2) * 2 + par
                        pt = Pt[pi]
                        P.act(lambda e, b=b, pt=pt: e.activation(out=pt[:], in_=ps[b][:], func=AF.Exp, scale=SWA_SCALE),
                              reads=psk(b), writes=KB(pi), banks=[b])
                        pts.append((pt, KB(pi)))

                    def fnD(e, pts=pts, h=h, bD=bD):
                        ins = None
                        for par in range(2):
                            pt = pts[par][0]
                            o = ps[bD][:, par * 256:(par + 1) * 256]
                            e.matmul(o, lhsT=ones1[:], rhs=pt[:, 0:256], start=True, stop=False)
                            e.matmul(o, lhsT=ones1[:], rhs=pt[:, 256:512], start=False, stop=False)
                            ins = e.matmul(o, lhsT=ones1[0:1, :], rhs=esink[0:1, l, h, par * 256:(par + 1) * 256],
                                           start=False, stop=True)
                        return ins
                    P.pe(fnD, reads=["ones1"] + ESK(l, h) + pts[0][1] + pts[1][1], writes=psk(bD), banks=[bD])

                    def fnO(e, pts=pts, h=h, bO=bO, n=n):
                        ins = None
                        for par in range(2):
                            pt = pts[par][0]
                            o = ps[bO][:, par * 256:(par + 1) * 256]
                            e.matmul(o, lhsT=Vd[:, n + 1, h, :], rhs=pt[:, 0:256], start=True, stop=False)
                            ins = e.matmul(o, lhsT=Vd[:, n, h, :], rhs=pt[:, 256:512], start=False, stop=True)
                        return ins
                    P.pe(fnO, reads=["Vd"] + pts[0][1] + pts[1][1], writes=psk(bO), banks=[bO])
                    yield
                    ri = it % 2
                    rd = rden[ri]
                    rdk = RDK[ri]
                    P.dve(lambda e, rd=rd, bD=bD: e.reciprocal(out=rd[:, 0:512], in_=ps[bD][:]),
                          reads=psk(bD), writes=rdk, banks=[bD])
                    for par in range(2):
                        p0 = par * 64
                        P.dve(lambda e, p0=p0, par=par, rd=rd, h=h, n=n, bO=bO: e.tensor_tensor(
                            out=yT.ap[p0:p0 + 64, 8 + 2 * h:10 + 2 * h, n * 128:(n + 1) * 128],
                            in0=ps[bO][p0:p0 + 64, par * 256:(par + 1) * 256].rearrange("p (g q) -> p g q", g=2),
                            in1=rd[p0:p0 + 64, par * 256:(par + 1) * 256].rearrange("p (g q) -> p g q", g=2),
                            op=ALU.mult),
                            reads=psk(bO) + rdk, writes=yT.keys(8 + 2 * h, 10 + 2 * h), banks=[bO])
                    yield
            P.dve(lambda e: e.tensor_copy(out=carry_k[:, l, :, :], in_=kext.ap[:, :, T:T + 128]),
                  reads=kext.keys(), writes=["carry_k"])
            P.dve(lambda e: e.tensor_copy(out=carry_V[:, l, :, :], in_=Vd[:, NG, :, :]),
                  reads=["Vd"], writes=["carry_V"])

        def mem_prompt(l, T):
            for hd in range(4):
                base_b = 0 if hd % 2 == 0 else 4
                state["fb"] = 4 - base_b
                bD, bO = base_b + 2, base_b + 3
                pts = []
                for kb in range(2):
                    b = base_b + kb
                    P.pe(lambda e, b=b, kb=kb, hd=hd: e.matmul(ps[b][:, 0:T], lhsT=mkT[:, l, hd, kb * 128:(kb + 1) * 128],
                                                              rhs=qmR.ap[:, hd, 0:T], start=True, stop=True),
                         reads=[("mkT", l)] + qmR.keys(hd, hd + 1), writes=psk(b), banks=[b])
                    pt = Pt[(hd % 2) * 2 + kb]
                    ptk = [("Bt", (hd % 2) * 2 + kb)]
                    P.act(lambda e, b=b, pt=pt: e.activation(out=pt[:, 0:T], in_=ps[b][:, 0:T], func=AF.Exp, scale=MEM_SCALE),
                          reads=psk(b), writes=ptk, banks=[b])
                    pts.append((pt, ptk))

                def fnD(e, pts=pts, bD=bD):
                    e.matmul(ps[bD][:, 0:T], lhsT=ones1[:], rhs=pts[0][0][:, 0:T], start=True, stop=False)
                    return e.matmul(ps[bD][:, 0:T], lhsT=ones1[:], rhs=pts[1][0][:, 0:T], start=False, stop=True)
                P.pe(fnD, reads=["ones1"] + pts[0][1] + pts[1][1], writes=psk(bD), banks=[bD])

                def fnO(e, pts=pts, hd=hd, bO=bO):
                    e.matmul(ps[bO][:, 0:T], lhsT=mvv[:, l, 0, hd * 128:(hd + 1) * 128], rhs=pts[0][0][:, 0:T],
                             start=True, stop=False)
                    return e.matmul(ps[bO][:, 0:T], lhsT=mvv[:, l, 1, hd * 128:(hd + 1) * 128], rhs=pts[1][0][:, 0:T],
                                    start=False, stop=True)
                P.pe(fnO, reads=[("mvv", l)] + pts[0][1] + pts[1][1], writes=psk(bO), banks=[bO])
                yield
                rd = rden[hd % 2]
                rdk = RDK[hd % 2]
                P.dve(lambda e, rd=rd, bD=bD: e.reciprocal(out=rd[:, 0:T], in_=ps[bD][:, 0:T]),
                      reads=psk(bD), writes=rdk, banks=[bD])
                P.dve(lambda e, rd=rd, bO=bO, hd=hd: e.tensor_tensor(out=yT.ap[:, 12 + hd, 0:T], in0=ps[bO][:, 0:T],
                                                                     in1=rd[:, 0:T], op=ALU.mult),
                      reads=psk(bO) + rdk, writes=yT.keys(12 + hd, 13 + hd), banks=[bO])
                yield

        def pre1_chunk(l, T):
            def f(c):
                P.act(lambda e, c=c: e.activation(out=hT.ap[:, c, 0:T], in_=xT[:, c, 0:T], func=AF.Copy,
                                                  scale=gv[:, 0, l, c:c + 1]),
                      reads=[("xT", c), "gv"], writes=hT.keys(c, c + 1))
            return f

        def layer_prompt(l, ti):
            T = TP
            first = (ti == 0)
            last = (ti == NPT - 1)
            if l == 0:
                for c in range(16):
                    pre1_chunk(0, T)(c)
            P.dve(lambda e: e.tensor_copy(out=Vd[:, 0, :, :], in_=carry_V[:, l, :, :]), reads=["carry_V"], writes=["Vd"])
            RK = [("F", 0)]

            def evac_scaled(out_ap, b, wkeys):
                P.dve(lambda e: e.tensor_tensor(out=out_ap, in0=ps[b][:, 0:T], in1=rstd[:, 0:T], op=ALU.mult),
                      reads=psk(b) + RK, writes=wkeys, banks=[b])

            def dest(m):
                if m < 4:
                    return u_ext.ap[:, m, 16:16 + T], u_ext.keys(m, m + 1)
                if m < 8:
                    return hcR.ap[:, m - 4, 0:T], hcR.keys(m - 4, m - 3)
                if m < 12:
                    return gbR.ap[:, m - 8, 0:T], gbR.keys(m - 8, m - 7)
                if m < 16:
                    return v_ext.ap[:, m - 12, 2:2 + T], v_ext.keys(m - 12, m - 11)
                if m < 20:
                    return qR.ap[:, m - 16, 0:T], qR.keys(m - 16, m - 15)
                if m < 22:
                    return kext.ap[:, m - 20, 128:128 + T], kext.keys(m - 20, m - 19)
                return qmR.ap[:, m - 22, 0:T], qmR.keys(m - 22, m - 21)
            for s in range(WIN_NSLAB):
                sap, skeys = w_next(("in", l, s), 16)
                pend = []
                for j in range(4):
                    m = s * 4 + j
                    if m >= 26:
                        continue
                    b = nbank()
                    proj_chunk(sap, skeys, 16, j, hT, T, b, fine=(m == 0))
                    if s == 0:
                        pend.append((m, b))
                    else:
                        o_, k_ = dest(m)
                        evac_scaled(o_, b, k_)
                if last and s == 0:
                    tm_chunk(sap, skeys, 16, 0, 512, hT, T - 16, 16, 6)
                if s == 0:
                    a_, k_ = xT_ap(T)
                    norm_stats(a_, k_, T, mode="rstd")
                    bq = nbank()

                    def fnr(e, bq=bq):
                        ins = None
                        for g in range(T // 128):
                            ins = e.matmul(ps[bq][:, g:g + 1], lhsT=rstd[:, g * 128:(g + 1) * 128], rhs=ident[:, 0:1],
                                           start=True, stop=True)
                        if last:
                            ins = e.matmul(ps[bq][0:16, 4:5], lhsT=rstd[:, T - 16:T], rhs=ident[:, 0:1], start=True, stop=True)
                        return ins
                    P.pe(fnr, reads=RK + ["ident"], writes=psk(bq), banks=[bq])
                    P.act(lambda e, bq=bq: e.activation(out=rtm[:, 0:8], in_=ps[bq][:, 0:8], func=AF.Copy),
                          reads=psk(bq), writes=["rtm"], banks=[bq])
                    for (m, b) in pend:
                        o_, k_ = dest(m)
                        evac_scaled(o_, b, k_)
                if last and s == 0:
                    b = 6
                    P.act(lambda e, b=b: e.activation(out=tmst[0][0:16, 0:512], in_=ps[b][0:16, :], func=AF.Copy,
                                                      scale=rtm[0:16, 4:5]),
                          reads=psk(b) + ["rtm"], writes=[("F", 3)], banks=[b])
                    P.dma(lambda e: e.dma_start(out=o_pool_p[l], in_=tmst[0][1:16, 0:512]), reads=[("F", 3)],
                          writes=[("o_pool_p", l)], eng="act")
                if last and s == 1:
                    b = 6
                    tm_chunk(sap, skeys, 16, 0, 512, hT, T - 16, 16, b)
                    P.act(lambda e, b=b: e.activation(out=hcst[0:16, 0:512], in_=ps[b][0:16, :], func=AF.Copy,
                                                      scale=rtm[0:16, 4:5]),
                          reads=psk(b) + ["rtm"], writes=[("F", 5)], banks=[b])
                if last and s == 3:
                    b = 6
                    tm_chunk(sap, skeys, 16, 0, 512, hT, T - 16, 16, b)
                    P.dve(lambda e, b=b: e.scalar_tensor_tensor(out=tmst[1][0:16, 0:512], in0=ps[b][0:16, :],
                                                                scalar=rtm[0:16, 4:5], in1=hcst[0:16, 0:512],
                                                                op0=ALU.mult, op1=ALU.mult),
                          reads=psk(b) + [("F", 5), "rtm"], writes=[("F", 4)], banks=[b])
                    P.dma(lambda e: e.dma_start(out=o_conv_p[l], in_=tmst[1][14:16, 0:512]), reads=[("F", 4)],
                          writes=[("o_conv_p", l)], eng="act")
                if s == 6:
                    for g in range(T // 128):
                        b = 6
                        tm_chunk(sap, skeys, 16, 256, 256, hT, g * 128, 128, b)
                        for h in range(2):
                            P.dve(lambda e, g=g, h=h, b=b: e.tensor_scalar(
                                out=Vd[:, g + 1, h, :].rearrange("p (r d) -> p r d", r=2),
                                in0=ps[b][:, 128 + h * 64:128 + (h + 1) * 64].unsqueeze(1).to_broadcast([128, 2, 64]),
                                scalar1=rtm[:, g:g + 1], scalar2=None, op0=ALU.mult),
                                reads=psk(b) + ["rtm"], writes=["Vd"], banks=[b])
                        if last and g == T // 128 - 1:
                            P.act(lambda e, b=b, g=g: e.activation(out=tmst[0][:, 0:256], in_=ps[b][:, 0:256], func=AF.Copy,
                                                                   scale=rtm[:, g:g + 1]),
                                  reads=psk(b) + ["rtm"], writes=[("F", 3)], banks=[b])
                            P.dma(lambda e: e.dma_start(out=o_k_p[l], in_=tmst[0][:, 0:128]), reads=[("F", 3)],
                                  writes=[("o_k_p", l)], eng="act")
                            P.dma(lambda e: e.dma_start(out=o_v_p[l], in_=tmst[0][:, 128:256]), reads=[("F", 3)],
                                  writes=[("o_v_p", l)], eng="act")
            if DBG["mixers"]:
                import itertools
                A = itertools.chain(swa_prompt(l, T, first), mem_prompt(l, T))
                B = itertools.chain(pool_prompt(l, T, first), conv_prompt(l, T))
                a_alive, b_alive = True, True
                while a_alive or b_alive:
                    if a_alive:
                        a_alive = next(A, "end") != "end"
                    if b_alive:
                        b_alive = next(B, "end") != "end"
                    if a_alive:
                        a_alive = next(A, "end") != "end"
            if DBG["ffn"]:
                ffn_and_out(RP, l, T, post2_hook=(pre1_chunk(l + 1, T) if l + 1 < DBG["nlayers"] else None))

        def ffn_and_out(R, l, T, post2_hook=None):
            for s in range(4):
                sap, skeys = w_next(("out", l, s), 16)
                for j in range(4):
                    m = s * 4 + j
                    b = nbank()
                    proj_chunk(sap, skeys, 16, j, R.yT, T, b)
                    evac_copy(alt(), R.mixT.ap[:, m, 0:T], ps[b][:, 0:T], psk(b), R.mixT.keys(m, m + 1), [b])
            def h2_chunk(c):
                P.act(lambda e, c=c: e.activation(out=R.hT.ap[:, c, 0:T], in_=xT[:, c, 0:T], func=AF.Copy,
                                                  scale=gv[:, 3, l, c:c + 1]),
                      reads=[("xT", c), "gv"], writes=R.hT.keys(c, c + 1))
            postnorm_residual(R, 2, l, T, after_chunk=h2_chunk)
            for s in range(16):
                sap, skeys = w_next(("up", l, s), 16)
                if s == 1:
                    a_, k_ = xT_ap(T)
                    norm_stats(a_, k_, T, mode="epsq")
                for j in range(4):
                    m = s * 4 + j
                    b = nbank()
                    proj_chunk(sap, skeys, 16, j, R.hT, T, b, fine=(m == 0))
                    rt = rtmp[m % 2]
                    rk = [("F", 1 + m % 2)]
                    P.act(lambda e, b=b, rt=rt: e.activation(out=rt[:, 0:T], in_=ps[b][:, 0:T], func=AF.Relu),
                          reads=psk(b), writes=rk, banks=[b])
                    P.dve(lambda e, m=m, rt=rt: e.tensor_tensor(out=R.hidT.ap[:, m, 0:T], in0=rt[:, 0:T], in1=rt[:, 0:T],
                                                                op=ALU.mult),
                          reads=rk, writes=R.hidT.keys(m, m + 1))
            for G in range(4):
                base = 0 if G % 2 == 0 else 4
                for q in range(4):
                    sap, skeys = w_next(("down", l, G * 4 + q), 16)
                    for j in range(4):
                        b = base + j
                        if G == 0 and q == 0 and j == 0:
                            for kc in range(16):
                                P.pe(lambda e, kc=kc, sap=sap, b=b: e.matmul(
                                    ps[b][:, 0:T], lhsT=sap[:, kc, 0:128], rhs=R.hidT.ap[:, kc, 0:T],
                                    start=(kc == 0), stop=False),
                                    reads=skeys + R.hidT.keys(kc, kc + 1), writes=psk(b), banks=[b])
                            continue

                        def fn(e, sap=sap, q=q, j=j, b=b):
                            ins = None
                            for kc in range(16):
                                ins = e.matmul(ps[b][:, 0:T], lhsT=sap[:, kc, j * 128:(j + 1) * 128],
                                               rhs=R.hidT.ap[:, q * 16 + kc, 0:T],
                                               start=(q == 0 and kc == 0), stop=(q == 3 and kc == 15))
                            return ins
                        P.pe(fn, reads=skeys + R.hidT.keys(q * 16, q * 16 + 16), writes=psk(b), banks=[b])
                for j in range(4):
                    b = base + j
                    m = G * 4 + j
                    evac_copy(alt(), R.mixT.ap[:, m, 0:T], ps[b][:, 0:T], psk(b), R.mixT.keys(m, m + 1), [b])
            postnorm_residual(R, 4, l, T, mode="rstd_q", after_chunk=post2_hook)

        psb = [p_[:].bitcast(BF16) for p_ in ps]
        s_hidT = Region(BIG, "BIG", 0, 64, 64, BF16)
        s_yT = Region(BIG, "BIG", 8192, 16, 64, BF16)
        s_q = Region(BIG, "BIG", 10240, 4, 64, BF16)
        s_kx = Region(BIG, "BIG", 10752, 2, 64, BF16)
        s_qm = Region(BIG, "BIG", 11008, 4, 64, BF16)
        s_hc = Region(BIG, "BIG", 11520, 4, 64, F32)
        s_gb = Region(BIG, "BIG", 12544, 4, 64, F32)
        s_vx = Region(BIG, "BIG", 13568, 4, 96, F32)
        s_ux = Region(BIG, "BIG", 15104, 4, 304, F32)
        pst = Region(BIG, "BIG", 20480, 2, 512, F32)
        cst = Region(BIG, "BIG", 24576, 1, 512, F32)
        Kd = Region(BIG, "BIG", 26624, 16, 256, BF16)
        KTd = Region(BIG, "BIG", 34816, 32, 128, BF16)
        Vds = Region(BIG, "BIG", 43008, 16, 256, BF16)
        mc = [dict(K=Region(BIG, "BIG", 51200, 4, 512, BF16), KT=Region(BIG, "BIG", 55296, 8, 256, BF16),
                   V=Region(BIG, "BIG", 59392, 4, 512, BF16)),
              dict(K=Region(U1, "U1", 8192, 4, 512, BF16), KT=Region(U1, "U1", 12288, 8, 256, BF16),
                   V=Region(U1, "U1", 16384, 4, 512, BF16))]
        Vn = Region(U1, "U1", 20480, 16, 256, BF16)
        s_hT = Region(U1, "U1", 0, 16, 64, BF16)
        s_mixT = Region(U1, "U1", 2048, 16, 64, F32)
        RS = SimpleNamespace(hT=s_hT, mixT=s_mixT, yT=s_yT, hidT=s_hidT)

        def bt4(ap):
            return ap.rearrange("p (b t) -> p b t", t=4)

        def layer_sample(l):
            T = TS
            prenorm_to_hT(RS, 0, l, T)
            sp2 = spool[l].rearrange("b r f -> (b r) f")
            P.dma(lambda e: e.dma_start(out=pst.ap[0:128, 0, :], in_=sp2[0:128, :]), writes=pst.keys(0, 1))
            P.dma(lambda e: e.dma_start(out=pst.ap[0:112, 1, :], in_=sp2[128:240, :]), writes=pst.keys(1, 2))
            P.dma(lambda e: e.dma_start(out=cst.ap[0:32, 0, :], in_=sconv[l].rearrange("b r f -> (b r) f")), writes=cst.keys())
            Kd5 = Kd.ap.rearrange("p b (h r d) -> p b h r d", h=2, r=2)
            Vd5 = Vds.ap.rearrange("p b (h r d) -> p b h r d", h=2, r=2)
            for r in range(2):
                for h in range(2):
                    P.dma(lambda e, r=r, h=h: e.dma_start(out=Kd5[:, :, h, r, :],
                                                          in_=ck[l][:, :, h * 64:(h + 1) * 64].rearrange("b k d -> k b d")),
                          writes=Kd.keys(), eng="pool")
                    P.dma(lambda e, r=r, h=h: e.dma_start(out=Vd5[:, :, h, r, :],
                                                          in_=cv[l][:, :, h * 64:(h + 1) * 64].rearrange("b k d -> k b d")),
                          writes=Vds.keys(), eng="pool")
            P.dma(lambda e: e.dma_start(out=o_pool_s[l][:, 0:11, :], in_=spool[l][:, 4:15, :]), writes=[("o_pool_s", l, 0)], eng="pool")
            P.dma(lambda e: e.dma_start(out=o_k_s[l][:, 0:124, :], in_=ck[l][:, 4:128, :]), writes=[("o_k_s", l, 0)], eng="pool")
            P.dma(lambda e: e.dma_start(out=o_v_s[l][:, 0:124, :], in_=cv[l][:, 4:128, :]), writes=[("o_v_s", l, 0)], eng="pool")
            for gi in range(4):
                b = nbank(0, 4)

                def fn(e, gi=gi, b=b):
                    e.transpose(out=ps[b][:, 0:128], in_=pst.ap[0:128, 0, gi * 128:(gi + 1) * 128], identity=ident[:])
                    return e.transpose(out=ps[b][:, 128:240], in_=pst.ap[0:112, 1, gi * 128:(gi + 1) * 128],
                                       identity=ident[0:112, 0:112])
                P.pe(fn, reads=pst.keys() + ["ident"], writes=psk(b), banks=[b])
                u3 = s_ux.ap[:, gi, :].rearrange("p (b r) -> p b r", r=19)
                evac_copy(alt(), u3[:, :, 0:15], ps[b][:, 0:240].rearrange("p (b r) -> p b r", r=15), psk(b),
                          s_ux.keys(gi, gi + 1), [b])
            for c in range(4):
                b = nbank(0, 4)
                P.pe(lambda e, c=c, b=b: e.transpose(out=ps[b][:, 0:32], in_=cst.ap[0:32, 0, c * 128:(c + 1) * 128],
                                                     identity=ident[0:32, 0:32]),
                     reads=cst.keys() + ["ident"], writes=psk(b), banks=[b])
                v3 = s_vx.ap[:, c, :].rearrange("p (b r) -> p b r", r=6)
                evac_copy(alt(), v3[:, :, 0:2], ps[b][:, 0:32].rearrange("p (b r) -> p b r", r=2), psk(b),
                          s_vx.keys(c, c + 1), [b])
            for q4 in range(4):
                b = 4 + q4

                def fnk(e, q4=q4, b=b):
                    ins = None
                    for i in range(8):
                        idx = q4 * 8 + i
                        bb, h = idx // 2, idx % 2
                        ins = e.transpose(out=psb[b][:, i * 128:(i + 1) * 128], in_=Kd.ap[:, bb, h * 128:(h + 1) * 128],
                                          identity=identb[:])
                    return ins
                P.pe(fnk, reads=Kd.keys() + ["identb"], writes=psk(b), banks=[b])
                evac_copy(alt(), KTd.ap[:, q4 * 8:(q4 + 1) * 8, :], psb[b][:].rearrange("p (i k) -> p i k", i=8), psk(b),
                          KTd.keys(q4 * 8, q4 * 8 + 8), [b])
            for s in range(WIN_NSLAB):
                sap, skeys = w_next(("in", l, s), 16)
                for j in range(4):
                    m = s * 4 + j
                    if m >= 26:
                        continue
                    b = nbank(0, 4)
                    proj_chunk(sap, skeys, 16, j, s_hT, T, b, fine=(m == 0))
                    eng = alt()
                    src = ps[b][:, 0:T]
                    if m < 4:
                        u3 = s_ux.ap[:, m, :].rearrange("p (b r) -> p b r", r=19)
                        evac_copy(eng, u3[:, :, 15:19], bt4(src), psk(b), s_ux.keys(m, m + 1), [b])
                    elif m < 8:
                        evac_copy(eng, s_hc.ap[:, m - 4, :], src, psk(b), s_hc.keys(m - 4, m - 3), [b])
                    elif m < 12:
                        evac_copy(eng, s_gb.ap[:, m - 8, :], src, psk(b), s_gb.keys(m - 8, m - 7), [b])
                    elif m < 16:
                        v3 = s_vx.ap[:, m - 12, :].rearrange("p (b r) -> p b r", r=6)
                        evac_copy(eng, v3[:, :, 2:6], bt4(src), psk(b), s_vx.keys(m - 12, m - 11), [b])
                    elif m < 20:
                        evac_copy(eng, s_q.ap[:, m - 16, :], src, psk(b), s_q.keys(m - 16, m - 15), [b])
                    elif m < 22:
                        evac_copy(eng, s_kx.ap[:, m - 20, :], src, psk(b), s_kx.keys(m - 20, m - 19), [b])
                    else:
                        evac_copy(eng, s_qm.ap[:, m - 22, :], src, psk(b), s_qm.keys(m - 22, m - 21), [b])
                if s == 0:
                    b = 6
                    tm_chunk(sap, skeys, 16, 0, 512, s_hT, 0, 64, b)
                    evac_copy("act", tmst[0][0:64, 0:512], ps[b][0:64, :], psk(b), KF(3), [b])
                    P.dma(lambda e: e.dma_start(out=o_pool_s[l][:, 11:15, :], in_=tmst[0][0:64, 0:512]), reads=KF(3),
                          writes=[("o_pool_s", l, 1)], eng="pool")
                if s == 1:
                    b = 6
                    tm_chunk(sap, skeys, 16, 0, 512, s_hT, 0, 64, b)
                    evac_copy("act", hcst[0:64, 0:512], ps[b][0:64, :], psk(b), KF(5), [b])
                if s == 3:
                    b = 6
                    tm_chunk(sap, skeys, 16, 0, 512, s_hT, 0, 64, b)
                    P.dve(lambda e, b=b: e.tensor_tensor(out=tmst[1][0:64, 0:512], in0=ps[b][0:64, :], in1=hcst[0:64, 0:512],
                                                         op=ALU.mult), reads=psk(b) + KF(5), writes=KF(4), banks=[b])
                    for t in (2, 3):
                        for bb in range(SB):
                            P.dma(lambda e, t=t, bb=bb: e.dma_start(out=o_conv_s[l][bb, t - 2:t - 1, :],
                                                                    in_=tmst[1][bb * 4 + t:bb * 4 + t + 1, 0:512]),
                                  reads=KF(4), writes=[("o_conv_s", l, t, bb)], eng="pool")
                if s == 6:
                    b = 6
                    tm_chunk(sap, skeys, 16, 256, 256, s_hT, 0, 64, b)
                    evac_copy("act", tmst[0][0:64, 0:256], ps[b][0:64, 0:256], psk(b), KF(3), [b])
                    P.dma(lambda e: e.dma_start(out=o_k_s[l][:, 124:128, :], in_=tmst[0][0:64, 0:128]), reads=KF(3),
                          writes=[("o_k_s", l, 1)], eng="pool")
                    P.dma(lambda e: e.dma_start(out=o_v_s[l][:, 124:128, :], in_=tmst[0][0:64, 128:256]), reads=KF(3),
                          writes=[("o_v_s", l, 1)], eng="pool")
                    for q4 in range(4):
                        bk = nbank(0, 4)

                        def fnv(e, q4=q4, bk=bk, sap=sap):
                            ins = None
                            for i in range(4):
                                bb = q4 * 4 + i
                                for kc in range(16):
                                    ins = e.matmul(ps[bk][0:4, i * 128:(i + 1) * 128], lhsT=s_hT.ap[:, kc, bb * 4:bb * 4 + 4],
                                                   rhs=sap[:, kc, 384:512], start=(kc == 0), stop=(kc == 15))
                            return ins
                        P.pe(fnv, reads=skeys + s_hT.keys(), writes=psk(bk), banks=[bk])
                        for h in range(2):
                            P.dve(lambda e, q4=q4, bk=bk, h=h: e.tensor_copy(
                                out=Vn.ap[0:4, q4 * 4:(q4 + 1) * 4, h * 128:(h + 1) * 128].rearrange("p b (r d) -> p b r d", r=2),
                                in_=ps[bk][0:4, :].rearrange("p (i h d) -> p i h d", i=4, h=2)[:, :, h, :]
                                .unsqueeze(2).to_broadcast([4, 4, 2, 64])),
                                reads=psk(bk), writes=Vn.keys(q4 * 4, q4 * 4 + 4), banks=[bk])
            for gi in range(4):
                w = 2 << gi
                u3 = s_ux.ap[:, gi, :].rearrange("p (b r) -> p b r", r=19)
                uk = s_ux.keys(gi, gi + 1)
                cur, curk = u3, uk
                for k in range(1, gi + 2):
                    sh = 1 << (k - 1)
                    lo = (1 << k) - 1
                    dst = ptmp[k % 2][:, 0:304].rearrange("p (b r) -> p b r", r=19)
                    P.dve(lambda e, cur=cur, dst=dst, lo=lo, sh=sh: e.tensor_tensor(
                        out=dst[:, :, lo:19], in0=cur[:, :, lo:19], in1=cur[:, :, lo - sh:19 - sh], op=ALU.add),
                        reads=curk, writes=KF(3 + k % 2))
                    cur, curk = dst, KF(3 + k % 2)
                d = dT[gi % 2]
                P.dve(lambda e, cur=cur, d=d, u3=u3, w=w: e.scalar_tensor_tensor(
                    out=bt4(d[:, 0:64]), in0=cur[:, :, 15:19], scalar=1.0 / w, in1=u3[:, :, 15:19],
                    op0=ALU.mult, op1=ALU.subtract), reads=curk + uk, writes=KB(gi % 2))
                b = nbank(0, 4)
                P.pe(lambda e, b=b, gi=gi, d=d: e.matmul(ps[b][:, 0:64], lhsT=wpool[:, l, gi, :], rhs=d[:, 0:64],
                                                        start=True, stop=True),
                     reads=KB(gi % 2) + ["wpool"], writes=psk(b), banks=[b])
                P.act(lambda e, b=b, gi=gi: e.activation(out=s_yT.ap[:, gi, :], in_=ps[b][:, 0:64], func=AF.Copy,
                                                         scale=pscale[:, l, gi:gi + 1]),
                      reads=psk(b) + ["pscale"], writes=s_yT.keys(gi, gi + 1), banks=[b])
            for c in range(4):
                v3 = s_vx.ap[:, c, :].rearrange("p (b r) -> p b r", r=6)
                vk = s_vx.keys(c, c + 1)
                ca = bt4(cacc[c % 2][:, 0:64])
                cak = KF(1 + c % 2)
                P.dve(lambda e, v3=v3, c=c: e.tensor_tensor(out=v3[:, :, 2:6], in0=v3[:, :, 2:6], in1=bt4(s_hc.ap[:, c, :]),
                                                           op=ALU.mult), reads=vk + s_hc.keys(c, c + 1), writes=vk)
                P.act(lambda e, v3=v3, c=c, ca=ca: e.activation(out=ca, in_=v3[:, :, 0:4], func=AF.Copy,
                                                               scale=convw[:, l, 0, c:c + 1]),
                      reads=vk + ["convw"], writes=cak)
                for kk in (1, 2):
                    P.dve(lambda e, v3=v3, c=c, ca=ca, kk=kk: e.scalar_tensor_tensor(
                        out=ca, in0=v3[:, :, kk:kk + 4], scalar=convw[:, l, kk, c:c + 1], in1=ca,
                        op0=ALU.mult, op1=ALU.add), reads=vk + ["convw"] + cak, writes=cak)
                P.dve(lambda e, c=c, ca=ca: e.tensor_tensor(out=bt4(s_yT.ap[:, 4 + c, :]), in0=ca, in1=bt4(s_gb.ap[:, c, :]),
                                                            op=ALU.mult),
                      reads=cak + s_gb.keys(c, c + 1), writes=s_yT.keys(4 + c, 5 + c))
            bA, bC, bE, bF = [0, 1], [2, 3], 4, 5
            for par in range(2):
                p0 = par * 64

                def fnS(e, par=par, p0=p0):
                    e.matmul(ps[bA[par]][:, 0:256], lhsT=identb[:], rhs=mask_sc[:], start=True, stop=False)
                    ins = None
                    for bb in range(SB):
                        for h in range(2):
                            c0 = h * 128 + bb * 8
                            ins = e.matmul(ps[bA[par]][:, c0:c0 + 8].rearrange("p (g t) -> p g t", g=2),
                                           lhsT=KTd.ap[p0:p0 + 64, bb * 2 + h, :],
                                           rhs=s_q.ap[p0:p0 + 64, 2 * h:2 * h + 2, bb * 4:bb * 4 + 4],
                                           start=False, stop=(bb == SB - 1 and h == 1))
                    return ins
                P.pe(fnS, reads=["identb", "mask_sc"] + KTd.keys() + s_q.keys(), writes=psk(bA[par]), banks=[bA[par]])

                def fnN(e, par=par, p0=p0):
                    e.matmul(ps[bC[par]][0:4, 0:256], lhsT=identb[0:4, 0:4], rhs=mask_sn[0:4, :], start=True, stop=False)
                    ins = None
                    for bb in range(SB):
                        for h in range(2):
                            c0 = h * 128 + bb * 8
                            ins = e.matmul(ps[bC[par]][0:4, c0:c0 + 8].rearrange("p (g t) -> p g t", g=2),
                                           lhsT=s_kx.ap[p0:p0 + 64, h, bb * 4:bb * 4 + 4],
                                           rhs=s_q.ap[p0:p0 + 64, 2 * h:2 * h + 2, bb * 4:bb * 4 + 4],
                                           start=False, stop=(bb == SB - 1 and h == 1))
                    return ins
                P.pe(fnN, reads=["identb", "mask_sn"] + s_kx.keys() + s_q.keys(), writes=psk(bC[par]), banks=[bC[par]])
                P.act(lambda e, par=par: e.activation(out=Pt[par][:, 0:256], in_=ps[bA[par]][:, 0:256], func=AF.Exp,
                                                      scale=SWA_SCALE), reads=psk(bA[par]), writes=KB(par), banks=[bA[par]])
                P.act(lambda e, par=par: e.activation(out=Pt[2 + par][0:4, 0:256], in_=ps[bC[par]][0:4, 0:256], func=AF.Exp,
                                                      scale=SWA_SCALE), reads=psk(bC[par]), writes=KB(2 + par), banks=[bC[par]])

            def fnDs(e):
                ins = None
                for par in range(2):
                    o = ps[bE][:, par * 256:(par + 1) * 256]
                    e.matmul(o, lhsT=ones1[:], rhs=Pt[par][:, 0:256], start=True, stop=False)
                    e.matmul(o, lhsT=ones1[0:4, :], rhs=Pt[2 + par][0:4, 0:256], start=False, stop=False)
                    ins = e.matmul(o, lhsT=ones1[0:1, :], rhs=esink_s[0:1, l, par, :], start=False, stop=True)
                return ins
            P.pe(fnDs, reads=["ones1", ("esink_s", l)] + KB(0) + KB(1) + KB(2) + KB(3), writes=psk(bE), banks=[bE])

            def fnOs(e):
                ins = None
                for par in range(2):
                    for bb in range(SB):
                        for h in range(2):
                            cc = h * 128 + bb * 8
                            c0 = par * 256 + cc
                            e.matmul(ps[bF][:, c0:c0 + 8], lhsT=Vds.ap[:, bb, h * 128:(h + 1) * 128], rhs=Pt[par][:, cc:cc + 8],
                                     start=True, stop=False)
                            ins = e.matmul(ps[bF][:, c0:c0 + 8], lhsT=Vn.ap[0:4, bb, h * 128:(h + 1) * 128],
                                           rhs=Pt[2 + par][0:4, cc:cc + 8], start=False, stop=True)
                return ins
            P.pe(fnOs, reads=Vds.keys() + Vn.keys() + KB(0) + KB(1) + KB(2) + KB(3), writes=psk(bF), banks=[bF])
            P.dve(lambda e: e.reciprocal(out=rden[0][:, 0:512], in_=ps[bE][:]), reads=psk(bE), writes=RDK[0], banks=[bE])
            for par in range(2):
                for h in range(2):
                    p0 = par * 64
                    c0 = par * 256 + h * 128
                    P.dve(lambda e, p0=p0, c0=c0, h=h: e.tensor_tensor(
                        out=s_yT.ap[p0:p0 + 64, 8 + 2 * h:10 + 2 * h, :].rearrange("p g (b t) -> p b g t", t=4),
                        in0=ps[bF][p0:p0 + 64, c0:c0 + 128].rearrange("p (b g t) -> p b g t", b=SB, g=2),
                        in1=rden[0][p0:p0 + 64, c0:c0 + 128].rearrange("p (b g t) -> p b g t", b=SB, g=2),
                        op=ALU.mult), reads=psk(bF) + RDK[0], writes=s_yT.keys(8 + 2 * h, 10 + 2 * h), banks=[bF])
            bS, bDn, bOm = 2, 3, 6
            Pm = Bt[0]
            for g in range(SB // 2):
                M_ = mc[g % 2]
                b0 = 2 * g
                P.dma(lambda e, M_=M_, b0=b0: e.dma_start(out=M_["K"].ap.rearrange("p (b k) f -> p b k f", b=2),
                                                          in_=cmk[l][b0:b0 + 2].rearrange("b (k p) f -> p b k f", p=128)),
                      writes=M_["K"].keys(), eng="pool")
                P.dma(lambda e, M_=M_, b0=b0: e.dma_start(out=M_["V"].ap.rearrange("p (b k) f -> p b k f", b=2),
                                                          in_=cmv[l][b0:b0 + 2].rearrange("b (k p) f -> p b k f", p=128)),
                      writes=M_["V"].keys(), eng="pool")
                for b2 in range(2):
                    bk = b2

                    def fnT(e, M_=M_, b2=b2, bk=bk):
                        ins = None
                        for hd in range(4):
                            for blk in range(2):
                                i = hd * 2 + blk
                                ins = e.transpose(out=psb[bk][:, i * 128:(i + 1) * 128],
                                                  in_=M_["K"].ap[:, b2 * 2 + blk, hd * 128:(hd + 1) * 128], identity=identb[:])
                        return ins
                    P.pe(fnT, reads=M_["K"].keys() + ["identb"], writes=psk(bk), banks=[bk])
                    evac_copy(alt(), M_["KT"].ap[:, b2 * 4:(b2 + 1) * 4, :], psb[bk][:].rearrange("p (h k) -> p h k", h=4),
                              psk(bk), M_["KT"].keys(b2 * 4, b2 * 4 + 4), [bk])

                def fnSm(e, M_=M_, b0=b0):
                    ins = None
                    for b2 in range(2):
                        bb = b0 + b2
                        for blk in range(2):
                            for hd in range(4):
                                c0 = bb * 32 + blk * 16 + hd * 4
                                ins = e.matmul(ps[bS][:, c0:c0 + 4], lhsT=M_["KT"].ap[:, b2 * 4 + hd, blk * 128:(blk + 1) * 128],
                                               rhs=s_qm.ap[:, hd, bb * 4:bb * 4 + 4], start=True, stop=True)
                    return ins
                P.pe(fnSm, reads=M_["KT"].keys() + s_qm.keys(), writes=psk(bS), banks=[bS])
                P.act(lambda e, b0=b0: e.activation(out=Pm[:, b0 * 32:b0 * 32 + 64], in_=ps[bS][:, b0 * 32:b0 * 32 + 64],
                                                    func=AF.Exp, scale=MEM_SCALE), reads=psk(bS), writes=KB(0), banks=[bS])

                def fnDm(e, b0=b0):
                    v = Pm[:, b0 * 32:b0 * 32 + 64].rearrange("p (b k x) -> p b k x", b=2, k=2)
                    o = ps[bDn][:, b0 * 16:b0 * 16 + 32].rearrange("p (b x) -> p b x", b=2)
                    e.matmul(o, lhsT=ones1[:], rhs=v[:, :, 0, :], start=True, stop=False)
                    return e.matmul(o, lhsT=ones1[:], rhs=v[:, :, 1, :], start=False, stop=True)
                P.pe(fnDm, reads=["ones1"] + KB(0), writes=psk(bDn), banks=[bDn])

                def fnOm(e, M_=M_, b0=b0):
                    ins = None
                    for b2 in range(2):
                        bb = b0 + b2
                        for hd in range(4):
                            oc = bb * 16 + hd * 4
                            for blk in range(2):
                                c0 = bb * 32 + blk * 16 + hd * 4
                                ins = e.matmul(ps[bOm][:, oc:oc + 4], lhsT=M_["V"].ap[:, b2 * 2 + blk, hd * 128:(hd + 1) * 128],
                                               rhs=Pm[:, c0:c0 + 4], start=(blk == 0), stop=(blk == 1))
                    return ins
                P.pe(fnOm, reads=M_["V"].keys() + KB(0), writes=psk(bOm), banks=[bOm])
            P.dve(lambda e: e.reciprocal(out=rden[1][:, 0:256], in_=ps[bDn][:, 0:256]), reads=psk(bDn), writes=RDK[1], banks=[bDn])
            P.dve(lambda e: e.tensor_tensor(
                out=s_yT.ap[:, 12:16, :].rearrange("p h (b t) -> p b h t", t=4),
                in0=ps[bOm][:, 0:256].rearrange("p (b h t) -> p b h t", b=SB, h=4),
                in1=rden[1][:, 0:256].rearrange("p (b h t) -> p b h t", b=SB, h=4), op=ALU.mult),
                reads=psk(bOm) + RDK[1], writes=s_yT.keys(12, 16), banks=[bOm])
            ffn_and_out(RS, l, T)

        if DBG["mem"]:
            mem_phase()
        for ti in range(DBG["ntiles"]):
            state["use_pool"] = ti > 0
            load_xT(xp[ti * TP:(ti + 1) * TP, :], TP)
            for l in range(DBG["nlayers"]):
                layer_prompt(l, ti)
            store_xT(yp[ti * TP:(ti + 1) * TP, :], TP)
        if with_sample:
            load_xT(xs, TS)
            for l in range(DBG["nlayers"]):
                layer_sample(l)
            store_xT(ys, TS)
        assert wst["next"] == len(seq), (wst["next"], len(seq))
        P.emit()
    return nc


_CACHE = {}


def kernel(**inputs):
    f = lambda a: np.ascontiguousarray(np.asarray(a, dtype=np.float32))
    inp = {k: f(v) for k, v in inputs.items()}
    with_sample = True
    if "nc" not in _CACHE:
        _CACHE["nc"] = build_program(with_sample)
    nc = _CACHE["nc"]
    shared = {k: inp[k] for k in ("g_mix_pre", "w_in", "w_pool", "pool_scale", "conv_w", "swa_sinks", "g_mem",
                                  "w_mem_kv", "w_out", "g_mix_post", "g_mlp_pre", "w_up", "w_down", "g_mlp_post")}
    in_maps = []
    for c in range(NCORES):
        m = dict(shared)
        m["xp"] = inp["x_prompt"][c]
        m["xs"] = inp["x_sample"][c * SB:(c + 1) * SB].reshape(TS, D)
        m["mem"] = inp["mem_prompt"][c]
        m["spool"] = np.ascontiguousarray(inp["state_pool"][:, c * SB:(c + 1) * SB])
        m["sconv"] = np.ascontiguousarray(inp["state_conv"][:, c * SB:(c + 1) * SB])
        m["ck"] = np.ascontiguousarray(inp["cache_swa_k"][:, c * SB:(c + 1) * SB].reshape(DEPTH, SB, 128, 128))
        m["cv"] = np.ascontiguousarray(inp["cache_swa_v"][:, c * SB:(c + 1) * SB].reshape(DEPTH, SB, 128, 128))
        m["cmk"] = np.ascontiguousarray(inp["cache_mem_k"][:, c * SB:(c + 1) * SB].reshape(DEPTH, SB, MEMT, DG))
        m["cmv"] = np.ascontiguousarray(inp["cache_mem_v"][:, c * SB:(c + 1) * SB].reshape(DEPTH, SB, MEMT, DG))
        in_maps.append(m)
    res = run_bass_kernel_spmd(nc, in_maps, core_ids=list(range(NCORES)))
    R = res.results
    cat = lambda name, axis: np.concatenate([np.asarray(R[c][name], dtype=np.float32) for c in range(NCORES)], axis=axis)
    stk = lambda name: np.stack([np.asarray(R[c][name], dtype=np.float32) for c in range(NCORES)], axis=1)
    y_prompt = np.stack([np.asarray(R[c]["yp"], dtype=np.float32) for c in range(NCORES)], axis=0)
    y_sample = cat("ys", 0).reshape(NCORES * SB, ST, D)
    return (
        y_prompt,
        y_sample,
        stk("o_pool_p"),
        cat("o_pool_s", 1),
        stk("o_conv_p"),
        cat("o_conv_s", 1),
        stk("o_k_p").reshape(DEPTH, NCORES, 128, 2, 64),
        cat("o_k_s", 1).reshape(DEPTH, NCORES * SB, 128, 2, 64),
        stk("o_v_p").reshape(DEPTH, NCORES, 128, 2, 64),
        cat("o_v_s", 1).reshape(DEPTH, NCORES * SB, 128, 2, 64),
        stk("o_mk_p").reshape(DEPTH, NCORES, MEMT, 4, 128),
        stk("o_mv_p").reshape(DEPTH, NCORES, MEMT, 4, 128),
    )
```
